# Optimizing a Trainium2 kernel written in Bass

```python
import jax, jax.numpy as jnp
from jax import lax
import numpy as np

D_MODEL = 1024
BATCH = 4
SEQ = 8192
DEPTH = 1
DEC_BATCH = 128
DEC_SEQ = 8
PAST_LEN = 16384
PAGE_SIZE = 128

HEAD_DIM = 64
D_ATTN = D_MODEL // 2
D_RWKV = D_MODEL - D_ATTN
N_HEADS = D_ATTN // HEAD_DIM
N_KV_HEADS = 2
GQA_GROUP = N_HEADS // N_KV_HEADS
D_KV = N_KV_HEADS * HEAD_DIM
WINDOW = 128
BLOCK = 128
ROPE_THETA = 10000.0
RWKV_HEADS = D_RWKV // HEAD_DIM
D_DECAY_LORA = 32
D_A_LORA = 32
D_GATE_LORA = 96
D_SHIFT = 3 * D_RWKV + D_DECAY_LORA + D_A_LORA + D_GATE_LORA
D_IN = D_ATTN + 2 * D_KV + D_SHIFT
D_FF = 2816
CONV_W = 3
RMS_EPS = 1e-6
GN_EPS = 64e-5
NEG_INF = -1e30

kernel_name = "hymba_swa_sink_rwkv7_convffn_step"


def _rms_norm(x, g):
    xf = x.astype(jnp.float32)
    y = xf * lax.rsqrt(jnp.mean(xf * xf, axis=-1, keepdims=True) + RMS_EPS)
    return (y * g.astype(jnp.float32)).astype(x.dtype)


def _rope(x, pos):
    half = HEAD_DIM // 2
    inv = ROPE_THETA ** (-jnp.arange(half, dtype=jnp.float32) / half)
    ang = pos.astype(jnp.float32)[:, None] * inv[None, :]
    cos = jnp.cos(ang)[None, :, None, :]
    sin = jnp.sin(ang)[None, :, None, :]
    xf = x.astype(jnp.float32)
    x1, x2 = xf[..., :half], xf[..., half:]
    return jnp.concatenate([x1 * cos - x2 * sin, x2 * cos + x1 * sin], axis=-1).astype(x.dtype)


def _attend_with_sinks(q, k, v, mask, sinks):
    s = jnp.einsum('...qkgd,...skd->...kgqs', q, k,
                   preferred_element_type=jnp.float32) * (HEAD_DIM ** -0.5)
    s = jnp.where(mask[..., None, None, :, :], s, NEG_INF)
    sink = sinks.astype(jnp.float32)[:, :, None, None]
    m = jnp.maximum(jnp.max(s, axis=-1, keepdims=True), sink)
    p = jnp.exp(s - m)
    p = (p / (jnp.sum(p, axis=-1, keepdims=True) + jnp.exp(sink - m))).astype(v.dtype)
    return jnp.einsum('...kgqs,...skd->...qkgd', p, v)


def _attn_prompt(q, k, v, sinks):
    B, T = q.shape[:2]
    nb = T // BLOCK
    qb = q.reshape(B, nb, BLOCK, N_KV_HEADS, GQA_GROUP, HEAD_DIM)
    kb = k.reshape(B, nb, BLOCK, N_KV_HEADS, HEAD_DIM)
    vb = v.reshape(B, nb, BLOCK, N_KV_HEADS, HEAD_DIM)
    pad = ((0, 0), (1, 0), (0, 0), (0, 0), (0, 0))
    kcat = jnp.concatenate([jnp.pad(kb, pad)[:, :-1], kb], axis=2)
    vcat = jnp.concatenate([jnp.pad(vb, pad)[:, :-1], vb], axis=2)
    qi = jnp.arange(BLOCK)[:, None]
    sj = jnp.arange(2 * BLOCK)[None, :]
    dist = qi + BLOCK - sj
    band = (dist >= 0) & (dist <= WINDOW)
    valid = (jnp.arange(nb)[:, None, None] > 0) | (sj >= BLOCK)[None]
    mask = band[None] & valid
    sk = sinks.reshape(N_KV_HEADS, GQA_GROUP)
    o = _attend_with_sinks(qb, kcat, vcat, mask, sk)
    return o.reshape(B, T, D_ATTN)


def _attn_sample(q, k, v, ck, cv, sinks):
    Bd, L = q.shape[:2]
    kcat = jnp.concatenate([ck, k], axis=1)
    vcat = jnp.concatenate([cv, v], axis=1)
    qi = jnp.arange(L)[:, None]
    sj = jnp.arange(WINDOW + L)[None, :]
    dist = qi + WINDOW - sj
    mask = (dist >= 0) & (dist <= WINDOW)
    sk = sinks.reshape(N_KV_HEADS, GQA_GROUP)
    o = _attend_with_sinks(q.reshape(Bd, L, N_KV_HEADS, GQA_GROUP, HEAD_DIM), kcat, vcat, mask, sk)
    return o.reshape(Bd, L, D_ATTN), kcat[:, -WINDOW:], vcat[:, -WINDOW:]


def _rwkv7(h, shift_prev, s0, mu, w0, w_decay_up, a0, w_a_up, w_g_up, k_k, k_a, r_k, gn_w, gn_b):
    B, T, _ = h.shape
    f32 = jnp.float32
    hprev = jnp.concatenate([shift_prev[:, None, :], h[:, :-1]], axis=1)
    hs = h + (hprev - h) * mu
    cuts = [D_RWKV, 2 * D_RWKV, 3 * D_RWKV, 3 * D_RWKV + D_DECAY_LORA,
            3 * D_RWKV + D_DECAY_LORA + D_A_LORA]
    r, k, v, wd, ad, gd = jnp.split(hs, cuts, axis=-1)
    w = -jax.nn.softplus(-(w0 + jnp.tanh(wd) @ w_decay_up)) - 0.5
    decay = jnp.exp(-jnp.exp(w.astype(f32)))
    a = jax.nn.sigmoid(a0 + ad @ w_a_up)
    g = jax.nn.sigmoid(gd) @ w_g_up
    heads = lambda t: t.astype(f32).reshape(B, T, RWKV_HEADS, HEAD_DIM)
    kk = heads(k * k_k)
    kk = kk * lax.rsqrt(jnp.maximum(jnp.sum(kk * kk, axis=-1, keepdims=True), 1e-24))
    k = k * (1.0 + (a - 1.0) * k_a)
    r_h, k_h, v_h, a_h, w_h = heads(r), heads(k), heads(v), heads(a), heads(decay)
    xs = tuple(jnp.moveaxis(t, 1, 0) for t in (r_h, w_h, k_h, v_h, kk, a_h))

    def step(S, inp):
        r_t, w_t, k_t, v_t, kk_t, a_t = inp
        sa = jnp.einsum('bhij,bhj->bhi', S, kk_t)
        S = (S * w_t[:, :, None, :] - sa[..., None] * (kk_t * a_t)[:, :, None, :]
             + v_t[..., None] * k_t[:, :, None, :])
        return S, jnp.einsum('bhij,bhj->bhi', S, r_t)

    S_T, ys = lax.scan(step, s0.astype(f32), xs)
    y = jnp.moveaxis(ys, 0, 1)
    mean = jnp.mean(y, axis=-1, keepdims=True)
    var = jnp.mean(jnp.square(y - mean), axis=-1, keepdims=True)
    y = ((y - mean) * lax.rsqrt(var + GN_EPS)).reshape(B, T, D_RWKV) * gn_w + gn_b
    bonus = jnp.sum(r_h * k_h * r_k.astype(f32), axis=-1, keepdims=True) * v_h
    y = (y + bonus.reshape(B, T, D_RWKV)) * g
    return y.astype(h.dtype), h[:, -1], S_T.astype(s0.dtype)


def _conv_ffn(x, conv_prev, w_ffn_in, conv_w, conv_b, w_ffn_out):
    T = x.shape[1]
    gu = x @ w_ffn_in
    z, u = gu[..., :D_FF], gu[..., D_FF:]
    zp = jnp.concatenate([conv_prev, z], axis=1)
    zc = conv_b + conv_w[0] * zp[:, 0:T]
    for j in range(1, CONV_W):
        zc = zc + conv_w[j] * zp[:, j:j + T]
    hid = jax.nn.silu(zc) * u
    return hid @ w_ffn_out, zp[:, -(CONV_W - 1):]


def _layer(x, pos, ck, cv, shift_prev, s0, conv_prev,
           g_pre_mix, w_in, attn_sinks, mu_shift, w0, w_decay_up, a0, w_a_up, w_g_up,
           k_k, k_a, r_k, gn_w, gn_b, w_out, g_post_mix, g_pre_ffn, w_ffn_in, conv_w,
           conv_b, w_ffn_out, g_post_ffn):
    B, T, _ = x.shape
    hn = _rms_norm(x, g_pre_mix)
    proj = hn @ w_in
    q = proj[..., :D_ATTN].reshape(B, T, N_HEADS, HEAD_DIM)
    k = proj[..., D_ATTN:D_ATTN + D_KV].reshape(B, T, N_KV_HEADS, HEAD_DIM)
    v = proj[..., D_ATTN + D_KV:D_ATTN + 2 * D_KV].reshape(B, T, N_KV_HEADS, HEAD_DIM)
    h_rw = proj[..., D_ATTN + 2 * D_KV:]
    q = _rope(q, pos)
    k = _rope(k, pos)
    if ck is None:
        o_attn = _attn_prompt(q, k, v, attn_sinks)
        new_k, new_v = k[:, -WINDOW:], v[:, -WINDOW:]
    else:
        o_attn, new_k, new_v = _attn_sample(q, k, v, ck, cv, attn_sinks)
    o_rw, new_shift, new_S = _rwkv7(h_rw, shift_prev, s0, mu_shift, w0, w_decay_up, a0,
                                    w_a_up, w_g_up, k_k, k_a, r_k, gn_w, gn_b)
    mix = jnp.concatenate([o_attn, o_rw], axis=-1) @ w_out
    x = x + _rms_norm(mix, g_post_mix)
    f, new_conv = _conv_ffn(_rms_norm(x, g_pre_ffn), conv_prev, w_ffn_in, conv_w, conv_b, w_ffn_out)
    x = x + _rms_norm(f, g_post_ffn)
    return x, (new_k, new_v, new_shift, new_S, new_conv)


def setup_inputs(seed: int = 0) -> dict:
    key = jax.random.key(seed)
    ks = jax.random.split(key, 32)
    nrm = lambda i, shape, s: jax.random.normal(ks[i], shape, jnp.float32) * s
    unif = lambda i, shape, lo, hi: jax.random.uniform(ks[i], shape, jnp.float32, lo, hi)
    L = DEPTH
    return {
        "x_prompt": nrm(0, (BATCH, SEQ, D_MODEL), 1.0),
        "x_sample": nrm(1, (DEC_BATCH, DEC_SEQ, D_MODEL), 1.0),
        "cache_k_win": nrm(2, (L, DEC_BATCH, WINDOW, N_KV_HEADS, HEAD_DIM), 1.0),
        "cache_v_win": nrm(3, (L, DEC_BATCH, WINDOW, N_KV_HEADS, HEAD_DIM), 1.0),
        "state_shift": nrm(4, (L, DEC_BATCH, D_SHIFT), 1.0),
        "state_wkv": nrm(5, (L, DEC_BATCH, RWKV_HEADS, HEAD_DIM, HEAD_DIM), 0.3),
        "state_conv": nrm(6, (L, DEC_BATCH, CONV_W - 1, D_FF), 1.0),
        "g_pre_mix": 1.0 + nrm(7, (L, D_MODEL), 0.05),
        "w_in": nrm(8, (L, D_MODEL, D_IN), D_MODEL ** -0.5),
        "attn_sinks": nrm(9, (L, N_HEADS), 0.5),
        "mu_shift": unif(10, (L, D_SHIFT), 0.0, 1.0),
        "w0": unif(11, (L, D_RWKV), -1.5, 1.5),
        "w_decay_up": nrm(12, (L, D_DECAY_LORA, D_RWKV), 0.1),
        "a0": nrm(13, (L, D_RWKV), 0.1),
        "w_a_up": nrm(14, (L, D_A_LORA, D_RWKV), 0.1),
        "w_g_up": nrm(15, (L, D_GATE_LORA, D_RWKV), D_GATE_LORA ** -0.5),
        "k_k": 0.85 + nrm(16, (L, D_RWKV), 0.05),
        "k_a": 1.0 + nrm(17, (L, D_RWKV), 0.05),
        "r_k": nrm(18, (L, RWKV_HEADS, HEAD_DIM), 0.1),
        "gn_w": 1.0 + nrm(19, (L, D_RWKV), 0.05),
        "gn_b": nrm(20, (L, D_RWKV), 0.01),
        "w_out": nrm(21, (L, D_MODEL, D_MODEL), D_MODEL ** -0.5),
        "g_post_mix": 1.0 + nrm(22, (L, D_MODEL), 0.05),
        "g_pre_ffn": 1.0 + nrm(23, (L, D_MODEL), 0.05),
        "w_ffn_in": nrm(24, (L, D_MODEL, 2 * D_FF), D_MODEL ** -0.5),
        "conv_w": nrm(25, (L, CONV_W, D_FF), CONV_W ** -0.5),
        "conv_b": nrm(26, (L, D_FF), 0.01),
        "w_ffn_out": nrm(27, (L, D_FF, D_MODEL), D_FF ** -0.5),
        "g_post_ffn": 1.0 + nrm(28, (L, D_MODEL), 0.05),
    }


def reference(x_prompt, x_sample, cache_k_win, cache_v_win, state_shift, state_wkv, state_conv,
              g_pre_mix, w_in, attn_sinks, mu_shift, w0, w_decay_up, a0, w_a_up, w_g_up,
              k_k, k_a, r_k, gn_w, gn_b, w_out, g_post_mix, g_pre_ffn, w_ffn_in, conv_w,
              conv_b, w_ffn_out, g_post_ffn):
    B, T = x_prompt.shape[:2]
    pos_p = jnp.arange(T, dtype=jnp.int32)
    pos_s = PAST_LEN + jnp.arange(x_sample.shape[1], dtype=jnp.int32)
    xp, xs = x_prompt, x_sample
    st_p, st_s = [], []
    for l in range(DEPTH):
        lw = (g_pre_mix[l], w_in[l], attn_sinks[l], mu_shift[l], w0[l], w_decay_up[l], a0[l],
              w_a_up[l], w_g_up[l], k_k[l], k_a[l], r_k[l], gn_w[l], gn_b[l], w_out[l],
              g_post_mix[l], g_pre_ffn[l], w_ffn_in[l], conv_w[l], conv_b[l], w_ffn_out[l],
              g_post_ffn[l])
        zero_shift = jnp.zeros((B, D_SHIFT), xp.dtype)
        zero_wkv = jnp.zeros((B, RWKV_HEADS, HEAD_DIM, HEAD_DIM), xp.dtype)
        zero_conv = jnp.zeros((B, CONV_W - 1, D_FF), xp.dtype)
        xp, sp = _layer(xp, pos_p, None, None, zero_shift, zero_wkv, zero_conv, *lw)
        xs, ss = _layer(xs, pos_s, cache_k_win[l], cache_v_win[l], state_shift[l],
                        state_wkv[l], state_conv[l], *lw)
        st_p.append(sp)
        st_s.append(ss)
    stk = lambda outs, i: jnp.stack([o[i] for o in outs], axis=0)
    return (xp, xs,
            stk(st_p, 0), stk(st_p, 1), stk(st_p, 2), stk(st_p, 3), stk(st_p, 4),
            stk(st_s, 0), stk(st_s, 1), stk(st_s, 2), stk(st_s, 3), stk(st_s, 4))
```

```python
import os
import contextlib
import numpy as np
import concourse.bass as bass
import concourse.mybir as mybir
from concourse.bass_utils import run_bass_kernel_spmd

F32 = mybir.dt.float32
BF16 = mybir.dt.bfloat16
ALU = mybir.AluOpType
AF = mybir.ActivationFunctionType
AX = mybir.AxisListType

ENGS = ("pe", "act", "dve", "pool", "sp")


class Op:
    __slots__ = ("eng", "fn", "deps", "dma", "signal", "sigval", "waits", "has_dependents")

    def __init__(self, eng, fn, dma):
        self.eng = eng
        self.fn = fn
        self.dma = dma
        self.deps = set()
        self.signal = False
        self.sigval = 0
        self.waits = []
        self.has_dependents = False


class Prog:
    def __init__(self, nc, same_engine_sync=True):
        self.nc = nc
        self.ops = {e: [] for e in ENGS}
        self.last_w = {}
        self.readers = {}
        self.same_engine_sync = same_engine_sync
        self.nops = 0

    def add(self, eng, fn, reads=(), writes=(), dma=None):
        op = Op(eng, fn, dma)
        deps = op.deps
        for k in reads:
            w = self.last_w.get(k)
            if w is not None:
                deps.add(w)
            if isinstance(k, tuple) and k[0] == "ps":
                for r in self.readers.get(k, ()):
                    if r.eng != eng:
                        deps.add(r)
        for k in writes:
            w = self.last_w.get(k)
            if w is not None:
                deps.add(w)
            for r in self.readers.get(k, ()):
                deps.add(r)
        deps.discard(op)
        for k in reads:
            lst = self.readers.setdefault(k, [])
            if dma is None:
                for i_, r_ in enumerate(lst):
                    if r_.dma is None and r_.eng == eng:
                        lst[i_] = op
                        break
                else:
                    lst.append(op)
            else:
                lst.append(op)
        for k in writes:
            self.last_w[k] = op
            self.readers[k] = []
        self.ops[eng].append(op)
        self.nops += 1
        return op

    def pe(self, fn, r=(), w=()):
        return self.add("pe", fn, r, w)

    def act(self, fn, r=(), w=()):
        return self.add("act", fn, r, w)

    def dve(self, fn, r=(), w=()):
        return self.add("dve", fn, r, w)

    def pool(self, fn, r=(), w=()):
        return self.add("pool", fn, r, w)

    def dma(self, q, sem, out, in_, r=(), w=(), **kw):
        return self.add(q, lambda e: e.dma_start(out=out, in_=in_, **kw), r, w, dma=sem)

    def _skip(self, d, op):
        return d.dma is None and d.eng == op.eng and (op.eng == "pe" or not self.same_engine_sync)

    def finalize(self):
        for e in ENGS:
            for op in self.ops[e]:
                for d in op.deps:
                    if not self._skip(d, op):
                        d.has_dependents = True
        eng_cnt = {e: 0 for e in ENGS}
        dma_cnt = {}
        for e in ENGS:
            for op in self.ops[e]:
                if op.dma is not None:
                    dma_cnt[op.dma] = dma_cnt.get(op.dma, 0) + 16
                    op.sigval = dma_cnt[op.dma]
                    op.signal = True
                elif op.has_dependents:
                    eng_cnt[e] += 1
                    op.sigval = eng_cnt[e]
                    op.signal = True
        for e in ENGS:
            for op in self.ops[e]:
                ws = {}
                for d in op.deps:
                    if self._skip(d, op):
                        continue
                    key = ("dma", d.dma) if d.dma is not None else ("eng", d.eng)
                    if ws.get(key, 0) < d.sigval:
                        ws[key] = d.sigval
                op.waits = list(ws.items())
        self.dma_names = sorted(dma_cnt.keys())

    def emit(self):
        nc = self.nc
        self.finalize()
        with contextlib.ExitStack() as st:
            sems = {}
            for e in ENGS:
                sems[("eng", e)] = st.enter_context(nc.semaphore("s_" + e))
            for n in self.dma_names:
                sems[("dma", n)] = st.enter_context(nc.semaphore("d_" + n))
            block = st.enter_context(nc.Block())

            def replay(eobj, ename):
                known = {}
                for op in self.ops[ename]:
                    for key, val in op.waits:
                        if known.get(key, 0) < val:
                            eobj.wait_ge(sems[key], val)
                            known[key] = val
                    ins = op.fn(eobj)
                    if op.signal and ins is not None:
                        if op.dma is not None:
                            ins.then_inc(sems[("dma", op.dma)], 16)
                        else:
                            ins.then_inc(sems[("eng", ename)], 1)

            @block.tensor
            def _(e):
                replay(e, "pe")

            @block.scalar
            def _(e):
                replay(e, "act")

            @block.vector
            def _(e):
                replay(e, "dve")

            @block.gpsimd
            def _(e):
                replay(e, "pool")

            @block.sync
            def _(e):
                replay(e, "sp")


D = 1024
DIN = 2464
DFF = 2816
NFC = 22
DSH = 1696
SEQ = 8192
NSEQ_S = 16
LS = 8
KAPPA = float(np.exp(-0.5))
RMS_EPS = 1e-6
GN_EPS = 64e-5
RING = 5
N_TILES = int(os.environ.get("MK_NTILES", "16"))
DO_SAMPLE = int(os.environ.get("MK_SAMPLE", "1"))
STOP = int(os.environ.get("MK_STOP", "99"))
HPS = int(os.environ.get("MK_HPS", "4"))
VAR = int(os.environ.get("MK_VAR", "0"))

RW0 = 768
WIN_CHUNKS = {}
for c in range(4):
    WIN_CHUNKS[("q", c)] = [(64 * c, 64, 0), (256 + 64 * c, 64, 64)]
    WIN_CHUNKS[("qs", c)] = [(64 * c + 32, 32, 0), (64 * c, 32, 32), (256 + 64 * c + 32, 32, 64), (256 + 64 * c, 32, 96)]
    WIN_CHUNKS[("r", c)] = [(RW0 + 128 * c, 128, 0)]
    WIN_CHUNKS[("k", c)] = [(RW0 + 512 + 128 * c, 128, 0)]
    WIN_CHUNKS[("v", c)] = [(RW0 + 1024 + 128 * c, 128, 0)]
WIN_CHUNKS[("ak", 0)] = [(512, 128, 0)]
WIN_CHUNKS[("aks", 0)] = [(544, 32, 0), (512, 32, 32), (608, 32, 64), (576, 32, 96)]
WIN_CHUNKS[("av", 0)] = [(640, 128, 0)]
WIN_CHUNKS[("lo", 0)] = [(RW0 + 1536, 64, 0)]
WIN_CHUNKS[("lg", 0)] = [(RW0 + 1600, 96, 0)]
WIN_ORDER = [("lo", 0), ("lg", 0)]
for c in range(4):
    WIN_ORDER += [("r", c), ("k", c), ("v", c)]
WIN_ORDER += [("ak", 0), ("aks", 0), ("av", 0)]
for c in range(4):
    WIN_ORDER += [("q", c), ("qs", c)]
WIN_IDX = {k: i for i, k in enumerate(WIN_ORDER)}
NWIN = len(WIN_ORDER)
RC = {}
for c in range(4):
    RC[("r", c)] = c
    RC[("k", c)] = 4 + c
    RC[("v", c)] = 8 + c
RC[("lo", 0)] = 12
RC[("lg", 0)] = 13
RC_NP = {12: 64, 13: 96}


def build_program():
    nc = bass.Bass("TRN2", target_bir_lowering=False)
    P = Prog(nc)
    st = contextlib.ExitStack()

    def din(name, shape, dt=F32):
        return nc.dram_tensor(name, list(shape), dt, kind="ExternalInput").ap()

    def dout(name, shape, dt=F32):
        return nc.dram_tensor(name, list(shape), dt, kind="ExternalOutput").ap()

    def dint(name, shape, dt=BF16):
        return nc.dram_tensor(name, list(shape), dt, kind="Internal").ap()

    def T(name, shape, dt=F32):
        return st.enter_context(nc.sbuf_tensor(name, list(shape), dt))

    xp = din("xp", [SEQ, D])
    xs = din("xs", [128, D])
    ck = din("ck", [NSEQ_S, 128, 128])
    cv = din("cv", [NSEQ_S, 128, 128])
    sshift = din("sshift", [NSEQ_S, DSH])
    swkv = din("swkv", [NSEQ_S, 8, 64, 64])
    sconv = din("sconv", [NSEQ_S * 2, DFF])
    g_pre_mix = din("g_pre_mix", [D])
    w_in = din("w_in", [D, DIN])
    attn_sinks = din("attn_sinks", [8])
    mu_shift = din("mu_shift", [DSH])
    w0 = din("w0", [512])
    w_decay_up = din("w_decay_up", [32, 512])
    a0 = din("a0", [512])
    w_a_up = din("w_a_up", [32, 512])
    w_g_up = din("w_g_up", [96, 512])
    k_k = din("k_k", [512])
    k_a = din("k_a", [512])
    r_k = din("r_k", [512])
    gn_w = din("gn_w", [512])
    gn_b = din("gn_b", [512])
    w_out = din("w_out", [D, D])
    g_post_mix = din("g_post_mix", [D])
    g_pre_ffn = din("g_pre_ffn", [D])
    w_ffn_in = din("w_ffn_in", [D, 2 * DFF])
    conv_w = din("conv_w", [3, DFF])
    conv_b = din("conv_b", [DFF])
    w_ffn_out = din("w_ffn_out", [DFF, D])
    g_post_ffn = din("g_post_ffn", [D])
    c_ident = din("c_ident", [128, 128])
    c_cos_p = din("c_cos_p", [128, SEQ])
    c_sin_p = din("c_sin_p", [128, SEQ])
    c_cos_s = din("c_cos_s", [128, 128])
    c_sin_s = din("c_sin_s", [128, 128])
    c_mask_own = din("c_mask_own", [128, 128])
    c_mask_prev = din("c_mask_prev", [128, 128])
    c_negmask_own = din("c_negmask_own", [128, 128])
    c_negmask_prev = din("c_negmask_prev", [128, 128])
    c_mask_sown = din("c_mask_sown", [128, 128])
    c_mask_scache = din("c_mask_scache", [128, NSEQ_S, 128])
    c_blockones = din("c_blockones", [128, 128])
    c_m2b = din("c_m2b", [128, 2, 128])
    c_m2k = din("c_m2k", [128, 2, 128])
    c_mT = din("c_mT", [128, 128])
    c_scanm_p = din("c_scanm_p", [128, 512])
    c_scanm_s = din("c_scanm_s", [128, 128])

    yp = dout("yp", [SEQ, D])
    ys = dout("ys", [128, D])
    kwp = dout("kwp", [128, 128])
    vwp = dout("vwp", [128, 128])
    shp = dout("shp", [1, DSH])
    wkvp = dout("wkvp", [8, 64, 64])
    convp = dout("convp", [2, DFF])
    kws = dout("kws", [NSEQ_S, 128, 128])
    vws = dout("vws", [NSEQ_S, 128, 128])
    shs = dout("shs", [NSEQ_S, DSH])
    wkvs = dout("wkvs", [NSEQ_S, 8, 64, 64])
    convs = dout("convs", [NSEQ_S * 2, DFF])

    wsc_in = dint("wsc_in", [NWIN, 128, 1024])
    wsc_out = dint("wsc_out", [8, 128, 1024])
    wsc_fin = dint("wsc_fin", [2 * NFC, 128, 1024])
    wsc_fout = dint("wsc_fout", [NFC, 128, 1024])

    pp = [st.enter_context(nc.psum_tensor("pp%d" % i, [128, 1024], F32)) for i in range(4)]
    nb = [0]

    def bank():
        b = nb[0] % 8
        nb[0] += 1
        return b

    def pair():
        if nb[0] % 2:
            nb[0] += 1
        b = nb[0] % 8
        nb[0] += 2
        return b

    def PS(b, n=512, np_=128, p0=0):
        o = (b % 2) * 512
        return pp[b // 2][p0:p0 + np_, o:o + n]

    def PSP(b):
        return pp[b // 2][:, :]

    def PSB(b, np_=128):
        return pp[b // 2].bitcast(BF16)[0:np_, (b % 2) * 1024:(b % 2) * 1024 + 1024]

    def kb(b):
        return ("ps", b)

    idf = T("idf", [128, 128])
    idb = T("idb", [128, 128], BF16)
    gTpre = T("gTpre", [128, 8])
    gTffn = T("gTffn", [128, 8])
    gpost_bc = T("gpost_bc", [128, D])
    gpostf_bc = T("gpostf_bc", [128, D])
    gnw_bc = T("gnw_bc", [128, 512])
    gnb_bc = T("gnb_bc", [128, 512])
    esink = T("esink", [128, 8])
    w0T = T("w0T", [128, 4])
    a0T = T("a0T", [128, 4])
    kkT = T("kkT", [128, 4])
    kaT = T("kaT", [128, 4])
    rkT = T("rkT", [128, 4])
    muT = T("muT", [128, 14])
    ommT = T("ommT", [128, 14])
    cwT = T("cwT", [128, 3, NFC])
    cbT = T("cbT", [128, NFC])
    Wd_b = T("Wd_b", [32, 512], BF16)
    Wa_b = T("Wa_b", [64, 512], BF16)
    Wg_b = T("Wg_b", [96, 512], BF16)
    blk1 = T("blk1", [128, 128], BF16)
    m_own = T("m_own", [128, 4, 128], BF16)
    m_prev = T("m_prev", [128, 4, 128], BF16)
    m2b = T("m2b", [128, 2, 128], BF16)
    m2k = T("m2k", [128, 2, 128], BF16)
    mTl = T("mTl", [128, 128], BF16)
    scanm_p = T("scanm_p", [128, 512])
    cstage = T("cstage", [128, 512])

    ld_n = [0]

    def cload(dst, src, xw=(), **kw):
        i = ld_n[0]
        ld_n[0] += 1
        k = ("c", i)
        P.dma("sp", "cl%d" % i, dst, src, w=[k] + list(xw), **kw)
        return k

    def cload_cast(dst_bf, src, np_, shape_free, eng="dve"):
        nfree = int(np.prod(shape_free))
        stg = cstage[0:np_, 0:nfree]
        k = ("c", ld_n[0])
        ld_n[0] += 1
        P.dma("sp", "cst", stg, src, w=["cstage"])
        P.dve(lambda e: e.tensor_copy(out=dst_bf, in_=stg), r=["cstage"], w=[k, "cstage_rd"])
        return k

    NSC = ALLOW = dict(allow_slow_non_contiguous=True)
    CK = []
    CK.append(cload(idf[:], c_ident))
    P.dve(lambda e: e.tensor_copy(out=idb[:], in_=idf[:]), r=[CK[-1]], w=["idb"])
    CK.append(cload(gTpre[:], g_pre_mix.rearrange("(c p) -> p c", p=128), **NSC))
    kgpre = CK[-1]
    CK.append(cload(gTffn[:], g_pre_ffn.rearrange("(c p) -> p c", p=128), **NSC))
    kgffn = CK[-1]
    CK.append(cload(gpost_bc[:], g_post_mix.partition_broadcast(128)))
    CK.append(cload(gpostf_bc[:], g_post_ffn.partition_broadcast(128)))
    CK.append(cload(gnw_bc[:], gn_w.partition_broadcast(128)))
    CK.append(cload(gnb_bc[:], gn_b.partition_broadcast(128)))
    CK.append(cload(esink[:], attn_sinks.partition_broadcast(128)))
    P.act(lambda e: e.activation(out=esink[:], in_=esink[:], func=AF.Exp), r=[CK[-1]], w=["esink"])
    for tt, src in ((w0T, w0), (a0T, a0), (kkT, k_k), (kaT, k_a), (rkT, r_k)):
        CK.append(cload(tt[:], src.rearrange("(c p) -> p c", p=128), **NSC))
    P.pool(lambda e: e.memset(muT[:], 0.0), w=["muT"])
    CK.append(cload(muT[:, 0:12], mu_shift[0:1536].rearrange("(c p) -> p c", p=128), xw=["muT"], **NSC))
    CK.append(cload(muT[0:64, 12:13], mu_shift[1536:1600].rearrange("(p c) -> p c", c=1), xw=["muT"], **NSC))
    CK.append(cload(muT[0:96, 13:14], mu_shift[1600:1696].rearrange("(p c) -> p c", c=1), xw=["muT"], **NSC))
    P.dve(lambda e: e.tensor_scalar(out=ommT[:], in0=muT[:], scalar1=-1.0, scalar2=1.0, op0=ALU.mult, op1=ALU.add),
          r=["muT"], w=["muT2"])
    CK.append(cload(cwT[:], conv_w.rearrange("j (c p) -> p j c", p=128), **NSC))
    CK.append(cload(cbT[:], conv_b.rearrange("(c p) -> p c", p=128), **NSC))
    CK.append(cload(scanm_p[:], c_scanm_p))

    def cast_const(dst, src, np_, nfree, bcast4=False):
        stg = cstage[0:np_, 0:nfree]
        P.dma("sp", "cst", stg, src, w=["cstage"])
        if bcast4:
            P.dve(lambda e: e.tensor_copy(out=dst, in_=stg.unsqueeze(1).to_broadcast([np_, 4, nfree])),
                  r=["cstage"], w=["cconst", "cstage"])
        else:
            P.dve(lambda e: e.tensor_copy(out=dst, in_=stg), r=["cstage"], w=["cconst", "cstage"])

    cast_const(blk1[:], c_blockones, 128, 128)
    cast_const(m_own[:], c_negmask_own, 128, 128, bcast4=True)
    cast_const(m_prev[:], c_negmask_prev, 128, 128, bcast4=True)
    cast_const(m2b[:].rearrange("p a b -> p (a b)"), c_m2b.rearrange("p a b -> p (a b)"), 128, 256)
    cast_const(m2k[:].rearrange("p a b -> p (a b)"), c_m2k.rearrange("p a b -> p (a b)"), 128, 256)
    cast_const(mTl[:], c_mT, 128, 128)
    cast_const(Wd_b[:], w_decay_up, 32, 512)
    P.dma("sp", "cst", cstage[32:64, 0:512], w_a_up, w=["cstage"])
    P.dve(lambda e: e.tensor_copy(out=Wa_b[32:64, :], in_=cstage[32:64, 0:512]), r=["cstage"], w=["cconst", "cstage"])
    cast_const(Wg_b[:], w_g_up, 96, 512)
    CONST = CK + ["idb", "esink", "cconst", "muT", "muT2"]

    x_t = T("x_t", [128, 4, D])
    bufA = T("bufA", [128, 8, 512], BF16)
    BUFA_ALL = [("bufA", b) for b in range(4)]
    wp_n = [0]

    def prep_chunk(dst_dram, dkey, loads, gT=None, gkey=None):
        i = wp_n[0] % 4
        wp_n[0] += 1
        stage = x_t[:, i, :]
        wbs_i = bufA[:, 2 * i:2 * i + 2, :].rearrange("p a b -> p (a b)")
        for mk, src in loads:
            P.dma("sp", "wl%d" % i, mk(stage), src, w=[("x", i)])
        if gT is not None:
            P.dve(lambda e: e.tensor_tensor(out=wbs_i.rearrange("p (c n) -> p c n", c=8),
                                            in0=stage.rearrange("p (c n) -> p c n", c=8),
                                            in1=gT[:].unsqueeze(2).to_broadcast([128, 8, 128]), op=ALU.mult),
                  r=[("x", i), gkey], w=[("wbs", i)])
        else:
            P.dve(lambda e: e.tensor_copy(out=wbs_i, in_=stage), r=[("x", i)], w=[("wbs", i)])
        P.dma("pool", "ws%d" % i, dst_dram, wbs_i, r=[("wbs", i)], w=[dkey])

    P.pool(lambda e: e.memset(x_t[:, :, :], 0.0), w=[("x", b_) for b_ in range(4)])
    win3 = w_in.rearrange("(c p) n -> p c n", p=128)
    for ci, key in enumerate(WIN_ORDER):
        loads = []
        for (cs, n, o) in WIN_CHUNKS[key]:
            loads.append((lambda s, o=o, n=n: s.rearrange("p (c n) -> p c n", c=8)[:, :, o:o + n], win3[:, :, cs:cs + n]))
        if key[0] in ("lo", "lg"):
            pass
        prep_chunk(wsc_in[ci], ("wsc_in", ci), loads, gTpre, kgpre)
    wfi3 = w_ffn_in.rearrange("(c p) n -> p c n", p=128)
    for fc in range(NFC):
        for zu in range(2):
            cs = zu * DFF + fc * 128
            prep_chunk(wsc_fin[2 * fc + zu], ("wsc_fin", 2 * fc + zu),
                       [(lambda s: s.rearrange("p (c n) -> p c n", c=8), wfi3[:, :, cs:cs + 128])], gTffn, kgffn)
    for c in range(8):
        prep_chunk(wsc_out[c], ("wsc_out", c), [(lambda s: s, w_out[c * 128:(c + 1) * 128, :])])
    for c in range(NFC):
        prep_chunk(wsc_fout[c], ("wsc_fout", c), [(lambda s: s, w_ffn_out[c * 128:(c + 1) * 128, :])])

    P.dve(lambda e: e.memset(bufA[0:1, 0, 0:2], 0.0), r=[("wbs", i_) for i_ in range(4)], w=BUFA_ALL + [("wbs", i_) for i_ in range(4)])
    ring = T("ring", [128, RING, 1024], BF16)
    xn = T("xn", [128, D], BF16)
    junk = T("junk", [128, D], BF16)
    cs_t = T("cs_t", [128, 2, 512])
    qT = T("qT", [128, 4, 512], BF16)
    kbuf = T("kbuf", [128, 5 * 128], BF16)
    vtokf = T("vtokf", [128, 128])
    Vaug = T("Vaug", [128, 5, 2, 65], BF16)
    Et = T("Et", [128, 4, 512], BF16)
    oacc = T("oacc", [128, 2, 4, 65])
    den = T("den", [128, 8])
    mix = T("mix", [128, 4, D], BF16)
    hidT = T("hidT", [128, NFC, 512], BF16)
    zbuf = T("zbuf", [128, 1, 640])
    za = T("za", [128, 1, 512])
    zcar_p = T("zcar_p", [128, NFC, 1, 2])
    zcar_s = T("zcar_s", [128, NFC, NSEQ_S, 2])
    sstat = T("sstat", [128, 16])
    hbuf = T("hbuf", [128, 1, 640])
    hcar_p = T("hcar_p", [128, 14, 1])
    hcar_s = T("hcar_s", [128, 14, NSEQ_S])
    dtmp = T("dtmp", [128, 512])
    hs_lo = T("hs_lo", [64, 512])
    hs_lg = T("hs_lg", [96, 512])
    lorab = T("lorab", [64, 512], BF16)
    sgb = T("sgb", [96, 512], BF16)
    rkv = T("rkv", [128, 1, 3, 512])
    tq = [T("tq%d" % i, [128, 512]) for i in range(7)]
    yq, yq2, kf32, vT32 = tq[0], tq[1], tq[3], tq[4]
    vblk_s = T("vblk_s", [128, 512], BF16)
    Sld = tq[5][0:64, 0:256].rearrange("p (a b) -> p a b", a=2)
    sqb = T("sqb", [128, 512], BF16)
    Kt = T("Kt", [128, 512], BF16)
    Bt = T("Bt", [128, 512], BF16)
    KR = T("KR", [128, 1024], BF16)
    KgL = T("KgL", [128, 512], BF16)
    BgLn = T("BgLn", [128, 512], BF16)
    vb = T("vb", [128, 512], BF16)
    prodb = T("prodb", [128, 512], BF16)
    gL2 = T("gL2", [128, 2, 16])
    nkcl = T("nkcl", [128, 16])
    vtok = T("vtok", [128, 2048], BF16)
    KgLtok = T("KgLtok", [128, 512], BF16)
    BgLtok = T("BgLtok", [128, 512], BF16)
    NQ2 = T("NQ2", [128, 2048], BF16)
    KQ2 = T("KQ2", [128, 2048], BF16)
    NT2 = T("NT2", [128, 1024], BF16)
    T32 = T("T32", [128, 1024])
    Tb = T("Tb", [128, 1024], BF16)
    XTs = T("XTs", [128, 128], BF16)
    SATs = T("SATs", [128, 128], BF16)
    ytok = T("ytok", [128, 4, 512])
    rkb = T("rkb", [128, 4, 8])
    Sm = T("Sm", [128, 4, 64])
    Sb = T("Sb", [128, 4, 64], BF16)
    gstat = T("gstat", [128, 4, 8])
    hid32 = hidT[:].rearrange("p a b -> p (a b)").bitcast(F32)
    HID_ALL = [("hidT", fc) for fc in range(NFC)]
    rowbuf = hid32[0:32, 0:DSH]
    sst = hid32[0:32, 1792:1792 + DFF]
    scanm_s = T("scanm_s", [128, 128])
    m_sown = T("m_sown", [128, 4, 128], BF16)
    m_scache = T("m_scache", [128, NSEQ_S, 128], BF16)
    ckT = T("ckT", [128, 128], BF16)
    ckf = T("ckf", [128, 2, 128])
    cVaug = T("cVaug", [128, 2, 2, 65], BF16)
    ytok_s = hid32[0:8, 0:NSEQ_S * 128].rearrange("p (s n) -> p s n", s=NSEQ_S)

    state = dict(ring_n=0, xld=0, ost=0)

    def stream(src, skey):
        n = state["ring_n"]
        state["ring_n"] += 1
        s = n % RING
        P.dma("sp", "rg%d" % s, ring[:, s, :], src, r=[skey], w=[("ring", s)])
        return s

    def emit_tile(kind, ti):
        NT = 512 if kind == "p" else 128
        NB = NT // 128
        nseq, Lseq = (1, 512) if kind == "p" else (NSEQ_S, LS)
        L = 128 if kind == "p" else LS
        nch = NT // L
        t0 = ti * 512
        first = (kind == "p" and ti == 0)
        last = (kind == "s") or (ti == N_TILES - 1)
        xsrc = xp if kind == "p" else xs
        ydst = yp if kind == "p" else ys
        hcar = hcar_p if kind == "p" else hcar_s
        zcar = zcar_p if kind == "p" else zcar_s
        scanm = scanm_p if kind == "p" else scanm_s
        S_ = slice(0, NT)

        if kind == "p":
            P.dma("pool", "cs", cs_t[:, 0, S_], c_cos_p[:, t0:t0 + NT], w=["cs_t"])
            P.dma("pool", "cs", cs_t[:, 1, S_], c_sin_p[:, t0:t0 + NT], w=["cs_t"])
        else:
            P.dma("pool", "cs", cs_t[:, 0, S_], c_cos_s, w=["cs_t"])
            P.dma("pool", "cs", cs_t[:, 1, S_], c_sin_s, w=["cs_t"])

        for b in range(NB):
            P.dma("sp", "xl%d" % b, x_t[:, b, :], xsrc[t0 + b * 128:t0 + (b + 1) * 128, :] if kind == "p" else xsrc,
                  w=[("x", b)])
            sc = sstat[:, b:b + 1]
            P.act(lambda e, b=b, sc=sc: e.activation(out=junk[:], in_=x_t[:, b, :], func=AF.Square, accum_out=sc),
                  r=[("x", b)], w=["junk", ("ss", b)])
            P.act(lambda e, sc=sc: e.activation(out=sc, in_=sc, func=AF.Sqrt, scale=1.0 / D, bias=RMS_EPS),
                  r=[("ss", b)], w=[("ss", b)])
            P.dve(lambda e, sc=sc: e.reciprocal(out=sc, in_=sc), r=[("ss", b)], w=[("ss", b)])
        for b in range(NB):
            sc = sstat[:, b:b + 1]
            P.dve(lambda e, b=b, sc=sc: e.tensor_scalar(out=xn[:], in0=x_t[:, b, :], scalar1=sc, scalar2=None, op0=ALU.mult),
                  r=[("x", b), ("ss", b)], w=["xn"])
            bk = bank()
            for c in range(8):
                P.pe(lambda e, c=c, bk=bk: e.transpose(out=PSB(bk)[:, c * 128:(c + 1) * 128], in_=xn[:, c * 128:(c + 1) * 128],
                                                      identity=idb[:]), r=["xn", "idb"], w=[kb(bk)])
            P.act(lambda e, b=b, bk=bk: e.copy(out=bufA[:, :, b * 128:(b + 1) * 128],
                                               in_=PSB(bk).rearrange("p (c n) -> p c n", c=8)),
                  r=[kb(bk)], w=[("bufA", b)])
        bufA_keys = [("bufA", b) for b in range(NB)]

        if kind == "s" and STOP <= 1:
            return
        def inproj(key, ncols):
            ci = WIN_IDX[key]
            s = stream(wsc_in[ci], ("wsc_in", ci))
            bk = bank()
            for c in range(8):
                P.pe(lambda e, c=c, s=s, bk=bk: e.matmul(PS(bk, NT, ncols), lhsT=ring[:, s, c * 128:c * 128 + ncols],
                                                         rhs=bufA[:, c, S_], start=(c == 0), stop=(c == 7)),
                     r=[("ring", s)] + bufA_keys, w=[kb(bk)])
            return bk

        hb_n = [0]

        def shift_evac(bk, np_, rc, out_ap, okey, xw=()):
            i = 0
            hb = hbuf[0:np_, i, 0:nseq * (Lseq + 1)].rearrange("p (s l) -> p s l", s=nseq)
            hk = ("hbuf", i)
            P.act(lambda e: e.copy(out=hb[:, :, 1:Lseq + 1], in_=PS(bk, NT, np_).rearrange("p (s l) -> p s l", s=nseq)),
                  r=[kb(bk)], w=[hk])
            if VAR != 1:
                P.pool(lambda e: e.tensor_copy(out=hb[:, :, 0:1], in_=hcar[0:np_, rc, :].unsqueeze(2)),
                       r=[("hcar", rc)], w=[hk])
                P.pool(lambda e: e.tensor_copy(out=hcar[0:np_, rc, :].unsqueeze(2), in_=hb[:, :, Lseq:Lseq + 1]),
                       r=[hk], w=[("hcar", rc)])
            if VAR == 2:
                return
            d3 = dtmp[0:np_, S_].rearrange("p (s l) -> p s l", s=nseq)
            P.act(lambda e: e.activation(out=d3, in_=hb[:, :, 0:Lseq], func=AF.Copy, scale=muT[0:np_, rc:rc + 1]),
                  r=[hk, "muT"], w=["dtmp"])
            P.dve(lambda e: e.scalar_tensor_tensor(out=out_ap.rearrange("p (s l) -> p s l", s=nseq), in0=hb[:, :, 1:Lseq + 1],
                                                   scalar=ommT[0:np_, rc:rc + 1], in1=d3,
                                                   op0=ALU.mult, op1=ALU.add),
                  r=["dtmp", hk, "muT", "muT2"], w=[okey] + list(xw))

        bk = inproj(("lo", 0), 64)
        shift_evac(bk, 64, 12, hs_lo[:, S_], "hs_lo")
        P.act(lambda e: e.activation(out=lorab[0:32, S_], in_=hs_lo[0:32, S_], func=AF.Tanh), r=["hs_lo"], w=["lorab0"])
        P.act(lambda e: e.copy(out=lorab[32:64, S_], in_=hs_lo[32:64, S_]), r=["hs_lo"], w=["lorab1"])
        bk = inproj(("lg", 0), 96)
        shift_evac(bk, 96, 13, hs_lg[:, S_], "hs_lg")
        P.act(lambda e: e.activation(out=sgb[:, S_], in_=hs_lg[:, S_], func=AF.Sigmoid), r=["hs_lg"], w=["sgb"])

        if kind == "s" and STOP <= 2:
            return
        def hp_gen(hp):
            rs_i = (hp % 2) if PIPE else 0
            gLv = gL2[:, rs_i, :]
            gk = ("gL", rs_i)
            rkv_v = (lambda j: rkv[:, 0, j, S_]) if rs_i == 0 else (lambda j: hid32[:, j * 512:j * 512 + NT])
            rs, ks, vs = rkv_v(0), rkv_v(1), rkv_v(2)
            for j, nm in enumerate(("r", "k", "v")):
                bk = inproj((nm, hp), 128)
                shift_evac(bk, 128, RC[(nm, hp)], rkv_v(j), ("rkv", rs_i, j), xw=([("hidT", fc_) for fc_ in range(6)] if rs_i == 1 else []))
                yield "p0"
            kr_, kk_, kv_ = ("rkv", rs_i, 0), ("rkv", rs_i, 1), ("rkv", rs_i, 2)
            t = [x[:, S_] for x in tq]
            if kind == "s" and STOP == 32:
                return
            bw = bank()
            P.pe(lambda e, bw=bw, hp=hp: e.matmul(PS(bw, NT), lhsT=Wd_b[0:32, hp * 128:(hp + 1) * 128], rhs=lorab[0:32, S_],
                                                  start=True, stop=True), r=["lorab0", "cconst"], w=[kb(bw)])
            P.act(lambda e, bw=bw, hp=hp: e.activation(out=t[0], in_=PS(bw, NT), func=AF.Sigmoid, bias=w0T[:, hp:hp + 1]),
                  r=[kb(bw)] + CONST, w=["t0"])
            P.dve(lambda e: e.tensor_tensor_scan(out=t[1], data0=scanm[:, S_], data1=t[0], initial=0.0, op0=ALU.mult, op1=ALU.add),
                  r=["t0"] + CONST, w=["t1"])
            P.dve(lambda e: e.tensor_tensor(out=t[2], in0=t[1], in1=t[0], op=ALU.subtract), r=["t0", "t1"], w=["t2"])
            yield "p1"
            P.act(lambda e: e.activation(out=t[0], in_=t[1], func=AF.Exp, scale=-KAPPA), r=["t1", "t2"], w=["t0"])
            P.act(lambda e: e.activation(out=t[3], in_=t[1], func=AF.Exp, scale=KAPPA), r=["t1"], w=["t3"])
            P.act(lambda e: e.activation(out=t[2], in_=t[2], func=AF.Exp, scale=-KAPPA), r=["t2"], w=["t2"])
            yield "p1"
            cp3 = t[1].rearrange("p (c l) -> p c l", l=L)
            eg3 = t[0].rearrange("p (c l) -> p c l", l=L)
            P.dve(lambda e: e.tensor_copy(out=gLv[:, 0:nch].unsqueeze(2), in_=eg3[:, :, L - 1:L]), r=["t0"], w=[gk])
            P.dve(lambda e: e.tensor_scalar(out=nkcl[:, 0:nch].unsqueeze(2), in0=cp3[:, :, L - 1:L], scalar1=-KAPPA, scalar2=None,
                                            op0=ALU.mult), r=["t1"], w=["nkcl"])
            P.dve(lambda e: e.scalar_tensor_tensor(out=t[4].rearrange("p (c l) -> p c l", l=L), in0=cp3, scalar=KAPPA,
                                                   in1=nkcl[:, 0:nch].unsqueeze(2).to_broadcast([128, nch, L]),
                                                   op0=ALU.mult, op1=ALU.add), r=["t1", "nkcl"], w=["t4"])
            P.act(lambda e: e.activation(out=t[4], in_=t[4], func=AF.Exp), r=["t4"], w=["t4"])
            if kind == "s" and STOP == 33:
                return
            yield "p1"
            ba = bank()
            P.pe(lambda e, ba=ba, hp=hp: e.matmul(PS(ba, NT), lhsT=Wa_b[32:64, hp * 128:(hp + 1) * 128], rhs=lorab[32:64, S_],
                                                  start=True, stop=True), r=["lorab1", "cconst"], w=[kb(ba)])
            P.act(lambda e, ba=ba, hp=hp: e.activation(out=t[1], in_=PS(ba, NT), func=AF.Sigmoid, bias=a0T[:, hp:hp + 1]),
                  r=[kb(ba), "t4", gk, "nkcl"] + CONST, w=["t1"])
            yield "p1"
            P.dve(lambda e, hp=hp: e.tensor_scalar(out=t[5], in0=ks, scalar1=kkT[:, hp:hp + 1], scalar2=None, op0=ALU.mult),
                  r=[kk_] + CONST, w=["t5"])
            P.act(lambda e: e.activation(out=sqb[:, S_], in_=t[5], func=AF.Square), r=["t5"], w=["sqb"])
            bn = bank()
            P.pe(lambda e, bn=bn: e.matmul(PS(bn, NT), lhsT=blk1[:], rhs=sqb[:, S_], start=True, stop=True),
                 r=["sqb", "cconst"], w=[kb(bn)])
            yield "p1"
            P.dve(lambda e, bn=bn: e.tensor_scalar(out=t[6], in0=PS(bn, NT), scalar1=1e-24, scalar2=None, op0=ALU.max),
                  r=[kb(bn)], w=["t6"])
            P.act(lambda e: e.activation(out=t[6], in_=t[6], func=AF.Sqrt), r=["t6"], w=["t6"])
            P.dve(lambda e: e.reciprocal(out=t[6], in_=t[6]), r=["t6"], w=["t6"])
            P.dve(lambda e: e.tensor_tensor(out=t[5], in0=t[5], in1=t[6], op=ALU.mult), r=["t5", "t6"], w=["t5"])
            yield "p1"
            P.dve(lambda e, hp=hp: e.tensor_scalar(out=t[6], in0=t[1], scalar1=1.0, scalar2=kaT[:, hp:hp + 1],
                                                   op0=ALU.subtract, op1=ALU.mult), r=["t1", "t5"] + CONST, w=["t6"])
            P.dve(lambda e: e.scalar_tensor_tensor(out=t[6], in0=t[6], scalar=1.0, in1=ks, op0=ALU.add, op1=ALU.mult),
                  r=["t6", kk_], w=["t6"])
            yield "p1"
            P.dve(lambda e: e.tensor_tensor(out=t[1], in0=t[5], in1=t[1], op=ALU.mult), r=["t5", "t1"], w=["t1"])
            if kind == "s" and STOP == 34:
                return
            yield "p1done"
            KR4 = KR[:, 0:2 * NT].rearrange("p (c a l) -> p c a l", a=2, l=L)
            P.dve(lambda e: e.tensor_tensor(out=Kt[:, S_], in0=t[6], in1=t[3], op=ALU.mult), r=["t6", "t3"], w=["Kt"])
            P.dve(lambda e: e.tensor_tensor(out=Bt[:, S_], in0=t[1], in1=t[3], op=ALU.mult), r=["t1", "t3"], w=["Bt"])
            P.dve(lambda e: e.tensor_tensor(out=KR4[:, :, 0, :], in0=t[5].rearrange("p (c l) -> p c l", l=L),
                                            in1=t[2].rearrange("p (c l) -> p c l", l=L), op=ALU.mult), r=["t5", "t2"], w=["KR"])
            P.dve(lambda e: e.tensor_tensor(out=KR4[:, :, 1, :], in0=rs.rearrange("p (c l) -> p c l", l=L),
                                            in1=t[0].rearrange("p (c l) -> p c l", l=L), op=ALU.mult), r=[kr_, "t0"], w=["KR"])
            P.dve(lambda e: e.tensor_tensor(out=KgL[:, S_], in0=t[6], in1=t[4], op=ALU.mult), r=["t6", "t4"], w=["KgL"])
            P.dve(lambda e: e.scalar_tensor_tensor(out=BgLn[:, S_], in0=t[1], scalar=-1.0, in1=t[4], op0=ALU.mult, op1=ALU.mult),
                  r=["t1", "t4"], w=["BgLn"])
            P.act(lambda e: e.copy(out=vb[:, S_], in_=vs), r=[kv_], w=["vb"])
            P.dve(lambda e, hp=hp: e.scalar_tensor_tensor(out=prodb[:, S_], in0=rs, scalar=rkT[:, hp:hp + 1], in1=t[6],
                                                          op0=ALU.mult, op1=ALU.mult), r=[kr_, "t6"] + CONST, w=["prodb"])
            if kind == "s" and STOP == 35:
                return
            brk = bank()
            for b in range(NB):
                P.pe(lambda e, b=b, brk=brk: e.matmul(PS(brk, 2 * NB)[:, 2 * b:2 * b + 2], lhsT=prodb[:, b * 128:(b + 1) * 128],
                                                      rhs=blk1[:, 0:128:64], start=True, stop=True),
                     r=["prodb", "cconst"], w=[kb(brk)])
            P.act(lambda e, brk=brk, hp=hp: e.copy(out=rkb[:, 0:NB, 2 * hp:2 * hp + 2],
                                                   in_=PS(brk, 2 * NB).rearrange("p (b a) -> p b a", a=2)),
                  r=[kb(brk)], w=[("rkb", hp)])
            if kind == "s" and STOP == 36:
                return
            if kind == "p":
                vtok_hp = vtok[0:L, hp * nch * 128:(hp + 1) * nch * 128].rearrange("p (c n) -> p c n", n=128)
                vkey = ("vtok", hp)
            else:
                vtok_hp = vtok[0:L, 0:nch * 128].rearrange("p (c n) -> p c n", n=128)
                vkey = "vtok_s"
            if kind == "p":
                KgLtok3 = KgLtok[0:L, 0:nch * 128].rearrange("p (c n) -> p c n", n=128)
                BgLtok3 = BgLtok[0:L, 0:nch * 128].rearrange("p (c n) -> p c n", n=128)
                kBg = ["BgLtok"]
            else:
                KgLtok3 = hidT[:].rearrange("p a b -> p (a b)")[0:L, 8192:8192 + nch * 128].rearrange("p (c n) -> p c n", n=128)
                BgLtok3 = Et[:].rearrange("p a b -> p (a b)")[0:L, 0:nch * 128].rearrange("p (c n) -> p c n", n=128)
                kBg = ["BgLtok"] + [("Et", i_) for i_ in range(4)]
            for src, skey, dst3, dkey in ((vb, "vb", vtok_hp, vkey), (KgL, "KgL", KgLtok3, "KgLtok"), (BgLn, "BgLn", BgLtok3, kBg)):
                for g0 in range(0, nch, 8):
                    g1 = min(nch, g0 + 8)
                    bt = bank()
                    for c in range(g0, g1):
                        P.pe(lambda e, c=c, bt=bt, src=src, g0=g0: e.transpose(
                            out=PSB(bt, L)[:, (c - g0) * 128:(c - g0 + 1) * 128], in_=src[:, c * L:(c + 1) * L], identity=idb[:]),
                            r=[skey, "idb"], w=[kb(bt)])
                    P.act(lambda e, bt=bt, g0=g0, g1=g1, dst3=dst3: e.copy(
                        out=dst3[:, g0:g1, :], in_=PSB(bt, L)[:, 0:(g1 - g0) * 128].rearrange("p (c n) -> p c n", n=128)),
                        r=[kb(bt), "XA"], w=(dkey if isinstance(dkey, list) else [dkey]))
            if kind == "s" and STOP == 31:
                return
            if kind == "s":
                btv = bank()
                P.pe(lambda e, btv=btv: e.transpose(out=PSB(btv)[:, 0:128], in_=vb[:, 0:128], identity=idb[:]), r=["vb", "idb"], w=[kb(btv)])
                P.act(lambda e, btv=btv, hp=hp: e.copy(out=vblk_s[:, hp * 128:(hp + 1) * 128], in_=PSB(btv)[:, 0:128]), r=[kb(btv)], w=["vblk_s"])
                Sst = hid32[0:64, 2048:4096]
                if VAR == 3:
                    return
                for s_ in range(NSEQ_S):
                    P.dma("sp", "sst_in", Sst.rearrange("i (s h j) -> i s h j", s=NSEQ_S, h=2)[:, s_, :, :],
                          swkv[s_, 2 * hp:2 * hp + 2, :, :].rearrange("h i j -> i h j"), r=["XA"], w=["Sstage"] + HID_ALL)
                if VAR == 4:
                    return
                for g in range(2):
                    bts = bank()
                    for s8 in range(8):
                        s_ = g * 8 + s8
                        P.pe(lambda e, bts=bts, s8=s8, s_=s_, Sst=Sst: e.transpose(out=PS(bts, 512)[:, s8 * 64:(s8 + 1) * 64],
                                                                                 in_=Sst[:, s_ * 128:(s_ + 1) * 128], identity=idf[0:64, 0:64]),
                             r=["Sstage", "XA"] + CONST, w=[kb(bts)])
                    if VAR == 5:
                        return
                    P.act(lambda e, bts=bts, g=g: e.copy(out=SmS[:, g * 8:(g + 1) * 8, :], in_=PS(bts, 512).rearrange("p (s i) -> p s i", i=64)),
                          r=[kb(bts)], w=[("SmS", s_) for s_ in range(g * 8, g * 8 + 8)])
                    if VAR == 6:
                        return
                    P.dve(lambda e, bts=bts, g=g: e.tensor_copy(out=SbS[:, g * 8:(g + 1) * 8, :], in_=PS(bts, 512).rearrange("p (s i) -> p s i", i=64)),
                          r=[kb(bts)], w=[("SbS", s_) for s_ in range(g * 8, g * 8 + 8)])
            if kind == "s" and STOP == 3:
                return
            yield "p2done"
            nu = 2 * nch
            W = nu * L
            NQv = NQ2[0:L, 0:2 * W]
            KQv = KQ2[0:L, 0:2 * W]
            NTv = NT2[0:L, 0:W]
            T32v = T32[0:L, 0:W]
            Tbv = Tb[0:L, 0:W]
            NQ4 = NQv.rearrange("p (u a l) -> p u a l", a=2, l=L)
            KQ4 = KQv.rearrange("p (u a l) -> p u a l", a=2, l=L)
            kNQ, kKQ, kNT, kT32, kTb = "NQ2", "KQ2", "NT2", "T32", "Tb"
            for p in range(2):
                P0 = 64 * p
                for (lhs, lkey, dst4, dkey, msk) in ((Bt, "Bt", NQ4, kNQ, m2b), (Kt, "Kt", KQ4, kKQ, m2k)):
                    bq = pair()
                    for c in range(nch):
                        col = c * 2 * L
                        P.pe(lambda e, P0=P0, c=c, bq=bq, lhs=lhs, col=col: e.matmul(
                            PSP(bq)[0:L, col:col + 2 * L], lhsT=lhs[P0:P0 + 64, c * L:(c + 1) * L],
                            rhs=KR[P0:P0 + 64, c * 2 * L:(c + 1) * 2 * L], start=True, stop=True),
                            r=[lkey, "KR"], w=[kb(bq), kb(bq + 1)])
                    P.dve(lambda e, bq=bq, dst4=dst4, msk=msk, p=p: e.tensor_tensor(
                        out=dst4[:, p * nch:(p + 1) * nch, :, :],
                        in0=PSP(bq)[0:L, 0:nch * 2 * L].rearrange("p (c a l) -> p c a l", a=2, l=L),
                        in1=msk[0:L, :, 0:L].unsqueeze(1).to_broadcast([L, nch, 2, L]),
                        op=ALU.mult), r=[kb(bq), kb(bq + 1), "cconst"], w=[(dkey, p)])
                    pump()
            pump()
            bT2 = pair()
            for p in range(2):
                P0 = 64 * p
                for c in range(nch):
                    u = p * nch + c
                    P.pe(lambda e, P0=P0, c=c, u=u, bT2=bT2: e.matmul(PSP(bT2)[0:L, u * L:(u + 1) * L],
                                                                     lhsT=KR[P0:P0 + 64, c * 2 * L:c * 2 * L + L],
                                                                     rhs=Bt[P0:P0 + 64, c * L:(c + 1) * L], start=True, stop=True),
                         r=["KR", "Bt"], w=[kb(bT2), kb(bT2 + 1)])
            P.dve(lambda e, bT2=bT2: e.tensor_tensor(out=NTv.rearrange("p (u l) -> p u l", l=L),
                                                     in0=PSP(bT2)[0:L, 0:W].rearrange("p (u l) -> p u l", l=L),
                                                     in1=mTl[0:L, 0:L].unsqueeze(1).to_broadcast([L, nu, L]), op=ALU.mult),
                  r=[kb(bT2), kb(bT2 + 1), "cconst"], w=[kNT])
            kNQb = [(kNQ, 0), (kNQ, 1)]
            kKQb = [(kKQ, 0), (kKQ, 1)]
            P.dve(lambda e: e.tensor_tensor(out=T32v.rearrange("p (u l) -> p u l", l=L),
                                            in0=idf[0:L, 0:L].unsqueeze(1).to_broadcast([L, nu, L]),
                                            in1=NQ4[:, :, 0, :], op=ALU.subtract), r=kNQb + CONST, w=[kT32])
            P.act(lambda e: e.copy(out=Tbv, in_=T32v), r=[kT32], w=[kTb])
            nlev = int(np.log2(L)) - 1

            def emit_sq(lastlev):
                bPT = pair()
                for u in range(nu):
                    P.pe(lambda e, u=u, bPT=bPT: e.matmul(PSP(bPT)[0:L, u * L:(u + 1) * L], lhsT=NQ4[:, u, 0, :],
                                                          rhs=NTv[:, u * L:(u + 1) * L], start=True, stop=True),
                         r=kNQb + [kNT], w=[kb(bPT), kb(bPT + 1)])
                bP = None
                if not lastlev:
                    bP = pair()
                    for u in range(nu):
                        P.pe(lambda e, u=u, bP=bP: e.matmul(PSP(bP)[0:L, u * L:(u + 1) * L], lhsT=NTv[:, u * L:(u + 1) * L],
                                                            rhs=NQ4[:, u, 0, :], start=True, stop=True),
                             r=kNQb + [kNT], w=[kb(bP), kb(bP + 1)])
                return bPT, bP

            def emit_sq_evac(bPT, bP):
                P.act(lambda e, bPT=bPT: e.copy(out=NTv, in_=PSP(bPT)[0:L, 0:W]), r=[kb(bPT), kb(bPT + 1)], w=[kNT])
                if bP is not None:
                    P.act(lambda e, bP=bP: e.copy(out=NQ4[:, :, 0, :], in_=PSP(bP)[0:L, 0:W].rearrange("p (u l) -> p u l", l=L)),
                          r=[kb(bP), kb(bP + 1)], w=kNQb)

            bb = emit_sq(nlev == 1)
            emit_sq_evac(*bb)
            for lev in range(nlev):
                nxt = None
                if lev + 1 < nlev:
                    nxt = emit_sq(lev + 1 == nlev - 1)
                bT = pair()
                for u in range(nu):
                    P.pe(lambda e, u=u, bT=bT: e.matmul(PSP(bT)[0:L, u * L:(u + 1) * L], lhsT=NTv[:, u * L:(u + 1) * L],
                                                        rhs=Tbv[:, u * L:(u + 1) * L], start=True, stop=True),
                         r=[kNT, kTb], w=[kb(bT), kb(bT + 1)])
                if nxt is not None:
                    emit_sq_evac(*nxt)
                P.dve(lambda e, bT=bT: e.tensor_tensor(out=T32v, in0=T32v, in1=PSP(bT)[0:L, 0:W], op=ALU.add),
                      r=[kb(bT), kb(bT + 1), kT32], w=[kT32])
                P.act(lambda e: e.copy(out=Tbv, in_=T32v), r=[kT32], w=[kTb])
                pump()
            for c in range(nch):
                if kind == "p":
                    Smv, Sbv, skm, skb = Sm[:, hp, :], Sb[:, hp, :], ("Sm", hp), ("Sb", hp)
                    zero_state = first and c == 0
                    ydst_ap, ykey = ytok[:, c, hp * 128:(hp + 1) * 128], ("ytok", c)
                else:
                    Smv, Sbv, skm, skb = SmS[:, c, :], SbS[:, c, :], ("SmS", c), ("SbS", c)
                    zero_state = False
                    ydst_ap, ykey = ytok_s[0:L, c, :], "ytok_s"
                bX = bank()
                for p in range(2):
                    P0 = 64 * p
                    u = p * nch + c
                    if not zero_state:
                        P.pe(lambda e, P0=P0, bX=bX, c=c, p=p, Sbv=Sbv: e.matmul(PS(bX, 128, L)[:, p * 64:(p + 1) * 64],
                                                                       lhsT=KR[P0:P0 + 64, c * 2 * L:c * 2 * L + L], rhs=Sbv[P0:P0 + 64, :],
                                                                       start=True, stop=False), r=["KR", skb], w=[kb(bX)])
                    P.pe(lambda e, P0=P0, bX=bX, c=c, p=p, u=u, zs=zero_state, vtok_hp=vtok_hp: e.matmul(PS(bX, 128, L)[:, p * 64:(p + 1) * 64], lhsT=KQ4[:, u, 0, :],
                                                                                      rhs=vtok_hp[:, c, P0:P0 + 64], start=zs, stop=True),
                         r=kKQb + [vkey], w=[kb(bX)])
                P.act(lambda e, bX=bX: e.copy(out=XTs[0:L, :], in_=PS(bX, 128, L)), r=[kb(bX)], w=["XTs"])
                bS = bank()
                for p in range(2):
                    u = p * nch + c
                    P.pe(lambda e, bS=bS, p=p, u=u: e.matmul(PS(bS, 128, L)[:, p * 64:(p + 1) * 64], lhsT=Tbv[:, u * L:(u + 1) * L],
                                                            rhs=XTs[0:L, p * 64:(p + 1) * 64], start=True, stop=True), r=[kTb, "XTs"], w=[kb(bS)])
                P.act(lambda e, bS=bS: e.copy(out=SATs[0:L, :], in_=PS(bS, 128, L)), r=[kb(bS)], w=["SATs"])
                bY = bank()
                for p in range(2):
                    P0 = 64 * p
                    u = p * nch + c
                    if not zero_state:
                        P.pe(lambda e, P0=P0, bY=bY, c=c, p=p, Sbv=Sbv: e.matmul(PS(bY, 128, L)[:, p * 64:(p + 1) * 64],
                                                                       lhsT=KR[P0:P0 + 64, c * 2 * L + L:(c + 1) * 2 * L], rhs=Sbv[P0:P0 + 64, :],
                                                                       start=True, stop=False), r=["KR", skb], w=[kb(bY)])
                    P.pe(lambda e, P0=P0, bY=bY, c=c, p=p, u=u, zs=zero_state, vtok_hp=vtok_hp: e.matmul(PS(bY, 128, L)[:, p * 64:(p + 1) * 64], lhsT=KQ4[:, u, 1, :],
                                                                                      rhs=vtok_hp[:, c, P0:P0 + 64], start=zs, stop=False),
                         r=kKQb + [vkey], w=[kb(bY)])
                    P.pe(lambda e, bY=bY, p=p, u=u: e.matmul(PS(bY, 128, L)[:, p * 64:(p + 1) * 64], lhsT=NQ4[:, u, 1, :],
                                                            rhs=SATs[0:L, p * 64:(p + 1) * 64], start=False, stop=True),
                         r=kNQb + ["SATs"], w=[kb(bY)])
                P.dve(lambda e, bY=bY, ydst_ap=ydst_ap: e.tensor_copy(out=ydst_ap, in_=PS(bY, 128, L)),
                      r=[kb(bY)] + (["XA"] if kind == "s" else []), w=[ykey])
                bZ = bank()
                for p in range(2):
                    P0 = 64 * p
                    P.pe(lambda e, P0=P0, bZ=bZ, c=c, vtok_hp=vtok_hp, KgLtok3=KgLtok3: e.matmul(PS(bZ, 64, 64, P0), lhsT=KgLtok3[:, c, P0:P0 + 64], rhs=vtok_hp[:, c, P0:P0 + 64],
                                                               start=True, stop=False), r=["KgLtok", vkey, "XA"], w=[kb(bZ)])
                    P.pe(lambda e, P0=P0, bZ=bZ, c=c, p=p, BgLtok3=BgLtok3: e.matmul(PS(bZ, 64, 64, P0), lhsT=BgLtok3[:, c, P0:P0 + 64], rhs=SATs[0:L, p * 64:(p + 1) * 64],
                                                                   start=False, stop=True), r=kBg + ["SATs"], w=[kb(bZ)])
                if zero_state:
                    P.dve(lambda e, bZ=bZ, Smv=Smv: e.tensor_copy(out=Smv, in_=PS(bZ, 64)), r=[kb(bZ)], w=[skm])
                else:
                    P.dve(lambda e, bZ=bZ, Smv=Smv, c=c: e.scalar_tensor_tensor(out=Smv, in0=Smv, scalar=gLv[:, c:c + 1], in1=PS(bZ, 64),
                                                                                op0=ALU.mult, op1=ALU.add), r=[kb(bZ), skm, gk], w=[skm])
                P.act(lambda e, Smv=Smv, Sbv=Sbv: e.copy(out=Sbv, in_=Smv), r=[skm], w=[skb])
                pump()
            if kind == "s":
                for s_ in range(NSEQ_S):
                    P.dma("sp", "yrl", ytok[s_ * LS:(s_ + 1) * LS, 0, hp * 128:(hp + 1) * 128], ytok_s[0:LS, s_, :],
                          r=["ytok_s", "XA"] + HID_ALL, w=[("ytok", 0)])
                Sst = hid32[0:64, 2048:4096]
                for g in range(4):
                    bts = bank()
                    for s4 in range(4):
                        s_ = g * 4 + s4
                        P.pe(lambda e, bts=bts, s4=s4, s_=s_: e.transpose(out=PS(bts, 512, 64)[:, s4 * 128:(s4 + 1) * 128], in_=SmS[:, s_, :], identity=idf[:]),
                             r=[("SmS", s_)] + CONST, w=[kb(bts)])
                    P.act(lambda e, bts=bts, g=g, Sst=Sst: e.copy(out=Sst[:, g * 512:(g + 1) * 512], in_=PS(bts, 512, 64)), r=[kb(bts), "XA"], w=["Sstage"])
                for s_ in range(NSEQ_S):
                    P.dma("sp", "sst_out", wkvs[s_, 2 * hp:2 * hp + 2, :, :].rearrange("h i j -> i h j"),
                          Sst.rearrange("i (s h j) -> i s h j", s=NSEQ_S, h=2)[:, s_, :, :], r=["Sstage", "XA"] + HID_ALL, w=["o_wkvs"])


        nhp = HPS if kind == "s" else 4
        PIPE = (kind == "p")
        pstate = {"g": None, "done": True}

        def pump():
            if pstate["done"] or pstate["g"] is None:
                return
            try:
                v = next(pstate["g"])
            except StopIteration:
                pstate["done"] = True
                return
            if v == "p1done":
                pstate["done"] = True

        def run_until(g, tag):
            while True:
                try:
                    v = next(g)
                except StopIteration:
                    return False
                if v == tag:
                    return True

        gens = [hp_gen(h) for h in range(nhp)]
        alive = run_until(gens[0], "p2done")
        for h in range(nhp):
            if not alive:
                break
            nxt = gens[h + 1] if (h + 1 < nhp) else None
            if nxt is not None and PIPE:
                pstate["g"], pstate["done"] = nxt, False
            else:
                pstate["g"], pstate["done"] = None, True
            run_until(gens[h], "__end__")
            if nxt is not None:
                if PIPE and not pstate["done"]:
                    run_until(nxt, "p1done")
                pstate["done"] = True
                alive = run_until(nxt, "p2done")
        if kind == "s" and (STOP <= 4 or 30 <= STOP < 40):
            return
        bkk = inproj(("ak", 0), 128)
        bks = inproj(("aks", 0), 128)
        P.dve(lambda e: e.tensor_tensor(out=kf32[:, S_], in0=PS(bkk, NT), in1=cs_t[:, 0, S_], op=ALU.mult), r=[kb(bkk), "cs_t"], w=["t3"])
        P.dve(lambda e: e.tensor_tensor(out=dtmp[:, S_], in0=PS(bks, NT), in1=cs_t[:, 1, S_], op=ALU.mult), r=[kb(bks), "cs_t"], w=["dtmp"])
        P.dve(lambda e: e.tensor_tensor(out=kf32[:, S_], in0=kf32[:, S_], in1=dtmp[:, S_], op=ALU.add), r=["t3", "dtmp"], w=["t3"])
        P.act(lambda e: e.copy(out=kbuf[:, 128:128 + NT], in_=kf32[:, S_]), r=["t3"], w=["kbuf"])
        bv = inproj(("av", 0), 128)
        P.act(lambda e: e.copy(out=vT32[:, S_], in_=PS(bv, NT)), r=[kb(bv)], w=["t4"])
        if first or kind == "s":
            P.pool(lambda e: e.memset(Vaug[:], 1.0), w=["Vaug"])
        for b in range(NB):
            bt = bank()
            P.pe(lambda e, b=b, bt=bt: e.transpose(out=PS(bt, 128), in_=vT32[:, b * 128:(b + 1) * 128], identity=idf[:]),
                 r=["t4"] + CONST, w=[kb(bt)])
            P.act(lambda e, b=b, bt=bt: e.copy(out=Vaug[:, b + 1, :, 0:64], in_=PS(bt, 128).rearrange("p (k d) -> p k d", k=2)),
                  r=[kb(bt)], w=["Vaug"])
            if last and b == NB - 1:
                P.dve(lambda e, bt=bt: e.tensor_copy(out=vtokf[:], in_=PS(bt, 128)), r=[kb(bt)], w=["vtokf"])
                if kind == "p":
                    P.dma("pool", "o_vw", vwp, vtokf[:], r=["vtokf"], w=["o_vwp"])
                else:
                    for s in range(NSEQ_S):
                        P.dma("sp", "o_vws", vws[s, 120:128, :], vtokf[s * LS:(s + 1) * LS, :], r=["vtokf"], w=["o_vws"])
                    P.dma("sp", "o_vw2", vws[:, 0:120, :].rearrange("s r c -> s (r c)"), cv[:, 8:128, :].rearrange("s r c -> s (r c)"), w=["o_vws2"])
                bt2 = bank()
                P.pe(lambda e, b=b, bt2=bt2: e.transpose(out=PS(bt2, 128), in_=kf32[:, b * 128:(b + 1) * 128], identity=idf[:]),
                     r=["t3"] + CONST, w=[kb(bt2)])
                P.dve(lambda e, bt2=bt2: e.tensor_copy(out=yq[:, 0:128], in_=PS(bt2, 128)), r=[kb(bt2)], w=["t0"])
                if kind == "p":
                    P.dma("pool", "o_kw", kwp, yq[:, 0:128], r=["t0"], w=["o_kwp"])
                else:
                    for s in range(NSEQ_S):
                        P.dma("sp", "o_kws", kws[s, 120:128, :], yq[s * LS:(s + 1) * LS, 0:128], r=["t0"], w=["o_kws"])
                    P.dma("sp", "o_kw2", kws[:, 0:120, :].rearrange("s r c -> s (r c)"), ck[:, 8:128, :].rearrange("s r c -> s (r c)"), w=["o_kws2"])
        for c in range(4):
            bq_ = inproj(("q", c), 128)
            bqs = inproj(("qs", c), 128)
            P.dve(lambda e, bq_=bq_: e.tensor_tensor(out=yq[:, S_], in0=PS(bq_, NT), in1=cs_t[:, 0, S_], op=ALU.mult),
                  r=[kb(bq_), "cs_t", "t0"], w=["t0"])
            P.dve(lambda e, bqs=bqs: e.tensor_tensor(out=yq2[:, S_], in0=PS(bqs, NT), in1=cs_t[:, 1, S_], op=ALU.mult),
                  r=[kb(bqs), "cs_t"], w=["t1"])
            P.pool(lambda e, c=c: e.tensor_tensor(out=qT[:, c, S_], in0=yq[:, S_], in1=yq2[:, S_], op=ALU.add),
                   r=["t0", "t1"], w=[("qT", c)])
        qkeys = [("qT", c) for c in range(4)]

        def epilogue(b):
            y3 = ytok[:, b, :].rearrange("p (h d) -> p h d", d=64)
            yk = ("ytok", b)
            gs = gstat[:, 0, :]
            P.dve(lambda e, y3=y3: e.tensor_reduce(out=gstat[:, 0, :], in_=y3, axis=AX.X, op=ALU.add), r=[yk], w=["gstat"])
            P.act(lambda e, b=b: e.activation(out=yq[:, :], in_=ytok[:, b, :], func=AF.Square), r=[yk, "t0"], w=["t0"])
            P.dve(lambda e: e.tensor_reduce(out=gstat[:, 1, :], in_=yq[:, :].rearrange("p (h d) -> p h d", d=64), axis=AX.X, op=ALU.add),
                  r=["t0"], w=["gstat"])
            P.dve(lambda e: e.tensor_scalar(out=gstat[:, 0, :], in0=gstat[:, 0, :], scalar1=1.0 / 64, scalar2=None, op0=ALU.mult),
                  r=["gstat"], w=["gstat"])
            P.dve(lambda e: e.tensor_tensor(out=gstat[:, 2, :], in0=gstat[:, 0, :], in1=gstat[:, 0, :], op=ALU.mult), r=["gstat"], w=["gstat"])
            P.dve(lambda e: e.scalar_tensor_tensor(out=gstat[:, 1, :], in0=gstat[:, 1, :], scalar=1.0 / 64, in1=gstat[:, 2, :],
                                                   op0=ALU.mult, op1=ALU.subtract), r=["gstat"], w=["gstat"])
            P.act(lambda e: e.activation(out=gstat[:, 1, :], in_=gstat[:, 1, :], func=AF.Sqrt, bias=GN_EPS), r=["gstat"], w=["gstat"])
            P.dve(lambda e: e.reciprocal(out=gstat[:, 1, :], in_=gstat[:, 1, :]), r=["gstat"], w=["gstat"])
            for h in range(8):
                P.dve(lambda e, h=h, y3=y3: e.tensor_scalar(out=yq2[:, h * 64:(h + 1) * 64], in0=y3[:, h, :], scalar1=gstat[:, 0, h:h + 1],
                                                            scalar2=gstat[:, 1, h:h + 1], op0=ALU.subtract, op1=ALU.mult),
                      r=[yk, "gstat", "t1"], w=["t1"])
            P.dve(lambda e: e.tensor_tensor(out=yq2[:, :], in0=yq2[:, :], in1=gnw_bc[:], op=ALU.mult), r=["t1"] + CONST, w=["t1"])
            P.dve(lambda e: e.tensor_tensor(out=yq2[:, :], in0=yq2[:, :], in1=gnb_bc[:], op=ALU.add), r=["t1"] + CONST, w=["t1"])
            for hp in range(4):
                if kind == "p":
                    vt_blk = vtok[:, (hp * nch + b) * 128:(hp * nch + b + 1) * 128]
                    vkeys = [("vtok", hp)]
                else:
                    vt_blk = None
                    vkeys = []
                for p in range(2):
                    h = 2 * hp + p
                    if kind == "p":
                        P.dve(lambda e, h=h, p=p, vt_blk=vt_blk, b=b: e.scalar_tensor_tensor(
                            out=yq2[:, h * 64:(h + 1) * 64], in0=vt_blk[:, p * 64:(p + 1) * 64], scalar=rkb[:, b, h:h + 1],
                            in1=yq2[:, h * 64:(h + 1) * 64], op0=ALU.mult, op1=ALU.add),
                            r=vkeys + [("rkb", hp), "t1"], w=["t1"])
                    else:
                        P.dve(lambda e, h=h, b=b: e.scalar_tensor_tensor(
                            out=yq2[:, h * 64:(h + 1) * 64], in0=vblk_s[:, h * 64:(h + 1) * 64], scalar=rkb[:, b, h:h + 1],
                            in1=yq2[:, h * 64:(h + 1) * 64], op0=ALU.mult, op1=ALU.add),
                            r=["vblk_s", ("rkb", hp), "t1"], w=["t1"])
            bg = bank()
            P.pe(lambda e, b=b, bg=bg: e.matmul(PS(bg, 512), lhsT=sgb[:, b * 128:(b + 1) * 128], rhs=Wg_b[:], start=True, stop=True),
                 r=["sgb", "cconst"], w=[kb(bg)])
            P.dve(lambda e, b=b, bg=bg: e.tensor_tensor(out=mix[:, b, 512:1024], in0=yq2[:, :], in1=PS(bg, 512), op=ALU.mult),
                  r=["t1", kb(bg)], w=[("mixr", b)])


        def attn_group(b, kblocks, first_grp, only_grp, additive=False):
            ng = len(kblocks)
            for kv in range(2):
                P0 = 64 * kv
                for i, (kfn, vfn, msk, kkeys) in enumerate(kblocks):
                    bs = bank()
                    P.pe(lambda e, P0=P0, bs=bs, kfn=kfn, kv=kv: e.matmul(PS(bs, 512), lhsT=kfn(kv),
                                                                   rhs=qT[P0:P0 + 64, :, b * 128:(b + 1) * 128], start=True, stop=not additive),
                         r=qkeys + kkeys, w=[kb(bs)])
                    if additive:
                        P.pe(lambda e, bs=bs, msk=msk: e.matmul(PS(bs, 512), lhsT=idb[:], rhs=msk.rearrange("p a b -> p (a b)"),
                                                                start=False, stop=True), r=["idb", "cconst"], w=[kb(bs)])
                    Ei = Et[:, kv * 2 + i, :]
                    ek = ("Et", kv * 2 + i)
                    P.act(lambda e, bs=bs, Ei=Ei: e.activation(out=Ei, in_=PS(bs, 512), func=AF.Exp, scale=0.125), r=[kb(bs)], w=[ek])
                    if not additive:
                      P.pool(lambda e, Ei=Ei, msk=msk: e.tensor_tensor(out=Ei.rearrange("p (a b) -> p a b", a=4), in0=Ei.rearrange("p (a b) -> p a b", a=4), in1=msk, op=ALU.mult),
                           r=[ek, "cconst", "scconst"], w=[ek])
            bo = pair()
            for kv in range(2):
                for c4 in range(4):
                    for i, (kfn, vfn, msk, kkeys) in enumerate(kblocks):
                        P.pe(lambda e, kv=kv, c4=c4, i=i, vfn=vfn, bo=bo: e.matmul(
                            PS(bo + kv, 260)[:, c4 * 65:(c4 + 1) * 65], lhsT=Et[:, kv * 2 + i, c4 * 128:(c4 + 1) * 128], rhs=vfn(kv),
                            start=(i == 0), stop=(i == ng - 1)), r=[("Et", kv * 2 + i)] + kkeys, w=[kb(bo + kv)])
            return bo

        def attn_finish(b, src_fn, rkeys):
            for kv in range(2):
                s3 = src_fn(kv)
                P.dve(lambda e, kv=kv, s3=s3: e.tensor_tensor(out=den[:, kv * 4:(kv + 1) * 4].unsqueeze(2), in0=s3[:, :, 64:65],
                                                              in1=esink[:, kv * 4:(kv + 1) * 4].unsqueeze(2), op=ALU.add),
                      r=rkeys + ["esink"], w=[("den", kv)])
                P.dve(lambda e, kv=kv: e.reciprocal(out=den[:, kv * 4:(kv + 1) * 4], in_=den[:, kv * 4:(kv + 1) * 4]),
                      r=[("den", kv)], w=[("den", kv)])
                for c4 in range(4):
                    h = kv * 4 + c4
                    P.dve(lambda e, s3=s3, c4=c4, h=h: e.tensor_scalar(out=mix[:, b, h * 64:(h + 1) * 64], in0=s3[:, c4, 0:64],
                                                                       scalar1=den[:, h:h + 1], scalar2=None, op0=ALU.mult),
                          r=rkeys + [("den", kv)], w=[("mixa", b)])

        if kind == "p":
            for b in range(NB):
                gb = ti * NB + b
                kbl = []
                if gb > 0:
                    kbl.append((lambda kv, b=b: kbuf[64 * kv:64 * kv + 64, b * 128:(b + 1) * 128],
                                lambda kv, b=b: Vaug[:, b, kv, :], m_prev[:], ["kbuf", "Vaug"]))
                kbl.append((lambda kv, b=b: kbuf[64 * kv:64 * kv + 64, (b + 1) * 128:(b + 2) * 128],
                            lambda kv, b=b: Vaug[:, b + 1, kv, :], m_own[:], ["kbuf", "Vaug"]))
                bo = attn_group(b, kbl, True, True, additive=True)
                epilogue(b)
                attn_finish(b, lambda kv, bo=bo: PS(bo + kv, 260).rearrange("p (c d) -> p c d", d=65), [kb(bo), kb(bo + 1)])
            P.act(lambda e: e.copy(out=kbuf[:, 0:128], in_=kbuf[:, NB * 128:(NB + 1) * 128]), r=["kbuf"], w=["kbuf"])
            P.pool(lambda e: e.tensor_copy(out=Vaug[:, 0, :, :], in_=Vaug[:, NB, :, :]), r=["Vaug"], w=["Vaug"])
        else:
            kbl = [(lambda kv: kbuf[64 * kv:64 * kv + 64, 128:256], lambda kv: Vaug[:, 1, kv, :], m_sown[:], ["kbuf", "Vaug"])]
            bo = attn_group(0, kbl, True, False)
            for kv in range(2):
                P.dve(lambda e, kv=kv, bo=bo: e.tensor_copy(out=oacc[:, kv, :, :].rearrange("p c d -> p (c d)"), in_=PS(bo + kv, 260)),
                      r=[kb(bo + kv)], w=[("oacc", kv)])
            for s in range(NSEQ_S):
                i = s % 2
                P.dma("pool", "ck%d" % i, ckf[:, i, :], ck[s], w=[("ckf", i)])
                btk = bank()
                P.pe(lambda e, i=i, btk=btk: e.transpose(out=PS(btk, 128), in_=ckf[:, i, :], identity=idf[:]), r=[("ckf", i)] + CONST, w=[kb(btk)])
                P.act(lambda e, btk=btk: e.copy(out=ckT[:], in_=PS(btk, 128)), r=[kb(btk)], w=["ckT"])
                P.dma("pool", "cv%d" % i, ckf[:, i, :], cv[s], r=[], w=[("ckf", i)])
                P.pool(lambda e, i=i: e.memset(cVaug[:, i, :, 64:65], 1.0), w=[("cVaug", i)])
                P.act(lambda e, i=i: e.copy(out=cVaug[:, i, :, 0:64], in_=ckf[:, i, :].rearrange("p (k d) -> p k d", k=2)),
                      r=[("ckf", i)], w=[("cVaug", i)])
                kbl = [(lambda kv: ckT[64 * kv:64 * kv + 64, :], lambda kv, i=i: cVaug[:, i, kv, :],
                        m_scache[:, s:s + 1, :].to_broadcast([128, 4, 128]), ["ckT", ("cVaug", i)])]
                bo = attn_group(0, kbl, False, False)
                for kv in range(2):
                    P.dve(lambda e, kv=kv, bo=bo: e.tensor_tensor(out=oacc[:, kv, :, :].rearrange("p c d -> p (c d)"),
                                                                  in0=oacc[:, kv, :, :].rearrange("p c d -> p (c d)"),
                                                                  in1=PS(bo + kv, 260), op=ALU.add),
                          r=[kb(bo + kv), ("oacc", kv)], w=[("oacc", kv)])
            attn_finish(0, lambda kv: oacc[:, kv, :, :], [("oacc", 0), ("oacc", 1)])
            epilogue(0)

        if kind == "s" and STOP <= 5:
            return
        if kind == "s" and STOP <= 6:
            return
        for b in range(NB):
            bk = bank()
            for c in range(8):
                P.pe(lambda e, c=c, bk=bk, b=b: e.transpose(out=PSB(bk)[:, c * 128:(c + 1) * 128], in_=mix[:, b, c * 128:(c + 1) * 128],
                                                            identity=idb[:]), r=[("mixa", b), ("mixr", b), "idb"], w=[kb(bk)])
            P.act(lambda e, b=b, bk=bk: e.copy(out=bufA[:, :, b * 128:(b + 1) * 128], in_=PSB(bk).rearrange("p (c n) -> p c n", c=8)),
                  r=[kb(bk)], w=[("bufA", b)])
        nb[0] = 0
        for c in range(8):
            s = stream(wsc_out[c], ("wsc_out", c))
            for b in range(NB):
                for hf in range(2):
                    P.pe(lambda e, c=c, s=s, b=b, hf=hf: e.matmul(PS(2 * b + hf, 512), lhsT=bufA[:, c, b * 128:(b + 1) * 128],
                                                                 rhs=ring[:, s, hf * 512:(hf + 1) * 512], start=(c == 0), stop=(c == 7)),
                         r=[("ring", s), ("bufA", b)], w=[kb(2 * b + hf)])

        def ost(b):
            return hid32[:, 1536 + b * 1024:1536 + (b + 1) * 1024]

        def ostk(b):
            return [("hidT", fc_) for fc_ in range(6 + 4 * b, 10 + 4 * b)]

        def norm_res(b, src_pair, gbc, dst, dkey, xkey_r):
            sc = sstat[:, 4 + b:5 + b]
            pk = [kb(2 * b), kb(2 * b + 1)]
            P.act(lambda e: e.activation(out=junk[:], in_=src_pair, func=AF.Square, accum_out=sc), r=pk, w=["junk", ("ss2", b)])
            P.act(lambda e: e.activation(out=sc, in_=sc, func=AF.Sqrt, scale=1.0 / D, bias=RMS_EPS), r=[("ss2", b)], w=[("ss2", b)])
            P.dve(lambda e: e.reciprocal(out=sc, in_=sc), r=[("ss2", b)], w=[("ss2", b)])
            tmp = ost(b)
            P.dve(lambda e: e.scalar_tensor_tensor(out=tmp, in0=src_pair, scalar=sc, in1=gbc[:], op0=ALU.mult, op1=ALU.mult),
                  r=pk + [("ss2", b)] + CONST, w=ostk(b))
            P.pool(lambda e: e.tensor_tensor(out=dst, in0=tmp, in1=x_t[:, b, :], op=ALU.add), r=ostk(b) + [xkey_r], w=dkey)

        for b in range(NB):
            norm_res(b, PSP(2 * b), gpost_bc, x_t[:, b, :], [("x", b)], ("x", b))
        for b in range(NB):
            sc = sstat[:, 8 + b:9 + b]
            P.act(lambda e, b=b, sc=sc: e.activation(out=junk[:], in_=x_t[:, b, :], func=AF.Square, accum_out=sc),
                  r=[("x", b)], w=["junk", ("ss3", b)])
            P.act(lambda e, sc=sc: e.activation(out=sc, in_=sc, func=AF.Sqrt, scale=1.0 / D, bias=RMS_EPS), r=[("ss3", b)], w=[("ss3", b)])
            P.dve(lambda e, sc=sc: e.reciprocal(out=sc, in_=sc), r=[("ss3", b)], w=[("ss3", b)])
        for b in range(NB):
            sc = sstat[:, 8 + b:9 + b]
            P.dve(lambda e, b=b, sc=sc: e.tensor_scalar(out=xn[:], in0=x_t[:, b, :], scalar1=sc, scalar2=None, op0=ALU.mult),
                  r=[("x", b), ("ss3", b)], w=["xn"])
            bk = (2 * b) % 8
            for c in range(8):
                P.pe(lambda e, c=c, bk=bk: e.transpose(out=PSB(bk)[:, c * 128:(c + 1) * 128], in_=xn[:, c * 128:(c + 1) * 128],
                                                      identity=idb[:]), r=["xn", "idb"], w=[kb(bk)])
            P.act(lambda e, b=b, bk=bk: e.copy(out=bufA[:, :, b * 128:(b + 1) * 128], in_=PSB(bk).rearrange("p (c n) -> p c n", c=8)),
                  r=[kb(bk)], w=[("bufA", b)])
        nb[0] = 0

        if kind == "s" and STOP <= 7:
            return
        for fc in range(NFC):
            sz = stream(wsc_fin[2 * fc], ("wsc_fin", 2 * fc))
            su = stream(wsc_fin[2 * fc + 1], ("wsc_fin", 2 * fc + 1))
            bz = bank()
            bu = bank()
            for (s, bk_) in ((sz, bz), (su, bu)):
                for c in range(8):
                    P.pe(lambda e, c=c, s=s, bk_=bk_: e.matmul(PS(bk_, NT), lhsT=ring[:, s, c * 128:(c + 1) * 128], rhs=bufA[:, c, S_],
                                                               start=(c == 0), stop=(c == 7)), r=[("ring", s)] + bufA_keys, w=[kb(bk_)])
            i = 0
            zb = zbuf[:, i, 0:nseq * (Lseq + 2)].rearrange("p (s l) -> p s l", s=nseq)
            zk = ("zbuf", i)
            P.act(lambda e, bz=bz, zb=zb: e.copy(out=zb[:, :, 2:Lseq + 2], in_=PS(bz, NT).rearrange("p (s l) -> p s l", s=nseq)),
                  r=[kb(bz)], w=[zk])
            P.pool(lambda e, zb=zb, fc=fc: e.tensor_copy(out=zb[:, :, 0:2], in_=zcar[:, fc, :, :]), r=[("zcar", fc)], w=[zk])
            P.pool(lambda e, zb=zb, fc=fc: e.tensor_copy(out=zcar[:, fc, :, :], in_=zb[:, :, Lseq:Lseq + 2]), r=[zk], w=[("zcar", fc)])
            a3 = za[:, i, S_].rearrange("p (s l) -> p s l", s=nseq)
            ak = ("za", i)
            P.act(lambda e, bz=bz, a3=a3, fc=fc: e.activation(out=a3, in_=PS(bz, NT).rearrange("p (s l) -> p s l", s=nseq), func=AF.Identity,
                                                              scale=cwT[:, 2, fc:fc + 1], bias=cbT[:, fc:fc + 1]),
                  r=[kb(bz)] + CONST, w=[ak])
            P.dve(lambda e, zb=zb, a3=a3, fc=fc: e.scalar_tensor_tensor(out=a3, in0=zb[:, :, 1:Lseq + 1], scalar=cwT[:, 1, fc:fc + 1], in1=a3,
                                                                        op0=ALU.mult, op1=ALU.add), r=[zk, ak], w=[ak])
            P.dve(lambda e, zb=zb, a3=a3, fc=fc: e.scalar_tensor_tensor(out=a3, in0=zb[:, :, 0:Lseq], scalar=cwT[:, 0, fc:fc + 1], in1=a3,
                                                                        op0=ALU.mult, op1=ALU.add), r=[zk, ak], w=[ak])
            P.act(lambda e, i=i: e.activation(out=za[:, i, S_], in_=za[:, i, S_], func=AF.Silu), r=[ak], w=[ak])
            P.dve(lambda e, i=i, bu=bu, fc=fc: e.tensor_tensor(out=hidT[:, fc, S_], in0=za[:, i, S_], in1=PS(bu, NT), op=ALU.mult),
                  r=[ak, kb(bu)], w=[("hidT", fc)])
        nb[0] = 0
        for fc in range(NFC):
            s = stream(wsc_fout[fc], ("wsc_fout", fc))
            for b in range(NB):
                for hf in range(2):
                    P.pe(lambda e, fc=fc, s=s, b=b, hf=hf: e.matmul(PS(2 * b + hf, 512), lhsT=hidT[:, fc, b * 128:(b + 1) * 128],
                                                                   rhs=ring[:, s, hf * 512:(hf + 1) * 512], start=(fc == 0), stop=(fc == NFC - 1)),
                         r=[("ring", s), ("hidT", fc)], w=[kb(2 * b + hf)])
        for b in range(NB):
            oi = 0
            state["ost"] += 1
            norm_res(b, PSP(2 * b), gpostf_bc, ost(b), ostk(b), ("x", b))
        for b in range(NB):
            P.dma("pool", "oy%d" % b, ydst[t0 + b * 128:t0 + (b + 1) * 128, :] if kind == "p" else ydst, ost(b),
                  r=ostk(b), w=["o_y"])
        nb[0] = 0

        if last:
            emit_state_outputs(kind, nseq)

    def emit_state_outputs(kind, nseq):
        hcar = hcar_p if kind == "p" else hcar_s
        zcar = zcar_p if kind == "p" else zcar_s
        for rc in range(14):
            np_ = RC_NP.get(rc, 128)
            bt = bank()
            P.pe(lambda e, rc=rc, np_=np_, bt=bt: e.transpose(out=PS(bt, np_, nseq), in_=hcar[0:np_, rc, :], identity=idf[0:np_, 0:np_]),
                 r=[("hcar", rc)] + CONST, w=[kb(bt)])
            c0 = rc * 128 if rc < 13 else 1600
            P.act(lambda e, bt=bt, np_=np_, c0=c0: e.copy(out=rowbuf[0:nseq, c0:c0 + np_], in_=PS(bt, np_, nseq)), r=[kb(bt)], w=["rowbuf", "XA"] + (HID_ALL if rc == 0 else []))
        P.dma("pool", "o_sh", shp if kind == "p" else shs, rowbuf[0:nseq, 0:DSH], r=["rowbuf"] + HID_ALL, w=["o_sh" + kind, "XA"])
        for fc in range(NFC):
            bt = bank()
            P.pe(lambda e, fc=fc, bt=bt: e.transpose(out=PS(bt, 128, 2 * nseq), in_=zcar[:, fc, :, :].rearrange("p s j -> p (s j)"),
                                                     identity=idf[:]), r=[("zcar", fc)] + CONST, w=[kb(bt)])
            P.act(lambda e, fc=fc, bt=bt: e.copy(out=sst[0:2 * nseq, fc * 128:(fc + 1) * 128], in_=PS(bt, 128, 2 * nseq)), r=[kb(bt)], w=["sst", "XA"] + (HID_ALL if fc == 0 else []))
        P.dma("pool", "o_cv", convp if kind == "p" else convs, sst[0:2 * nseq, :], r=["sst"] + HID_ALL, w=["o_cv" + kind, "XA"])
        if kind == "p":
            for hp in range(4):
                bt = bank()
                P.pe(lambda e, hp=hp, bt=bt: e.transpose(out=PS(bt, 128, 64), in_=Sm[:, hp, :], identity=idf[:]),
                     r=[("Sm", hp)] + CONST, w=[kb(bt)])
                i = hp % 2
                P.act(lambda e, bt=bt, i=i: e.copy(out=Sld[:, i, :], in_=PS(bt, 128, 64)), r=[kb(bt)], w=["t5"])
                P.dma("pool", "o_wk%d" % i, wkvp[2 * hp:2 * hp + 2].rearrange("h i j -> i h j"),
                      Sld[:, i, :].rearrange("p (h j) -> p h j", h=2), r=["t5"], w=["o_wkvp"])

    SmS = T("SmS", [128, NSEQ_S, 64])
    SbS = T("SbS", [128, NSEQ_S, 64], BF16)

    P.pool(lambda e: e.memset(hcar_p[:], 0.0), w=[("hcar", rc) for rc in range(14)])
    P.pool(lambda e: e.memset(zcar_p[:], 0.0), w=[("zcar", fc) for fc in range(NFC)])
    P.pool(lambda e: e.memset(kbuf[:], 0.0), w=["kbuf"])

    for ti in range(N_TILES):
        emit_tile("p", ti)

    if DO_SAMPLE:
        P.dma("sp", "clss", scanm_s[:], c_scanm_s, w=["scconst"])
        cast_const(m_sown[:], c_mask_sown, 128, 128, bcast4=True)
        for g in range(4):
            cast_const(m_scache[:, 4 * g:4 * g + 4, :].rearrange("p a b -> p (a b)"), c_mask_scache[:, 4 * g:4 * g + 4, :].rearrange("p a b -> p (a b)"), 128, 512)
        P.ops["dve"][-1].deps.add(P.last_w["scconst"])
        P.dma("pool", "hst", rowbuf[0:NSEQ_S, :], sshift, r=[], w=["rowbuf", "XA"] + HID_ALL)
        for rc in range(14):
            np_ = RC_NP.get(rc, 128)
            c0 = rc * 128 if rc < 13 else 1600
            bt = bank()
            P.pe(lambda e, bt=bt, np_=np_, c0=c0: e.transpose(out=PS(bt, NSEQ_S, np_), in_=rowbuf[0:NSEQ_S, c0:c0 + np_], identity=idf[0:NSEQ_S, 0:NSEQ_S]),
                 r=["rowbuf"] + CONST, w=[kb(bt), "XA"])
            P.act(lambda e, bt=bt, np_=np_, rc=rc: e.copy(out=hcar_s[0:np_, rc, :], in_=PS(bt, NSEQ_S, np_)), r=[kb(bt)], w=[("hcar", rc)])
        P.dma("pool", "hst2", sst[0:2 * NSEQ_S, :], sconv, r=[], w=["sst", "XA"] + HID_ALL)
        for fc in range(NFC):
            bt = bank()
            P.pe(lambda e, bt=bt, fc=fc: e.transpose(out=PS(bt, 2 * NSEQ_S, 128), in_=sst[0:2 * NSEQ_S, fc * 128:(fc + 1) * 128],
                                                     identity=idf[0:2 * NSEQ_S, 0:2 * NSEQ_S]), r=["sst"] + CONST, w=[kb(bt), "XA"])
            P.act(lambda e, bt=bt, fc=fc: e.copy(out=zcar_s[:, fc, :, :].rearrange("p s j -> p (s j)"), in_=PS(bt, 2 * NSEQ_S, 128)),
                  r=[kb(bt)], w=[("zcar", fc)])
        emit_tile("s", 0)

    okeys = [k for k in P.last_w.keys() if isinstance(k, str) and k.startswith("o_")]
    P.add("sp", lambda e: None, reads=okeys)
    P.add("pool", lambda e: None, reads=okeys)
    P.emit()
    return nc, st


def _consts():
    c = {}
    c["c_ident"] = np.eye(128, dtype=np.float32)
    half = 32
    inv = (np.float32(10000.0) ** (-np.arange(half, dtype=np.float32) / np.float32(half))).astype(np.float32)
    p = np.arange(128)
    f = (p % 64) % 32
    sign = np.where((p % 64) < 32, -1.0, 1.0).astype(np.float32)

    def tabs(pos):
        ang = pos.astype(np.float32)[None, :] * inv[f][:, None]
        return np.cos(ang).astype(np.float32), (np.sin(ang).astype(np.float32) * sign[:, None]).astype(np.float32)

    c["c_cos_p"], c["c_sin_p"] = tabs(np.arange(SEQ))
    pos_s = 16384 + (np.arange(128) % LS)
    c["c_cos_s"], c["c_sin_s"] = tabs(pos_s)
    s = np.arange(128)[:, None]
    q = np.arange(128)[None, :]
    c["c_mask_own"] = (s <= q).astype(np.float32)
    c["c_mask_prev"] = (s >= q).astype(np.float32)
    c["c_negmask_own"] = np.where(s <= q, 0.0, -30000.0).astype(np.float32)
    c["c_negmask_prev"] = np.where(s >= q, 0.0, -30000.0).astype(np.float32)
    c["c_mask_sown"] = ((s // LS == q // LS) & (s <= q)).astype(np.float32)
    msc = np.zeros((128, NSEQ_S, 128), np.float32)
    for sq in range(NSEQ_S):
        msc[:, sq, :] = ((q // LS == sq) & (s >= (q % LS))).astype(np.float32)
    c["c_mask_scache"] = msc
    bo = np.zeros((128, 128), np.float32)
    bo[:64, :64] = 1
    bo[64:, 64:] = 1
    c["c_blockones"] = bo
    strict = (s < q).astype(np.float32)
    incl = (s <= q).astype(np.float32)
    c["c_m2b"] = np.stack([strict, -incl], axis=1).astype(np.float32)
    c["c_m2k"] = np.stack([strict, incl], axis=1).astype(np.float32)
    c["c_mT"] = (q < s).astype(np.float32)
    mp = np.ones((128, 512), np.float32)
    mp[:, 0::128] = 0
    c["c_scanm_p"] = mp
    ms = np.ones((128, 128), np.float32)
    ms[:, 0::LS] = 0
    c["c_scanm_s"] = ms
    return c


_CACHE = {}


def kernel(**inputs):
    f = lambda a: np.ascontiguousarray(np.asarray(a, dtype=np.float32))
    if "nc" not in _CACHE:
        _CACHE["nc"] = build_program()
        _CACHE["consts"] = _consts()
    nc, _st = _CACHE["nc"]
    consts = _CACHE["consts"]
    x_prompt = f(inputs["x_prompt"])
    x_sample = f(inputs["x_sample"])
    wnames = ["g_pre_mix", "w_in", "attn_sinks", "mu_shift", "w0", "w_decay_up", "a0", "w_a_up", "w_g_up", "k_k", "k_a", "r_k",
              "gn_w", "gn_b", "w_out", "g_post_mix", "g_pre_ffn", "w_ffn_in", "conv_w", "conv_b", "w_ffn_out", "g_post_ffn"]
    shared = {}
    for n in wnames:
        a = f(inputs[n])[0]
        if n == "r_k":
            a = a.reshape(512)
        shared[n] = np.ascontiguousarray(a)
    shared.update(consts)
    in_maps = []
    for c in range(8):
        m = dict(shared)
        m["xp"] = x_prompt[c % 4]
        sl = slice(c * NSEQ_S, (c + 1) * NSEQ_S)
        m["xs"] = np.ascontiguousarray(x_sample[sl].reshape(128, D))
        m["ck"] = np.ascontiguousarray(f(inputs["cache_k_win"])[0, sl].reshape(NSEQ_S, 128, 128))
        m["cv"] = np.ascontiguousarray(f(inputs["cache_v_win"])[0, sl].reshape(NSEQ_S, 128, 128))
        m["sshift"] = np.ascontiguousarray(f(inputs["state_shift"])[0, sl])
        m["swkv"] = np.ascontiguousarray(f(inputs["state_wkv"])[0, sl])
        m["sconv"] = np.ascontiguousarray(f(inputs["state_conv"])[0, sl].reshape(NSEQ_S * 2, DFF))
        in_maps.append(m)
    ncores = int(os.environ.get("MK_CORES", "8"))
    res = run_bass_kernel_spmd(nc, in_maps[:ncores], core_ids=list(range(ncores)))
    R = list(res.results) + [res.results[0]] * (8 - ncores)
    cat = lambda k, rng: np.stack([R[c][k] for c in rng], axis=0)
    y_prompt = cat("yp", range(4)).reshape(4, SEQ, D)
    y_sample = np.concatenate([R[c]["ys"].reshape(NSEQ_S, LS, D) for c in range(8)], axis=0)
    nkp = cat("kwp", range(4)).reshape(1, 4, 128, 2, 64)
    nvp = cat("vwp", range(4)).reshape(1, 4, 128, 2, 64)
    nsp = cat("shp", range(4)).reshape(1, 4, DSH)
    nwp = cat("wkvp", range(4)).reshape(1, 4, 8, 64, 64)
    ncp = cat("convp", range(4)).reshape(1, 4, 2, DFF)
    nks = np.concatenate([R[c]["kws"] for c in range(8)], axis=0).reshape(1, 128, 128, 2, 64)
    nvs = np.concatenate([R[c]["vws"] for c in range(8)], axis=0).reshape(1, 128, 128, 2, 64)
    nss = np.concatenate([R[c]["shs"] for c in range(8)], axis=0).reshape(1, 128, DSH)
    nws = np.concatenate([R[c]["wkvs"] for c in range(8)], axis=0).reshape(1, 128, 8, 64, 64)
    ncs = np.concatenate([R[c]["convs"].reshape(NSEQ_S, 2, DFF) for c in range(8)], axis=0).reshape(1, 128, 2, DFF)
    outs = (y_prompt, y_sample, nkp, nvp, nsp, nwp, ncp, nks, nvs, nss, nws, ncs)
    return tuple(np.ascontiguousarray(o.astype(np.float32)) for o in outs)
```

```python
import os
import contextlib
import numpy as np
import concourse.bass as bass
import concourse.mybir as mybir
from concourse.bass_utils import run_bass_kernel_spmd

F32 = mybir.dt.float32
BF16 = mybir.dt.bfloat16
ALU = mybir.AluOpType
AF = mybir.ActivationFunctionType
AX = mybir.AxisListType

ENGS = ("pe", "act", "dve", "pool", "sp")


class Op:
    __slots__ = ("eng", "fn", "deps", "dma", "signal", "sigval", "waits", "has_dependents")

    def __init__(self, eng, fn, dma):
        self.eng = eng
        self.fn = fn
        self.dma = dma
        self.deps = set()
        self.signal = False
        self.sigval = 0
        self.waits = []
        self.has_dependents = False


class Prog:
    def __init__(self, nc, same_engine_sync=True):
        self.nc = nc
        self.ops = {e: [] for e in ENGS}
        self.last_w = {}
        self.readers = {}
        self.same_engine_sync = same_engine_sync
        self.nops = 0

    def add(self, eng, fn, reads=(), writes=(), dma=None):
        op = Op(eng, fn, dma)
        deps = op.deps
        for k in reads:
            w = self.last_w.get(k)
            if w is not None:
                deps.add(w)
            if isinstance(k, tuple) and k[0] == "ps":
                for r in self.readers.get(k, ()):
                    if r.eng != eng:
                        deps.add(r)
        for k in writes:
            w = self.last_w.get(k)
            if w is not None:
                deps.add(w)
            for r in self.readers.get(k, ()):
                deps.add(r)
        deps.discard(op)
        for k in reads:
            lst = self.readers.setdefault(k, [])
            if dma is None:
                for i_, r_ in enumerate(lst):
                    if r_.dma is None and r_.eng == eng:
                        lst[i_] = op
                        break
                else:
                    lst.append(op)
            else:
                lst.append(op)
        for k in writes:
            self.last_w[k] = op
            self.readers[k] = []
        self.ops[eng].append(op)
        self.nops += 1
        return op

    def pe(self, fn, r=(), w=()):
        return self.add("pe", fn, r, w)

    def act(self, fn, r=(), w=()):
        return self.add("act", fn, r, w)

    def dve(self, fn, r=(), w=()):
        return self.add("dve", fn, r, w)

    def pool(self, fn, r=(), w=()):
        return self.add("pool", fn, r, w)

    def dma(self, q, sem, out, in_, r=(), w=(), **kw):
        return self.add(q, lambda e: e.dma_start(out=out, in_=in_, **kw), r, w, dma=sem)

    def _skip(self, d, op):
        return d.dma is None and d.eng == op.eng and (op.eng == "pe" or not self.same_engine_sync)

    def finalize(self):
        for e in ENGS:
            for op in self.ops[e]:
                for d in op.deps:
                    if not self._skip(d, op):
                        d.has_dependents = True
        eng_cnt = {e: 0 for e in ENGS}
        dma_cnt = {}
        for e in ENGS:
            for op in self.ops[e]:
                if op.dma is not None:
                    dma_cnt[op.dma] = dma_cnt.get(op.dma, 0) + 16
                    op.sigval = dma_cnt[op.dma]
                    op.signal = True
                elif op.has_dependents:
                    eng_cnt[e] += 1
                    op.sigval = eng_cnt[e]
                    op.signal = True
        for e in ENGS:
            for op in self.ops[e]:
                ws = {}
                for d in op.deps:
                    if self._skip(d, op):
                        continue
                    key = ("dma", d.dma) if d.dma is not None else ("eng", d.eng)
                    if ws.get(key, 0) < d.sigval:
                        ws[key] = d.sigval
                op.waits = list(ws.items())
        self.dma_names = sorted(dma_cnt.keys())

    def emit(self):
        nc = self.nc
        self.finalize()
        with contextlib.ExitStack() as st:
            sems = {}
            for e in ENGS:
                sems[("eng", e)] = st.enter_context(nc.semaphore("s_" + e))
            for n in self.dma_names:
                sems[("dma", n)] = st.enter_context(nc.semaphore("d_" + n))
            block = st.enter_context(nc.Block())

            def replay(eobj, ename):
                known = {}
                for op in self.ops[ename]:
                    for key, val in op.waits:
                        if known.get(key, 0) < val:
                            eobj.wait_ge(sems[key], val)
                            known[key] = val
                    ins = op.fn(eobj)
                    if op.signal and ins is not None:
                        if op.dma is not None:
                            ins.then_inc(sems[("dma", op.dma)], 16)
                        else:
                            ins.then_inc(sems[("eng", ename)], 1)

            @block.tensor
            def _(e):
                replay(e, "pe")

            @block.scalar
            def _(e):
                replay(e, "act")

            @block.vector
            def _(e):
                replay(e, "dve")

            @block.gpsimd
            def _(e):
                replay(e, "pool")

            @block.sync
            def _(e):
                replay(e, "sp")


D = 1024
DIN = 2464
DFF = 2816
NFC = 22
DSH = 1696
SEQ = 8192
NSEQ_S = 16
LS = 8
KAPPA = float(np.exp(-0.5))
RMS_EPS = 1e-6
GN_EPS = 64e-5
RING = 5
N_TILES = int(os.environ.get("MK_NTILES", "16"))
DO_SAMPLE = int(os.environ.get("MK_SAMPLE", "1"))
STOP = int(os.environ.get("MK_STOP", "99"))
HPS = int(os.environ.get("MK_HPS", "4"))
VAR = int(os.environ.get("MK_VAR", "0"))

RW0 = 768
WIN_CHUNKS = {}
for c in range(4):
    WIN_CHUNKS[("q", c)] = [(64 * c, 64, 0), (256 + 64 * c, 64, 64)]
    WIN_CHUNKS[("qs", c)] = [(64 * c + 32, 32, 0), (64 * c, 32, 32), (256 + 64 * c + 32, 32, 64), (256 + 64 * c, 32, 96)]
    WIN_CHUNKS[("r", c)] = [(RW0 + 128 * c, 128, 0)]
    WIN_CHUNKS[("k", c)] = [(RW0 + 512 + 128 * c, 128, 0)]
    WIN_CHUNKS[("v", c)] = [(RW0 + 1024 + 128 * c, 128, 0)]
WIN_CHUNKS[("ak", 0)] = [(512, 128, 0)]
WIN_CHUNKS[("aks", 0)] = [(544, 32, 0), (512, 32, 32), (608, 32, 64), (576, 32, 96)]
WIN_CHUNKS[("av", 0)] = [(640, 128, 0)]
WIN_CHUNKS[("lo", 0)] = [(RW0 + 1536, 64, 0)]
WIN_CHUNKS[("lg", 0)] = [(RW0 + 1600, 96, 0)]
WIN_ORDER = [("lo", 0), ("lg", 0)]
for c in range(4):
    WIN_ORDER += [("r", c), ("k", c), ("v", c)]
WIN_ORDER += [("ak", 0), ("aks", 0), ("av", 0)]
for c in range(4):
    WIN_ORDER += [("q", c), ("qs", c)]
WIN_IDX = {k: i for i, k in enumerate(WIN_ORDER)}
NWIN = len(WIN_ORDER)
RC = {}
for c in range(4):
    RC[("r", c)] = c
    RC[("k", c)] = 4 + c
    RC[("v", c)] = 8 + c
RC[("lo", 0)] = 12
RC[("lg", 0)] = 13
RC_NP = {12: 64, 13: 96}


def build_program():
    nc = bass.Bass("TRN2", target_bir_lowering=False)
    P = Prog(nc)
    st = contextlib.ExitStack()

    def din(name, shape, dt=F32):
        return nc.dram_tensor(name, list(shape), dt, kind="ExternalInput").ap()

    def dout(name, shape, dt=F32):
        return nc.dram_tensor(name, list(shape), dt, kind="ExternalOutput").ap()

    def dint(name, shape, dt=BF16):
        return nc.dram_tensor(name, list(shape), dt, kind="Internal").ap()

    def T(name, shape, dt=F32):
        return st.enter_context(nc.sbuf_tensor(name, list(shape), dt))

    xp = din("xp", [SEQ, D])
    xs = din("xs", [128, D])
    ck = din("ck", [NSEQ_S, 128, 128])
    cv = din("cv", [NSEQ_S, 128, 128])
    sshift = din("sshift", [NSEQ_S, DSH])
    swkv = din("swkv", [NSEQ_S, 8, 64, 64])
    sconv = din("sconv", [NSEQ_S * 2, DFF])
    g_pre_mix = din("g_pre_mix", [D])
    w_in = din("w_in", [D, DIN])
    attn_sinks = din("attn_sinks", [8])
    mu_shift = din("mu_shift", [DSH])
    w0 = din("w0", [512])
    w_decay_up = din("w_decay_up", [32, 512])
    a0 = din("a0", [512])
    w_a_up = din("w_a_up", [32, 512])
    w_g_up = din("w_g_up", [96, 512])
    k_k = din("k_k", [512])
    k_a = din("k_a", [512])
    r_k = din("r_k", [512])
    gn_w = din("gn_w", [512])
    gn_b = din("gn_b", [512])
    w_out = din("w_out", [D, D])
    g_post_mix = din("g_post_mix", [D])
    g_pre_ffn = din("g_pre_ffn", [D])
    w_ffn_in = din("w_ffn_in", [D, 2 * DFF])
    conv_w = din("conv_w", [3, DFF])
    conv_b = din("conv_b", [DFF])
    w_ffn_out = din("w_ffn_out", [DFF, D])
    g_post_ffn = din("g_post_ffn", [D])
    c_ident = din("c_ident", [128, 128])
    c_cos_p = din("c_cos_p", [128, SEQ])
    c_sin_p = din("c_sin_p", [128, SEQ])
    c_cos_s = din("c_cos_s", [128, 128])
    c_sin_s = din("c_sin_s", [128, 128])
    c_mask_own = din("c_mask_own", [128, 128])
    c_mask_prev = din("c_mask_prev", [128, 128])
    c_negmask_own = din("c_negmask_own", [128, 128])
    c_negmask_prev = din("c_negmask_prev", [128, 128])
    c_mask_sown = din("c_mask_sown", [128, 128])
    c_mask_scache = din("c_mask_scache", [128, NSEQ_S, 128])
    c_blockones = din("c_blockones", [128, 128])
    c_m2b = din("c_m2b", [128, 2, 128])
    c_m2k = din("c_m2k", [128, 2, 128])
    c_mT = din("c_mT", [128, 128])
    c_scanm_p = din("c_scanm_p", [128, 512])
    c_scanm_s = din("c_scanm_s", [128, 128])

    yp = dout("yp", [SEQ, D])
    ys = dout("ys", [128, D])
    kwp = dout("kwp", [128, 128])
    vwp = dout("vwp", [128, 128])
    shp = dout("shp", [1, DSH])
    wkvp = dout("wkvp", [8, 64, 64])
    convp = dout("convp", [2, DFF])
    kws = dout("kws", [NSEQ_S, 128, 128])
    vws = dout("vws", [NSEQ_S, 128, 128])
    shs = dout("shs", [NSEQ_S, DSH])
    wkvs = dout("wkvs", [NSEQ_S, 8, 64, 64])
    convs = dout("convs", [NSEQ_S * 2, DFF])

    wsc_in = dint("wsc_in", [NWIN, 128, 1024])
    wsc_out = dint("wsc_out", [8, 128, 1024])
    wsc_fin = dint("wsc_fin", [2 * NFC, 128, 1024])
    wsc_fout = dint("wsc_fout", [NFC, 128, 1024])

    pp = [st.enter_context(nc.psum_tensor("pp%d" % i, [128, 1024], F32)) for i in range(4)]
    nb = [0]

    def bank():
        b = nb[0] % 8
        nb[0] += 1
        return b

    def pair():
        if nb[0] % 2:
            nb[0] += 1
        b = nb[0] % 8
        nb[0] += 2
        return b

    def PS(b, n=512, np_=128, p0=0):
        o = (b % 2) * 512
        return pp[b // 2][p0:p0 + np_, o:o + n]

    def PSP(b):
        return pp[b // 2][:, :]

    def PSB(b, np_=128):
        return pp[b // 2].bitcast(BF16)[0:np_, (b % 2) * 1024:(b % 2) * 1024 + 1024]

    def kb(b):
        return ("ps", b)

    idf = T("idf", [128, 128])
    idb = T("idb", [128, 128], BF16)
    gTpre = T("gTpre", [128, 8])
    gTffn = T("gTffn", [128, 8])
    gpost_bc = T("gpost_bc", [128, D])
    gpostf_bc = T("gpostf_bc", [128, D])
    gnw_bc = T("gnw_bc", [128, 512])
    gnb_bc = T("gnb_bc", [128, 512])
    esink = T("esink", [128, 8])
    w0T = T("w0T", [128, 4])
    a0T = T("a0T", [128, 4])
    kkT = T("kkT", [128, 4])
    kaT = T("kaT", [128, 4])
    rkT = T("rkT", [128, 4])
    muT = T("muT", [128, 14])
    ommT = T("ommT", [128, 14])
    cwT = T("cwT", [128, 3, NFC])
    cbT = T("cbT", [128, NFC])
    Wd_b = T("Wd_b", [32, 512], BF16)
    Wa_b = T("Wa_b", [64, 512], BF16)
    Wg_b = T("Wg_b", [96, 512], BF16)
    blk1 = T("blk1", [128, 128], BF16)
    m_own = T("m_own", [128, 4, 128], BF16)
    m_prev = T("m_prev", [128, 4, 128], BF16)
    m2b = T("m2b", [128, 2, 128], BF16)
    m2k = T("m2k", [128, 2, 128], BF16)
    mTl = T("mTl", [128, 128], BF16)
    scanm_p = T("scanm_p", [128, 512])
    cstage = T("cstage", [128, 512])

    ld_n = [0]

    def cload(dst, src, xw=(), **kw):
        i = ld_n[0]
        ld_n[0] += 1
        k = ("c", i)
        P.dma("sp", "cl%d" % i, dst, src, w=[k] + list(xw), **kw)
        return k

    def cload_cast(dst_bf, src, np_, shape_free, eng="dve"):
        nfree = int(np.prod(shape_free))
        stg = cstage[0:np_, 0:nfree]
        k = ("c", ld_n[0])
        ld_n[0] += 1
        P.dma("sp", "cst", stg, src, w=["cstage"])
        P.dve(lambda e: e.tensor_copy(out=dst_bf, in_=stg), r=["cstage"], w=[k, "cstage_rd"])
        return k

    NSC = ALLOW = dict(allow_slow_non_contiguous=True)
    CK = []
    CK.append(cload(idf[:], c_ident))
    P.dve(lambda e: e.tensor_copy(out=idb[:], in_=idf[:]), r=[CK[-1]], w=["idb"])
    CK.append(cload(gTpre[:], g_pre_mix.rearrange("(c p) -> p c", p=128), **NSC))
    kgpre = CK[-1]
    CK.append(cload(gTffn[:], g_pre_ffn.rearrange("(c p) -> p c", p=128), **NSC))
    kgffn = CK[-1]
    CK.append(cload(gpost_bc[:], g_post_mix.partition_broadcast(128)))
    CK.append(cload(gpostf_bc[:], g_post_ffn.partition_broadcast(128)))
    CK.append(cload(gnw_bc[:], gn_w.partition_broadcast(128)))
    CK.append(cload(gnb_bc[:], gn_b.partition_broadcast(128)))
    CK.append(cload(esink[:], attn_sinks.partition_broadcast(128)))
    P.act(lambda e: e.activation(out=esink[:], in_=esink[:], func=AF.Exp), r=[CK[-1]], w=["esink"])
    for tt, src in ((w0T, w0), (a0T, a0), (kkT, k_k), (kaT, k_a), (rkT, r_k)):
        CK.append(cload(tt[:], src.rearrange("(c p) -> p c", p=128), **NSC))
    P.pool(lambda e: e.memset(muT[:], 0.0), w=["muT"])
    CK.append(cload(muT[:, 0:12], mu_shift[0:1536].rearrange("(c p) -> p c", p=128), xw=["muT"], **NSC))
    CK.append(cload(muT[0:64, 12:13], mu_shift[1536:1600].rearrange("(p c) -> p c", c=1), xw=["muT"], **NSC))
    CK.append(cload(muT[0:96, 13:14], mu_shift[1600:1696].rearrange("(p c) -> p c", c=1), xw=["muT"], **NSC))
    P.dve(lambda e: e.tensor_scalar(out=ommT[:], in0=muT[:], scalar1=-1.0, scalar2=1.0, op0=ALU.mult, op1=ALU.add),
          r=["muT"], w=["muT2"])
    CK.append(cload(cwT[:], conv_w.rearrange("j (c p) -> p j c", p=128), **NSC))
    CK.append(cload(cbT[:], conv_b.rearrange("(c p) -> p c", p=128), **NSC))
    CK.append(cload(scanm_p[:], c_scanm_p))

    def cast_const(dst, src, np_, nfree, bcast4=False):
        stg = cstage[0:np_, 0:nfree]
        P.dma("sp", "cst", stg, src, w=["cstage"])
        if bcast4:
            P.dve(lambda e: e.tensor_copy(out=dst, in_=stg.unsqueeze(1).to_broadcast([np_, 4, nfree])),
                  r=["cstage"], w=["cconst", "cstage"])
        else:
            P.dve(lambda e: e.tensor_copy(out=dst, in_=stg), r=["cstage"], w=["cconst", "cstage"])

    cast_const(blk1[:], c_blockones, 128, 128)
    cast_const(m_own[:], c_negmask_own, 128, 128, bcast4=True)
    cast_const(m_prev[:], c_negmask_prev, 128, 128, bcast4=True)
    cast_const(m2b[:].rearrange("p a b -> p (a b)"), c_m2b.rearrange("p a b -> p (a b)"), 128, 256)
    cast_const(m2k[:].rearrange("p a b -> p (a b)"), c_m2k.rearrange("p a b -> p (a b)"), 128, 256)
    cast_const(mTl[:], c_mT, 128, 128)
    cast_const(Wd_b[:], w_decay_up, 32, 512)
    P.dma("sp", "cst", cstage[32:64, 0:512], w_a_up, w=["cstage"])
    P.dve(lambda e: e.tensor_copy(out=Wa_b[32:64, :], in_=cstage[32:64, 0:512]), r=["cstage"], w=["cconst", "cstage"])
    cast_const(Wg_b[:], w_g_up, 96, 512)
    CONST = CK + ["idb", "esink", "cconst", "muT", "muT2"]

    x_t = T("x_t", [128, 4, D])
    bufA = T("bufA", [128, 8, 512], BF16)
    BUFA_ALL = [("bufA", b) for b in range(4)]
    wp_n = [0]

    def prep_chunk(dst_dram, dkey, loads, gT=None, gkey=None):
        i = wp_n[0] % 4
        wp_n[0] += 1
        stage = x_t[:, i, :]
        wbs_i = bufA[:, 2 * i:2 * i + 2, :].rearrange("p a b -> p (a b)")
        for mk, src in loads:
            P.dma("sp", "wl%d" % i, mk(stage), src, w=[("x", i)])
        if gT is not None:
            P.dve(lambda e: e.tensor_tensor(out=wbs_i.rearrange("p (c n) -> p c n", c=8),
                                            in0=stage.rearrange("p (c n) -> p c n", c=8),
                                            in1=gT[:].unsqueeze(2).to_broadcast([128, 8, 128]), op=ALU.mult),
                  r=[("x", i), gkey], w=[("wbs", i)])
        else:
            P.dve(lambda e: e.tensor_copy(out=wbs_i, in_=stage), r=[("x", i)], w=[("wbs", i)])
        P.dma("pool", "ws%d" % i, dst_dram, wbs_i, r=[("wbs", i)], w=[dkey])

    P.pool(lambda e: e.memset(x_t[:, :, :], 0.0), w=[("x", b_) for b_ in range(4)])
    win3 = w_in.rearrange("(c p) n -> p c n", p=128)
    for ci, key in enumerate(WIN_ORDER):
        loads = []
        for (cs, n, o) in WIN_CHUNKS[key]:
            loads.append((lambda s, o=o, n=n: s.rearrange("p (c n) -> p c n", c=8)[:, :, o:o + n], win3[:, :, cs:cs + n]))
        if key[0] in ("lo", "lg"):
            pass
        prep_chunk(wsc_in[ci], ("wsc_in", ci), loads, gTpre, kgpre)
    wfi3 = w_ffn_in.rearrange("(c p) n -> p c n", p=128)
    for fc in range(NFC):
        for zu in range(2):
            cs = zu * DFF + fc * 128
            prep_chunk(wsc_fin[2 * fc + zu], ("wsc_fin", 2 * fc + zu),
                       [(lambda s: s.rearrange("p (c n) -> p c n", c=8), wfi3[:, :, cs:cs + 128])], gTffn, kgffn)
    for c in range(8):
        prep_chunk(wsc_out[c], ("wsc_out", c), [(lambda s: s, w_out[c * 128:(c + 1) * 128, :])])
    for c in range(NFC):
        prep_chunk(wsc_fout[c], ("wsc_fout", c), [(lambda s: s, w_ffn_out[c * 128:(c + 1) * 128, :])])

    P.dve(lambda e: e.memset(bufA[0:1, 0, 0:2], 0.0), r=[("wbs", i_) for i_ in range(4)], w=BUFA_ALL + [("wbs", i_) for i_ in range(4)])
    ring = T("ring", [128, RING, 1024], BF16)
    xn = T("xn", [128, D], BF16)
    junk = T("junk", [128, D], BF16)
    cs_t = T("cs_t", [128, 2, 512])
    qT = T("qT", [128, 4, 512], BF16)
    kbuf = T("kbuf", [128, 5 * 128], BF16)
    vtokf = T("vtokf", [128, 128])
    Vaug = T("Vaug", [128, 5, 2, 65], BF16)
    Et = T("Et", [128, 4, 512], BF16)
    oacc = T("oacc", [128, 2, 4, 65])
    den = T("den", [128, 8])
    mix = T("mix", [128, 4, D], BF16)
    hidT = T("hidT", [128, NFC, 512], BF16)
    zbuf = T("zbuf", [128, 1, 640])
    za = T("za", [128, 1, 512])
    zcar_p = T("zcar_p", [128, NFC, 1, 2])
    zcar_s = T("zcar_s", [128, NFC, NSEQ_S, 2])
    sstat = T("sstat", [128, 16])
    hbuf = T("hbuf", [128, 1, 640])
    hcar_p = T("hcar_p", [128, 14, 1])
    hcar_s = T("hcar_s", [128, 14, NSEQ_S])
    dtmp = T("dtmp", [128, 512])
    hs_lo = T("hs_lo", [64, 512])
    hs_lg = T("hs_lg", [96, 512])
    lorab = T("lorab", [64, 512], BF16)
    sgb = T("sgb", [96, 512], BF16)
    rkv = T("rkv", [128, 1, 3, 512])
    tq = [T("tq%d" % i, [128, 512]) for i in range(7)]
    yq, yq2, kf32, vT32 = tq[0], tq[1], tq[3], tq[4]
    vblk_s = T("vblk_s", [128, 512], BF16)
    Sld = tq[5][0:64, 0:256].rearrange("p (a b) -> p a b", a=2)
    sqb = T("sqb", [128, 512], BF16)
    Kt = T("Kt", [128, 512], BF16)
    Bt = T("Bt", [128, 512], BF16)
    KR = T("KR", [128, 1024], BF16)
    KgL = T("KgL", [128, 512], BF16)
    BgLn = T("BgLn", [128, 512], BF16)
    vb = T("vb", [128, 512], BF16)
    prodb = T("prodb", [128, 512], BF16)
    gL2 = T("gL2", [128, 2, 16])
    nkcl = T("nkcl", [128, 16])
    vtok = T("vtok", [128, 2048], BF16)
    KgLtok = T("KgLtok", [128, 512], BF16)
    BgLtok = T("BgLtok", [128, 512], BF16)
    NQ2 = T("NQ2", [128, 2048], BF16)
    KQ2 = T("KQ2", [128, 2048], BF16)
    NT2 = T("NT2", [128, 1024], BF16)
    T32 = T("T32", [128, 1024])
    Tb = T("Tb", [128, 1024], BF16)
    XTs = T("XTs", [128, 128], BF16)
    SATs = T("SATs", [128, 128], BF16)
    ytok = T("ytok", [128, 4, 512])
    rkb = T("rkb", [128, 4, 8])
    Sm = T("Sm", [128, 4, 64])
    Sb = T("Sb", [128, 4, 64], BF16)
    gstat = T("gstat", [128, 4, 8])
    hid32 = hidT[:].rearrange("p a b -> p (a b)").bitcast(F32)
    HID_ALL = [("hidT", fc) for fc in range(NFC)]
    rowbuf = hid32[0:32, 0:DSH]
    sst = hid32[0:32, 1792:1792 + DFF]
    scanm_s = T("scanm_s", [128, 128])
    m_sown = T("m_sown", [128, 4, 128], BF16)
    m_scache = T("m_scache", [128, NSEQ_S, 128], BF16)
    ckT = T("ckT", [128, 128], BF16)
    ckf = T("ckf", [128, 2, 128])
    cVaug = T("cVaug", [128, 2, 2, 65], BF16)
    ytok_s = hid32[0:8, 0:NSEQ_S * 128].rearrange("p (s n) -> p s n", s=NSEQ_S)

    state = dict(ring_n=0, xld=0, ost=0)

    def stream(src, skey):
        n = state["ring_n"]
        state["ring_n"] += 1
        s = n % RING
        P.dma("sp", "rg%d" % s, ring[:, s, :], src, r=[skey], w=[("ring", s)])
        return s

    def emit_tile(kind, ti):
        NT = 512 if kind == "p" else 128
        NB = NT // 128
        nseq, Lseq = (1, 512) if kind == "p" else (NSEQ_S, LS)
        L = 128 if kind == "p" else LS
        nch = NT // L
        t0 = ti * 512
        first = (kind == "p" and ti == 0)
        last = (kind == "s") or (ti == N_TILES - 1)
        xsrc = xp if kind == "p" else xs
        ydst = yp if kind == "p" else ys
        hcar = hcar_p if kind == "p" else hcar_s
        zcar = zcar_p if kind == "p" else zcar_s
        scanm = scanm_p if kind == "p" else scanm_s
        S_ = slice(0, NT)

        if kind == "p":
            P.dma("pool", "cs", cs_t[:, 0, S_], c_cos_p[:, t0:t0 + NT], w=["cs_t"])
            P.dma("pool", "cs", cs_t[:, 1, S_], c_sin_p[:, t0:t0 + NT], w=["cs_t"])
        else:
            P.dma("pool", "cs", cs_t[:, 0, S_], c_cos_s, w=["cs_t"])
            P.dma("pool", "cs", cs_t[:, 1, S_], c_sin_s, w=["cs_t"])

        for b in range(NB):
            P.dma("sp", "xl%d" % b, x_t[:, b, :], xsrc[t0 + b * 128:t0 + (b + 1) * 128, :] if kind == "p" else xsrc,
                  w=[("x", b)])
            sc = sstat[:, b:b + 1]
            P.act(lambda e, b=b, sc=sc: e.activation(out=junk[:], in_=x_t[:, b, :], func=AF.Square, accum_out=sc),
                  r=[("x", b)], w=["junk", ("ss", b)])
            P.act(lambda e, sc=sc: e.activation(out=sc, in_=sc, func=AF.Sqrt, scale=1.0 / D, bias=RMS_EPS),
                  r=[("ss", b)], w=[("ss", b)])
            P.dve(lambda e, sc=sc: e.reciprocal(out=sc, in_=sc), r=[("ss", b)], w=[("ss", b)])
        for b in range(NB):
            sc = sstat[:, b:b + 1]
            P.dve(lambda e, b=b, sc=sc: e.tensor_scalar(out=xn[:], in0=x_t[:, b, :], scalar1=sc, scalar2=None, op0=ALU.mult),
                  r=[("x", b), ("ss", b)], w=["xn"])
            bk = bank()
            for c in range(8):
                P.pe(lambda e, c=c, bk=bk: e.transpose(out=PSB(bk)[:, c * 128:(c + 1) * 128], in_=xn[:, c * 128:(c + 1) * 128],
                                                      identity=idb[:]), r=["xn", "idb"], w=[kb(bk)])
            P.act(lambda e, b=b, bk=bk: e.copy(out=bufA[:, :, b * 128:(b + 1) * 128],
                                               in_=PSB(bk).rearrange("p (c n) -> p c n", c=8)),
                  r=[kb(bk)], w=[("bufA", b)])
        bufA_keys = [("bufA", b) for b in range(NB)]

        if kind == "s" and STOP <= 1:
            return
        def inproj(key, ncols):
            ci = WIN_IDX[key]
            s = stream(wsc_in[ci], ("wsc_in", ci))
            bk = bank()
            for c in range(8):
                P.pe(lambda e, c=c, s=s, bk=bk: e.matmul(PS(bk, NT, ncols), lhsT=ring[:, s, c * 128:c * 128 + ncols],
                                                         rhs=bufA[:, c, S_], start=(c == 0), stop=(c == 7)),
                     r=[("ring", s)] + bufA_keys, w=[kb(bk)])
            return bk

        hb_n = [0]

        def shift_evac(bk, np_, rc, out_ap, okey, xw=()):
            i = 0
            hb = hbuf[0:np_, i, 0:nseq * (Lseq + 1)].rearrange("p (s l) -> p s l", s=nseq)
            hk = ("hbuf", i)
            P.act(lambda e: e.copy(out=hb[:, :, 1:Lseq + 1], in_=PS(bk, NT, np_).rearrange("p (s l) -> p s l", s=nseq)),
                  r=[kb(bk)], w=[hk])
            if VAR != 1:
                P.pool(lambda e: e.tensor_copy(out=hb[:, :, 0:1], in_=hcar[0:np_, rc, :].unsqueeze(2)),
                       r=[("hcar", rc)], w=[hk])
                P.pool(lambda e: e.tensor_copy(out=hcar[0:np_, rc, :].unsqueeze(2), in_=hb[:, :, Lseq:Lseq + 1]),
                       r=[hk], w=[("hcar", rc)])
            if VAR == 2:
                return
            d3 = dtmp[0:np_, S_].rearrange("p (s l) -> p s l", s=nseq)
            P.act(lambda e: e.activation(out=d3, in_=hb[:, :, 0:Lseq], func=AF.Copy, scale=muT[0:np_, rc:rc + 1]),
                  r=[hk, "muT"], w=["dtmp"])
            P.dve(lambda e: e.scalar_tensor_tensor(out=out_ap.rearrange("p (s l) -> p s l", s=nseq), in0=hb[:, :, 1:Lseq + 1],
                                                   scalar=ommT[0:np_, rc:rc + 1], in1=d3,
                                                   op0=ALU.mult, op1=ALU.add),
                  r=["dtmp", hk, "muT", "muT2"], w=[okey] + list(xw))

        bk = inproj(("lo", 0), 64)
        shift_evac(bk, 64, 12, hs_lo[:, S_], "hs_lo")
        P.act(lambda e: e.activation(out=lorab[0:32, S_], in_=hs_lo[0:32, S_], func=AF.Tanh), r=["hs_lo"], w=["lorab0"])
        P.act(lambda e: e.copy(out=lorab[32:64, S_], in_=hs_lo[32:64, S_]), r=["hs_lo"], w=["lorab1"])
        bk = inproj(("lg", 0), 96)
        shift_evac(bk, 96, 13, hs_lg[:, S_], "hs_lg")
        P.act(lambda e: e.activation(out=sgb[:, S_], in_=hs_lg[:, S_], func=AF.Sigmoid), r=["hs_lg"], w=["sgb"])

        if kind == "s" and STOP <= 2:
            return
        def hp_gen(hp):
            rs_i = (hp % 2) if PIPE else 0
            gLv = gL2[:, rs_i, :]
            gk = ("gL", rs_i)
            rkv_v = (lambda j: rkv[:, 0, j, S_]) if rs_i == 0 else (lambda j: hid32[:, j * 512:j * 512 + NT])
            rs, ks, vs = rkv_v(0), rkv_v(1), rkv_v(2)
            for j, nm in enumerate(("r", "k", "v")):
                bk = inproj((nm, hp), 128)
                shift_evac(bk, 128, RC[(nm, hp)], rkv_v(j), ("rkv", rs_i, j), xw=([("hidT", fc_) for fc_ in range(6)] if rs_i == 1 else []))
                yield "p0"
            kr_, kk_, kv_ = ("rkv", rs_i, 0), ("rkv", rs_i, 1), ("rkv", rs_i, 2)
            t = [x[:, S_] for x in tq]
            if kind == "s" and STOP == 32:
                return
            bw = bank()
            P.pe(lambda e, bw=bw, hp=hp: e.matmul(PS(bw, NT), lhsT=Wd_b[0:32, hp * 128:(hp + 1) * 128], rhs=lorab[0:32, S_],
                                                  start=True, stop=True), r=["lorab0", "cconst"], w=[kb(bw)])
            P.act(lambda e, bw=bw, hp=hp: e.activation(out=t[0], in_=PS(bw, NT), func=AF.Sigmoid, bias=w0T[:, hp:hp + 1]),
                  r=[kb(bw)] + CONST, w=["t0"])
            ba = bank()
            P.pe(lambda e, ba=ba, hp=hp: e.matmul(PS(ba, NT), lhsT=Wa_b[32:64, hp * 128:(hp + 1) * 128], rhs=lorab[32:64, S_],
                                                  start=True, stop=True), r=["lorab1", "cconst"], w=[kb(ba)])
            P.act(lambda e, ba=ba, hp=hp: e.activation(out=t[6], in_=PS(ba, NT), func=AF.Sigmoid, bias=a0T[:, hp:hp + 1]),
                  r=[kb(ba)] + CONST, w=["t6"])
            P.dve(lambda e: e.tensor_tensor_scan(out=t[1], data0=scanm[:, S_], data1=t[0], initial=0.0, op0=ALU.mult, op1=ALU.add),
                  r=["t0"] + CONST, w=["t1"])
            P.dve(lambda e: e.tensor_tensor(out=t[2], in0=t[1], in1=t[0], op=ALU.subtract), r=["t0", "t1"], w=["t2"])
            yield "p1"
            P.act(lambda e: e.activation(out=t[0], in_=t[1], func=AF.Exp, scale=-KAPPA), r=["t1", "t2"], w=["t0"])
            P.act(lambda e: e.activation(out=t[3], in_=t[1], func=AF.Exp, scale=KAPPA), r=["t1"], w=["t3"])
            P.act(lambda e: e.activation(out=t[2], in_=t[2], func=AF.Exp, scale=-KAPPA), r=["t2"], w=["t2"])
            yield "p1"
            cp3 = t[1].rearrange("p (c l) -> p c l", l=L)
            eg3 = t[0].rearrange("p (c l) -> p c l", l=L)
            P.dve(lambda e: e.tensor_copy(out=gLv[:, 0:nch].unsqueeze(2), in_=eg3[:, :, L - 1:L]), r=["t0"], w=[gk])
            P.dve(lambda e: e.tensor_scalar(out=nkcl[:, 0:nch].unsqueeze(2), in0=cp3[:, :, L - 1:L], scalar1=-KAPPA, scalar2=None,
                                            op0=ALU.mult), r=["t1"], w=["nkcl"])
            P.dve(lambda e: e.scalar_tensor_tensor(out=t[4].rearrange("p (c l) -> p c l", l=L), in0=cp3, scalar=KAPPA,
                                                   in1=nkcl[:, 0:nch].unsqueeze(2).to_broadcast([128, nch, L]),
                                                   op0=ALU.mult, op1=ALU.add), r=["t1", "nkcl"], w=["t4"])
            P.act(lambda e: e.activation(out=t[4], in_=t[4], func=AF.Exp), r=["t4"], w=["t4"])
            if kind == "s" and STOP == 33:
                return
            yield "p1"
            yield "p1"
            P.dve(lambda e, hp=hp: e.tensor_scalar(out=t[5], in0=ks, scalar1=kkT[:, hp:hp + 1], scalar2=None, op0=ALU.mult),
                  r=[kk_] + CONST, w=["t5"])
            P.act(lambda e: e.activation(out=sqb[:, S_], in_=t[5], func=AF.Square), r=["t5"], w=["sqb"])
            bn = bank()
            P.pe(lambda e, bn=bn: e.matmul(PS(bn, NT), lhsT=blk1[:], rhs=sqb[:, S_], start=True, stop=True),
                 r=["sqb", "cconst"], w=[kb(bn)])
            yield "p1"
            P.dve(lambda e, bn=bn: e.tensor_scalar(out=t[1], in0=PS(bn, NT), scalar1=1e-24, scalar2=None, op0=ALU.max),
                  r=[kb(bn), "t4", gk, "nkcl"], w=["t1"])
            P.act(lambda e: e.activation(out=t[1], in_=t[1], func=AF.Sqrt), r=["t1"], w=["t1"])
            P.dve(lambda e: e.reciprocal(out=t[1], in_=t[1]), r=["t1"], w=["t1"])
            P.dve(lambda e: e.tensor_tensor(out=t[5], in0=t[5], in1=t[1], op=ALU.mult), r=["t5", "t1"], w=["t5"])
            yield "p1"
            P.dve(lambda e: e.tensor_tensor(out=t[1], in0=t[5], in1=t[6], op=ALU.mult), r=["t5", "t6", "t1"], w=["t1"])
            yield "p1"
            P.dve(lambda e, hp=hp: e.tensor_scalar(out=t[6], in0=t[6], scalar1=1.0, scalar2=kaT[:, hp:hp + 1],
                                                   op0=ALU.subtract, op1=ALU.mult), r=["t6", "t1"] + CONST, w=["t6"])
            P.dve(lambda e: e.scalar_tensor_tensor(out=t[6], in0=t[6], scalar=1.0, in1=ks, op0=ALU.add, op1=ALU.mult),
                  r=["t6", kk_], w=["t6"])
            if kind == "s" and STOP == 34:
                return
            yield "p1done"
            KR4 = KR[:, 0:2 * NT].rearrange("p (c a l) -> p c a l", a=2, l=L)
            P.dve(lambda e: e.tensor_tensor(out=Kt[:, S_], in0=t[6], in1=t[3], op=ALU.mult), r=["t6", "t3"], w=["Kt"])
            P.dve(lambda e: e.tensor_tensor(out=Bt[:, S_], in0=t[1], in1=t[3], op=ALU.mult), r=["t1", "t3"], w=["Bt"])
            P.dve(lambda e: e.tensor_tensor(out=KR4[:, :, 0, :], in0=t[5].rearrange("p (c l) -> p c l", l=L),
                                            in1=t[2].rearrange("p (c l) -> p c l", l=L), op=ALU.mult), r=["t5", "t2"], w=["KR"])
            P.dve(lambda e: e.tensor_tensor(out=KR4[:, :, 1, :], in0=rs.rearrange("p (c l) -> p c l", l=L),
                                            in1=t[0].rearrange("p (c l) -> p c l", l=L), op=ALU.mult), r=[kr_, "t0"], w=["KR"])
            P.dve(lambda e: e.tensor_tensor(out=KgL[:, S_], in0=t[6], in1=t[4], op=ALU.mult), r=["t6", "t4"], w=["KgL"])
            P.dve(lambda e: e.scalar_tensor_tensor(out=BgLn[:, S_], in0=t[1], scalar=-1.0, in1=t[4], op0=ALU.mult, op1=ALU.mult),
                  r=["t1", "t4"], w=["BgLn"])
            P.act(lambda e: e.copy(out=vb[:, S_], in_=vs), r=[kv_], w=["vb"])
            P.dve(lambda e, hp=hp: e.scalar_tensor_tensor(out=prodb[:, S_], in0=rs, scalar=rkT[:, hp:hp + 1], in1=t[6],
                                                          op0=ALU.mult, op1=ALU.mult), r=[kr_, "t6"] + CONST, w=["prodb"])
            if kind == "s" and STOP == 35:
                return
            brk = bank()
            for b in range(NB):
                P.pe(lambda e, b=b, brk=brk: e.matmul(PS(brk, 2 * NB)[:, 2 * b:2 * b + 2], lhsT=prodb[:, b * 128:(b + 1) * 128],
                                                      rhs=blk1[:, 0:128:64], start=True, stop=True),
                     r=["prodb", "cconst"], w=[kb(brk)])
            P.act(lambda e, brk=brk, hp=hp: e.copy(out=rkb[:, 0:NB, 2 * hp:2 * hp + 2],
                                                   in_=PS(brk, 2 * NB).rearrange("p (b a) -> p b a", a=2)),
                  r=[kb(brk)], w=[("rkb", hp)])
            if kind == "s" and STOP == 36:
                return
            if kind == "p":
                vtok_hp = vtok[0:L, hp * nch * 128:(hp + 1) * nch * 128].rearrange("p (c n) -> p c n", n=128)
                vkey = ("vtok", hp)
            else:
                vtok_hp = vtok[0:L, 0:nch * 128].rearrange("p (c n) -> p c n", n=128)
                vkey = "vtok_s"
            if kind == "p":
                KgLtok3 = KgLtok[0:L, 0:nch * 128].rearrange("p (c n) -> p c n", n=128)
                BgLtok3 = BgLtok[0:L, 0:nch * 128].rearrange("p (c n) -> p c n", n=128)
                kBg = ["BgLtok"]
            else:
                KgLtok3 = hidT[:].rearrange("p a b -> p (a b)")[0:L, 8192:8192 + nch * 128].rearrange("p (c n) -> p c n", n=128)
                BgLtok3 = Et[:].rearrange("p a b -> p (a b)")[0:L, 0:nch * 128].rearrange("p (c n) -> p c n", n=128)
                kBg = ["BgLtok"] + [("Et", i_) for i_ in range(4)]
            for src, skey, dst3, dkey in ((vb, "vb", vtok_hp, vkey), (KgL, "KgL", KgLtok3, "KgLtok"), (BgLn, "BgLn", BgLtok3, kBg)):
                for g0 in range(0, nch, 8):
                    g1 = min(nch, g0 + 8)
                    bt = bank()
                    for c in range(g0, g1):
                        P.pe(lambda e, c=c, bt=bt, src=src, g0=g0: e.transpose(
                            out=PSB(bt, L)[:, (c - g0) * 128:(c - g0 + 1) * 128], in_=src[:, c * L:(c + 1) * L], identity=idb[:]),
                            r=[skey, "idb"], w=[kb(bt)])
                    P.act(lambda e, bt=bt, g0=g0, g1=g1, dst3=dst3: e.copy(
                        out=dst3[:, g0:g1, :], in_=PSB(bt, L)[:, 0:(g1 - g0) * 128].rearrange("p (c n) -> p c n", n=128)),
                        r=[kb(bt), "XA"], w=(dkey if isinstance(dkey, list) else [dkey]))
            if kind == "s" and STOP == 31:
                return
            if kind == "s":
                btv = bank()
                P.pe(lambda e, btv=btv: e.transpose(out=PSB(btv)[:, 0:128], in_=vb[:, 0:128], identity=idb[:]), r=["vb", "idb"], w=[kb(btv)])
                P.act(lambda e, btv=btv, hp=hp: e.copy(out=vblk_s[:, hp * 128:(hp + 1) * 128], in_=PSB(btv)[:, 0:128]), r=[kb(btv)], w=["vblk_s"])
                Sst = hid32[0:64, 2048:4096]
                if VAR == 3:
                    return
                for s_ in range(NSEQ_S):
                    P.dma("sp", "sst_in", Sst.rearrange("i (s h j) -> i s h j", s=NSEQ_S, h=2)[:, s_, :, :],
                          swkv[s_, 2 * hp:2 * hp + 2, :, :].rearrange("h i j -> i h j"), r=["XA"], w=["Sstage"] + HID_ALL)
                if VAR == 4:
                    return
                for g in range(2):
                    bts = bank()
                    for s8 in range(8):
                        s_ = g * 8 + s8
                        P.pe(lambda e, bts=bts, s8=s8, s_=s_, Sst=Sst: e.transpose(out=PS(bts, 512)[:, s8 * 64:(s8 + 1) * 64],
                                                                                 in_=Sst[:, s_ * 128:(s_ + 1) * 128], identity=idf[0:64, 0:64]),
                             r=["Sstage", "XA"] + CONST, w=[kb(bts)])
                    if VAR == 5:
                        return
                    P.act(lambda e, bts=bts, g=g: e.copy(out=SmS[:, g * 8:(g + 1) * 8, :], in_=PS(bts, 512).rearrange("p (s i) -> p s i", i=64)),
                          r=[kb(bts)], w=[("SmS", s_) for s_ in range(g * 8, g * 8 + 8)])
                    if VAR == 6:
                        return
                    P.dve(lambda e, bts=bts, g=g: e.tensor_copy(out=SbS[:, g * 8:(g + 1) * 8, :], in_=PS(bts, 512).rearrange("p (s i) -> p s i", i=64)),
                          r=[kb(bts)], w=[("SbS", s_) for s_ in range(g * 8, g * 8 + 8)])
            if kind == "s" and STOP == 3:
                return
            yield "p2done"
            nu = 2 * nch
            W = nu * L
            NQv = NQ2[0:L, 0:2 * W]
            KQv = KQ2[0:L, 0:2 * W]
            NTv = NT2[0:L, 0:W]
            T32v = T32[0:L, 0:W]
            Tbv = Tb[0:L, 0:W]
            NQ4 = NQv.rearrange("p (u a l) -> p u a l", a=2, l=L)
            KQ4 = KQv.rearrange("p (u a l) -> p u a l", a=2, l=L)
            kNQ, kKQ, kNT, kT32, kTb = "NQ2", "KQ2", "NT2", "T32", "Tb"
            for p in range(2):
                P0 = 64 * p
                for (lhs, lkey, dst4, dkey, msk) in ((Bt, "Bt", NQ4, kNQ, m2b), (Kt, "Kt", KQ4, kKQ, m2k)):
                    bq = pair()
                    for c in range(nch):
                        col = c * 2 * L
                        P.pe(lambda e, P0=P0, c=c, bq=bq, lhs=lhs, col=col: e.matmul(
                            PSP(bq)[0:L, col:col + 2 * L], lhsT=lhs[P0:P0 + 64, c * L:(c + 1) * L],
                            rhs=KR[P0:P0 + 64, c * 2 * L:(c + 1) * 2 * L], start=True, stop=True),
                            r=[lkey, "KR"], w=[kb(bq), kb(bq + 1)])
                    P.dve(lambda e, bq=bq, dst4=dst4, msk=msk, p=p: e.tensor_tensor(
                        out=dst4[:, p * nch:(p + 1) * nch, :, :],
                        in0=PSP(bq)[0:L, 0:nch * 2 * L].rearrange("p (c a l) -> p c a l", a=2, l=L),
                        in1=msk[0:L, :, 0:L].unsqueeze(1).to_broadcast([L, nch, 2, L]),
                        op=ALU.mult), r=[kb(bq), kb(bq + 1), "cconst"], w=[(dkey, p)])
                    pump()
            pump()
            bT2 = pair()
            for p in range(2):
                P0 = 64 * p
                for c in range(nch):
                    u = p * nch + c
                    P.pe(lambda e, P0=P0, c=c, u=u, bT2=bT2: e.matmul(PSP(bT2)[0:L, u * L:(u + 1) * L],
                                                                     lhsT=KR[P0:P0 + 64, c * 2 * L:c * 2 * L + L],
                                                                     rhs=Bt[P0:P0 + 64, c * L:(c + 1) * L], start=True, stop=True),
                         r=["KR", "Bt"], w=[kb(bT2), kb(bT2 + 1)])
            P.dve(lambda e, bT2=bT2: e.tensor_tensor(out=NTv.rearrange("p (u l) -> p u l", l=L),
                                                     in0=PSP(bT2)[0:L, 0:W].rearrange("p (u l) -> p u l", l=L),
                                                     in1=mTl[0:L, 0:L].unsqueeze(1).to_broadcast([L, nu, L]), op=ALU.mult),
                  r=[kb(bT2), kb(bT2 + 1), "cconst"], w=[kNT])
            kNQb = [(kNQ, 0), (kNQ, 1)]
            kKQb = [(kKQ, 0), (kKQ, 1)]
            P.dve(lambda e: e.tensor_tensor(out=T32v.rearrange("p (u l) -> p u l", l=L),
                                            in0=idf[0:L, 0:L].unsqueeze(1).to_broadcast([L, nu, L]),
                                            in1=NQ4[:, :, 0, :], op=ALU.subtract), r=kNQb + CONST, w=[kT32])
            P.act(lambda e: e.copy(out=Tbv, in_=T32v), r=[kT32], w=[kTb])
            nlev = int(np.log2(L)) - 1

            def emit_sq(lastlev):
                bPT = pair()
                for u in range(nu):
                    P.pe(lambda e, u=u, bPT=bPT: e.matmul(PSP(bPT)[0:L, u * L:(u + 1) * L], lhsT=NQ4[:, u, 0, :],
                                                          rhs=NTv[:, u * L:(u + 1) * L], start=True, stop=True),
                         r=kNQb + [kNT], w=[kb(bPT), kb(bPT + 1)])
                bP = None
                if not lastlev:
                    bP = pair()
                    for u in range(nu):
                        P.pe(lambda e, u=u, bP=bP: e.matmul(PSP(bP)[0:L, u * L:(u + 1) * L], lhsT=NTv[:, u * L:(u + 1) * L],
                                                            rhs=NQ4[:, u, 0, :], start=True, stop=True),
                             r=kNQb + [kNT], w=[kb(bP), kb(bP + 1)])
                return bPT, bP

            def emit_sq_evac(bPT, bP):
                P.act(lambda e, bPT=bPT: e.copy(out=NTv, in_=PSP(bPT)[0:L, 0:W]), r=[kb(bPT), kb(bPT + 1)], w=[kNT])
                if bP is not None:
                    P.dve(lambda e, bP=bP: e.tensor_copy(out=NQ4[:, :, 0, :], in_=PSP(bP)[0:L, 0:W].rearrange("p (u l) -> p u l", l=L)),
                          r=[kb(bP), kb(bP + 1)], w=kNQb)

            bb = emit_sq(nlev == 1)
            emit_sq_evac(*bb)
            for lev in range(nlev):
                nxt = None
                if lev + 1 < nlev:
                    nxt = emit_sq(lev + 1 == nlev - 1)
                bT = pair()
                for u in range(nu):
                    P.pe(lambda e, u=u, bT=bT: e.matmul(PSP(bT)[0:L, u * L:(u + 1) * L], lhsT=NTv[:, u * L:(u + 1) * L],
                                                        rhs=Tbv[:, u * L:(u + 1) * L], start=True, stop=True),
                         r=[kNT, kTb], w=[kb(bT), kb(bT + 1)])
                if nxt is not None:
                    emit_sq_evac(*nxt)
                P.dve(lambda e, bT=bT: e.tensor_tensor(out=T32v, in0=T32v, in1=PSP(bT)[0:L, 0:W], op=ALU.add),
                      r=[kb(bT), kb(bT + 1), kT32], w=[kT32])
                P.act(lambda e: e.copy(out=Tbv, in_=T32v), r=[kT32], w=[kTb])
                pump()
            for c in range(nch):
                if kind == "p":
                    Smv, Sbv, skm, skb = Sm[:, hp, :], Sb[:, hp, :], ("Sm", hp), ("Sb", hp)
                    zero_state = first and c == 0
                    ydst_ap, ykey = ytok[:, c, hp * 128:(hp + 1) * 128], ("ytok", c)
                else:
                    Smv, Sbv, skm, skb = SmS[:, c, :], SbS[:, c, :], ("SmS", c), ("SbS", c)
                    zero_state = False
                    ydst_ap, ykey = ytok_s[0:L, c, :], "ytok_s"
                bX = bank()
                for p in range(2):
                    P0 = 64 * p
                    u = p * nch + c
                    if not zero_state:
                        P.pe(lambda e, P0=P0, bX=bX, c=c, p=p, Sbv=Sbv: e.matmul(PS(bX, 128, L)[:, p * 64:(p + 1) * 64],
                                                                       lhsT=KR[P0:P0 + 64, c * 2 * L:c * 2 * L + L], rhs=Sbv[P0:P0 + 64, :],
                                                                       start=True, stop=False), r=["KR", skb], w=[kb(bX)])
                    P.pe(lambda e, P0=P0, bX=bX, c=c, p=p, u=u, zs=zero_state, vtok_hp=vtok_hp: e.matmul(PS(bX, 128, L)[:, p * 64:(p + 1) * 64], lhsT=KQ4[:, u, 0, :],
                                                                                      rhs=vtok_hp[:, c, P0:P0 + 64], start=zs, stop=True),
                         r=kKQb + [vkey], w=[kb(bX)])
                P.act(lambda e, bX=bX: e.copy(out=XTs[0:L, :], in_=PS(bX, 128, L)), r=[kb(bX)], w=["XTs"])
                bS = bank()
                for p in range(2):
                    u = p * nch + c
                    P.pe(lambda e, bS=bS, p=p, u=u: e.matmul(PS(bS, 128, L)[:, p * 64:(p + 1) * 64], lhsT=Tbv[:, u * L:(u + 1) * L],
                                                            rhs=XTs[0:L, p * 64:(p + 1) * 64], start=True, stop=True), r=[kTb, "XTs"], w=[kb(bS)])
                P.act(lambda e, bS=bS: e.copy(out=SATs[0:L, :], in_=PS(bS, 128, L)), r=[kb(bS)], w=["SATs"])
                bY = bank()
                for p in range(2):
                    P0 = 64 * p
                    u = p * nch + c
                    if not zero_state:
                        P.pe(lambda e, P0=P0, bY=bY, c=c, p=p, Sbv=Sbv: e.matmul(PS(bY, 128, L)[:, p * 64:(p + 1) * 64],
                                                                       lhsT=KR[P0:P0 + 64, c * 2 * L + L:(c + 1) * 2 * L], rhs=Sbv[P0:P0 + 64, :],
                                                                       start=True, stop=False), r=["KR", skb], w=[kb(bY)])
                    P.pe(lambda e, P0=P0, bY=bY, c=c, p=p, u=u, zs=zero_state, vtok_hp=vtok_hp: e.matmul(PS(bY, 128, L)[:, p * 64:(p + 1) * 64], lhsT=KQ4[:, u, 1, :],
                                                                                      rhs=vtok_hp[:, c, P0:P0 + 64], start=zs, stop=False),
                         r=kKQb + [vkey], w=[kb(bY)])
                    P.pe(lambda e, bY=bY, p=p, u=u: e.matmul(PS(bY, 128, L)[:, p * 64:(p + 1) * 64], lhsT=NQ4[:, u, 1, :],
                                                            rhs=SATs[0:L, p * 64:(p + 1) * 64], start=False, stop=True),
                         r=kNQb + ["SATs"], w=[kb(bY)])
                P.dve(lambda e, bY=bY, ydst_ap=ydst_ap: e.tensor_copy(out=ydst_ap, in_=PS(bY, 128, L)),
                      r=[kb(bY)] + (["XA"] if kind == "s" else []), w=[ykey])
                bZ = bank()
                for p in range(2):
                    P0 = 64 * p
                    P.pe(lambda e, P0=P0, bZ=bZ, c=c, vtok_hp=vtok_hp, KgLtok3=KgLtok3: e.matmul(PS(bZ, 64, 64, P0), lhsT=KgLtok3[:, c, P0:P0 + 64], rhs=vtok_hp[:, c, P0:P0 + 64],
                                                               start=True, stop=False), r=["KgLtok", vkey, "XA"], w=[kb(bZ)])
                    P.pe(lambda e, P0=P0, bZ=bZ, c=c, p=p, BgLtok3=BgLtok3: e.matmul(PS(bZ, 64, 64, P0), lhsT=BgLtok3[:, c, P0:P0 + 64], rhs=SATs[0:L, p * 64:(p + 1) * 64],
                                                                   start=False, stop=True), r=kBg + ["SATs"], w=[kb(bZ)])
                if zero_state:
                    P.dve(lambda e, bZ=bZ, Smv=Smv: e.tensor_copy(out=Smv, in_=PS(bZ, 64)), r=[kb(bZ)], w=[skm])
                else:
                    P.dve(lambda e, bZ=bZ, Smv=Smv, c=c: e.scalar_tensor_tensor(out=Smv, in0=Smv, scalar=gLv[:, c:c + 1], in1=PS(bZ, 64),
                                                                                op0=ALU.mult, op1=ALU.add), r=[kb(bZ), skm, gk], w=[skm])
                P.act(lambda e, Smv=Smv, Sbv=Sbv: e.copy(out=Sbv, in_=Smv), r=[skm], w=[skb])
                pump()
            if kind == "s":
                for s_ in range(NSEQ_S):
                    P.dma("sp", "yrl", ytok[s_ * LS:(s_ + 1) * LS, 0, hp * 128:(hp + 1) * 128], ytok_s[0:LS, s_, :],
                          r=["ytok_s", "XA"] + HID_ALL, w=[("ytok", 0)])
                Sst = hid32[0:64, 2048:4096]
                for g in range(4):
                    bts = bank()
                    for s4 in range(4):
                        s_ = g * 4 + s4
                        P.pe(lambda e, bts=bts, s4=s4, s_=s_: e.transpose(out=PS(bts, 512, 64)[:, s4 * 128:(s4 + 1) * 128], in_=SmS[:, s_, :], identity=idf[:]),
                             r=[("SmS", s_)] + CONST, w=[kb(bts)])
                    P.act(lambda e, bts=bts, g=g, Sst=Sst: e.copy(out=Sst[:, g * 512:(g + 1) * 512], in_=PS(bts, 512, 64)), r=[kb(bts), "XA"], w=["Sstage"])
                for s_ in range(NSEQ_S):
                    P.dma("sp", "sst_out", wkvs[s_, 2 * hp:2 * hp + 2, :, :].rearrange("h i j -> i h j"),
                          Sst.rearrange("i (s h j) -> i s h j", s=NSEQ_S, h=2)[:, s_, :, :], r=["Sstage", "XA"] + HID_ALL, w=["o_wkvs"])


        nhp = HPS if kind == "s" else 4
        PIPE = (kind == "p")
        pstate = {"g": None, "done": True}

        def pump():
            if pstate["done"] or pstate["g"] is None:
                return
            try:
                v = next(pstate["g"])
            except StopIteration:
                pstate["done"] = True
                return
            if v == "p1done":
                pstate["done"] = True

        def run_until(g, tag):
            while True:
                try:
                    v = next(g)
                except StopIteration:
                    return False
                if v == tag:
                    return True

        gens = [hp_gen(h) for h in range(nhp)]
        alive = run_until(gens[0], "p2done")
        for h in range(nhp):
            if not alive:
                break
            nxt = gens[h + 1] if (h + 1 < nhp) else None
            if nxt is not None and PIPE:
                pstate["g"], pstate["done"] = nxt, False
            else:
                pstate["g"], pstate["done"] = None, True
            run_until(gens[h], "__end__")
            if nxt is not None:
                if PIPE and not pstate["done"]:
                    run_until(nxt, "p1done")
                pstate["done"] = True
                alive = run_until(nxt, "p2done")
        if kind == "s" and (STOP <= 4 or 30 <= STOP < 40):
            return
        bkk = inproj(("ak", 0), 128)
        bks = inproj(("aks", 0), 128)
        P.dve(lambda e: e.tensor_tensor(out=kf32[:, S_], in0=PS(bkk, NT), in1=cs_t[:, 0, S_], op=ALU.mult), r=[kb(bkk), "cs_t"], w=["t3"])
        P.dve(lambda e: e.tensor_tensor(out=dtmp[:, S_], in0=PS(bks, NT), in1=cs_t[:, 1, S_], op=ALU.mult), r=[kb(bks), "cs_t"], w=["dtmp"])
        P.dve(lambda e: e.tensor_tensor(out=kf32[:, S_], in0=kf32[:, S_], in1=dtmp[:, S_], op=ALU.add), r=["t3", "dtmp"], w=["t3"])
        P.act(lambda e: e.copy(out=kbuf[:, 128:128 + NT], in_=kf32[:, S_]), r=["t3"], w=["kbuf"])
        bv = inproj(("av", 0), 128)
        P.act(lambda e: e.copy(out=vT32[:, S_], in_=PS(bv, NT)), r=[kb(bv)], w=["t4"])
        if first or kind == "s":
            P.pool(lambda e: e.memset(Vaug[:], 1.0), w=["Vaug"])
        for b in range(NB):
            bt = bank()
            P.pe(lambda e, b=b, bt=bt: e.transpose(out=PS(bt, 128), in_=vT32[:, b * 128:(b + 1) * 128], identity=idf[:]),
                 r=["t4"] + CONST, w=[kb(bt)])
            P.act(lambda e, b=b, bt=bt: e.copy(out=Vaug[:, b + 1, :, 0:64], in_=PS(bt, 128).rearrange("p (k d) -> p k d", k=2)),
                  r=[kb(bt)], w=["Vaug"])
            if last and b == NB - 1:
                P.dve(lambda e, bt=bt: e.tensor_copy(out=vtokf[:], in_=PS(bt, 128)), r=[kb(bt)], w=["vtokf"])
                if kind == "p":
                    P.dma("pool", "o_vw", vwp, vtokf[:], r=["vtokf"], w=["o_vwp"])
                else:
                    for s in range(NSEQ_S):
                        P.dma("sp", "o_vws", vws[s, 120:128, :], vtokf[s * LS:(s + 1) * LS, :], r=["vtokf"], w=["o_vws"])
                    P.dma("sp", "o_vw2", vws[:, 0:120, :].rearrange("s r c -> s (r c)"), cv[:, 8:128, :].rearrange("s r c -> s (r c)"), w=["o_vws2"])
                bt2 = bank()
                P.pe(lambda e, b=b, bt2=bt2: e.transpose(out=PS(bt2, 128), in_=kf32[:, b * 128:(b + 1) * 128], identity=idf[:]),
                     r=["t3"] + CONST, w=[kb(bt2)])
                P.dve(lambda e, bt2=bt2: e.tensor_copy(out=yq[:, 0:128], in_=PS(bt2, 128)), r=[kb(bt2)], w=["t0"])
                if kind == "p":
                    P.dma("pool", "o_kw", kwp, yq[:, 0:128], r=["t0"], w=["o_kwp"])
                else:
                    for s in range(NSEQ_S):
                        P.dma("sp", "o_kws", kws[s, 120:128, :], yq[s * LS:(s + 1) * LS, 0:128], r=["t0"], w=["o_kws"])
                    P.dma("sp", "o_kw2", kws[:, 0:120, :].rearrange("s r c -> s (r c)"), ck[:, 8:128, :].rearrange("s r c -> s (r c)"), w=["o_kws2"])
        for c in range(4):
            bq_ = inproj(("q", c), 128)
            bqs = inproj(("qs", c), 128)
            P.dve(lambda e, bq_=bq_: e.tensor_tensor(out=yq[:, S_], in0=PS(bq_, NT), in1=cs_t[:, 0, S_], op=ALU.mult),
                  r=[kb(bq_), "cs_t", "t0"], w=["t0"])
            P.dve(lambda e, bqs=bqs: e.tensor_tensor(out=yq2[:, S_], in0=PS(bqs, NT), in1=cs_t[:, 1, S_], op=ALU.mult),
                  r=[kb(bqs), "cs_t"], w=["t1"])
            P.pool(lambda e, c=c: e.tensor_tensor(out=qT[:, c, S_], in0=yq[:, S_], in1=yq2[:, S_], op=ALU.add),
                   r=["t0", "t1"], w=[("qT", c)])
        qkeys = [("qT", c) for c in range(4)]

        def epilogue(b):
            y3 = ytok[:, b, :].rearrange("p (h d) -> p h d", d=64)
            yk = ("ytok", b)
            gs = gstat[:, 0, :]
            P.dve(lambda e, y3=y3: e.tensor_reduce(out=gstat[:, 0, :], in_=y3, axis=AX.X, op=ALU.add), r=[yk], w=["gstat"])
            P.act(lambda e, b=b: e.activation(out=yq[:, :], in_=ytok[:, b, :], func=AF.Square), r=[yk, "t0"], w=["t0"])
            P.dve(lambda e: e.tensor_reduce(out=gstat[:, 1, :], in_=yq[:, :].rearrange("p (h d) -> p h d", d=64), axis=AX.X, op=ALU.add),
                  r=["t0"], w=["gstat"])
            P.dve(lambda e: e.tensor_scalar(out=gstat[:, 0, :], in0=gstat[:, 0, :], scalar1=1.0 / 64, scalar2=None, op0=ALU.mult),
                  r=["gstat"], w=["gstat"])
            P.dve(lambda e: e.tensor_tensor(out=gstat[:, 2, :], in0=gstat[:, 0, :], in1=gstat[:, 0, :], op=ALU.mult), r=["gstat"], w=["gstat"])
            P.dve(lambda e: e.scalar_tensor_tensor(out=gstat[:, 1, :], in0=gstat[:, 1, :], scalar=1.0 / 64, in1=gstat[:, 2, :],
                                                   op0=ALU.mult, op1=ALU.subtract), r=["gstat"], w=["gstat"])
            P.act(lambda e: e.activation(out=gstat[:, 1, :], in_=gstat[:, 1, :], func=AF.Sqrt, bias=GN_EPS), r=["gstat"], w=["gstat"])
            P.dve(lambda e: e.reciprocal(out=gstat[:, 1, :], in_=gstat[:, 1, :]), r=["gstat"], w=["gstat"])
            for h in range(8):
                P.dve(lambda e, h=h, y3=y3: e.tensor_scalar(out=yq2[:, h * 64:(h + 1) * 64], in0=y3[:, h, :], scalar1=gstat[:, 0, h:h + 1],
                                                            scalar2=gstat[:, 1, h:h + 1], op0=ALU.subtract, op1=ALU.mult),
                      r=[yk, "gstat", "t1"], w=["t1"])
            P.dve(lambda e: e.tensor_tensor(out=yq2[:, :], in0=yq2[:, :], in1=gnw_bc[:], op=ALU.mult), r=["t1"] + CONST, w=["t1"])
            P.dve(lambda e: e.tensor_tensor(out=yq2[:, :], in0=yq2[:, :], in1=gnb_bc[:], op=ALU.add), r=["t1"] + CONST, w=["t1"])
            for hp in range(4):
                if kind == "p":
                    vt_blk = vtok[:, (hp * nch + b) * 128:(hp * nch + b + 1) * 128]
                    vkeys = [("vtok", hp)]
                else:
                    vt_blk = None
                    vkeys = []
                for p in range(2):
                    h = 2 * hp + p
                    if kind == "p":
                        P.dve(lambda e, h=h, p=p, vt_blk=vt_blk, b=b: e.scalar_tensor_tensor(
                            out=yq2[:, h * 64:(h + 1) * 64], in0=vt_blk[:, p * 64:(p + 1) * 64], scalar=rkb[:, b, h:h + 1],
                            in1=yq2[:, h * 64:(h + 1) * 64], op0=ALU.mult, op1=ALU.add),
                            r=vkeys + [("rkb", hp), "t1"], w=["t1"])
                    else:
                        P.dve(lambda e, h=h, b=b: e.scalar_tensor_tensor(
                            out=yq2[:, h * 64:(h + 1) * 64], in0=vblk_s[:, h * 64:(h + 1) * 64], scalar=rkb[:, b, h:h + 1],
                            in1=yq2[:, h * 64:(h + 1) * 64], op0=ALU.mult, op1=ALU.add),
                            r=["vblk_s", ("rkb", hp), "t1"], w=["t1"])
            bg = bank()
            P.pe(lambda e, b=b, bg=bg: e.matmul(PS(bg, 512), lhsT=sgb[:, b * 128:(b + 1) * 128], rhs=Wg_b[:], start=True, stop=True),
                 r=["sgb", "cconst"], w=[kb(bg)])
            P.dve(lambda e, b=b, bg=bg: e.tensor_tensor(out=mix[:, b, 512:1024], in0=yq2[:, :], in1=PS(bg, 512), op=ALU.mult),
                  r=["t1", kb(bg)], w=[("mixr", b)])


        def attn_group(b, kblocks, first_grp, only_grp, additive=False):
            ng = len(kblocks)
            for kv in range(2):
                P0 = 64 * kv
                for i, (kfn, vfn, msk, kkeys) in enumerate(kblocks):
                    bs = bank()
                    P.pe(lambda e, P0=P0, bs=bs, kfn=kfn, kv=kv: e.matmul(PS(bs, 512), lhsT=kfn(kv),
                                                                   rhs=qT[P0:P0 + 64, :, b * 128:(b + 1) * 128], start=True, stop=not additive),
                         r=qkeys + kkeys, w=[kb(bs)])
                    if additive:
                        P.pe(lambda e, bs=bs, msk=msk: e.matmul(PS(bs, 512), lhsT=idb[:], rhs=msk.rearrange("p a b -> p (a b)"),
                                                                start=False, stop=True), r=["idb", "cconst"], w=[kb(bs)])
                    Ei = Et[:, kv * 2 + i, :]
                    ek = ("Et", kv * 2 + i)
                    P.act(lambda e, bs=bs, Ei=Ei: e.activation(out=Ei, in_=PS(bs, 512), func=AF.Exp, scale=0.125), r=[kb(bs)], w=[ek])
                    if not additive:
                      P.pool(lambda e, Ei=Ei, msk=msk: e.tensor_tensor(out=Ei.rearrange("p (a b) -> p a b", a=4), in0=Ei.rearrange("p (a b) -> p a b", a=4), in1=msk, op=ALU.mult),
                           r=[ek, "cconst", "scconst"], w=[ek])
            bo = pair()
            for kv in range(2):
                for c4 in range(4):
                    for i, (kfn, vfn, msk, kkeys) in enumerate(kblocks):
                        P.pe(lambda e, kv=kv, c4=c4, i=i, vfn=vfn, bo=bo: e.matmul(
                            PS(bo + kv, 260)[:, c4 * 65:(c4 + 1) * 65], lhsT=Et[:, kv * 2 + i, c4 * 128:(c4 + 1) * 128], rhs=vfn(kv),
                            start=(i == 0), stop=(i == ng - 1)), r=[("Et", kv * 2 + i)] + kkeys, w=[kb(bo + kv)])
            return bo

        def attn_finish(b, src_fn, rkeys):
            for kv in range(2):
                s3 = src_fn(kv)
                P.dve(lambda e, kv=kv, s3=s3: e.tensor_tensor(out=den[:, kv * 4:(kv + 1) * 4].unsqueeze(2), in0=s3[:, :, 64:65],
                                                              in1=esink[:, kv * 4:(kv + 1) * 4].unsqueeze(2), op=ALU.add),
                      r=rkeys + ["esink"], w=[("den", kv)])
                P.dve(lambda e, kv=kv: e.reciprocal(out=den[:, kv * 4:(kv + 1) * 4], in_=den[:, kv * 4:(kv + 1) * 4]),
                      r=[("den", kv)], w=[("den", kv)])
                for c4 in range(4):
                    h = kv * 4 + c4
                    P.dve(lambda e, s3=s3, c4=c4, h=h: e.tensor_scalar(out=mix[:, b, h * 64:(h + 1) * 64], in0=s3[:, c4, 0:64],
                                                                       scalar1=den[:, h:h + 1], scalar2=None, op0=ALU.mult),
                          r=rkeys + [("den", kv)], w=[("mixa", b)])

        if kind == "p":
            for b in range(NB):
                gb = ti * NB + b
                kbl = []
                if gb > 0:
                    kbl.append((lambda kv, b=b: kbuf[64 * kv:64 * kv + 64, b * 128:(b + 1) * 128],
                                lambda kv, b=b: Vaug[:, b, kv, :], m_prev[:], ["kbuf", "Vaug"]))
                kbl.append((lambda kv, b=b: kbuf[64 * kv:64 * kv + 64, (b + 1) * 128:(b + 2) * 128],
                            lambda kv, b=b: Vaug[:, b + 1, kv, :], m_own[:], ["kbuf", "Vaug"]))
                bo = attn_group(b, kbl, True, True, additive=True)
                epilogue(b)
                attn_finish(b, lambda kv, bo=bo: PS(bo + kv, 260).rearrange("p (c d) -> p c d", d=65), [kb(bo), kb(bo + 1)])
            P.act(lambda e: e.copy(out=kbuf[:, 0:128], in_=kbuf[:, NB * 128:(NB + 1) * 128]), r=["kbuf"], w=["kbuf"])
            P.pool(lambda e: e.tensor_copy(out=Vaug[:, 0, :, :], in_=Vaug[:, NB, :, :]), r=["Vaug"], w=["Vaug"])
        else:
            kbl = [(lambda kv: kbuf[64 * kv:64 * kv + 64, 128:256], lambda kv: Vaug[:, 1, kv, :], m_sown[:], ["kbuf", "Vaug"])]
            bo = attn_group(0, kbl, True, False)
            for kv in range(2):
                P.dve(lambda e, kv=kv, bo=bo: e.tensor_copy(out=oacc[:, kv, :, :].rearrange("p c d -> p (c d)"), in_=PS(bo + kv, 260)),
                      r=[kb(bo + kv)], w=[("oacc", kv)])
            for s in range(NSEQ_S):
                i = s % 2
                P.dma("pool", "ck%d" % i, ckf[:, i, :], ck[s], w=[("ckf", i)])
                btk = bank()
                P.pe(lambda e, i=i, btk=btk: e.transpose(out=PS(btk, 128), in_=ckf[:, i, :], identity=idf[:]), r=[("ckf", i)] + CONST, w=[kb(btk)])
                P.act(lambda e, btk=btk: e.copy(out=ckT[:], in_=PS(btk, 128)), r=[kb(btk)], w=["ckT"])
                P.dma("pool", "cv%d" % i, ckf[:, i, :], cv[s], r=[], w=[("ckf", i)])
                P.pool(lambda e, i=i: e.memset(cVaug[:, i, :, 64:65], 1.0), w=[("cVaug", i)])
                P.act(lambda e, i=i: e.copy(out=cVaug[:, i, :, 0:64], in_=ckf[:, i, :].rearrange("p (k d) -> p k d", k=2)),
                      r=[("ckf", i)], w=[("cVaug", i)])
                kbl = [(lambda kv: ckT[64 * kv:64 * kv + 64, :], lambda kv, i=i: cVaug[:, i, kv, :],
                        m_scache[:, s:s + 1, :].to_broadcast([128, 4, 128]), ["ckT", ("cVaug", i)])]
                bo = attn_group(0, kbl, False, False)
                for kv in range(2):
                    P.dve(lambda e, kv=kv, bo=bo: e.tensor_tensor(out=oacc[:, kv, :, :].rearrange("p c d -> p (c d)"),
                                                                  in0=oacc[:, kv, :, :].rearrange("p c d -> p (c d)"),
                                                                  in1=PS(bo + kv, 260), op=ALU.add),
                          r=[kb(bo + kv), ("oacc", kv)], w=[("oacc", kv)])
            attn_finish(0, lambda kv: oacc[:, kv, :, :], [("oacc", 0), ("oacc", 1)])
            epilogue(0)

        if kind == "s" and STOP <= 5:
            return
        if kind == "s" and STOP <= 6:
            return
        for b in range(NB):
            bk = bank()
            for c in range(8):
                P.pe(lambda e, c=c, bk=bk, b=b: e.transpose(out=PSB(bk)[:, c * 128:(c + 1) * 128], in_=mix[:, b, c * 128:(c + 1) * 128],
                                                            identity=idb[:]), r=[("mixa", b), ("mixr", b), "idb"], w=[kb(bk)])
            P.act(lambda e, b=b, bk=bk: e.copy(out=bufA[:, :, b * 128:(b + 1) * 128], in_=PSB(bk).rearrange("p (c n) -> p c n", c=8)),
                  r=[kb(bk)], w=[("bufA", b)])
        nb[0] = 0
        for c in range(8):
            s = stream(wsc_out[c], ("wsc_out", c))
            for b in range(NB):
                for hf in range(2):
                    P.pe(lambda e, c=c, s=s, b=b, hf=hf: e.matmul(PS(2 * b + hf, 512), lhsT=bufA[:, c, b * 128:(b + 1) * 128],
                                                                 rhs=ring[:, s, hf * 512:(hf + 1) * 512], start=(c == 0), stop=(c == 7)),
                         r=[("ring", s), ("bufA", b)], w=[kb(2 * b + hf)])

        def ost(b):
            return hid32[:, 1536 + b * 1024:1536 + (b + 1) * 1024]

        def ostk(b):
            return [("hidT", fc_) for fc_ in range(6 + 4 * b, 10 + 4 * b)]

        def norm_res(b, src_pair, gbc, dst, dkey, xkey_r):
            sc = sstat[:, 4 + b:5 + b]
            pk = [kb(2 * b), kb(2 * b + 1)]
            P.act(lambda e: e.activation(out=junk[:], in_=src_pair, func=AF.Square, accum_out=sc), r=pk, w=["junk", ("ss2", b)])
            P.act(lambda e: e.activation(out=sc, in_=sc, func=AF.Sqrt, scale=1.0 / D, bias=RMS_EPS), r=[("ss2", b)], w=[("ss2", b)])
            P.dve(lambda e: e.reciprocal(out=sc, in_=sc), r=[("ss2", b)], w=[("ss2", b)])
            tmp = ost(b)
            P.dve(lambda e: e.scalar_tensor_tensor(out=tmp, in0=src_pair, scalar=sc, in1=gbc[:], op0=ALU.mult, op1=ALU.mult),
                  r=pk + [("ss2", b)] + CONST, w=ostk(b))
            P.pool(lambda e: e.tensor_tensor(out=dst, in0=tmp, in1=x_t[:, b, :], op=ALU.add), r=ostk(b) + [xkey_r], w=dkey)

        for b in range(NB):
            norm_res(b, PSP(2 * b), gpost_bc, x_t[:, b, :], [("x", b)], ("x", b))
        for b in range(NB):
            sc = sstat[:, 8 + b:9 + b]
            P.act(lambda e, b=b, sc=sc: e.activation(out=junk[:], in_=x_t[:, b, :], func=AF.Square, accum_out=sc),
                  r=[("x", b)], w=["junk", ("ss3", b)])
            P.act(lambda e, sc=sc: e.activation(out=sc, in_=sc, func=AF.Sqrt, scale=1.0 / D, bias=RMS_EPS), r=[("ss3", b)], w=[("ss3", b)])
            P.dve(lambda e, sc=sc: e.reciprocal(out=sc, in_=sc), r=[("ss3", b)], w=[("ss3", b)])
        for b in range(NB):
            sc = sstat[:, 8 + b:9 + b]
            P.dve(lambda e, b=b, sc=sc: e.tensor_scalar(out=xn[:], in0=x_t[:, b, :], scalar1=sc, scalar2=None, op0=ALU.mult),
                  r=[("x", b), ("ss3", b)], w=["xn"])
            bk = (2 * b) % 8
            for c in range(8):
                P.pe(lambda e, c=c, bk=bk: e.transpose(out=PSB(bk)[:, c * 128:(c + 1) * 128], in_=xn[:, c * 128:(c + 1) * 128],
                                                      identity=idb[:]), r=["xn", "idb"], w=[kb(bk)])
            P.act(lambda e, b=b, bk=bk: e.copy(out=bufA[:, :, b * 128:(b + 1) * 128], in_=PSB(bk).rearrange("p (c n) -> p c n", c=8)),
                  r=[kb(bk)], w=[("bufA", b)])
        nb[0] = 0

        if kind == "s" and STOP <= 7:
            return
        for fc in range(NFC):
            sz = stream(wsc_fin[2 * fc], ("wsc_fin", 2 * fc))
            su = stream(wsc_fin[2 * fc + 1], ("wsc_fin", 2 * fc + 1))
            bz = bank()
            bu = bank()
            for (s, bk_) in ((sz, bz), (su, bu)):
                for c in range(8):
                    P.pe(lambda e, c=c, s=s, bk_=bk_: e.matmul(PS(bk_, NT), lhsT=ring[:, s, c * 128:(c + 1) * 128], rhs=bufA[:, c, S_],
                                                               start=(c == 0), stop=(c == 7)), r=[("ring", s)] + bufA_keys, w=[kb(bk_)])
            i = 0
            zb = zbuf[:, i, 0:nseq * (Lseq + 2)].rearrange("p (s l) -> p s l", s=nseq)
            zk = ("zbuf", i)
            P.act(lambda e, bz=bz, zb=zb: e.copy(out=zb[:, :, 2:Lseq + 2], in_=PS(bz, NT).rearrange("p (s l) -> p s l", s=nseq)),
                  r=[kb(bz)], w=[zk])
            P.pool(lambda e, zb=zb, fc=fc: e.tensor_copy(out=zb[:, :, 0:2], in_=zcar[:, fc, :, :]), r=[("zcar", fc)], w=[zk])
            P.pool(lambda e, zb=zb, fc=fc: e.tensor_copy(out=zcar[:, fc, :, :], in_=zb[:, :, Lseq:Lseq + 2]), r=[zk], w=[("zcar", fc)])
            a3 = za[:, i, S_].rearrange("p (s l) -> p s l", s=nseq)
            ak = ("za", i)
            P.act(lambda e, bz=bz, a3=a3, fc=fc: e.activation(out=a3, in_=PS(bz, NT).rearrange("p (s l) -> p s l", s=nseq), func=AF.Identity,
                                                              scale=cwT[:, 2, fc:fc + 1], bias=cbT[:, fc:fc + 1]),
                  r=[kb(bz)] + CONST, w=[ak])
            P.dve(lambda e, zb=zb, a3=a3, fc=fc: e.scalar_tensor_tensor(out=a3, in0=zb[:, :, 1:Lseq + 1], scalar=cwT[:, 1, fc:fc + 1], in1=a3,
                                                                        op0=ALU.mult, op1=ALU.add), r=[zk, ak], w=[ak])
            P.dve(lambda e, zb=zb, a3=a3, fc=fc: e.scalar_tensor_tensor(out=a3, in0=zb[:, :, 0:Lseq], scalar=cwT[:, 0, fc:fc + 1], in1=a3,
                                                                        op0=ALU.mult, op1=ALU.add), r=[zk, ak], w=[ak])
            P.act(lambda e, i=i: e.activation(out=za[:, i, S_], in_=za[:, i, S_], func=AF.Silu), r=[ak], w=[ak])
            P.dve(lambda e, i=i, bu=bu, fc=fc: e.tensor_tensor(out=hidT[:, fc, S_], in0=za[:, i, S_], in1=PS(bu, NT), op=ALU.mult),
                  r=[ak, kb(bu)], w=[("hidT", fc)])
        nb[0] = 0
        for fc in range(NFC):
            s = stream(wsc_fout[fc], ("wsc_fout", fc))
            for b in range(NB):
                for hf in range(2):
                    P.pe(lambda e, fc=fc, s=s, b=b, hf=hf: e.matmul(PS(2 * b + hf, 512), lhsT=hidT[:, fc, b * 128:(b + 1) * 128],
                                                                   rhs=ring[:, s, hf * 512:(hf + 1) * 512], start=(fc == 0), stop=(fc == NFC - 1)),
                         r=[("ring", s), ("hidT", fc)], w=[kb(2 * b + hf)])
        for b in range(NB):
            oi = 0
            state["ost"] += 1
            norm_res(b, PSP(2 * b), gpostf_bc, ost(b), ostk(b), ("x", b))
        for b in range(NB):
            P.dma("pool", "oy%d" % b, ydst[t0 + b * 128:t0 + (b + 1) * 128, :] if kind == "p" else ydst, ost(b),
                  r=ostk(b), w=["o_y"])
        nb[0] = 0

        if last:
            emit_state_outputs(kind, nseq)

    def emit_state_outputs(kind, nseq):
        hcar = hcar_p if kind == "p" else hcar_s
        zcar = zcar_p if kind == "p" else zcar_s
        for rc in range(14):
            np_ = RC_NP.get(rc, 128)
            bt = bank()
            P.pe(lambda e, rc=rc, np_=np_, bt=bt: e.transpose(out=PS(bt, np_, nseq), in_=hcar[0:np_, rc, :], identity=idf[0:np_, 0:np_]),
                 r=[("hcar", rc)] + CONST, w=[kb(bt)])
            c0 = rc * 128 if rc < 13 else 1600
            P.act(lambda e, bt=bt, np_=np_, c0=c0: e.copy(out=rowbuf[0:nseq, c0:c0 + np_], in_=PS(bt, np_, nseq)), r=[kb(bt)], w=["rowbuf", "XA"] + (HID_ALL if rc == 0 else []))
        P.dma("pool", "o_sh", shp if kind == "p" else shs, rowbuf[0:nseq, 0:DSH], r=["rowbuf"] + HID_ALL, w=["o_sh" + kind, "XA"])
        for fc in range(NFC):
            bt = bank()
            P.pe(lambda e, fc=fc, bt=bt: e.transpose(out=PS(bt, 128, 2 * nseq), in_=zcar[:, fc, :, :].rearrange("p s j -> p (s j)"),
                                                     identity=idf[:]), r=[("zcar", fc)] + CONST, w=[kb(bt)])
            P.act(lambda e, fc=fc, bt=bt: e.copy(out=sst[0:2 * nseq, fc * 128:(fc + 1) * 128], in_=PS(bt, 128, 2 * nseq)), r=[kb(bt)], w=["sst", "XA"] + (HID_ALL if fc == 0 else []))
        P.dma("pool", "o_cv", convp if kind == "p" else convs, sst[0:2 * nseq, :], r=["sst"] + HID_ALL, w=["o_cv" + kind, "XA"])
        if kind == "p":
            for hp in range(4):
                bt = bank()
                P.pe(lambda e, hp=hp, bt=bt: e.transpose(out=PS(bt, 128, 64), in_=Sm[:, hp, :], identity=idf[:]),
                     r=[("Sm", hp)] + CONST, w=[kb(bt)])
                i = hp % 2
                P.act(lambda e, bt=bt, i=i: e.copy(out=Sld[:, i, :], in_=PS(bt, 128, 64)), r=[kb(bt)], w=["t5"])
                P.dma("pool", "o_wk%d" % i, wkvp[2 * hp:2 * hp + 2].rearrange("h i j -> i h j"),
                      Sld[:, i, :].rearrange("p (h j) -> p h j", h=2), r=["t5"], w=["o_wkvp"])

    SmS = T("SmS", [128, NSEQ_S, 64])
    SbS = T("SbS", [128, NSEQ_S, 64], BF16)

    P.pool(lambda e: e.memset(hcar_p[:], 0.0), w=[("hcar", rc) for rc in range(14)])
    P.pool(lambda e: e.memset(zcar_p[:], 0.0), w=[("zcar", fc) for fc in range(NFC)])
    P.pool(lambda e: e.memset(kbuf[:], 0.0), w=["kbuf"])

    for ti in range(N_TILES):
        emit_tile("p", ti)

    if DO_SAMPLE:
        P.dma("sp", "clss", scanm_s[:], c_scanm_s, w=["scconst"])
        cast_const(m_sown[:], c_mask_sown, 128, 128, bcast4=True)
        for g in range(4):
            cast_const(m_scache[:, 4 * g:4 * g + 4, :].rearrange("p a b -> p (a b)"), c_mask_scache[:, 4 * g:4 * g + 4, :].rearrange("p a b -> p (a b)"), 128, 512)
        P.ops["dve"][-1].deps.add(P.last_w["scconst"])
        P.dma("pool", "hst", rowbuf[0:NSEQ_S, :], sshift, r=[], w=["rowbuf", "XA"] + HID_ALL)
        for rc in range(14):
            np_ = RC_NP.get(rc, 128)
            c0 = rc * 128 if rc < 13 else 1600
            bt = bank()
            P.pe(lambda e, bt=bt, np_=np_, c0=c0: e.transpose(out=PS(bt, NSEQ_S, np_), in_=rowbuf[0:NSEQ_S, c0:c0 + np_], identity=idf[0:NSEQ_S, 0:NSEQ_S]),
                 r=["rowbuf"] + CONST, w=[kb(bt), "XA"])
            P.act(lambda e, bt=bt, np_=np_, rc=rc: e.copy(out=hcar_s[0:np_, rc, :], in_=PS(bt, NSEQ_S, np_)), r=[kb(bt)], w=[("hcar", rc)])
        P.dma("pool", "hst2", sst[0:2 * NSEQ_S, :], sconv, r=[], w=["sst", "XA"] + HID_ALL)
        for fc in range(NFC):
            bt = bank()
            P.pe(lambda e, bt=bt, fc=fc: e.transpose(out=PS(bt, 2 * NSEQ_S, 128), in_=sst[0:2 * NSEQ_S, fc * 128:(fc + 1) * 128],
                                                     identity=idf[0:2 * NSEQ_S, 0:2 * NSEQ_S]), r=["sst"] + CONST, w=[kb(bt), "XA"])
            P.act(lambda e, bt=bt, fc=fc: e.copy(out=zcar_s[:, fc, :, :].rearrange("p s j -> p (s j)"), in_=PS(bt, 2 * NSEQ_S, 128)),
                  r=[kb(bt)], w=[("zcar", fc)])
        emit_tile("s", 0)

    okeys = [k for k in P.last_w.keys() if isinstance(k, str) and k.startswith("o_")]
    P.add("sp", lambda e: None, reads=okeys)
    P.add("pool", lambda e: None, reads=okeys)
    P.emit()
    return nc, st


def _consts():
    c = {}
    c["c_ident"] = np.eye(128, dtype=np.float32)
    half = 32
    inv = (np.float32(10000.0) ** (-np.arange(half, dtype=np.float32) / np.float32(half))).astype(np.float32)
    p = np.arange(128)
    f = (p % 64) % 32
    sign = np.where((p % 64) < 32, -1.0, 1.0).astype(np.float32)

    def tabs(pos):
        ang = pos.astype(np.float32)[None, :] * inv[f][:, None]
        return np.cos(ang).astype(np.float32), (np.sin(ang).astype(np.float32) * sign[:, None]).astype(np.float32)

    c["c_cos_p"], c["c_sin_p"] = tabs(np.arange(SEQ))
    pos_s = 16384 + (np.arange(128) % LS)
    c["c_cos_s"], c["c_sin_s"] = tabs(pos_s)
    s = np.arange(128)[:, None]
    q = np.arange(128)[None, :]
    c["c_mask_own"] = (s <= q).astype(np.float32)
    c["c_mask_prev"] = (s >= q).astype(np.float32)
    c["c_negmask_own"] = np.where(s <= q, 0.0, -30000.0).astype(np.float32)
    c["c_negmask_prev"] = np.where(s >= q, 0.0, -30000.0).astype(np.float32)
    c["c_mask_sown"] = ((s // LS == q // LS) & (s <= q)).astype(np.float32)
    msc = np.zeros((128, NSEQ_S, 128), np.float32)
    for sq in range(NSEQ_S):
        msc[:, sq, :] = ((q // LS == sq) & (s >= (q % LS))).astype(np.float32)
    c["c_mask_scache"] = msc
    bo = np.zeros((128, 128), np.float32)
    bo[:64, :64] = 1
    bo[64:, 64:] = 1
    c["c_blockones"] = bo
    strict = (s < q).astype(np.float32)
    incl = (s <= q).astype(np.float32)
    c["c_m2b"] = np.stack([strict, -incl], axis=1).astype(np.float32)
    c["c_m2k"] = np.stack([strict, incl], axis=1).astype(np.float32)
    c["c_mT"] = (q < s).astype(np.float32)
    mp = np.ones((128, 512), np.float32)
    mp[:, 0::128] = 0
    c["c_scanm_p"] = mp
    ms = np.ones((128, 128), np.float32)
    ms[:, 0::LS] = 0
    c["c_scanm_s"] = ms
    return c


_CACHE = {}


def kernel(**inputs):
    f = lambda a: np.ascontiguousarray(np.asarray(a, dtype=np.float32))
    if "nc" not in _CACHE:
        _CACHE["nc"] = build_program()
        _CACHE["consts"] = _consts()
    nc, _st = _CACHE["nc"]
    consts = _CACHE["consts"]
    x_prompt = f(inputs["x_prompt"])
    x_sample = f(inputs["x_sample"])
    wnames = ["g_pre_mix", "w_in", "attn_sinks", "mu_shift", "w0", "w_decay_up", "a0", "w_a_up", "w_g_up", "k_k", "k_a", "r_k",
              "gn_w", "gn_b", "w_out", "g_post_mix", "g_pre_ffn", "w_ffn_in", "conv_w", "conv_b", "w_ffn_out", "g_post_ffn"]
    shared = {}
    for n in wnames:
        a = f(inputs[n])[0]
        if n == "r_k":
            a = a.reshape(512)
        shared[n] = np.ascontiguousarray(a)
    shared.update(consts)
    in_maps = []
    for c in range(8):
        m = dict(shared)
        m["xp"] = x_prompt[c % 4]
        sl = slice(c * NSEQ_S, (c + 1) * NSEQ_S)
        m["xs"] = np.ascontiguousarray(x_sample[sl].reshape(128, D))
        m["ck"] = np.ascontiguousarray(f(inputs["cache_k_win"])[0, sl].reshape(NSEQ_S, 128, 128))
        m["cv"] = np.ascontiguousarray(f(inputs["cache_v_win"])[0, sl].reshape(NSEQ_S, 128, 128))
        m["sshift"] = np.ascontiguousarray(f(inputs["state_shift"])[0, sl])
        m["swkv"] = np.ascontiguousarray(f(inputs["state_wkv"])[0, sl])
        m["sconv"] = np.ascontiguousarray(f(inputs["state_conv"])[0, sl].reshape(NSEQ_S * 2, DFF))
        in_maps.append(m)
    ncores = int(os.environ.get("MK_CORES", "8"))
    res = run_bass_kernel_spmd(nc, in_maps[:ncores], core_ids=list(range(ncores)))
    R = list(res.results) + [res.results[0]] * (8 - ncores)
    cat = lambda k, rng: np.stack([R[c][k] for c in rng], axis=0)
    y_prompt = cat("yp", range(4)).reshape(4, SEQ, D)
    y_sample = np.concatenate([R[c]["ys"].reshape(NSEQ_S, LS, D) for c in range(8)], axis=0)
    nkp = cat("kwp", range(4)).reshape(1, 4, 128, 2, 64)
    nvp = cat("vwp", range(4)).reshape(1, 4, 128, 2, 64)
    nsp = cat("shp", range(4)).reshape(1, 4, DSH)
    nwp = cat("wkvp", range(4)).reshape(1, 4, 8, 64, 64)
    ncp = cat("convp", range(4)).reshape(1, 4, 2, DFF)
    nks = np.concatenate([R[c]["kws"] for c in range(8)], axis=0).reshape(1, 128, 128, 2, 64)
    nvs = np.concatenate([R[c]["vws"] for c in range(8)], axis=0).reshape(1, 128, 128, 2, 64)
    nss = np.concatenate([R[c]["shs"] for c in range(8)], axis=0).reshape(1, 128, DSH)
    nws = np.concatenate([R[c]["wkvs"] for c in range(8)], axis=0).reshape(1, 128, 8, 64, 64)
    ncs = np.concatenate([R[c]["convs"].reshape(NSEQ_S, 2, DFF) for c in range(8)], axis=0).reshape(1, 128, 2, DFF)
    outs = (y_prompt, y_sample, nkp, nvp, nsp, nwp, ncp, nks, nvs, nss, nws, ncs)
    return tuple(np.ascontiguousarray(o.astype(np.float32)) for o in outs)
```

```python
import os
import contextlib
import numpy as np
import concourse.bass as bass
import concourse.mybir as mybir
from concourse.bass_utils import run_bass_kernel_spmd

F32 = mybir.dt.float32
BF16 = mybir.dt.bfloat16
ALU = mybir.AluOpType
AF = mybir.ActivationFunctionType
AX = mybir.AxisListType

ENGS = ("pe", "act", "dve", "pool", "sp")


class Op:
    __slots__ = ("eng", "fn", "deps", "dma", "signal", "sigval", "waits", "has_dependents")

    def __init__(self, eng, fn, dma):
        self.eng = eng
        self.fn = fn
        self.dma = dma
        self.deps = set()
        self.signal = False
        self.sigval = 0
        self.waits = []
        self.has_dependents = False


class Prog:
    def __init__(self, nc, same_engine_sync=True):
        self.nc = nc
        self.ops = {e: [] for e in ENGS}
        self.last_w = {}
        self.readers = {}
        self.same_engine_sync = same_engine_sync
        self.nops = 0

    def add(self, eng, fn, reads=(), writes=(), dma=None):
        op = Op(eng, fn, dma)
        deps = op.deps
        for k in reads:
            w = self.last_w.get(k)
            if w is not None:
                deps.add(w)
            if isinstance(k, tuple) and k[0] == "ps":
                for r in self.readers.get(k, ()):
                    if r.eng != eng:
                        deps.add(r)
        for k in writes:
            w = self.last_w.get(k)
            if w is not None:
                deps.add(w)
            for r in self.readers.get(k, ()):
                deps.add(r)
        deps.discard(op)
        for k in reads:
            lst = self.readers.setdefault(k, [])
            if dma is None:
                for i_, r_ in enumerate(lst):
                    if r_.dma is None and r_.eng == eng:
                        lst[i_] = op
                        break
                else:
                    lst.append(op)
            else:
                lst.append(op)
        for k in writes:
            self.last_w[k] = op
            self.readers[k] = []
        self.ops[eng].append(op)
        self.nops += 1
        return op

    def pe(self, fn, r=(), w=()):
        return self.add("pe", fn, r, w)

    def act(self, fn, r=(), w=()):
        return self.add("act", fn, r, w)

    def dve(self, fn, r=(), w=()):
        return self.add("dve", fn, r, w)

    def pool(self, fn, r=(), w=()):
        return self.add("pool", fn, r, w)

    def dma(self, q, sem, out, in_, r=(), w=(), **kw):
        return self.add(q, lambda e: e.dma_start(out=out, in_=in_, **kw), r, w, dma=sem)

    def _skip(self, d, op):
        return d.dma is None and d.eng == op.eng and (op.eng == "pe" or not self.same_engine_sync)

    def finalize(self):
        for e in ENGS:
            for op in self.ops[e]:
                for d in op.deps:
                    if not self._skip(d, op):
                        d.has_dependents = True
        eng_cnt = {e: 0 for e in ENGS}
        dma_cnt = {}
        for e in ENGS:
            for op in self.ops[e]:
                if op.dma is not None:
                    dma_cnt[op.dma] = dma_cnt.get(op.dma, 0) + 16
                    op.sigval = dma_cnt[op.dma]
                    op.signal = True
                elif op.has_dependents:
                    eng_cnt[e] += 1
                    op.sigval = eng_cnt[e]
                    op.signal = True
        for e in ENGS:
            for op in self.ops[e]:
                ws = {}
                for d in op.deps:
                    if self._skip(d, op):
                        continue
                    key = ("dma", d.dma) if d.dma is not None else ("eng", d.eng)
                    if ws.get(key, 0) < d.sigval:
                        ws[key] = d.sigval
                op.waits = list(ws.items())
        self.dma_names = sorted(dma_cnt.keys())

    def emit(self):
        nc = self.nc
        self.finalize()
        with contextlib.ExitStack() as st:
            sems = {}
            for e in ENGS:
                sems[("eng", e)] = st.enter_context(nc.semaphore("s_" + e))
            for n in self.dma_names:
                sems[("dma", n)] = st.enter_context(nc.semaphore("d_" + n))
            block = st.enter_context(nc.Block())

            def replay(eobj, ename):
                known = {}
                for op in self.ops[ename]:
                    for key, val in op.waits:
                        if known.get(key, 0) < val:
                            eobj.wait_ge(sems[key], val)
                            known[key] = val
                    ins = op.fn(eobj)
                    if op.signal and ins is not None:
                        if op.dma is not None:
                            ins.then_inc(sems[("dma", op.dma)], 16)
                        else:
                            ins.then_inc(sems[("eng", ename)], 1)

            @block.tensor
            def _(e):
                replay(e, "pe")

            @block.scalar
            def _(e):
                replay(e, "act")

            @block.vector
            def _(e):
                replay(e, "dve")

            @block.gpsimd
            def _(e):
                replay(e, "pool")

            @block.sync
            def _(e):
                replay(e, "sp")


D = 1024
DIN = 2464
DFF = 2816
NFC = 22
DSH = 1696
SEQ = 8192
NSEQ_S = 16
LS = 8
KAPPA = float(np.exp(-0.5))
RMS_EPS = 1e-6
GN_EPS = 64e-5
RING = 5
N_TILES = int(os.environ.get("MK_NTILES", "16"))
DO_SAMPLE = int(os.environ.get("MK_SAMPLE", "1"))
STOP = int(os.environ.get("MK_STOP", "99"))
HPS = int(os.environ.get("MK_HPS", "4"))
VAR = int(os.environ.get("MK_VAR", "0"))

RW0 = 768
WIN_CHUNKS = {}
for c in range(4):
    WIN_CHUNKS[("q", c)] = [(64 * c, 64, 0), (256 + 64 * c, 64, 64)]
    WIN_CHUNKS[("qs", c)] = [(64 * c + 32, 32, 0), (64 * c, 32, 32), (256 + 64 * c + 32, 32, 64), (256 + 64 * c, 32, 96)]
    WIN_CHUNKS[("r", c)] = [(RW0 + 128 * c, 128, 0)]
    WIN_CHUNKS[("k", c)] = [(RW0 + 512 + 128 * c, 128, 0)]
    WIN_CHUNKS[("v", c)] = [(RW0 + 1024 + 128 * c, 128, 0)]
WIN_CHUNKS[("ak", 0)] = [(512, 128, 0)]
WIN_CHUNKS[("aks", 0)] = [(544, 32, 0), (512, 32, 32), (608, 32, 64), (576, 32, 96)]
WIN_CHUNKS[("av", 0)] = [(640, 128, 0)]
WIN_CHUNKS[("lo", 0)] = [(RW0 + 1536, 64, 0)]
WIN_CHUNKS[("lg", 0)] = [(RW0 + 1600, 96, 0)]
WIN_ORDER = [("lo", 0), ("lg", 0)]
for c in range(4):
    WIN_ORDER += [("r", c), ("k", c), ("v", c)]
WIN_ORDER += [("ak", 0), ("aks", 0), ("av", 0)]
for c in range(4):
    WIN_ORDER += [("q", c), ("qs", c)]
WIN_IDX = {k: i for i, k in enumerate(WIN_ORDER)}
NWIN = len(WIN_ORDER)
RC = {}
for c in range(4):
    RC[("r", c)] = c
    RC[("k", c)] = 4 + c
    RC[("v", c)] = 8 + c
RC[("lo", 0)] = 12
RC[("lg", 0)] = 13
RC_NP = {12: 64, 13: 96}


def build_program():
    nc = bass.Bass("TRN2", target_bir_lowering=False)
    P = Prog(nc)
    st = contextlib.ExitStack()

    def din(name, shape, dt=F32):
        return nc.dram_tensor(name, list(shape), dt, kind="ExternalInput").ap()

    def dout(name, shape, dt=F32):
        return nc.dram_tensor(name, list(shape), dt, kind="ExternalOutput").ap()

    def dint(name, shape, dt=BF16):
        return nc.dram_tensor(name, list(shape), dt, kind="Internal").ap()

    def T(name, shape, dt=F32):
        return st.enter_context(nc.sbuf_tensor(name, list(shape), dt))

    xp = din("xp", [SEQ, D])
    xs = din("xs", [128, D])
    ck = din("ck", [NSEQ_S, 128, 128])
    cv = din("cv", [NSEQ_S, 128, 128])
    sshift = din("sshift", [NSEQ_S, DSH])
    swkv = din("swkv", [NSEQ_S, 8, 64, 64])
    sconv = din("sconv", [NSEQ_S * 2, DFF])
    g_pre_mix = din("g_pre_mix", [D])
    w_in = din("w_in", [D, DIN])
    attn_sinks = din("attn_sinks", [8])
    mu_shift = din("mu_shift", [DSH])
    w0 = din("w0", [512])
    w_decay_up = din("w_decay_up", [32, 512])
    a0 = din("a0", [512])
    w_a_up = din("w_a_up", [32, 512])
    w_g_up = din("w_g_up", [96, 512])
    k_k = din("k_k", [512])
    k_a = din("k_a", [512])
    r_k = din("r_k", [512])
    gn_w = din("gn_w", [512])
    gn_b = din("gn_b", [512])
    w_out = din("w_out", [D, D])
    g_post_mix = din("g_post_mix", [D])
    g_pre_ffn = din("g_pre_ffn", [D])
    w_ffn_in = din("w_ffn_in", [D, 2 * DFF])
    conv_w = din("conv_w", [3, DFF])
    conv_b = din("conv_b", [DFF])
    w_ffn_out = din("w_ffn_out", [DFF, D])
    g_post_ffn = din("g_post_ffn", [D])
    c_ident = din("c_ident", [128, 128])
    c_cos_p = din("c_cos_p", [128, SEQ])
    c_sin_p = din("c_sin_p", [128, SEQ])
    c_cos_s = din("c_cos_s", [128, 128])
    c_sin_s = din("c_sin_s", [128, 128])
    c_mask_own = din("c_mask_own", [128, 128])
    c_mask_prev = din("c_mask_prev", [128, 128])
    c_negmask_own = din("c_negmask_own", [128, 128])
    c_negmask_prev = din("c_negmask_prev", [128, 128])
    c_mask_sown = din("c_mask_sown", [128, 128])
    c_mask_scache = din("c_mask_scache", [128, NSEQ_S, 128])
    c_blockones = din("c_blockones", [128, 128])
    c_m2b = din("c_m2b", [128, 2, 128])
    c_m2k = din("c_m2k", [128, 2, 128])
    c_mT = din("c_mT", [128, 128])
    c_scanm_p = din("c_scanm_p", [128, 512])
    c_scanm_s = din("c_scanm_s", [128, 128])

    yp = dout("yp", [SEQ, D])
    ys = dout("ys", [128, D])
    kwp = dout("kwp", [128, 128])
    vwp = dout("vwp", [128, 128])
    shp = dout("shp", [1, DSH])
    wkvp = dout("wkvp", [8, 64, 64])
    convp = dout("convp", [2, DFF])
    kws = dout("kws", [NSEQ_S, 128, 128])
    vws = dout("vws", [NSEQ_S, 128, 128])
    shs = dout("shs", [NSEQ_S, DSH])
    wkvs = dout("wkvs", [NSEQ_S, 8, 64, 64])
    convs = dout("convs", [NSEQ_S * 2, DFF])

    wsc_in = dint("wsc_in", [NWIN, 128, 1024])
    wsc_out = dint("wsc_out", [8, 128, 1024])
    wsc_fin = dint("wsc_fin", [2 * NFC, 128, 1024])
    wsc_fout = dint("wsc_fout", [NFC, 128, 1024])

    pp = [st.enter_context(nc.psum_tensor("pp%d" % i, [128, 1024], F32)) for i in range(4)]
    nb = [0]

    def bank():
        b = nb[0] % 8
        nb[0] += 1
        return b

    def pair():
        if nb[0] % 2:
            nb[0] += 1
        b = nb[0] % 8
        nb[0] += 2
        return b

    def PS(b, n=512, np_=128, p0=0):
        o = (b % 2) * 512
        return pp[b // 2][p0:p0 + np_, o:o + n]

    def PSP(b):
        return pp[b // 2][:, :]

    def PSB(b, np_=128):
        return pp[b // 2].bitcast(BF16)[0:np_, (b % 2) * 1024:(b % 2) * 1024 + 1024]

    def kb(b):
        return ("ps", b)

    idf = T("idf", [128, 128])
    idb = T("idb", [128, 128], BF16)
    gTpre = T("gTpre", [128, 8])
    gTffn = T("gTffn", [128, 8])
    gpost_bc = T("gpost_bc", [128, D])
    gpostf_bc = T("gpostf_bc", [128, D])
    gnw_bc = T("gnw_bc", [128, 512])
    gnb_bc = T("gnb_bc", [128, 512])
    esink = T("esink", [128, 8])
    w0T = T("w0T", [128, 4])
    a0T = T("a0T", [128, 4])
    kkT = T("kkT", [128, 4])
    kaT = T("kaT", [128, 4])
    rkT = T("rkT", [128, 4])
    ln2x20 = T("ln2x20", [128, 1])
    muT = T("muT", [128, 14])
    ommT = T("ommT", [128, 14])
    cwT = T("cwT", [128, 3, NFC])
    cbT = T("cbT", [128, NFC])
    Wd_b = T("Wd_b", [32, 512], BF16)
    Wa_b = T("Wa_b", [64, 512], BF16)
    Wg_b = T("Wg_b", [96, 512], BF16)
    blk1 = T("blk1", [128, 128], BF16)
    m_own = T("m_own", [128, 4, 128], BF16)
    m_prev = T("m_prev", [128, 4, 128], BF16)
    m2b = T("m2b", [128, 2, 128], BF16)
    m2k = T("m2k", [128, 2, 128], BF16)
    mTl = T("mTl", [128, 128], BF16)
    scanm_p = T("scanm_p", [128, 512])
    cstage = T("cstage", [128, 512])

    ld_n = [0]

    def cload(dst, src, xw=(), **kw):
        i = ld_n[0]
        ld_n[0] += 1
        k = ("c", i)
        P.dma("sp", "cl%d" % i, dst, src, w=[k] + list(xw), **kw)
        return k

    def cload_cast(dst_bf, src, np_, shape_free, eng="dve"):
        nfree = int(np.prod(shape_free))
        stg = cstage[0:np_, 0:nfree]
        k = ("c", ld_n[0])
        ld_n[0] += 1
        P.dma("sp", "cst", stg, src, w=["cstage"])
        P.dve(lambda e: e.tensor_copy(out=dst_bf, in_=stg), r=["cstage"], w=[k, "cstage_rd"])
        return k

    NSC = ALLOW = dict(allow_slow_non_contiguous=True)
    CK = []
    CK.append(cload(idf[:], c_ident))
    P.dve(lambda e: e.tensor_copy(out=idb[:], in_=idf[:]), r=[CK[-1]], w=["idb"])
    CK.append(cload(gTpre[:], g_pre_mix.rearrange("(c p) -> p c", p=128), **NSC))
    kgpre = CK[-1]
    CK.append(cload(gTffn[:], g_pre_ffn.rearrange("(c p) -> p c", p=128), **NSC))
    kgffn = CK[-1]
    CK.append(cload(gpost_bc[:], g_post_mix.partition_broadcast(128)))
    CK.append(cload(gpostf_bc[:], g_post_ffn.partition_broadcast(128)))
    CK.append(cload(gnw_bc[:], gn_w.partition_broadcast(128)))
    CK.append(cload(gnb_bc[:], gn_b.partition_broadcast(128)))
    CK.append(cload(esink[:], attn_sinks.partition_broadcast(128)))
    P.act(lambda e: e.activation(out=esink[:], in_=esink[:], func=AF.Exp), r=[CK[-1]], w=["esink"])
    for tt, src in ((w0T, w0), (a0T, a0), (kkT, k_k), (kaT, k_a), (rkT, r_k)):
        CK.append(cload(tt[:], src.rearrange("(c p) -> p c", p=128), **NSC))
    P.pool(lambda e: e.memset(muT[:], 0.0), w=["muT"])
    P.pool(lambda e: e.memset(ln2x20[:], float(20.0 * np.log(2.0))), w=["ln2c"])
    CK.append(cload(muT[:, 0:12], mu_shift[0:1536].rearrange("(c p) -> p c", p=128), xw=["muT"], **NSC))
    CK.append(cload(muT[0:64, 12:13], mu_shift[1536:1600].rearrange("(p c) -> p c", c=1), xw=["muT"], **NSC))
    CK.append(cload(muT[0:96, 13:14], mu_shift[1600:1696].rearrange("(p c) -> p c", c=1), xw=["muT"], **NSC))
    P.dve(lambda e: e.tensor_scalar(out=ommT[:], in0=muT[:], scalar1=-1.0, scalar2=1.0, op0=ALU.mult, op1=ALU.add),
          r=["muT"], w=["muT2"])
    CK.append(cload(cwT[:], conv_w.rearrange("j (c p) -> p j c", p=128), **NSC))
    CK.append(cload(cbT[:], conv_b.rearrange("(c p) -> p c", p=128), **NSC))
    CK.append(cload(scanm_p[:], c_scanm_p))

    def cast_const(dst, src, np_, nfree, bcast4=False):
        stg = cstage[0:np_, 0:nfree]
        P.dma("sp", "cst", stg, src, w=["cstage"])
        if bcast4:
            P.dve(lambda e: e.tensor_copy(out=dst, in_=stg.unsqueeze(1).to_broadcast([np_, 4, nfree])),
                  r=["cstage"], w=["cconst", "cstage"])
        else:
            P.dve(lambda e: e.tensor_copy(out=dst, in_=stg), r=["cstage"], w=["cconst", "cstage"])

    cast_const(blk1[:], c_blockones, 128, 128)
    cast_const(m_own[:], c_negmask_own, 128, 128, bcast4=True)
    cast_const(m_prev[:], c_negmask_prev, 128, 128, bcast4=True)
    cast_const(m2b[:].rearrange("p a b -> p (a b)"), c_m2b.rearrange("p a b -> p (a b)"), 128, 256)
    cast_const(m2k[:].rearrange("p a b -> p (a b)"), c_m2k.rearrange("p a b -> p (a b)"), 128, 256)
    cast_const(mTl[:], c_mT, 128, 128)
    cast_const(Wd_b[:], w_decay_up, 32, 512)
    P.dma("sp", "cst", cstage[32:64, 0:512], w_a_up, w=["cstage"])
    P.dve(lambda e: e.tensor_copy(out=Wa_b[32:64, :], in_=cstage[32:64, 0:512]), r=["cstage"], w=["cconst", "cstage"])
    cast_const(Wg_b[:], w_g_up, 96, 512)
    CONST = CK + ["idb", "esink", "cconst", "muT", "muT2", "ln2c"]

    x_t = T("x_t", [128, 4, D])
    bufA = T("bufA", [128, 8, 512], BF16)
    BUFA_ALL = [("bufA", b) for b in range(4)]
    wp_n = [0]

    def prep_chunk(dst_dram, dkey, loads, gT=None, gkey=None):
        i = wp_n[0] % 4
        wp_n[0] += 1
        stage = x_t[:, i, :]
        wbs_i = bufA[:, 2 * i:2 * i + 2, :].rearrange("p a b -> p (a b)")
        for mk, src in loads:
            P.dma("sp", "wl%d" % i, mk(stage), src, w=[("x", i)])
        if gT is not None:
            P.dve(lambda e: e.tensor_tensor(out=wbs_i.rearrange("p (c n) -> p c n", c=8),
                                            in0=stage.rearrange("p (c n) -> p c n", c=8),
                                            in1=gT[:].unsqueeze(2).to_broadcast([128, 8, 128]), op=ALU.mult),
                  r=[("x", i), gkey], w=[("wbs", i)])
        else:
            P.dve(lambda e: e.tensor_copy(out=wbs_i, in_=stage), r=[("x", i)], w=[("wbs", i)])
        P.dma("pool", "ws%d" % i, dst_dram, wbs_i, r=[("wbs", i)], w=[dkey])

    P.pool(lambda e: e.memset(x_t[:, :, :], 0.0), w=[("x", b_) for b_ in range(4)])
    win3 = w_in.rearrange("(c p) n -> p c n", p=128)
    for ci, key in enumerate(WIN_ORDER):
        loads = []
        for (cs, n, o) in WIN_CHUNKS[key]:
            loads.append((lambda s, o=o, n=n: s.rearrange("p (c n) -> p c n", c=8)[:, :, o:o + n], win3[:, :, cs:cs + n]))
        if key[0] in ("lo", "lg"):
            pass
        prep_chunk(wsc_in[ci], ("wsc_in", ci), loads, gTpre, kgpre)
    wfi3 = w_ffn_in.rearrange("(c p) n -> p c n", p=128)
    for fc in range(NFC):
        for zu in range(2):
            cs = zu * DFF + fc * 128
            prep_chunk(wsc_fin[2 * fc + zu], ("wsc_fin", 2 * fc + zu),
                       [(lambda s: s.rearrange("p (c n) -> p c n", c=8), wfi3[:, :, cs:cs + 128])], gTffn, kgffn)
    for c in range(8):
        prep_chunk(wsc_out[c], ("wsc_out", c), [(lambda s: s, w_out[c * 128:(c + 1) * 128, :])])
    for c in range(NFC):
        prep_chunk(wsc_fout[c], ("wsc_fout", c), [(lambda s: s, w_ffn_out[c * 128:(c + 1) * 128, :])])

    P.dve(lambda e: e.memset(bufA[0:1, 0, 0:2], 0.0), r=[("wbs", i_) for i_ in range(4)], w=BUFA_ALL + [("wbs", i_) for i_ in range(4)])
    ring = T("ring", [128, RING, 1024], BF16)
    xn = T("xn", [128, D], BF16)
    junk = T("junk", [128, D], BF16)
    cs_t = T("cs_t", [128, 2, 512])
    qT = T("qT", [128, 4, 512], BF16)
    kbuf = T("kbuf", [128, 5 * 128], BF16)
    vtokf = T("vtokf", [128, 128])
    Vaug = T("Vaug", [128, 5, 2, 65], BF16)
    Et = T("Et", [128, 4, 512], BF16)
    oacc = T("oacc", [128, 2, 4, 65])
    den = T("den", [128, 8])
    mix = T("mix", [128, 4, D], BF16)
    hidT = T("hidT", [128, NFC, 512], BF16)
    zbuf = T("zbuf", [128, 1, 640])
    za = T("za", [128, 1, 512])
    zcar_p = T("zcar_p", [128, NFC, 1, 2])
    zcar_s = T("zcar_s", [128, NFC, NSEQ_S, 2])
    sstat = T("sstat", [128, 16])
    hbuf = T("hbuf", [128, 1, 640])
    hcar_p = T("hcar_p", [128, 14, 1])
    hcar_s = T("hcar_s", [128, 14, NSEQ_S])
    dtmp = T("dtmp", [128, 512])
    hs_lo = T("hs_lo", [64, 512])
    hs_lg = T("hs_lg", [96, 512])
    lorab = T("lorab", [64, 512], BF16)
    sgb = T("sgb", [96, 512], BF16)
    rkv = T("rkv", [128, 1, 3, 512])
    tq = [T("tq%d" % i, [128, 512]) for i in range(7)]
    yq, yq2, kf32, vT32 = tq[0], tq[1], tq[3], tq[4]
    vblk_s = T("vblk_s", [128, 512], BF16)
    Sld = tq[5][0:64, 0:256].rearrange("p (a b) -> p a b", a=2)
    sqb = T("sqb", [128, 512], BF16)
    Kt = T("Kt", [128, 512], BF16)
    Bt = T("Bt", [128, 512], BF16)
    KR = T("KR", [128, 1024], BF16)
    KgL = T("KgL", [128, 512], BF16)
    BgLn = T("BgLn", [128, 512], BF16)
    vb = T("vb", [128, 512], BF16)
    prodb = T("prodb", [128, 512], BF16)
    gL2 = T("gL2", [128, 2, 16])
    nkcl = T("nkcl", [128, 16])
    vtok = T("vtok", [128, 2048], BF16)
    KgLtok = T("KgLtok", [128, 512], BF16)
    BgLtok = T("BgLtok", [128, 512], BF16)
    NQ2 = T("NQ2", [128, 2048], BF16)
    KQ2 = T("KQ2", [128, 2048], BF16)
    NT2 = T("NT2", [128, 1024], BF16)
    T32 = T("T32", [128, 1024])
    Tb = T("Tb", [128, 1024], BF16)
    XTs = T("XTs", [128, 128], BF16)
    SATs = T("SATs", [128, 128], BF16)
    ytok = T("ytok", [128, 4, 512])
    rkb = T("rkb", [128, 4, 8])
    Sm = T("Sm", [128, 4, 64])
    Sb = T("Sb", [128, 4, 64], BF16)
    gstat = T("gstat", [128, 4, 8])
    hid32 = hidT[:].rearrange("p a b -> p (a b)").bitcast(F32)
    HID_ALL = [("hidT", fc) for fc in range(NFC)]
    rowbuf = hid32[0:32, 0:DSH]
    sst = hid32[0:32, 1792:1792 + DFF]
    scanm_s = T("scanm_s", [128, 128])
    m_sown = T("m_sown", [128, 4, 128], BF16)
    m_scache = T("m_scache", [128, NSEQ_S, 128], BF16)
    ckT = T("ckT", [128, 128], BF16)
    ckf = T("ckf", [128, 2, 128])
    cVaug = T("cVaug", [128, 2, 2, 65], BF16)
    ytok_s = hid32[0:8, 0:NSEQ_S * 128].rearrange("p (s n) -> p s n", s=NSEQ_S)

    state = dict(ring_n=0, xld=0, ost=0)

    def stream(src, skey):
        n = state["ring_n"]
        state["ring_n"] += 1
        s = n % RING
        P.dma("sp", "rg%d" % s, ring[:, s, :], src, r=[skey], w=[("ring", s)])
        return s

    def emit_tile(kind, ti):
        NT = 512 if kind == "p" else 128
        NB = NT // 128
        nseq, Lseq = (1, 512) if kind == "p" else (NSEQ_S, LS)
        L = 128 if kind == "p" else LS
        nch = NT // L
        t0 = ti * 512
        first = (kind == "p" and ti == 0)
        last = (kind == "s") or (ti == N_TILES - 1)
        xsrc = xp if kind == "p" else xs
        ydst = yp if kind == "p" else ys
        hcar = hcar_p if kind == "p" else hcar_s
        zcar = zcar_p if kind == "p" else zcar_s
        scanm = scanm_p if kind == "p" else scanm_s
        S_ = slice(0, NT)

        if kind == "p":
            P.dma("pool", "cs", cs_t[:, 0, S_], c_cos_p[:, t0:t0 + NT], w=["cs_t"])
            P.dma("pool", "cs", cs_t[:, 1, S_], c_sin_p[:, t0:t0 + NT], w=["cs_t"])
        else:
            P.dma("pool", "cs", cs_t[:, 0, S_], c_cos_s, w=["cs_t"])
            P.dma("pool", "cs", cs_t[:, 1, S_], c_sin_s, w=["cs_t"])

        for b in range(NB):
            P.dma("sp", "xl%d" % b, x_t[:, b, :], xsrc[t0 + b * 128:t0 + (b + 1) * 128, :] if kind == "p" else xsrc,
                  w=[("x", b)])
            sc = sstat[:, b:b + 1]
            P.act(lambda e, b=b, sc=sc: e.activation(out=junk[:], in_=x_t[:, b, :], func=AF.Square, accum_out=sc),
                  r=[("x", b)], w=["junk", ("ss", b)])
            P.act(lambda e, sc=sc: e.activation(out=sc, in_=sc, func=AF.Sqrt, scale=1.0 / D, bias=RMS_EPS),
                  r=[("ss", b)], w=[("ss", b)])
            P.dve(lambda e, sc=sc: e.reciprocal(out=sc, in_=sc), r=[("ss", b)], w=[("ss", b)])
        for b in range(NB):
            sc = sstat[:, b:b + 1]
            P.dve(lambda e, b=b, sc=sc: e.tensor_scalar(out=xn[:], in0=x_t[:, b, :], scalar1=sc, scalar2=None, op0=ALU.mult),
                  r=[("x", b), ("ss", b)], w=["xn"])
            bk = bank()
            for c in range(8):
                P.pe(lambda e, c=c, bk=bk: e.transpose(out=PSB(bk)[:, c * 128:(c + 1) * 128], in_=xn[:, c * 128:(c + 1) * 128],
                                                      identity=idb[:]), r=["xn", "idb"], w=[kb(bk)])
            P.act(lambda e, b=b, bk=bk: e.copy(out=bufA[:, :, b * 128:(b + 1) * 128],
                                               in_=PSB(bk).rearrange("p (c n) -> p c n", c=8)),
                  r=[kb(bk)], w=[("bufA", b)])
        bufA_keys = [("bufA", b) for b in range(NB)]

        if kind == "s" and STOP <= 1:
            return
        def inproj(key, ncols):
            ci = WIN_IDX[key]
            s = stream(wsc_in[ci], ("wsc_in", ci))
            bk = bank()
            for c in range(8):
                P.pe(lambda e, c=c, s=s, bk=bk: e.matmul(PS(bk, NT, ncols), lhsT=ring[:, s, c * 128:c * 128 + ncols],
                                                         rhs=bufA[:, c, S_], start=(c == 0), stop=(c == 7)),
                     r=[("ring", s)] + bufA_keys, w=[kb(bk)])
            return bk

        hb_n = [0]

        def shift_evac(bk, np_, rc, out_ap, okey, xw=()):
            i = 0
            hb = hbuf[0:np_, i, 0:nseq * (Lseq + 1)].rearrange("p (s l) -> p s l", s=nseq)
            hk = ("hbuf", i)
            P.act(lambda e: e.copy(out=hb[:, :, 1:Lseq + 1], in_=PS(bk, NT, np_).rearrange("p (s l) -> p s l", s=nseq)),
                  r=[kb(bk)], w=[hk])
            if VAR != 1:
                P.pool(lambda e: e.tensor_copy(out=hb[:, :, 0:1], in_=hcar[0:np_, rc, :].unsqueeze(2)),
                       r=[("hcar", rc)], w=[hk])
                P.pool(lambda e: e.tensor_copy(out=hcar[0:np_, rc, :].unsqueeze(2), in_=hb[:, :, Lseq:Lseq + 1]),
                       r=[hk], w=[("hcar", rc)])
            if VAR == 2:
                return
            d3 = dtmp[0:np_, S_].rearrange("p (s l) -> p s l", s=nseq)
            P.act(lambda e: e.activation(out=d3, in_=hb[:, :, 0:Lseq], func=AF.Copy, scale=muT[0:np_, rc:rc + 1]),
                  r=[hk, "muT"], w=["dtmp"])
            P.dve(lambda e: e.scalar_tensor_tensor(out=out_ap.rearrange("p (s l) -> p s l", s=nseq), in0=hb[:, :, 1:Lseq + 1],
                                                   scalar=ommT[0:np_, rc:rc + 1], in1=d3,
                                                   op0=ALU.mult, op1=ALU.add),
                  r=["dtmp", hk, "muT", "muT2"], w=[okey] + list(xw))

        bk = inproj(("lo", 0), 64)
        shift_evac(bk, 64, 12, hs_lo[:, S_], "hs_lo")
        P.act(lambda e: e.activation(out=lorab[0:32, S_], in_=hs_lo[0:32, S_], func=AF.Tanh), r=["hs_lo"], w=["lorab0"])
        P.act(lambda e: e.copy(out=lorab[32:64, S_], in_=hs_lo[32:64, S_]), r=["hs_lo"], w=["lorab1"])
        bk = inproj(("lg", 0), 96)
        shift_evac(bk, 96, 13, hs_lg[:, S_], "hs_lg")
        P.act(lambda e: e.activation(out=sgb[:, S_], in_=hs_lg[:, S_], func=AF.Sigmoid), r=["hs_lg"], w=["sgb"])

        if kind == "s" and STOP <= 2:
            return
        def hp_gen(hp):
            rs_i = (hp % 2) if PIPE else 0
            gLv = gL2[:, rs_i, :]
            gk = ("gL", rs_i)
            rkv_v = (lambda j: rkv[:, 0, j, S_]) if rs_i == 0 else (lambda j: hid32[:, j * 512:j * 512 + NT])
            rs, ks, vs = rkv_v(0), rkv_v(1), rkv_v(2)
            for j, nm in enumerate(("r", "k", "v")):
                bk = inproj((nm, hp), 128)
                shift_evac(bk, 128, RC[(nm, hp)], rkv_v(j), ("rkv", rs_i, j), xw=([("hidT", fc_) for fc_ in range(6)] if rs_i == 1 else []))
                yield "p0"
            kr_, kk_, kv_ = ("rkv", rs_i, 0), ("rkv", rs_i, 1), ("rkv", rs_i, 2)
            t = [x[:, S_] for x in tq]
            if kind == "s" and STOP == 32:
                return
            bw = bank()
            P.pe(lambda e, bw=bw, hp=hp: e.matmul(PS(bw, NT), lhsT=Wd_b[0:32, hp * 128:(hp + 1) * 128], rhs=lorab[0:32, S_],
                                                  start=True, stop=True), r=["lorab0", "cconst"], w=[kb(bw)])
            P.act(lambda e, bw=bw, hp=hp: e.activation(out=t[0], in_=PS(bw, NT), func=AF.Sigmoid, bias=w0T[:, hp:hp + 1]),
                  r=[kb(bw)] + CONST, w=["t0"])
            ba = bank()
            P.pe(lambda e, ba=ba, hp=hp: e.matmul(PS(ba, NT), lhsT=Wa_b[32:64, hp * 128:(hp + 1) * 128], rhs=lorab[32:64, S_],
                                                  start=True, stop=True), r=["lorab1", "cconst"], w=[kb(ba)])
            P.act(lambda e, ba=ba, hp=hp: e.activation(out=t[6], in_=PS(ba, NT), func=AF.Sigmoid, bias=a0T[:, hp:hp + 1]),
                  r=[kb(ba)] + CONST, w=["t6"])
            P.dve(lambda e: e.tensor_tensor_scan(out=t[1], data0=scanm[:, S_], data1=t[0], initial=0.0, op0=ALU.mult, op1=ALU.add),
                  r=["t0"] + CONST, w=["t1"])
            P.dve(lambda e: e.tensor_tensor(out=t[2], in0=t[1], in1=t[0], op=ALU.subtract), r=["t0", "t1"], w=["t2"])
            yield "p1"
            P.act(lambda e: e.activation(out=t[0], in_=t[1], func=AF.Exp, scale=-KAPPA), r=["t1", "t2"], w=["t0"])
            P.act(lambda e: e.activation(out=t[3], in_=t[1], func=AF.Exp, scale=KAPPA), r=["t1"], w=["t3"])
            P.act(lambda e: e.activation(out=t[2], in_=t[2], func=AF.Exp, scale=-KAPPA), r=["t2"], w=["t2"])
            yield "p1"
            cp3 = t[1].rearrange("p (c l) -> p c l", l=L)
            eg3 = t[0].rearrange("p (c l) -> p c l", l=L)
            P.dve(lambda e: e.tensor_copy(out=gLv[:, 0:nch].unsqueeze(2), in_=eg3[:, :, L - 1:L]), r=["t0"], w=[gk])
            P.dve(lambda e: e.tensor_scalar(out=nkcl[:, 0:nch].unsqueeze(2), in0=cp3[:, :, L - 1:L], scalar1=-KAPPA, scalar2=None,
                                            op0=ALU.mult), r=["t1"], w=["nkcl"])
            P.dve(lambda e: e.scalar_tensor_tensor(out=t[4].rearrange("p (c l) -> p c l", l=L), in0=cp3, scalar=KAPPA,
                                                   in1=nkcl[:, 0:nch].unsqueeze(2).to_broadcast([128, nch, L]),
                                                   op0=ALU.mult, op1=ALU.add), r=["t1", "nkcl"], w=["t4"])
            P.act(lambda e: e.activation(out=t[4], in_=t[4], func=AF.Exp), r=["t4"], w=["t4"])
            if kind == "s" and STOP == 33:
                return
            yield "p1"
            yield "p1"
            P.dve(lambda e, hp=hp: e.tensor_scalar(out=t[5], in0=ks, scalar1=kkT[:, hp:hp + 1], scalar2=None, op0=ALU.mult),
                  r=[kk_] + CONST, w=["t5"])
            P.act(lambda e: e.activation(out=sqb[:, S_], in_=t[5], func=AF.Square), r=["t5"], w=["sqb"])
            bn = bank()
            P.pe(lambda e, bn=bn: e.matmul(PS(bn, NT), lhsT=blk1[:], rhs=sqb[:, S_], start=True, stop=True),
                 r=["sqb", "cconst"], w=[kb(bn)])
            yield "p1"
            P.dve(lambda e, bn=bn: e.tensor_scalar(out=t[1], in0=PS(bn, NT), scalar1=float(2.0 ** 40), scalar2=float(1e-24 * 2.0 ** 40),
                                                   op0=ALU.mult, op1=ALU.max),
                  r=[kb(bn), "t4", gk, "nkcl"], w=["t1"])
            P.act(lambda e: e.activation(out=t[1], in_=t[1], func=AF.Ln), r=["t1"], w=["t1"])
            P.act(lambda e: e.activation(out=t[1], in_=t[1], func=AF.Exp, scale=-0.5, bias=ln2x20[:, 0:1]), r=["t1"] + CONST, w=["t1"])
            P.dve(lambda e: e.tensor_tensor(out=t[5], in0=t[5], in1=t[1], op=ALU.mult), r=["t5", "t1"], w=["t5"])
            yield "p1"
            P.dve(lambda e: e.tensor_tensor(out=t[1], in0=t[5], in1=t[6], op=ALU.mult), r=["t5", "t6", "t1"], w=["t1"])
            yield "p1"
            P.dve(lambda e, hp=hp: e.tensor_scalar(out=t[6], in0=t[6], scalar1=1.0, scalar2=kaT[:, hp:hp + 1],
                                                   op0=ALU.subtract, op1=ALU.mult), r=["t6", "t1"] + CONST, w=["t6"])
            P.dve(lambda e: e.scalar_tensor_tensor(out=t[6], in0=t[6], scalar=1.0, in1=ks, op0=ALU.add, op1=ALU.mult),
                  r=["t6", kk_], w=["t6"])
            if kind == "s" and STOP == 34:
                return
            yield "p1done"
            KR4 = KR[:, 0:2 * NT].rearrange("p (c a l) -> p c a l", a=2, l=L)
            P.dve(lambda e: e.tensor_tensor(out=Kt[:, S_], in0=t[6], in1=t[3], op=ALU.mult), r=["t6", "t3"], w=["Kt"])
            P.dve(lambda e: e.tensor_tensor(out=Bt[:, S_], in0=t[1], in1=t[3], op=ALU.mult), r=["t1", "t3"], w=["Bt"])
            P.dve(lambda e: e.tensor_tensor(out=KR4[:, :, 0, :], in0=t[5].rearrange("p (c l) -> p c l", l=L),
                                            in1=t[2].rearrange("p (c l) -> p c l", l=L), op=ALU.mult), r=["t5", "t2"], w=["KR"])
            P.dve(lambda e: e.tensor_tensor(out=KR4[:, :, 1, :], in0=rs.rearrange("p (c l) -> p c l", l=L),
                                            in1=t[0].rearrange("p (c l) -> p c l", l=L), op=ALU.mult), r=[kr_, "t0"], w=["KR"])
            P.dve(lambda e: e.tensor_tensor(out=KgL[:, S_], in0=t[6], in1=t[4], op=ALU.mult), r=["t6", "t4"], w=["KgL"])
            P.dve(lambda e: e.scalar_tensor_tensor(out=BgLn[:, S_], in0=t[1], scalar=-1.0, in1=t[4], op0=ALU.mult, op1=ALU.mult),
                  r=["t1", "t4"], w=["BgLn"])
            P.act(lambda e: e.copy(out=vb[:, S_], in_=vs), r=[kv_], w=["vb"])
            P.dve(lambda e, hp=hp: e.scalar_tensor_tensor(out=prodb[:, S_], in0=rs, scalar=rkT[:, hp:hp + 1], in1=t[6],
                                                          op0=ALU.mult, op1=ALU.mult), r=[kr_, "t6"] + CONST, w=["prodb"])
            if kind == "s" and STOP == 35:
                return
            brk = bank()
            for b in range(NB):
                P.pe(lambda e, b=b, brk=brk: e.matmul(PS(brk, 2 * NB)[:, 2 * b:2 * b + 2], lhsT=prodb[:, b * 128:(b + 1) * 128],
                                                      rhs=blk1[:, 0:128:64], start=True, stop=True),
                     r=["prodb", "cconst"], w=[kb(brk)])
            P.act(lambda e, brk=brk, hp=hp: e.copy(out=rkb[:, 0:NB, 2 * hp:2 * hp + 2],
                                                   in_=PS(brk, 2 * NB).rearrange("p (b a) -> p b a", a=2)),
                  r=[kb(brk)], w=[("rkb", hp)])
            if kind == "s" and STOP == 36:
                return
            if kind == "p":
                vtok_hp = vtok[0:L, hp * nch * 128:(hp + 1) * nch * 128].rearrange("p (c n) -> p c n", n=128)
                vkey = ("vtok", hp)
            else:
                vtok_hp = vtok[0:L, 0:nch * 128].rearrange("p (c n) -> p c n", n=128)
                vkey = "vtok_s"
            if kind == "p":
                KgLtok3 = KgLtok[0:L, 0:nch * 128].rearrange("p (c n) -> p c n", n=128)
                BgLtok3 = BgLtok[0:L, 0:nch * 128].rearrange("p (c n) -> p c n", n=128)
                kBg = ["BgLtok"]
            else:
                KgLtok3 = hidT[:].rearrange("p a b -> p (a b)")[0:L, 8192:8192 + nch * 128].rearrange("p (c n) -> p c n", n=128)
                BgLtok3 = Et[:].rearrange("p a b -> p (a b)")[0:L, 0:nch * 128].rearrange("p (c n) -> p c n", n=128)
                kBg = ["BgLtok"] + [("Et", i_) for i_ in range(4)]
            for src, skey, dst3, dkey in ((vb, "vb", vtok_hp, vkey), (KgL, "KgL", KgLtok3, "KgLtok"), (BgLn, "BgLn", BgLtok3, kBg)):
                for g0 in range(0, nch, 8):
                    g1 = min(nch, g0 + 8)
                    bt = bank()
                    for c in range(g0, g1):
                        P.pe(lambda e, c=c, bt=bt, src=src, g0=g0: e.transpose(
                            out=PSB(bt, L)[:, (c - g0) * 128:(c - g0 + 1) * 128], in_=src[:, c * L:(c + 1) * L], identity=idb[:]),
                            r=[skey, "idb"], w=[kb(bt)])
                    P.act(lambda e, bt=bt, g0=g0, g1=g1, dst3=dst3: e.copy(
                        out=dst3[:, g0:g1, :], in_=PSB(bt, L)[:, 0:(g1 - g0) * 128].rearrange("p (c n) -> p c n", n=128)),
                        r=[kb(bt), "XA"], w=(dkey if isinstance(dkey, list) else [dkey]))
            if kind == "s" and STOP == 31:
                return
            if kind == "s":
                btv = bank()
                P.pe(lambda e, btv=btv: e.transpose(out=PSB(btv)[:, 0:128], in_=vb[:, 0:128], identity=idb[:]), r=["vb", "idb"], w=[kb(btv)])
                P.act(lambda e, btv=btv, hp=hp: e.copy(out=vblk_s[:, hp * 128:(hp + 1) * 128], in_=PSB(btv)[:, 0:128]), r=[kb(btv)], w=["vblk_s"])
                Sst = hid32[0:64, 2048:4096]
                if VAR == 3:
                    return
                for s_ in range(NSEQ_S):
                    P.dma("sp", "sst_in", Sst.rearrange("i (s h j) -> i s h j", s=NSEQ_S, h=2)[:, s_, :, :],
                          swkv[s_, 2 * hp:2 * hp + 2, :, :].rearrange("h i j -> i h j"), r=["XA"], w=["Sstage"] + HID_ALL)
                if VAR == 4:
                    return
                for g in range(2):
                    bts = bank()
                    for s8 in range(8):
                        s_ = g * 8 + s8
                        P.pe(lambda e, bts=bts, s8=s8, s_=s_, Sst=Sst: e.transpose(out=PS(bts, 512)[:, s8 * 64:(s8 + 1) * 64],
                                                                                 in_=Sst[:, s_ * 128:(s_ + 1) * 128], identity=idf[0:64, 0:64]),
                             r=["Sstage", "XA"] + CONST, w=[kb(bts)])
                    if VAR == 5:
                        return
                    P.act(lambda e, bts=bts, g=g: e.copy(out=SmS[:, g * 8:(g + 1) * 8, :], in_=PS(bts, 512).rearrange("p (s i) -> p s i", i=64)),
                          r=[kb(bts)], w=[("SmS", s_) for s_ in range(g * 8, g * 8 + 8)])
                    if VAR == 6:
                        return
                    P.dve(lambda e, bts=bts, g=g: e.tensor_copy(out=SbS[:, g * 8:(g + 1) * 8, :], in_=PS(bts, 512).rearrange("p (s i) -> p s i", i=64)),
                          r=[kb(bts)], w=[("SbS", s_) for s_ in range(g * 8, g * 8 + 8)])
            if kind == "s" and STOP == 3:
                return
            yield "p2done"
            nu = 2 * nch
            W = nu * L
            NQv = NQ2[0:L, 0:2 * W]
            KQv = KQ2[0:L, 0:2 * W]
            NTv = NT2[0:L, 0:W]
            T32v = T32[0:L, 0:W]
            Tbv = Tb[0:L, 0:W]
            NQ4 = NQv.rearrange("p (u a l) -> p u a l", a=2, l=L)
            KQ4 = KQv.rearrange("p (u a l) -> p u a l", a=2, l=L)
            kNQ, kKQ, kNT, kT32, kTb = "NQ2", "KQ2", "NT2", "T32", "Tb"
            for p in range(2):
                P0 = 64 * p
                for (lhs, lkey, dst4, dkey, msk) in ((Bt, "Bt", NQ4, kNQ, m2b), (Kt, "Kt", KQ4, kKQ, m2k)):
                    bq = pair()
                    for c in range(nch):
                        col = c * 2 * L
                        P.pe(lambda e, P0=P0, c=c, bq=bq, lhs=lhs, col=col: e.matmul(
                            PSP(bq)[0:L, col:col + 2 * L], lhsT=lhs[P0:P0 + 64, c * L:(c + 1) * L],
                            rhs=KR[P0:P0 + 64, c * 2 * L:(c + 1) * 2 * L], start=True, stop=True),
                            r=[lkey, "KR"], w=[kb(bq), kb(bq + 1)])
                    P.dve(lambda e, bq=bq, dst4=dst4, msk=msk, p=p: e.tensor_tensor(
                        out=dst4[:, p * nch:(p + 1) * nch, :, :],
                        in0=PSP(bq)[0:L, 0:nch * 2 * L].rearrange("p (c a l) -> p c a l", a=2, l=L),
                        in1=msk[0:L, :, 0:L].unsqueeze(1).to_broadcast([L, nch, 2, L]),
                        op=ALU.mult), r=[kb(bq), kb(bq + 1), "cconst"], w=[(dkey, p)])
                    pump()
            pump()
            bT2 = pair()
            for p in range(2):
                P0 = 64 * p
                for c in range(nch):
                    u = p * nch + c
                    P.pe(lambda e, P0=P0, c=c, u=u, bT2=bT2: e.matmul(PSP(bT2)[0:L, u * L:(u + 1) * L],
                                                                     lhsT=KR[P0:P0 + 64, c * 2 * L:c * 2 * L + L],
                                                                     rhs=Bt[P0:P0 + 64, c * L:(c + 1) * L], start=True, stop=True),
                         r=["KR", "Bt"], w=[kb(bT2), kb(bT2 + 1)])
            P.dve(lambda e, bT2=bT2: e.tensor_tensor(out=NTv.rearrange("p (u l) -> p u l", l=L),
                                                     in0=PSP(bT2)[0:L, 0:W].rearrange("p (u l) -> p u l", l=L),
                                                     in1=mTl[0:L, 0:L].unsqueeze(1).to_broadcast([L, nu, L]), op=ALU.mult),
                  r=[kb(bT2), kb(bT2 + 1), "cconst"], w=[kNT])
            kNQb = [(kNQ, 0), (kNQ, 1)]
            kKQb = [(kKQ, 0), (kKQ, 1)]
            P.dve(lambda e: e.tensor_tensor(out=T32v.rearrange("p (u l) -> p u l", l=L),
                                            in0=idf[0:L, 0:L].unsqueeze(1).to_broadcast([L, nu, L]),
                                            in1=NQ4[:, :, 0, :], op=ALU.subtract), r=kNQb + CONST, w=[kT32])
            P.act(lambda e: e.copy(out=Tbv, in_=T32v), r=[kT32], w=[kTb])
            nlev = int(np.log2(L)) - 1

            def emit_sq(lastlev):
                bPT = pair()
                for u in range(nu):
                    P.pe(lambda e, u=u, bPT=bPT: e.matmul(PSP(bPT)[0:L, u * L:(u + 1) * L], lhsT=NQ4[:, u, 0, :],
                                                          rhs=NTv[:, u * L:(u + 1) * L], start=True, stop=True),
                         r=kNQb + [kNT], w=[kb(bPT), kb(bPT + 1)])
                bP = None
                if not lastlev:
                    bP = pair()
                    for u in range(nu):
                        P.pe(lambda e, u=u, bP=bP: e.matmul(PSP(bP)[0:L, u * L:(u + 1) * L], lhsT=NTv[:, u * L:(u + 1) * L],
                                                            rhs=NQ4[:, u, 0, :], start=True, stop=True),
                             r=kNQb + [kNT], w=[kb(bP), kb(bP + 1)])
                return bPT, bP

            def emit_sq_evac(bPT, bP):
                P.act(lambda e, bPT=bPT: e.copy(out=NTv, in_=PSP(bPT)[0:L, 0:W]), r=[kb(bPT), kb(bPT + 1)], w=[kNT])
                if bP is not None:
                    P.dve(lambda e, bP=bP: e.tensor_copy(out=NQ4[:, :, 0, :], in_=PSP(bP)[0:L, 0:W].rearrange("p (u l) -> p u l", l=L)),
                          r=[kb(bP), kb(bP + 1)], w=kNQb)

            bb = emit_sq(nlev == 1)
            emit_sq_evac(*bb)
            for lev in range(nlev):
                nxt = None
                if lev + 1 < nlev:
                    nxt = emit_sq(lev + 1 == nlev - 1)
                bT = pair()
                for u in range(nu):
                    P.pe(lambda e, u=u, bT=bT: e.matmul(PSP(bT)[0:L, u * L:(u + 1) * L], lhsT=NTv[:, u * L:(u + 1) * L],
                                                        rhs=Tbv[:, u * L:(u + 1) * L], start=True, stop=True),
                         r=[kNT, kTb], w=[kb(bT), kb(bT + 1)])
                if nxt is not None:
                    emit_sq_evac(*nxt)
                P.dve(lambda e, bT=bT: e.tensor_tensor(out=T32v, in0=T32v, in1=PSP(bT)[0:L, 0:W], op=ALU.add),
                      r=[kb(bT), kb(bT + 1), kT32], w=[kT32])
                P.act(lambda e: e.copy(out=Tbv, in_=T32v), r=[kT32], w=[kTb])
                pump()
            for c in range(nch):
                if kind == "p":
                    Smv, Sbv, skm, skb = Sm[:, hp, :], Sb[:, hp, :], ("Sm", hp), ("Sb", hp)
                    zero_state = first and c == 0
                    ydst_ap, ykey = ytok[:, c, hp * 128:(hp + 1) * 128], ("ytok", c)
                else:
                    Smv, Sbv, skm, skb = SmS[:, c, :], SbS[:, c, :], ("SmS", c), ("SbS", c)
                    zero_state = False
                    ydst_ap, ykey = ytok_s[0:L, c, :], "ytok_s"
                bX = bank()
                for p in range(2):
                    P0 = 64 * p
                    u = p * nch + c
                    if not zero_state:
                        P.pe(lambda e, P0=P0, bX=bX, c=c, p=p, Sbv=Sbv: e.matmul(PS(bX, 128, L)[:, p * 64:(p + 1) * 64],
                                                                       lhsT=KR[P0:P0 + 64, c * 2 * L:c * 2 * L + L], rhs=Sbv[P0:P0 + 64, :],
                                                                       start=True, stop=False), r=["KR", skb], w=[kb(bX)])
                    P.pe(lambda e, P0=P0, bX=bX, c=c, p=p, u=u, zs=zero_state, vtok_hp=vtok_hp: e.matmul(PS(bX, 128, L)[:, p * 64:(p + 1) * 64], lhsT=KQ4[:, u, 0, :],
                                                                                      rhs=vtok_hp[:, c, P0:P0 + 64], start=zs, stop=True),
                         r=kKQb + [vkey], w=[kb(bX)])
                P.act(lambda e, bX=bX: e.copy(out=XTs[0:L, :], in_=PS(bX, 128, L)), r=[kb(bX)], w=["XTs"])
                bS = bank()
                for p in range(2):
                    u = p * nch + c
                    P.pe(lambda e, bS=bS, p=p, u=u: e.matmul(PS(bS, 128, L)[:, p * 64:(p + 1) * 64], lhsT=Tbv[:, u * L:(u + 1) * L],
                                                            rhs=XTs[0:L, p * 64:(p + 1) * 64], start=True, stop=True), r=[kTb, "XTs"], w=[kb(bS)])
                P.act(lambda e, bS=bS: e.copy(out=SATs[0:L, :], in_=PS(bS, 128, L)), r=[kb(bS)], w=["SATs"])
                bY = bank()
                for p in range(2):
                    P0 = 64 * p
                    u = p * nch + c
                    if not zero_state:
                        P.pe(lambda e, P0=P0, bY=bY, c=c, p=p, Sbv=Sbv: e.matmul(PS(bY, 128, L)[:, p * 64:(p + 1) * 64],
                                                                       lhsT=KR[P0:P0 + 64, c * 2 * L + L:(c + 1) * 2 * L], rhs=Sbv[P0:P0 + 64, :],
                                                                       start=True, stop=False), r=["KR", skb], w=[kb(bY)])
                    P.pe(lambda e, P0=P0, bY=bY, c=c, p=p, u=u, zs=zero_state, vtok_hp=vtok_hp: e.matmul(PS(bY, 128, L)[:, p * 64:(p + 1) * 64], lhsT=KQ4[:, u, 1, :],
                                                                                      rhs=vtok_hp[:, c, P0:P0 + 64], start=zs, stop=False),
                         r=kKQb + [vkey], w=[kb(bY)])
                    P.pe(lambda e, bY=bY, p=p, u=u: e.matmul(PS(bY, 128, L)[:, p * 64:(p + 1) * 64], lhsT=NQ4[:, u, 1, :],
                                                            rhs=SATs[0:L, p * 64:(p + 1) * 64], start=False, stop=True),
                         r=kNQb + ["SATs"], w=[kb(bY)])
                P.dve(lambda e, bY=bY, ydst_ap=ydst_ap: e.tensor_copy(out=ydst_ap, in_=PS(bY, 128, L)),
                      r=[kb(bY)] + (["XA"] if kind == "s" else []), w=[ykey])
                bZ = bank()
                for p in range(2):
                    P0 = 64 * p
                    P.pe(lambda e, P0=P0, bZ=bZ, c=c, vtok_hp=vtok_hp, KgLtok3=KgLtok3: e.matmul(PS(bZ, 64, 64, P0), lhsT=KgLtok3[:, c, P0:P0 + 64], rhs=vtok_hp[:, c, P0:P0 + 64],
                                                               start=True, stop=False), r=["KgLtok", vkey, "XA"], w=[kb(bZ)])
                    P.pe(lambda e, P0=P0, bZ=bZ, c=c, p=p, BgLtok3=BgLtok3: e.matmul(PS(bZ, 64, 64, P0), lhsT=BgLtok3[:, c, P0:P0 + 64], rhs=SATs[0:L, p * 64:(p + 1) * 64],
                                                                   start=False, stop=True), r=kBg + ["SATs"], w=[kb(bZ)])
                if zero_state:
                    P.dve(lambda e, bZ=bZ, Smv=Smv: e.tensor_copy(out=Smv, in_=PS(bZ, 64)), r=[kb(bZ)], w=[skm])
                else:
                    P.dve(lambda e, bZ=bZ, Smv=Smv, c=c: e.scalar_tensor_tensor(out=Smv, in0=Smv, scalar=gLv[:, c:c + 1], in1=PS(bZ, 64),
                                                                                op0=ALU.mult, op1=ALU.add), r=[kb(bZ), skm, gk], w=[skm])
                P.act(lambda e, Smv=Smv, Sbv=Sbv: e.copy(out=Sbv, in_=Smv), r=[skm], w=[skb])
                pump()
            if kind == "s":
                for s_ in range(NSEQ_S):
                    P.dma("sp", "yrl", ytok[s_ * LS:(s_ + 1) * LS, 0, hp * 128:(hp + 1) * 128], ytok_s[0:LS, s_, :],
                          r=["ytok_s", "XA"] + HID_ALL, w=[("ytok", 0)])
                Sst = hid32[0:64, 2048:4096]
                for g in range(4):
                    bts = bank()
                    for s4 in range(4):
                        s_ = g * 4 + s4
                        P.pe(lambda e, bts=bts, s4=s4, s_=s_: e.transpose(out=PS(bts, 512, 64)[:, s4 * 128:(s4 + 1) * 128], in_=SmS[:, s_, :], identity=idf[:]),
                             r=[("SmS", s_)] + CONST, w=[kb(bts)])
                    P.act(lambda e, bts=bts, g=g, Sst=Sst: e.copy(out=Sst[:, g * 512:(g + 1) * 512], in_=PS(bts, 512, 64)), r=[kb(bts), "XA"], w=["Sstage"])
                for s_ in range(NSEQ_S):
                    P.dma("sp", "sst_out", wkvs[s_, 2 * hp:2 * hp + 2, :, :].rearrange("h i j -> i h j"),
                          Sst.rearrange("i (s h j) -> i s h j", s=NSEQ_S, h=2)[:, s_, :, :], r=["Sstage", "XA"] + HID_ALL, w=["o_wkvs"])


        nhp = HPS if kind == "s" else 4
        PIPE = (kind == "p")
        pstate = {"g": None, "done": True}

        def pump():
            if pstate["done"] or pstate["g"] is None:
                return
            try:
                v = next(pstate["g"])
            except StopIteration:
                pstate["done"] = True
                return
            if v == "p1done":
                pstate["done"] = True

        def run_until(g, tag):
            while True:
                try:
                    v = next(g)
                except StopIteration:
                    return False
                if v == tag:
                    return True

        gens = [hp_gen(h) for h in range(nhp)]
        alive = run_until(gens[0], "p2done")
        for h in range(nhp):
            if not alive:
                break
            nxt = gens[h + 1] if (h + 1 < nhp) else None
            if nxt is not None and PIPE:
                pstate["g"], pstate["done"] = nxt, False
            else:
                pstate["g"], pstate["done"] = None, True
            run_until(gens[h], "__end__")
            if nxt is not None:
                if PIPE and not pstate["done"]:
                    run_until(nxt, "p1done")
                pstate["done"] = True
                alive = run_until(nxt, "p2done")
        if kind == "s" and (STOP <= 4 or 30 <= STOP < 40):
            return
        bkk = inproj(("ak", 0), 128)
        bks = inproj(("aks", 0), 128)
        P.dve(lambda e: e.tensor_tensor(out=kf32[:, S_], in0=PS(bkk, NT), in1=cs_t[:, 0, S_], op=ALU.mult), r=[kb(bkk), "cs_t"], w=["t3"])
        P.dve(lambda e: e.tensor_tensor(out=dtmp[:, S_], in0=PS(bks, NT), in1=cs_t[:, 1, S_], op=ALU.mult), r=[kb(bks), "cs_t"], w=["dtmp"])
        P.dve(lambda e: e.tensor_tensor(out=kf32[:, S_], in0=kf32[:, S_], in1=dtmp[:, S_], op=ALU.add), r=["t3", "dtmp"], w=["t3"])
        P.act(lambda e: e.copy(out=kbuf[:, 128:128 + NT], in_=kf32[:, S_]), r=["t3"], w=["kbuf"])
        bv = inproj(("av", 0), 128)
        P.act(lambda e: e.copy(out=vT32[:, S_], in_=PS(bv, NT)), r=[kb(bv)], w=["t4"])
        if first or kind == "s":
            P.pool(lambda e: e.memset(Vaug[:], 1.0), w=["Vaug"])
        for b in range(NB):
            bt = bank()
            P.pe(lambda e, b=b, bt=bt: e.transpose(out=PS(bt, 128), in_=vT32[:, b * 128:(b + 1) * 128], identity=idf[:]),
                 r=["t4"] + CONST, w=[kb(bt)])
            P.act(lambda e, b=b, bt=bt: e.copy(out=Vaug[:, b + 1, :, 0:64], in_=PS(bt, 128).rearrange("p (k d) -> p k d", k=2)),
                  r=[kb(bt)], w=["Vaug"])
            if last and b == NB - 1:
                P.dve(lambda e, bt=bt: e.tensor_copy(out=vtokf[:], in_=PS(bt, 128)), r=[kb(bt)], w=["vtokf"])
                if kind == "p":
                    P.dma("pool", "o_vw", vwp, vtokf[:], r=["vtokf"], w=["o_vwp"])
                else:
                    for s in range(NSEQ_S):
                        P.dma("sp", "o_vws", vws[s, 120:128, :], vtokf[s * LS:(s + 1) * LS, :], r=["vtokf"], w=["o_vws"])
                    P.dma("sp", "o_vw2", vws[:, 0:120, :].rearrange("s r c -> s (r c)"), cv[:, 8:128, :].rearrange("s r c -> s (r c)"), w=["o_vws2"])
                bt2 = bank()
                P.pe(lambda e, b=b, bt2=bt2: e.transpose(out=PS(bt2, 128), in_=kf32[:, b * 128:(b + 1) * 128], identity=idf[:]),
                     r=["t3"] + CONST, w=[kb(bt2)])
                P.dve(lambda e, bt2=bt2: e.tensor_copy(out=yq[:, 0:128], in_=PS(bt2, 128)), r=[kb(bt2)], w=["t0"])
                if kind == "p":
                    P.dma("pool", "o_kw", kwp, yq[:, 0:128], r=["t0"], w=["o_kwp"])
                else:
                    for s in range(NSEQ_S):
                        P.dma("sp", "o_kws", kws[s, 120:128, :], yq[s * LS:(s + 1) * LS, 0:128], r=["t0"], w=["o_kws"])
                    P.dma("sp", "o_kw2", kws[:, 0:120, :].rearrange("s r c -> s (r c)"), ck[:, 8:128, :].rearrange("s r c -> s (r c)"), w=["o_kws2"])
        for c in range(4):
            bq_ = inproj(("q", c), 128)
            bqs = inproj(("qs", c), 128)
            P.dve(lambda e, bq_=bq_: e.tensor_tensor(out=yq[:, S_], in0=PS(bq_, NT), in1=cs_t[:, 0, S_], op=ALU.mult),
                  r=[kb(bq_), "cs_t", "t0"], w=["t0"])
            P.dve(lambda e, bqs=bqs: e.tensor_tensor(out=yq2[:, S_], in0=PS(bqs, NT), in1=cs_t[:, 1, S_], op=ALU.mult),
                  r=[kb(bqs), "cs_t"], w=["t1"])
            P.pool(lambda e, c=c: e.tensor_tensor(out=qT[:, c, S_], in0=yq[:, S_], in1=yq2[:, S_], op=ALU.add),
                   r=["t0", "t1"], w=[("qT", c)])
        qkeys = [("qT", c) for c in range(4)]

        def epilogue(b):
            y3 = ytok[:, b, :].rearrange("p (h d) -> p h d", d=64)
            yk = ("ytok", b)
            gs = gstat[:, 0, :]
            P.dve(lambda e, y3=y3: e.tensor_reduce(out=gstat[:, 0, :], in_=y3, axis=AX.X, op=ALU.add), r=[yk], w=["gstat"])
            P.act(lambda e, b=b: e.activation(out=yq[:, :], in_=ytok[:, b, :], func=AF.Square), r=[yk, "t0"], w=["t0"])
            P.dve(lambda e: e.tensor_reduce(out=gstat[:, 1, :], in_=yq[:, :].rearrange("p (h d) -> p h d", d=64), axis=AX.X, op=ALU.add),
                  r=["t0"], w=["gstat"])
            P.dve(lambda e: e.tensor_scalar(out=gstat[:, 0, :], in0=gstat[:, 0, :], scalar1=1.0 / 64, scalar2=None, op0=ALU.mult),
                  r=["gstat"], w=["gstat"])
            P.dve(lambda e: e.tensor_tensor(out=gstat[:, 2, :], in0=gstat[:, 0, :], in1=gstat[:, 0, :], op=ALU.mult), r=["gstat"], w=["gstat"])
            P.dve(lambda e: e.scalar_tensor_tensor(out=gstat[:, 1, :], in0=gstat[:, 1, :], scalar=1.0 / 64, in1=gstat[:, 2, :],
                                                   op0=ALU.mult, op1=ALU.subtract), r=["gstat"], w=["gstat"])
            P.act(lambda e: e.activation(out=gstat[:, 1, :], in_=gstat[:, 1, :], func=AF.Sqrt, bias=GN_EPS), r=["gstat"], w=["gstat"])
            P.dve(lambda e: e.reciprocal(out=gstat[:, 1, :], in_=gstat[:, 1, :]), r=["gstat"], w=["gstat"])
            for h in range(8):
                P.dve(lambda e, h=h, y3=y3: e.tensor_scalar(out=yq2[:, h * 64:(h + 1) * 64], in0=y3[:, h, :], scalar1=gstat[:, 0, h:h + 1],
                                                            scalar2=gstat[:, 1, h:h + 1], op0=ALU.subtract, op1=ALU.mult),
                      r=[yk, "gstat", "t1"], w=["t1"])
            P.dve(lambda e: e.tensor_tensor(out=yq2[:, :], in0=yq2[:, :], in1=gnw_bc[:], op=ALU.mult), r=["t1"] + CONST, w=["t1"])
            P.dve(lambda e: e.tensor_tensor(out=yq2[:, :], in0=yq2[:, :], in1=gnb_bc[:], op=ALU.add), r=["t1"] + CONST, w=["t1"])
            for hp in range(4):
                if kind == "p":
                    vt_blk = vtok[:, (hp * nch + b) * 128:(hp * nch + b + 1) * 128]
                    vkeys = [("vtok", hp)]
                else:
                    vt_blk = None
                    vkeys = []
                for p in range(2):
                    h = 2 * hp + p
                    if kind == "p":
                        P.dve(lambda e, h=h, p=p, vt_blk=vt_blk, b=b: e.scalar_tensor_tensor(
                            out=yq2[:, h * 64:(h + 1) * 64], in0=vt_blk[:, p * 64:(p + 1) * 64], scalar=rkb[:, b, h:h + 1],
                            in1=yq2[:, h * 64:(h + 1) * 64], op0=ALU.mult, op1=ALU.add),
                            r=vkeys + [("rkb", hp), "t1"], w=["t1"])
                    else:
                        P.dve(lambda e, h=h, b=b: e.scalar_tensor_tensor(
                            out=yq2[:, h * 64:(h + 1) * 64], in0=vblk_s[:, h * 64:(h + 1) * 64], scalar=rkb[:, b, h:h + 1],
                            in1=yq2[:, h * 64:(h + 1) * 64], op0=ALU.mult, op1=ALU.add),
                            r=["vblk_s", ("rkb", hp), "t1"], w=["t1"])
            bg = bank()
            P.pe(lambda e, b=b, bg=bg: e.matmul(PS(bg, 512), lhsT=sgb[:, b * 128:(b + 1) * 128], rhs=Wg_b[:], start=True, stop=True),
                 r=["sgb", "cconst"], w=[kb(bg)])
            P.dve(lambda e, b=b, bg=bg: e.tensor_tensor(out=mix[:, b, 512:1024], in0=yq2[:, :], in1=PS(bg, 512), op=ALU.mult),
                  r=["t1", kb(bg)], w=[("mixr", b)])


        def attn_group(b, kblocks, first_grp, only_grp, additive=False):
            ng = len(kblocks)
            for kv in range(2):
                P0 = 64 * kv
                for i, (kfn, vfn, msk, kkeys) in enumerate(kblocks):
                    bs = bank()
                    P.pe(lambda e, P0=P0, bs=bs, kfn=kfn, kv=kv: e.matmul(PS(bs, 512), lhsT=kfn(kv),
                                                                   rhs=qT[P0:P0 + 64, :, b * 128:(b + 1) * 128], start=True, stop=not additive),
                         r=qkeys + kkeys, w=[kb(bs)])
                    if additive:
                        P.pe(lambda e, bs=bs, msk=msk: e.matmul(PS(bs, 512), lhsT=idb[:], rhs=msk.rearrange("p a b -> p (a b)"),
                                                                start=False, stop=True), r=["idb", "cconst"], w=[kb(bs)])
                    Ei = Et[:, kv * 2 + i, :]
                    ek = ("Et", kv * 2 + i)
                    P.act(lambda e, bs=bs, Ei=Ei: e.activation(out=Ei, in_=PS(bs, 512), func=AF.Exp, scale=0.125), r=[kb(bs)], w=[ek])
                    if not additive:
                      P.pool(lambda e, Ei=Ei, msk=msk: e.tensor_tensor(out=Ei.rearrange("p (a b) -> p a b", a=4), in0=Ei.rearrange("p (a b) -> p a b", a=4), in1=msk, op=ALU.mult),
                           r=[ek, "cconst", "scconst"], w=[ek])
            bo = pair()
            for kv in range(2):
                for c4 in range(4):
                    for i, (kfn, vfn, msk, kkeys) in enumerate(kblocks):
                        P.pe(lambda e, kv=kv, c4=c4, i=i, vfn=vfn, bo=bo: e.matmul(
                            PS(bo + kv, 260)[:, c4 * 65:(c4 + 1) * 65], lhsT=Et[:, kv * 2 + i, c4 * 128:(c4 + 1) * 128], rhs=vfn(kv),
                            start=(i == 0), stop=(i == ng - 1)), r=[("Et", kv * 2 + i)] + kkeys, w=[kb(bo + kv)])
            return bo

        def attn_finish(b, src_fn, rkeys):
            for kv in range(2):
                s3 = src_fn(kv)
                P.dve(lambda e, kv=kv, s3=s3: e.tensor_tensor(out=den[:, kv * 4:(kv + 1) * 4].unsqueeze(2), in0=s3[:, :, 64:65],
                                                              in1=esink[:, kv * 4:(kv + 1) * 4].unsqueeze(2), op=ALU.add),
                      r=rkeys + ["esink"], w=[("den", kv)])
                P.dve(lambda e, kv=kv: e.reciprocal(out=den[:, kv * 4:(kv + 1) * 4], in_=den[:, kv * 4:(kv + 1) * 4]),
                      r=[("den", kv)], w=[("den", kv)])
                for c4 in range(4):
                    h = kv * 4 + c4
                    P.dve(lambda e, s3=s3, c4=c4, h=h: e.tensor_scalar(out=mix[:, b, h * 64:(h + 1) * 64], in0=s3[:, c4, 0:64],
                                                                       scalar1=den[:, h:h + 1], scalar2=None, op0=ALU.mult),
                          r=rkeys + [("den", kv)], w=[("mixa", b)])

        if kind == "p":
            for b in range(NB):
                gb = ti * NB + b
                kbl = []
                if gb > 0:
                    kbl.append((lambda kv, b=b: kbuf[64 * kv:64 * kv + 64, b * 128:(b + 1) * 128],
                                lambda kv, b=b: Vaug[:, b, kv, :], m_prev[:], ["kbuf", "Vaug"]))
                kbl.append((lambda kv, b=b: kbuf[64 * kv:64 * kv + 64, (b + 1) * 128:(b + 2) * 128],
                            lambda kv, b=b: Vaug[:, b + 1, kv, :], m_own[:], ["kbuf", "Vaug"]))
                bo = attn_group(b, kbl, True, True, additive=True)
                epilogue(b)
                attn_finish(b, lambda kv, bo=bo: PS(bo + kv, 260).rearrange("p (c d) -> p c d", d=65), [kb(bo), kb(bo + 1)])
            P.act(lambda e: e.copy(out=kbuf[:, 0:128], in_=kbuf[:, NB * 128:(NB + 1) * 128]), r=["kbuf"], w=["kbuf"])
            P.pool(lambda e: e.tensor_copy(out=Vaug[:, 0, :, :], in_=Vaug[:, NB, :, :]), r=["Vaug"], w=["Vaug"])
        else:
            kbl = [(lambda kv: kbuf[64 * kv:64 * kv + 64, 128:256], lambda kv: Vaug[:, 1, kv, :], m_sown[:], ["kbuf", "Vaug"])]
            bo = attn_group(0, kbl, True, False)
            for kv in range(2):
                P.dve(lambda e, kv=kv, bo=bo: e.tensor_copy(out=oacc[:, kv, :, :].rearrange("p c d -> p (c d)"), in_=PS(bo + kv, 260)),
                      r=[kb(bo + kv)], w=[("oacc", kv)])
            for s in range(NSEQ_S):
                i = s % 2
                P.dma("pool", "ck%d" % i, ckf[:, i, :], ck[s], w=[("ckf", i)])
                btk = bank()
                P.pe(lambda e, i=i, btk=btk: e.transpose(out=PS(btk, 128), in_=ckf[:, i, :], identity=idf[:]), r=[("ckf", i)] + CONST, w=[kb(btk)])
                P.act(lambda e, btk=btk: e.copy(out=ckT[:], in_=PS(btk, 128)), r=[kb(btk)], w=["ckT"])
                P.dma("pool", "cv%d" % i, ckf[:, i, :], cv[s], r=[], w=[("ckf", i)])
                P.pool(lambda e, i=i: e.memset(cVaug[:, i, :, 64:65], 1.0), w=[("cVaug", i)])
                P.act(lambda e, i=i: e.copy(out=cVaug[:, i, :, 0:64], in_=ckf[:, i, :].rearrange("p (k d) -> p k d", k=2)),
                      r=[("ckf", i)], w=[("cVaug", i)])
                kbl = [(lambda kv: ckT[64 * kv:64 * kv + 64, :], lambda kv, i=i: cVaug[:, i, kv, :],
                        m_scache[:, s:s + 1, :].to_broadcast([128, 4, 128]), ["ckT", ("cVaug", i)])]
                bo = attn_group(0, kbl, False, False)
                for kv in range(2):
                    P.dve(lambda e, kv=kv, bo=bo: e.tensor_tensor(out=oacc[:, kv, :, :].rearrange("p c d -> p (c d)"),
                                                                  in0=oacc[:, kv, :, :].rearrange("p c d -> p (c d)"),
                                                                  in1=PS(bo + kv, 260), op=ALU.add),
                          r=[kb(bo + kv), ("oacc", kv)], w=[("oacc", kv)])
            attn_finish(0, lambda kv: oacc[:, kv, :, :], [("oacc", 0), ("oacc", 1)])
            epilogue(0)

        if kind == "s" and STOP <= 5:
            return
        if kind == "s" and STOP <= 6:
            return
        for b in range(NB):
            bk = bank()
            for c in range(8):
                P.pe(lambda e, c=c, bk=bk, b=b: e.transpose(out=PSB(bk)[:, c * 128:(c + 1) * 128], in_=mix[:, b, c * 128:(c + 1) * 128],
                                                            identity=idb[:]), r=[("mixa", b), ("mixr", b), "idb"], w=[kb(bk)])
            P.act(lambda e, b=b, bk=bk: e.copy(out=bufA[:, :, b * 128:(b + 1) * 128], in_=PSB(bk).rearrange("p (c n) -> p c n", c=8)),
                  r=[kb(bk)], w=[("bufA", b)])
        nb[0] = 0
        for c in range(8):
            s = stream(wsc_out[c], ("wsc_out", c))
            for b in range(NB):
                for hf in range(2):
                    P.pe(lambda e, c=c, s=s, b=b, hf=hf: e.matmul(PS(2 * b + hf, 512), lhsT=bufA[:, c, b * 128:(b + 1) * 128],
                                                                 rhs=ring[:, s, hf * 512:(hf + 1) * 512], start=(c == 0), stop=(c == 7)),
                         r=[("ring", s), ("bufA", b)], w=[kb(2 * b + hf)])

        def ost(b):
            return hid32[:, 1536 + b * 1024:1536 + (b + 1) * 1024]

        def ostk(b):
            return [("hidT", fc_) for fc_ in range(6 + 4 * b, 10 + 4 * b)]

        def norm_res(b, src_pair, gbc, dst, dkey, xkey_r):
            sc = sstat[:, 4 + b:5 + b]
            pk = [kb(2 * b), kb(2 * b + 1)]
            P.act(lambda e: e.activation(out=junk[:], in_=src_pair, func=AF.Square, accum_out=sc), r=pk, w=["junk", ("ss2", b)])
            P.act(lambda e: e.activation(out=sc, in_=sc, func=AF.Sqrt, scale=1.0 / D, bias=RMS_EPS), r=[("ss2", b)], w=[("ss2", b)])
            P.dve(lambda e: e.reciprocal(out=sc, in_=sc), r=[("ss2", b)], w=[("ss2", b)])
            tmp = ost(b)
            P.dve(lambda e: e.scalar_tensor_tensor(out=tmp, in0=src_pair, scalar=sc, in1=gbc[:], op0=ALU.mult, op1=ALU.mult),
                  r=pk + [("ss2", b)] + CONST, w=ostk(b))
            P.pool(lambda e: e.tensor_tensor(out=dst, in0=tmp, in1=x_t[:, b, :], op=ALU.add), r=ostk(b) + [xkey_r], w=dkey)

        for b in range(NB):
            norm_res(b, PSP(2 * b), gpost_bc, x_t[:, b, :], [("x", b)], ("x", b))
        for b in range(NB):
            sc = sstat[:, 8 + b:9 + b]
            P.act(lambda e, b=b, sc=sc: e.activation(out=junk[:], in_=x_t[:, b, :], func=AF.Square, accum_out=sc),
                  r=[("x", b)], w=["junk", ("ss3", b)])
            P.act(lambda e, sc=sc: e.activation(out=sc, in_=sc, func=AF.Sqrt, scale=1.0 / D, bias=RMS_EPS), r=[("ss3", b)], w=[("ss3", b)])
            P.dve(lambda e, sc=sc: e.reciprocal(out=sc, in_=sc), r=[("ss3", b)], w=[("ss3", b)])
        for b in range(NB):
            sc = sstat[:, 8 + b:9 + b]
            P.dve(lambda e, b=b, sc=sc: e.tensor_scalar(out=xn[:], in0=x_t[:, b, :], scalar1=sc, scalar2=None, op0=ALU.mult),
                  r=[("x", b), ("ss3", b)], w=["xn"])
            bk = (2 * b) % 8
            for c in range(8):
                P.pe(lambda e, c=c, bk=bk: e.transpose(out=PSB(bk)[:, c * 128:(c + 1) * 128], in_=xn[:, c * 128:(c + 1) * 128],
                                                      identity=idb[:]), r=["xn", "idb"], w=[kb(bk)])
            P.act(lambda e, b=b, bk=bk: e.copy(out=bufA[:, :, b * 128:(b + 1) * 128], in_=PSB(bk).rearrange("p (c n) -> p c n", c=8)),
                  r=[kb(bk)], w=[("bufA", b)])
        nb[0] = 0

        if kind == "s" and STOP <= 7:
            return
        for fc in range(NFC):
            sz = stream(wsc_fin[2 * fc], ("wsc_fin", 2 * fc))
            su = stream(wsc_fin[2 * fc + 1], ("wsc_fin", 2 * fc + 1))
            bz = bank()
            bu = bank()
            for (s, bk_) in ((sz, bz), (su, bu)):
                for c in range(8):
                    P.pe(lambda e, c=c, s=s, bk_=bk_: e.matmul(PS(bk_, NT), lhsT=ring[:, s, c * 128:(c + 1) * 128], rhs=bufA[:, c, S_],
                                                               start=(c == 0), stop=(c == 7)), r=[("ring", s)] + bufA_keys, w=[kb(bk_)])
            i = 0
            zb = zbuf[:, i, 0:nseq * (Lseq + 2)].rearrange("p (s l) -> p s l", s=nseq)
            zk = ("zbuf", i)
            P.act(lambda e, bz=bz, zb=zb: e.copy(out=zb[:, :, 2:Lseq + 2], in_=PS(bz, NT).rearrange("p (s l) -> p s l", s=nseq)),
                  r=[kb(bz)], w=[zk])
            P.pool(lambda e, zb=zb, fc=fc: e.tensor_copy(out=zb[:, :, 0:2], in_=zcar[:, fc, :, :]), r=[("zcar", fc)], w=[zk])
            P.pool(lambda e, zb=zb, fc=fc: e.tensor_copy(out=zcar[:, fc, :, :], in_=zb[:, :, Lseq:Lseq + 2]), r=[zk], w=[("zcar", fc)])
            a3 = za[:, i, S_].rearrange("p (s l) -> p s l", s=nseq)
            ak = ("za", i)
            P.act(lambda e, bz=bz, a3=a3, fc=fc: e.activation(out=a3, in_=PS(bz, NT).rearrange("p (s l) -> p s l", s=nseq), func=AF.Identity,
                                                              scale=cwT[:, 2, fc:fc + 1], bias=cbT[:, fc:fc + 1]),
                  r=[kb(bz)] + CONST, w=[ak])
            P.dve(lambda e, zb=zb, a3=a3, fc=fc: e.scalar_tensor_tensor(out=a3, in0=zb[:, :, 1:Lseq + 1], scalar=cwT[:, 1, fc:fc + 1], in1=a3,
                                                                        op0=ALU.mult, op1=ALU.add), r=[zk, ak], w=[ak])
            P.dve(lambda e, zb=zb, a3=a3, fc=fc: e.scalar_tensor_tensor(out=a3, in0=zb[:, :, 0:Lseq], scalar=cwT[:, 0, fc:fc + 1], in1=a3,
                                                                        op0=ALU.mult, op1=ALU.add), r=[zk, ak], w=[ak])
            P.act(lambda e, i=i: e.activation(out=za[:, i, S_], in_=za[:, i, S_], func=AF.Silu), r=[ak], w=[ak])
            P.dve(lambda e, i=i, bu=bu, fc=fc: e.tensor_tensor(out=hidT[:, fc, S_], in0=za[:, i, S_], in1=PS(bu, NT), op=ALU.mult),
                  r=[ak, kb(bu)], w=[("hidT", fc)])
        nb[0] = 0
        for fc in range(NFC):
            s = stream(wsc_fout[fc], ("wsc_fout", fc))
            for b in range(NB):
                for hf in range(2):
                    P.pe(lambda e, fc=fc, s=s, b=b, hf=hf: e.matmul(PS(2 * b + hf, 512), lhsT=hidT[:, fc, b * 128:(b + 1) * 128],
                                                                   rhs=ring[:, s, hf * 512:(hf + 1) * 512], start=(fc == 0), stop=(fc == NFC - 1)),
                         r=[("ring", s), ("hidT", fc)], w=[kb(2 * b + hf)])
        for b in range(NB):
            oi = 0
            state["ost"] += 1
            norm_res(b, PSP(2 * b), gpostf_bc, ost(b), ostk(b), ("x", b))
        for b in range(NB):
            P.dma("pool", "oy%d" % b, ydst[t0 + b * 128:t0 + (b + 1) * 128, :] if kind == "p" else ydst, ost(b),
                  r=ostk(b), w=["o_y"])
        nb[0] = 0

        if last:
            emit_state_outputs(kind, nseq)

    def emit_state_outputs(kind, nseq):
        hcar = hcar_p if kind == "p" else hcar_s
        zcar = zcar_p if kind == "p" else zcar_s
        for rc in range(14):
            np_ = RC_NP.get(rc, 128)
            bt = bank()
            P.pe(lambda e, rc=rc, np_=np_, bt=bt: e.transpose(out=PS(bt, np_, nseq), in_=hcar[0:np_, rc, :], identity=idf[0:np_, 0:np_]),
                 r=[("hcar", rc)] + CONST, w=[kb(bt)])
            c0 = rc * 128 if rc < 13 else 1600
            P.act(lambda e, bt=bt, np_=np_, c0=c0: e.copy(out=rowbuf[0:nseq, c0:c0 + np_], in_=PS(bt, np_, nseq)), r=[kb(bt)], w=["rowbuf", "XA"] + (HID_ALL if rc == 0 else []))
        P.dma("pool", "o_sh", shp if kind == "p" else shs, rowbuf[0:nseq, 0:DSH], r=["rowbuf"] + HID_ALL, w=["o_sh" + kind, "XA"])
        for fc in range(NFC):
            bt = bank()
            P.pe(lambda e, fc=fc, bt=bt: e.transpose(out=PS(bt, 128, 2 * nseq), in_=zcar[:, fc, :, :].rearrange("p s j -> p (s j)"),
                                                     identity=idf[:]), r=[("zcar", fc)] + CONST, w=[kb(bt)])
            P.act(lambda e, fc=fc, bt=bt: e.copy(out=sst[0:2 * nseq, fc * 128:(fc + 1) * 128], in_=PS(bt, 128, 2 * nseq)), r=[kb(bt)], w=["sst", "XA"] + (HID_ALL if fc == 0 else []))
        P.dma("pool", "o_cv", convp if kind == "p" else convs, sst[0:2 * nseq, :], r=["sst"] + HID_ALL, w=["o_cv" + kind, "XA"])
        if kind == "p":
            for hp in range(4):
                bt = bank()
                P.pe(lambda e, hp=hp, bt=bt: e.transpose(out=PS(bt, 128, 64), in_=Sm[:, hp, :], identity=idf[:]),
                     r=[("Sm", hp)] + CONST, w=[kb(bt)])
                i = hp % 2
                P.act(lambda e, bt=bt, i=i: e.copy(out=Sld[:, i, :], in_=PS(bt, 128, 64)), r=[kb(bt)], w=["t5"])
                P.dma("pool", "o_wk%d" % i, wkvp[2 * hp:2 * hp + 2].rearrange("h i j -> i h j"),
                      Sld[:, i, :].rearrange("p (h j) -> p h j", h=2), r=["t5"], w=["o_wkvp"])

    SmS = T("SmS", [128, NSEQ_S, 64])
    SbS = T("SbS", [128, NSEQ_S, 64], BF16)

    P.pool(lambda e: e.memset(hcar_p[:], 0.0), w=[("hcar", rc) for rc in range(14)])
    P.pool(lambda e: e.memset(zcar_p[:], 0.0), w=[("zcar", fc) for fc in range(NFC)])
    P.pool(lambda e: e.memset(kbuf[:], 0.0), w=["kbuf"])

    for ti in range(N_TILES):
        emit_tile("p", ti)

    if DO_SAMPLE:
        P.dma("sp", "clss", scanm_s[:], c_scanm_s, w=["scconst"])
        cast_const(m_sown[:], c_mask_sown, 128, 128, bcast4=True)
        for g in range(4):
            cast_const(m_scache[:, 4 * g:4 * g + 4, :].rearrange("p a b -> p (a b)"), c_mask_scache[:, 4 * g:4 * g + 4, :].rearrange("p a b -> p (a b)"), 128, 512)
        P.ops["dve"][-1].deps.add(P.last_w["scconst"])
        P.dma("pool", "hst", rowbuf[0:NSEQ_S, :], sshift, r=[], w=["rowbuf", "XA"] + HID_ALL)
        for rc in range(14):
            np_ = RC_NP.get(rc, 128)
            c0 = rc * 128 if rc < 13 else 1600
            bt = bank()
            P.pe(lambda e, bt=bt, np_=np_, c0=c0: e.transpose(out=PS(bt, NSEQ_S, np_), in_=rowbuf[0:NSEQ_S, c0:c0 + np_], identity=idf[0:NSEQ_S, 0:NSEQ_S]),
                 r=["rowbuf"] + CONST, w=[kb(bt), "XA"])
            P.act(lambda e, bt=bt, np_=np_, rc=rc: e.copy(out=hcar_s[0:np_, rc, :], in_=PS(bt, NSEQ_S, np_)), r=[kb(bt)], w=[("hcar", rc)])
        P.dma("pool", "hst2", sst[0:2 * NSEQ_S, :], sconv, r=[], w=["sst", "XA"] + HID_ALL)
        for fc in range(NFC):
            bt = bank()
            P.pe(lambda e, bt=bt, fc=fc: e.transpose(out=PS(bt, 2 * NSEQ_S, 128), in_=sst[0:2 * NSEQ_S, fc * 128:(fc + 1) * 128],
                                                     identity=idf[0:2 * NSEQ_S, 0:2 * NSEQ_S]), r=["sst"] + CONST, w=[kb(bt), "XA"])
            P.act(lambda e, bt=bt, fc=fc: e.copy(out=zcar_s[:, fc, :, :].rearrange("p s j -> p (s j)"), in_=PS(bt, 2 * NSEQ_S, 128)),
                  r=[kb(bt)], w=[("zcar", fc)])
        emit_tile("s", 0)

    okeys = [k for k in P.last_w.keys() if isinstance(k, str) and k.startswith("o_")]
    P.add("sp", lambda e: None, reads=okeys)
    P.add("pool", lambda e: None, reads=okeys)
    P.emit()
    return nc, st


def _consts():
    c = {}
    c["c_ident"] = np.eye(128, dtype=np.float32)
    half = 32
    inv = (np.float32(10000.0) ** (-np.arange(half, dtype=np.float32) / np.float32(half))).astype(np.float32)
    p = np.arange(128)
    f = (p % 64) % 32
    sign = np.where((p % 64) < 32, -1.0, 1.0).astype(np.float32)

    def tabs(pos):
        ang = pos.astype(np.float32)[None, :] * inv[f][:, None]
        return np.cos(ang).astype(np.float32), (np.sin(ang).astype(np.float32) * sign[:, None]).astype(np.float32)

    c["c_cos_p"], c["c_sin_p"] = tabs(np.arange(SEQ))
    pos_s = 16384 + (np.arange(128) % LS)
    c["c_cos_s"], c["c_sin_s"] = tabs(pos_s)
    s = np.arange(128)[:, None]
    q = np.arange(128)[None, :]
    c["c_mask_own"] = (s <= q).astype(np.float32)
    c["c_mask_prev"] = (s >= q).astype(np.float32)
    c["c_negmask_own"] = np.where(s <= q, 0.0, -30000.0).astype(np.float32)
    c["c_negmask_prev"] = np.where(s >= q, 0.0, -30000.0).astype(np.float32)
    c["c_mask_sown"] = ((s // LS == q // LS) & (s <= q)).astype(np.float32)
    msc = np.zeros((128, NSEQ_S, 128), np.float32)
    for sq in range(NSEQ_S):
        msc[:, sq, :] = ((q // LS == sq) & (s >= (q % LS))).astype(np.float32)
    c["c_mask_scache"] = msc
    bo = np.zeros((128, 128), np.float32)
    bo[:64, :64] = 1
    bo[64:, 64:] = 1
    c["c_blockones"] = bo
    strict = (s < q).astype(np.float32)
    incl = (s <= q).astype(np.float32)
    c["c_m2b"] = np.stack([strict, -incl], axis=1).astype(np.float32)
    c["c_m2k"] = np.stack([strict, incl], axis=1).astype(np.float32)
    c["c_mT"] = (q < s).astype(np.float32)
    mp = np.ones((128, 512), np.float32)
    mp[:, 0::128] = 0
    c["c_scanm_p"] = mp
    ms = np.ones((128, 128), np.float32)
    ms[:, 0::LS] = 0
    c["c_scanm_s"] = ms
    return c


_CACHE = {}


def kernel(**inputs):
    f = lambda a: np.ascontiguousarray(np.asarray(a, dtype=np.float32))
    if "nc" not in _CACHE:
        _CACHE["nc"] = build_program()
        _CACHE["consts"] = _consts()
    nc, _st = _CACHE["nc"]
    consts = _CACHE["consts"]
    x_prompt = f(inputs["x_prompt"])
    x_sample = f(inputs["x_sample"])
    wnames = ["g_pre_mix", "w_in", "attn_sinks", "mu_shift", "w0", "w_decay_up", "a0", "w_a_up", "w_g_up", "k_k", "k_a", "r_k",
              "gn_w", "gn_b", "w_out", "g_post_mix", "g_pre_ffn", "w_ffn_in", "conv_w", "conv_b", "w_ffn_out", "g_post_ffn"]
    shared = {}
    for n in wnames:
        a = f(inputs[n])[0]
        if n == "r_k":
            a = a.reshape(512)
        shared[n] = np.ascontiguousarray(a)
    shared.update(consts)
    in_maps = []
    for c in range(8):
        m = dict(shared)
        m["xp"] = x_prompt[c % 4]
        sl = slice(c * NSEQ_S, (c + 1) * NSEQ_S)
        m["xs"] = np.ascontiguousarray(x_sample[sl].reshape(128, D))
        m["ck"] = np.ascontiguousarray(f(inputs["cache_k_win"])[0, sl].reshape(NSEQ_S, 128, 128))
        m["cv"] = np.ascontiguousarray(f(inputs["cache_v_win"])[0, sl].reshape(NSEQ_S, 128, 128))
        m["sshift"] = np.ascontiguousarray(f(inputs["state_shift"])[0, sl])
        m["swkv"] = np.ascontiguousarray(f(inputs["state_wkv"])[0, sl])
        m["sconv"] = np.ascontiguousarray(f(inputs["state_conv"])[0, sl].reshape(NSEQ_S * 2, DFF))
        in_maps.append(m)
    ncores = int(os.environ.get("MK_CORES", "8"))
    res = run_bass_kernel_spmd(nc, in_maps[:ncores], core_ids=list(range(ncores)))
    R = list(res.results) + [res.results[0]] * (8 - ncores)
    cat = lambda k, rng: np.stack([R[c][k] for c in rng], axis=0)
    y_prompt = cat("yp", range(4)).reshape(4, SEQ, D)
    y_sample = np.concatenate([R[c]["ys"].reshape(NSEQ_S, LS, D) for c in range(8)], axis=0)
    nkp = cat("kwp", range(4)).reshape(1, 4, 128, 2, 64)
    nvp = cat("vwp", range(4)).reshape(1, 4, 128, 2, 64)
    nsp = cat("shp", range(4)).reshape(1, 4, DSH)
    nwp = cat("wkvp", range(4)).reshape(1, 4, 8, 64, 64)
    ncp = cat("convp", range(4)).reshape(1, 4, 2, DFF)
    nks = np.concatenate([R[c]["kws"] for c in range(8)], axis=0).reshape(1, 128, 128, 2, 64)
    nvs = np.concatenate([R[c]["vws"] for c in range(8)], axis=0).reshape(1, 128, 128, 2, 64)
    nss = np.concatenate([R[c]["shs"] for c in range(8)], axis=0).reshape(1, 128, DSH)
    nws = np.concatenate([R[c]["wkvs"] for c in range(8)], axis=0).reshape(1, 128, 8, 64, 64)
    ncs = np.concatenate([R[c]["convs"].reshape(NSEQ_S, 2, DFF) for c in range(8)], axis=0).reshape(1, 128, 2, DFF)
    outs = (y_prompt, y_sample, nkp, nvp, nsp, nwp, ncp, nks, nvs, nss, nws, ncs)
    return tuple(np.ascontiguousarray(o.astype(np.float32)) for o in outs)
```

```python
import os
import contextlib
import numpy as np
import concourse.bass as bass
import concourse.mybir as mybir
from concourse.bass_utils import run_bass_kernel_spmd

F32 = mybir.dt.float32
BF16 = mybir.dt.bfloat16
ALU = mybir.AluOpType
AF = mybir.ActivationFunctionType
AX = mybir.AxisListType

ENGS = ("pe", "act", "dve", "pool", "sp")


class Op:
    __slots__ = ("eng", "fn", "deps", "dma", "signal", "sigval", "waits", "has_dependents")

    def __init__(self, eng, fn, dma):
        self.eng = eng
        self.fn = fn
        self.dma = dma
        self.deps = set()
        self.signal = False
        self.sigval = 0
        self.waits = []
        self.has_dependents = False


class Prog:
    def __init__(self, nc, same_engine_sync=True):
        self.nc = nc
        self.ops = {e: [] for e in ENGS}
        self.last_w = {}
        self.readers = {}
        self.same_engine_sync = same_engine_sync
        self.nops = 0

    def add(self, eng, fn, reads=(), writes=(), dma=None):
        op = Op(eng, fn, dma)
        deps = op.deps
        for k in reads:
            w = self.last_w.get(k)
            if w is not None:
                deps.add(w)
            if isinstance(k, tuple) and k[0] == "ps":
                for r in self.readers.get(k, ()):
                    if r.eng != eng:
                        deps.add(r)
        for k in writes:
            w = self.last_w.get(k)
            if w is not None:
                deps.add(w)
            for r in self.readers.get(k, ()):
                deps.add(r)
        deps.discard(op)
        for k in reads:
            lst = self.readers.setdefault(k, [])
            if dma is None:
                for i_, r_ in enumerate(lst):
                    if r_.dma is None and r_.eng == eng:
                        lst[i_] = op
                        break
                else:
                    lst.append(op)
            else:
                lst.append(op)
        for k in writes:
            self.last_w[k] = op
            self.readers[k] = []
        self.ops[eng].append(op)
        self.nops += 1
        return op

    def pe(self, fn, r=(), w=()):
        return self.add("pe", fn, r, w)

    def act(self, fn, r=(), w=()):
        return self.add("act", fn, r, w)

    def dve(self, fn, r=(), w=()):
        return self.add("dve", fn, r, w)

    def pool(self, fn, r=(), w=()):
        return self.add("pool", fn, r, w)

    def dma(self, q, sem, out, in_, r=(), w=(), **kw):
        return self.add(q, lambda e: e.dma_start(out=out, in_=in_, **kw), r, w, dma=sem)

    def _skip(self, d, op):
        return d.dma is None and d.eng == op.eng and (op.eng == "pe" or not self.same_engine_sync)

    def finalize(self):
        for e in ENGS:
            for op in self.ops[e]:
                for d in op.deps:
                    if not self._skip(d, op):
                        d.has_dependents = True
        eng_cnt = {e: 0 for e in ENGS}
        dma_cnt = {}
        for e in ENGS:
            for op in self.ops[e]:
                if op.dma is not None:
                    dma_cnt[op.dma] = dma_cnt.get(op.dma, 0) + 16
                    op.sigval = dma_cnt[op.dma]
                    op.signal = True
                elif op.has_dependents:
                    eng_cnt[e] += 1
                    op.sigval = eng_cnt[e]
                    op.signal = True
        for e in ENGS:
            for op in self.ops[e]:
                ws = {}
                for d in op.deps:
                    if self._skip(d, op):
                        continue
                    key = ("dma", d.dma) if d.dma is not None else ("eng", d.eng)
                    if ws.get(key, 0) < d.sigval:
                        ws[key] = d.sigval
                op.waits = list(ws.items())
        self.dma_names = sorted(dma_cnt.keys())

    def emit(self):
        nc = self.nc
        self.finalize()
        with contextlib.ExitStack() as st:
            sems = {}
            for e in ENGS:
                sems[("eng", e)] = st.enter_context(nc.semaphore("s_" + e))
            for n in self.dma_names:
                sems[("dma", n)] = st.enter_context(nc.semaphore("d_" + n))
            block = st.enter_context(nc.Block())

            def replay(eobj, ename):
                known = {}
                for op in self.ops[ename]:
                    for key, val in op.waits:
                        if known.get(key, 0) < val:
                            eobj.wait_ge(sems[key], val)
                            known[key] = val
                    ins = op.fn(eobj)
                    if op.signal and ins is not None:
                        if op.dma is not None:
                            ins.then_inc(sems[("dma", op.dma)], 16)
                        else:
                            ins.then_inc(sems[("eng", ename)], 1)

            @block.tensor
            def _(e):
                replay(e, "pe")

            @block.scalar
            def _(e):
                replay(e, "act")

            @block.vector
            def _(e):
                replay(e, "dve")

            @block.gpsimd
            def _(e):
                replay(e, "pool")

            @block.sync
            def _(e):
                replay(e, "sp")


D = 1024
DIN = 2464
DFF = 2816
NFC = 22
DSH = 1696
SEQ = 8192
NSEQ_S = 16
LS = 8
KAPPA = float(np.exp(-0.5))
RMS_EPS = 1e-6
GN_EPS = 64e-5
RING = 5
N_TILES = int(os.environ.get("MK_NTILES", "16"))
DO_SAMPLE = int(os.environ.get("MK_SAMPLE", "1"))
STOP = int(os.environ.get("MK_STOP", "99"))
HPS = int(os.environ.get("MK_HPS", "4"))
VAR = int(os.environ.get("MK_VAR", "0"))

RW0 = 768
WIN_CHUNKS = {}
for c in range(4):
    WIN_CHUNKS[("q", c)] = [(64 * c, 64, 0), (256 + 64 * c, 64, 64)]
    WIN_CHUNKS[("qs", c)] = [(64 * c + 32, 32, 0), (64 * c, 32, 32), (256 + 64 * c + 32, 32, 64), (256 + 64 * c, 32, 96)]
    WIN_CHUNKS[("r", c)] = [(RW0 + 128 * c, 128, 0)]
    WIN_CHUNKS[("k", c)] = [(RW0 + 512 + 128 * c, 128, 0)]
    WIN_CHUNKS[("v", c)] = [(RW0 + 1024 + 128 * c, 128, 0)]
WIN_CHUNKS[("ak", 0)] = [(512, 128, 0)]
WIN_CHUNKS[("aks", 0)] = [(544, 32, 0), (512, 32, 32), (608, 32, 64), (576, 32, 96)]
WIN_CHUNKS[("av", 0)] = [(640, 128, 0)]
WIN_CHUNKS[("lo", 0)] = [(RW0 + 1536, 64, 0)]
WIN_CHUNKS[("lg", 0)] = [(RW0 + 1600, 96, 0)]
WIN_ORDER = [("lo", 0), ("lg", 0)]
for c in range(4):
    WIN_ORDER += [("r", c), ("k", c), ("v", c)]
WIN_ORDER += [("ak", 0), ("aks", 0), ("av", 0)]
for c in range(4):
    WIN_ORDER += [("q", c), ("qs", c)]
WIN_IDX = {k: i for i, k in enumerate(WIN_ORDER)}
NWIN = len(WIN_ORDER)
RC = {}
for c in range(4):
    RC[("r", c)] = c
    RC[("k", c)] = 4 + c
    RC[("v", c)] = 8 + c
RC[("lo", 0)] = 12
RC[("lg", 0)] = 13
RC_NP = {12: 64, 13: 96}


def build_program():
    nc = bass.Bass("TRN2", target_bir_lowering=False)
    P = Prog(nc)
    st = contextlib.ExitStack()

    def din(name, shape, dt=F32):
        return nc.dram_tensor(name, list(shape), dt, kind="ExternalInput").ap()

    def dout(name, shape, dt=F32):
        return nc.dram_tensor(name, list(shape), dt, kind="ExternalOutput").ap()

    def dint(name, shape, dt=BF16):
        return nc.dram_tensor(name, list(shape), dt, kind="Internal").ap()

    def T(name, shape, dt=F32):
        return st.enter_context(nc.sbuf_tensor(name, list(shape), dt))

    xp = din("xp", [SEQ, D])
    xs = din("xs", [128, D])
    ck = din("ck", [NSEQ_S, 128, 128])
    cv = din("cv", [NSEQ_S, 128, 128])
    sshift = din("sshift", [NSEQ_S, DSH])
    swkv = din("swkv", [NSEQ_S, 8, 64, 64])
    sconv = din("sconv", [NSEQ_S * 2, DFF])
    g_pre_mix = din("g_pre_mix", [D])
    w_in = din("w_in", [D, DIN])
    attn_sinks = din("attn_sinks", [8])
    mu_shift = din("mu_shift", [DSH])
    w0 = din("w0", [512])
    w_decay_up = din("w_decay_up", [32, 512])
    a0 = din("a0", [512])
    w_a_up = din("w_a_up", [32, 512])
    w_g_up = din("w_g_up", [96, 512])
    k_k = din("k_k", [512])
    k_a = din("k_a", [512])
    r_k = din("r_k", [512])
    gn_w = din("gn_w", [512])
    gn_b = din("gn_b", [512])
    w_out = din("w_out", [D, D])
    g_post_mix = din("g_post_mix", [D])
    g_pre_ffn = din("g_pre_ffn", [D])
    w_ffn_in = din("w_ffn_in", [D, 2 * DFF])
    conv_w = din("conv_w", [3, DFF])
    conv_b = din("conv_b", [DFF])
    w_ffn_out = din("w_ffn_out", [DFF, D])
    g_post_ffn = din("g_post_ffn", [D])
    c_ident = din("c_ident", [128, 128])
    c_cos_p = din("c_cos_p", [128, SEQ])
    c_sin_p = din("c_sin_p", [128, SEQ])
    c_cos_s = din("c_cos_s", [128, 128])
    c_sin_s = din("c_sin_s", [128, 128])
    c_mask_own = din("c_mask_own", [128, 128])
    c_mask_prev = din("c_mask_prev", [128, 128])
    c_negmask_own = din("c_negmask_own", [128, 128])
    c_negmask_prev = din("c_negmask_prev", [128, 128])
    c_mask_sown = din("c_mask_sown", [128, 128])
    c_mask_scache = din("c_mask_scache", [128, NSEQ_S, 128])
    c_blockones = din("c_blockones", [128, 128])
    c_m2b = din("c_m2b", [128, 2, 128])
    c_m2k = din("c_m2k", [128, 2, 128])
    c_mT = din("c_mT", [128, 128])
    c_scanm_p = din("c_scanm_p", [128, 512])
    c_scanm_s = din("c_scanm_s", [128, 128])

    yp = dout("yp", [SEQ, D])
    ys = dout("ys", [128, D])
    kwp = dout("kwp", [128, 128])
    vwp = dout("vwp", [128, 128])
    shp = dout("shp", [1, DSH])
    wkvp = dout("wkvp", [8, 64, 64])
    convp = dout("convp", [2, DFF])
    kws = dout("kws", [NSEQ_S, 128, 128])
    vws = dout("vws", [NSEQ_S, 128, 128])
    shs = dout("shs", [NSEQ_S, DSH])
    wkvs = dout("wkvs", [NSEQ_S, 8, 64, 64])
    convs = dout("convs", [NSEQ_S * 2, DFF])

    wsc_in = dint("wsc_in", [NWIN, 128, 1024])
    wsc_out = dint("wsc_out", [8, 128, 1024])
    wsc_fin = dint("wsc_fin", [2 * NFC, 128, 1024])
    wsc_fout = dint("wsc_fout", [NFC, 128, 1024])

    pp = [st.enter_context(nc.psum_tensor("pp%d" % i, [128, 1024], F32)) for i in range(4)]
    nb = [0]

    def bank():
        b = nb[0] % 8
        nb[0] += 1
        return b

    def pair():
        if nb[0] % 2:
            nb[0] += 1
        b = nb[0] % 8
        nb[0] += 2
        return b

    def PS(b, n=512, np_=128, p0=0):
        o = (b % 2) * 512
        return pp[b // 2][p0:p0 + np_, o:o + n]

    def PSP(b):
        return pp[b // 2][:, :]

    def PSB(b, np_=128):
        return pp[b // 2].bitcast(BF16)[0:np_, (b % 2) * 1024:(b % 2) * 1024 + 1024]

    def kb(b):
        return ("ps", b)

    idf = T("idf", [128, 128])
    idb = T("idb", [128, 128], BF16)
    gTpre = T("gTpre", [128, 8])
    gTffn = T("gTffn", [128, 8])
    gpost_bc = T("gpost_bc", [128, D])
    gpostf_bc = T("gpostf_bc", [128, D])
    gnw_bc = T("gnw_bc", [128, 512])
    gnb_bc = T("gnb_bc", [128, 512])
    esink = T("esink", [128, 8])
    w0T = T("w0T", [128, 4])
    a0T = T("a0T", [128, 4])
    kkT = T("kkT", [128, 4])
    kaT = T("kaT", [128, 4])
    rkT = T("rkT", [128, 4])
    ln2x20 = T("ln2x20", [128, 1])
    muT = T("muT", [128, 14])
    ommT = T("ommT", [128, 14])
    cwT = T("cwT", [128, 3, NFC])
    cbT = T("cbT", [128, NFC])
    Wd_b = T("Wd_b", [32, 512], BF16)
    Wa_b = T("Wa_b", [64, 512], BF16)
    Wg_b = T("Wg_b", [96, 512], BF16)
    blk1 = T("blk1", [128, 128], BF16)
    m_own = T("m_own", [128, 4, 128], BF16)
    m_prev = T("m_prev", [128, 4, 128], BF16)
    m2b = T("m2b", [128, 2, 128], BF16)
    m2k = T("m2k", [128, 2, 128], BF16)
    mTl = T("mTl", [128, 128], BF16)
    scanm_p = T("scanm_p", [128, 512])
    cstage = T("cstage", [128, 512])

    ld_n = [0]

    def cload(dst, src, xw=(), **kw):
        i = ld_n[0]
        ld_n[0] += 1
        k = ("c", i)
        P.dma("sp", "cl%d" % i, dst, src, w=[k] + list(xw), **kw)
        return k

    def cload_cast(dst_bf, src, np_, shape_free, eng="dve"):
        nfree = int(np.prod(shape_free))
        stg = cstage[0:np_, 0:nfree]
        k = ("c", ld_n[0])
        ld_n[0] += 1
        P.dma("sp", "cst", stg, src, w=["cstage"])
        P.dve(lambda e: e.tensor_copy(out=dst_bf, in_=stg), r=["cstage"], w=[k, "cstage_rd"])
        return k

    NSC = ALLOW = dict(allow_slow_non_contiguous=True)
    CK = []
    CK.append(cload(idf[:], c_ident))
    P.dve(lambda e: e.tensor_copy(out=idb[:], in_=idf[:]), r=[CK[-1]], w=["idb"])
    CK.append(cload(gTpre[:], g_pre_mix.rearrange("(c p) -> p c", p=128), **NSC))
    kgpre = CK[-1]
    CK.append(cload(gTffn[:], g_pre_ffn.rearrange("(c p) -> p c", p=128), **NSC))
    kgffn = CK[-1]
    CK.append(cload(gpost_bc[:], g_post_mix.partition_broadcast(128)))
    CK.append(cload(gpostf_bc[:], g_post_ffn.partition_broadcast(128)))
    CK.append(cload(gnw_bc[:], gn_w.partition_broadcast(128)))
    CK.append(cload(gnb_bc[:], gn_b.partition_broadcast(128)))
    CK.append(cload(esink[:], attn_sinks.partition_broadcast(128)))
    P.act(lambda e: e.activation(out=esink[:], in_=esink[:], func=AF.Exp), r=[CK[-1]], w=["esink"])
    for tt, src in ((w0T, w0), (a0T, a0), (kkT, k_k), (kaT, k_a), (rkT, r_k)):
        CK.append(cload(tt[:], src.rearrange("(c p) -> p c", p=128), **NSC))
    P.pool(lambda e: e.memset(muT[:], 0.0), w=["muT"])
    P.pool(lambda e: e.memset(ln2x20[:], float(20.0 * np.log(2.0))), w=["ln2c"])
    CK.append(cload(muT[:, 0:12], mu_shift[0:1536].rearrange("(c p) -> p c", p=128), xw=["muT"], **NSC))
    CK.append(cload(muT[0:64, 12:13], mu_shift[1536:1600].rearrange("(p c) -> p c", c=1), xw=["muT"], **NSC))
    CK.append(cload(muT[0:96, 13:14], mu_shift[1600:1696].rearrange("(p c) -> p c", c=1), xw=["muT"], **NSC))
    P.dve(lambda e: e.tensor_scalar(out=ommT[:], in0=muT[:], scalar1=-1.0, scalar2=1.0, op0=ALU.mult, op1=ALU.add),
          r=["muT"], w=["muT2"])
    CK.append(cload(cwT[:], conv_w.rearrange("j (c p) -> p j c", p=128), **NSC))
    CK.append(cload(cbT[:], conv_b.rearrange("(c p) -> p c", p=128), **NSC))
    CK.append(cload(scanm_p[:], c_scanm_p))

    def cast_const(dst, src, np_, nfree, bcast4=False):
        stg = cstage[0:np_, 0:nfree]
        P.dma("sp", "cst", stg, src, w=["cstage"])
        if bcast4:
            P.dve(lambda e: e.tensor_copy(out=dst, in_=stg.unsqueeze(1).to_broadcast([np_, 4, nfree])),
                  r=["cstage"], w=["cconst", "cstage"])
        else:
            P.dve(lambda e: e.tensor_copy(out=dst, in_=stg), r=["cstage"], w=["cconst", "cstage"])

    cast_const(blk1[:], c_blockones, 128, 128)
    cast_const(m_own[:], c_negmask_own, 128, 128, bcast4=True)
    cast_const(m_prev[:], c_negmask_prev, 128, 128, bcast4=True)
    cast_const(m2b[:].rearrange("p a b -> p (a b)"), c_m2b.rearrange("p a b -> p (a b)"), 128, 256)
    cast_const(m2k[:].rearrange("p a b -> p (a b)"), c_m2k.rearrange("p a b -> p (a b)"), 128, 256)
    cast_const(mTl[:], c_mT, 128, 128)
    cast_const(Wd_b[:], w_decay_up, 32, 512)
    P.dma("sp", "cst", cstage[32:64, 0:512], w_a_up, w=["cstage"])
    P.dve(lambda e: e.tensor_copy(out=Wa_b[32:64, :], in_=cstage[32:64, 0:512]), r=["cstage"], w=["cconst", "cstage"])
    cast_const(Wg_b[:], w_g_up, 96, 512)
    CONST = CK + ["idb", "esink", "cconst", "muT", "muT2", "ln2c"]

    x_t = T("x_t", [128, 4, D])
    bufA = T("bufA", [128, 8, 512], BF16)
    BUFA_ALL = [("bufA", b) for b in range(4)]
    wp_n = [0]

    def prep_chunk(dst_dram, dkey, loads, gT=None, gkey=None):
        i = wp_n[0] % 4
        wp_n[0] += 1
        stage = x_t[:, i, :]
        wbs_i = bufA[:, 2 * i:2 * i + 2, :].rearrange("p a b -> p (a b)")
        for mk, src in loads:
            P.dma("sp", "wl%d" % i, mk(stage), src, w=[("x", i)])
        if gT is not None:
            P.dve(lambda e: e.tensor_tensor(out=wbs_i.rearrange("p (c n) -> p c n", c=8),
                                            in0=stage.rearrange("p (c n) -> p c n", c=8),
                                            in1=gT[:].unsqueeze(2).to_broadcast([128, 8, 128]), op=ALU.mult),
                  r=[("x", i), gkey], w=[("wbs", i)])
        else:
            P.dve(lambda e: e.tensor_copy(out=wbs_i, in_=stage), r=[("x", i)], w=[("wbs", i)])
        P.dma("pool", "ws%d" % i, dst_dram, wbs_i, r=[("wbs", i)], w=[dkey])

    P.pool(lambda e: e.memset(x_t[:, :, :], 0.0), w=[("x", b_) for b_ in range(4)])
    win3 = w_in.rearrange("(c p) n -> p c n", p=128)
    for ci, key in enumerate(WIN_ORDER):
        loads = []
        for (cs, n, o) in WIN_CHUNKS[key]:
            loads.append((lambda s, o=o, n=n: s.rearrange("p (c n) -> p c n", c=8)[:, :, o:o + n], win3[:, :, cs:cs + n]))
        if key[0] in ("lo", "lg"):
            pass
        prep_chunk(wsc_in[ci], ("wsc_in", ci), loads, gTpre, kgpre)
    wfi3 = w_ffn_in.rearrange("(c p) n -> p c n", p=128)
    for fc in range(NFC):
        for zu in range(2):
            cs = zu * DFF + fc * 128
            prep_chunk(wsc_fin[2 * fc + zu], ("wsc_fin", 2 * fc + zu),
                       [(lambda s: s.rearrange("p (c n) -> p c n", c=8), wfi3[:, :, cs:cs + 128])], gTffn, kgffn)
    for c in range(8):
        prep_chunk(wsc_out[c], ("wsc_out", c), [(lambda s: s, w_out[c * 128:(c + 1) * 128, :])])
    for c in range(NFC):
        prep_chunk(wsc_fout[c], ("wsc_fout", c), [(lambda s: s, w_ffn_out[c * 128:(c + 1) * 128, :])])

    P.dve(lambda e: e.memset(bufA[0:1, 0, 0:2], 0.0), r=[("wbs", i_) for i_ in range(4)], w=BUFA_ALL + [("wbs", i_) for i_ in range(4)])
    ring = T("ring", [128, RING, 1024], BF16)
    xn = T("xn", [128, D], BF16)
    junk = T("junk", [128, D], BF16)
    cs_t = T("cs_t", [128, 2, 512])
    qT = T("qT", [128, 4, 512], BF16)
    kbuf = T("kbuf", [128, 5 * 128], BF16)
    vtokf = T("vtokf", [128, 128])
    Vaug = T("Vaug", [128, 5, 2, 65], BF16)
    Et = T("Et", [128, 4, 512], BF16)
    oacc = T("oacc", [128, 2, 4, 65])
    den = T("den", [128, 8])
    mix = T("mix", [128, 4, D], BF16)
    hidT = T("hidT", [128, NFC, 512], BF16)
    zbuf = T("zbuf", [128, 1, 640])
    za = T("za", [128, 1, 512])
    zcar_p = T("zcar_p", [128, NFC, 1, 2])
    zcar_s = T("zcar_s", [128, NFC, NSEQ_S, 2])
    sstat = T("sstat", [128, 16])
    hbuf = T("hbuf", [128, 1, 640])
    hcar_p = T("hcar_p", [128, 14, 1])
    hcar_s = T("hcar_s", [128, 14, NSEQ_S])
    dtmp = T("dtmp", [128, 512])
    hs_lo = T("hs_lo", [64, 512])
    hs_lg = T("hs_lg", [96, 512])
    lorab = T("lorab", [64, 512], BF16)
    sgb = T("sgb", [96, 512], BF16)
    rkv = T("rkv", [128, 1, 3, 512])
    tq = [T("tq%d" % i, [128, 512]) for i in range(7)]
    yq, yq2, kf32, vT32 = tq[0], tq[1], tq[3], tq[4]
    vblk_s = T("vblk_s", [128, 512], BF16)
    Sld = tq[5][0:64, 0:256].rearrange("p (a b) -> p a b", a=2)
    sqb = T("sqb", [128, 512], BF16)
    Kt = T("Kt", [128, 512], BF16)
    Bt = T("Bt", [128, 512], BF16)
    KR = T("KR", [128, 1024], BF16)
    KgL = T("KgL", [128, 512], BF16)
    BgLn = T("BgLn", [128, 512], BF16)
    vb = T("vb", [128, 512], BF16)
    prodb = T("prodb", [128, 512], BF16)
    gL2 = T("gL2", [128, 2, 16])
    nkcl = T("nkcl", [128, 16])
    vtok = T("vtok", [128, 2048], BF16)
    KgLtok = T("KgLtok", [128, 512], BF16)
    BgLtok = T("BgLtok", [128, 512], BF16)
    NQ2 = T("NQ2", [128, 2048], BF16)
    KQ2 = T("KQ2", [128, 2048], BF16)
    NT2 = T("NT2", [128, 1024], BF16)
    T32 = T("T32", [128, 1024])
    Tb = T("Tb", [128, 1024], BF16)
    XTs = T("XTs", [128, 128], BF16)
    SATs = T("SATs", [128, 128], BF16)
    ytok = T("ytok", [128, 4, 512])
    rkb = T("rkb", [128, 4, 8])
    Sm = T("Sm", [128, 4, 64])
    Sb = T("Sb", [128, 4, 64], BF16)
    gstat = T("gstat", [128, 4, 8])
    hid32 = hidT[:].rearrange("p a b -> p (a b)").bitcast(F32)
    HID_ALL = [("hidT", fc) for fc in range(NFC)]
    rowbuf = hid32[0:32, 0:DSH]
    sst = hid32[0:32, 1792:1792 + DFF]
    scanm_s = T("scanm_s", [128, 128])
    m_sown = T("m_sown", [128, 4, 128], BF16)
    m_scache = T("m_scache", [128, NSEQ_S, 128], BF16)
    ckT = T("ckT", [128, 128], BF16)
    ckf = T("ckf", [128, 2, 128])
    cVaug = T("cVaug", [128, 2, 2, 65], BF16)
    ytok_s = hid32[0:8, 0:NSEQ_S * 128].rearrange("p (s n) -> p s n", s=NSEQ_S)

    state = dict(ring_n=0, xld=0, ost=0)

    def stream(src, skey):
        n = state["ring_n"]
        state["ring_n"] += 1
        s = n % RING
        P.dma("sp", "rg%d" % s, ring[:, s, :], src, r=[skey], w=[("ring", s)])
        return s

    def emit_tile(kind, ti):
        NT = 512 if kind == "p" else 128
        NB = NT // 128
        nseq, Lseq = (1, 512) if kind == "p" else (NSEQ_S, LS)
        L = 128 if kind == "p" else LS
        nch = NT // L
        t0 = ti * 512
        first = (kind == "p" and ti == 0)
        last = (kind == "s") or (ti == N_TILES - 1)
        xsrc = xp if kind == "p" else xs
        ydst = yp if kind == "p" else ys
        hcar = hcar_p if kind == "p" else hcar_s
        zcar = zcar_p if kind == "p" else zcar_s
        scanm = scanm_p if kind == "p" else scanm_s
        S_ = slice(0, NT)

        if kind == "p":
            P.dma("pool", "cs", cs_t[:, 0, S_], c_cos_p[:, t0:t0 + NT], w=["cs_t"])
            P.dma("pool", "cs", cs_t[:, 1, S_], c_sin_p[:, t0:t0 + NT], w=["cs_t"])
        else:
            P.dma("pool", "cs", cs_t[:, 0, S_], c_cos_s, w=["cs_t"])
            P.dma("pool", "cs", cs_t[:, 1, S_], c_sin_s, w=["cs_t"])

        for b in range(NB):
            P.dma("sp", "xl%d" % b, x_t[:, b, :], xsrc[t0 + b * 128:t0 + (b + 1) * 128, :] if kind == "p" else xsrc,
                  w=[("x", b)])
            sc = sstat[:, b:b + 1]
            P.act(lambda e, b=b, sc=sc: e.activation(out=junk[:], in_=x_t[:, b, :], func=AF.Square, accum_out=sc),
                  r=[("x", b)], w=["junk", ("ss", b)])
            P.act(lambda e, sc=sc: e.activation(out=sc, in_=sc, func=AF.Ln, scale=1.0 / D, bias=RMS_EPS),
                  r=[("ss", b)], w=[("ss", b)])
            P.act(lambda e, sc=sc: e.activation(out=sc, in_=sc, func=AF.Exp, scale=-0.5), r=[("ss", b)], w=[("ss", b)])
        for b in range(NB):
            sc = sstat[:, b:b + 1]
            P.dve(lambda e, b=b, sc=sc: e.tensor_scalar(out=xn[:], in0=x_t[:, b, :], scalar1=sc, scalar2=None, op0=ALU.mult),
                  r=[("x", b), ("ss", b)], w=["xn"])
            bk = bank()
            for c in range(8):
                P.pe(lambda e, c=c, bk=bk: e.transpose(out=PSB(bk)[:, c * 128:(c + 1) * 128], in_=xn[:, c * 128:(c + 1) * 128],
                                                      identity=idb[:]), r=["xn", "idb"], w=[kb(bk)])
            P.act(lambda e, b=b, bk=bk: e.copy(out=bufA[:, :, b * 128:(b + 1) * 128],
                                               in_=PSB(bk).rearrange("p (c n) -> p c n", c=8)),
                  r=[kb(bk)], w=[("bufA", b)])
        bufA_keys = [("bufA", b) for b in range(NB)]

        if kind == "s" and STOP <= 1:
            return
        def inproj(key, ncols):
            ci = WIN_IDX[key]
            s = stream(wsc_in[ci], ("wsc_in", ci))
            bk = bank()
            for c in range(8):
                P.pe(lambda e, c=c, s=s, bk=bk: e.matmul(PS(bk, NT, ncols), lhsT=ring[:, s, c * 128:c * 128 + ncols],
                                                         rhs=bufA[:, c, S_], start=(c == 0), stop=(c == 7)),
                     r=[("ring", s)] + bufA_keys, w=[kb(bk)])
            return bk

        hb_n = [0]

        def shift_evac(bk, np_, rc, out_ap, okey, xw=()):
            i = 0
            hb = hbuf[0:np_, i, 0:nseq * (Lseq + 1)].rearrange("p (s l) -> p s l", s=nseq)
            hk = ("hbuf", i)
            P.act(lambda e: e.copy(out=hb[:, :, 1:Lseq + 1], in_=PS(bk, NT, np_).rearrange("p (s l) -> p s l", s=nseq)),
                  r=[kb(bk)], w=[hk])
            if VAR != 1:
                P.pool(lambda e: e.tensor_copy(out=hb[:, :, 0:1], in_=hcar[0:np_, rc, :].unsqueeze(2)),
                       r=[("hcar", rc)], w=[hk])
                P.pool(lambda e: e.tensor_copy(out=hcar[0:np_, rc, :].unsqueeze(2), in_=hb[:, :, Lseq:Lseq + 1]),
                       r=[hk], w=[("hcar", rc)])
            if VAR == 2:
                return
            d3 = dtmp[0:np_, S_].rearrange("p (s l) -> p s l", s=nseq)
            P.act(lambda e: e.activation(out=d3, in_=hb[:, :, 0:Lseq], func=AF.Copy, scale=muT[0:np_, rc:rc + 1]),
                  r=[hk, "muT"], w=["dtmp"])
            P.dve(lambda e: e.scalar_tensor_tensor(out=out_ap.rearrange("p (s l) -> p s l", s=nseq), in0=hb[:, :, 1:Lseq + 1],
                                                   scalar=ommT[0:np_, rc:rc + 1], in1=d3,
                                                   op0=ALU.mult, op1=ALU.add),
                  r=["dtmp", hk, "muT", "muT2"], w=[okey] + list(xw))

        bk = inproj(("lo", 0), 64)
        shift_evac(bk, 64, 12, hs_lo[:, S_], "hs_lo")
        P.act(lambda e: e.activation(out=lorab[0:32, S_], in_=hs_lo[0:32, S_], func=AF.Tanh), r=["hs_lo"], w=["lorab0"])
        P.act(lambda e: e.copy(out=lorab[32:64, S_], in_=hs_lo[32:64, S_]), r=["hs_lo"], w=["lorab1"])
        bk = inproj(("lg", 0), 96)
        shift_evac(bk, 96, 13, hs_lg[:, S_], "hs_lg")
        P.act(lambda e: e.activation(out=sgb[:, S_], in_=hs_lg[:, S_], func=AF.Sigmoid), r=["hs_lg"], w=["sgb"])

        if kind == "s" and STOP <= 2:
            return
        def hp_gen(hp):
            rs_i = (hp % 2) if PIPE else 0
            gLv = gL2[:, rs_i, :]
            gk = ("gL", rs_i)
            rkv_v = (lambda j: rkv[:, 0, j, S_]) if rs_i == 0 else (lambda j: hid32[:, j * 512:j * 512 + NT])
            rs, ks, vs = rkv_v(0), rkv_v(1), rkv_v(2)
            for j, nm in enumerate(("r", "k", "v")):
                bk = inproj((nm, hp), 128)
                shift_evac(bk, 128, RC[(nm, hp)], rkv_v(j), ("rkv", rs_i, j), xw=([("hidT", fc_) for fc_ in range(6)] if rs_i == 1 else []))
                yield "p0"
            kr_, kk_, kv_ = ("rkv", rs_i, 0), ("rkv", rs_i, 1), ("rkv", rs_i, 2)
            t = [x[:, S_] for x in tq]
            if kind == "s" and STOP == 32:
                return
            bw = bank()
            P.pe(lambda e, bw=bw, hp=hp: e.matmul(PS(bw, NT), lhsT=Wd_b[0:32, hp * 128:(hp + 1) * 128], rhs=lorab[0:32, S_],
                                                  start=True, stop=True), r=["lorab0", "cconst"], w=[kb(bw)])
            P.act(lambda e, bw=bw, hp=hp: e.activation(out=t[0], in_=PS(bw, NT), func=AF.Sigmoid, bias=w0T[:, hp:hp + 1]),
                  r=[kb(bw)] + CONST, w=["t0"])
            ba = bank()
            P.pe(lambda e, ba=ba, hp=hp: e.matmul(PS(ba, NT), lhsT=Wa_b[32:64, hp * 128:(hp + 1) * 128], rhs=lorab[32:64, S_],
                                                  start=True, stop=True), r=["lorab1", "cconst"], w=[kb(ba)])
            P.act(lambda e, ba=ba, hp=hp: e.activation(out=t[6], in_=PS(ba, NT), func=AF.Sigmoid, bias=a0T[:, hp:hp + 1]),
                  r=[kb(ba)] + CONST, w=["t6"])
            P.dve(lambda e: e.tensor_tensor_scan(out=t[1], data0=scanm[:, S_], data1=t[0], initial=0.0, op0=ALU.mult, op1=ALU.add),
                  r=["t0"] + CONST, w=["t1"])
            P.dve(lambda e: e.tensor_tensor(out=t[2], in0=t[1], in1=t[0], op=ALU.subtract), r=["t0", "t1"], w=["t2"])
            yield "p1"
            P.act(lambda e: e.activation(out=t[0], in_=t[1], func=AF.Exp, scale=-KAPPA), r=["t1", "t2"], w=["t0"])
            P.act(lambda e: e.activation(out=t[3], in_=t[1], func=AF.Exp, scale=KAPPA), r=["t1"], w=["t3"])
            P.act(lambda e: e.activation(out=t[2], in_=t[2], func=AF.Exp, scale=-KAPPA), r=["t2"], w=["t2"])
            yield "p1"
            cp3 = t[1].rearrange("p (c l) -> p c l", l=L)
            eg3 = t[0].rearrange("p (c l) -> p c l", l=L)
            P.dve(lambda e: e.tensor_copy(out=gLv[:, 0:nch].unsqueeze(2), in_=eg3[:, :, L - 1:L]), r=["t0"], w=[gk])
            P.dve(lambda e: e.tensor_scalar(out=nkcl[:, 0:nch].unsqueeze(2), in0=cp3[:, :, L - 1:L], scalar1=-KAPPA, scalar2=None,
                                            op0=ALU.mult), r=["t1"], w=["nkcl"])
            P.dve(lambda e: e.scalar_tensor_tensor(out=t[4].rearrange("p (c l) -> p c l", l=L), in0=cp3, scalar=KAPPA,
                                                   in1=nkcl[:, 0:nch].unsqueeze(2).to_broadcast([128, nch, L]),
                                                   op0=ALU.mult, op1=ALU.add), r=["t1", "nkcl"], w=["t4"])
            P.act(lambda e: e.activation(out=t[4], in_=t[4], func=AF.Exp), r=["t4"], w=["t4"])
            if kind == "s" and STOP == 33:
                return
            yield "p1"
            yield "p1"
            P.dve(lambda e, hp=hp: e.tensor_scalar(out=t[5], in0=ks, scalar1=kkT[:, hp:hp + 1], scalar2=None, op0=ALU.mult),
                  r=[kk_] + CONST, w=["t5"])
            P.act(lambda e: e.activation(out=sqb[:, S_], in_=t[5], func=AF.Square), r=["t5"], w=["sqb"])
            bn = bank()
            P.pe(lambda e, bn=bn: e.matmul(PS(bn, NT), lhsT=blk1[:], rhs=sqb[:, S_], start=True, stop=True),
                 r=["sqb", "cconst"], w=[kb(bn)])
            yield "p1"
            P.dve(lambda e, bn=bn: e.tensor_scalar(out=t[1], in0=PS(bn, NT), scalar1=float(2.0 ** 40), scalar2=float(1e-24 * 2.0 ** 40),
                                                   op0=ALU.mult, op1=ALU.max),
                  r=[kb(bn), "t4", gk, "nkcl"], w=["t1"])
            P.act(lambda e: e.activation(out=t[1], in_=t[1], func=AF.Ln), r=["t1"], w=["t1"])
            P.act(lambda e: e.activation(out=t[1], in_=t[1], func=AF.Exp, scale=-0.5, bias=ln2x20[:, 0:1]), r=["t1"] + CONST, w=["t1"])
            P.dve(lambda e: e.tensor_tensor(out=t[5], in0=t[5], in1=t[1], op=ALU.mult), r=["t5", "t1"], w=["t5"])
            yield "p1"
            P.dve(lambda e: e.tensor_tensor(out=t[1], in0=t[5], in1=t[6], op=ALU.mult), r=["t5", "t6", "t1"], w=["t1"])
            yield "p1"
            P.dve(lambda e, hp=hp: e.tensor_scalar(out=t[6], in0=t[6], scalar1=1.0, scalar2=kaT[:, hp:hp + 1],
                                                   op0=ALU.subtract, op1=ALU.mult), r=["t6", "t1"] + CONST, w=["t6"])
            P.dve(lambda e: e.scalar_tensor_tensor(out=t[6], in0=t[6], scalar=1.0, in1=ks, op0=ALU.add, op1=ALU.mult),
                  r=["t6", kk_], w=["t6"])
            if kind == "s" and STOP == 34:
                return
            yield "p1done"
            KR4 = KR[:, 0:2 * NT].rearrange("p (c a l) -> p c a l", a=2, l=L)
            P.dve(lambda e: e.tensor_tensor(out=Kt[:, S_], in0=t[6], in1=t[3], op=ALU.mult), r=["t6", "t3"], w=["Kt"])
            P.dve(lambda e: e.tensor_tensor(out=Bt[:, S_], in0=t[1], in1=t[3], op=ALU.mult), r=["t1", "t3"], w=["Bt"])
            P.dve(lambda e: e.tensor_tensor(out=KR4[:, :, 0, :], in0=t[5].rearrange("p (c l) -> p c l", l=L),
                                            in1=t[2].rearrange("p (c l) -> p c l", l=L), op=ALU.mult), r=["t5", "t2"], w=["KR"])
            P.dve(lambda e: e.tensor_tensor(out=KR4[:, :, 1, :], in0=rs.rearrange("p (c l) -> p c l", l=L),
                                            in1=t[0].rearrange("p (c l) -> p c l", l=L), op=ALU.mult), r=[kr_, "t0"], w=["KR"])
            P.dve(lambda e: e.tensor_tensor(out=KgL[:, S_], in0=t[6], in1=t[4], op=ALU.mult), r=["t6", "t4"], w=["KgL"])
            P.dve(lambda e: e.scalar_tensor_tensor(out=BgLn[:, S_], in0=t[1], scalar=-1.0, in1=t[4], op0=ALU.mult, op1=ALU.mult),
                  r=["t1", "t4"], w=["BgLn"])
            P.act(lambda e: e.copy(out=vb[:, S_], in_=vs), r=[kv_], w=["vb"])
            P.dve(lambda e, hp=hp: e.scalar_tensor_tensor(out=prodb[:, S_], in0=rs, scalar=rkT[:, hp:hp + 1], in1=t[6],
                                                          op0=ALU.mult, op1=ALU.mult), r=[kr_, "t6"] + CONST, w=["prodb"])
            if kind == "s" and STOP == 35:
                return
            brk = bank()
            for b in range(NB):
                P.pe(lambda e, b=b, brk=brk: e.matmul(PS(brk, 2 * NB)[:, 2 * b:2 * b + 2], lhsT=prodb[:, b * 128:(b + 1) * 128],
                                                      rhs=blk1[:, 0:128:64], start=True, stop=True),
                     r=["prodb", "cconst"], w=[kb(brk)])
            P.act(lambda e, brk=brk, hp=hp: e.copy(out=rkb[:, 0:NB, 2 * hp:2 * hp + 2],
                                                   in_=PS(brk, 2 * NB).rearrange("p (b a) -> p b a", a=2)),
                  r=[kb(brk)], w=[("rkb", hp)])
            if kind == "s" and STOP == 36:
                return
            if kind == "p":
                vtok_hp = vtok[0:L, hp * nch * 128:(hp + 1) * nch * 128].rearrange("p (c n) -> p c n", n=128)
                vkey = ("vtok", hp)
            else:
                vtok_hp = vtok[0:L, 0:nch * 128].rearrange("p (c n) -> p c n", n=128)
                vkey = "vtok_s"
            if kind == "p":
                KgLtok3 = KgLtok[0:L, 0:nch * 128].rearrange("p (c n) -> p c n", n=128)
                BgLtok3 = BgLtok[0:L, 0:nch * 128].rearrange("p (c n) -> p c n", n=128)
                kBg = ["BgLtok"]
            else:
                KgLtok3 = hidT[:].rearrange("p a b -> p (a b)")[0:L, 8192:8192 + nch * 128].rearrange("p (c n) -> p c n", n=128)
                BgLtok3 = Et[:].rearrange("p a b -> p (a b)")[0:L, 0:nch * 128].rearrange("p (c n) -> p c n", n=128)
                kBg = ["BgLtok"] + [("Et", i_) for i_ in range(4)]
            for src, skey, dst3, dkey in ((vb, "vb", vtok_hp, vkey), (KgL, "KgL", KgLtok3, "KgLtok"), (BgLn, "BgLn", BgLtok3, kBg)):
                for g0 in range(0, nch, 8):
                    g1 = min(nch, g0 + 8)
                    bt = bank()
                    for c in range(g0, g1):
                        P.pe(lambda e, c=c, bt=bt, src=src, g0=g0: e.transpose(
                            out=PSB(bt, L)[:, (c - g0) * 128:(c - g0 + 1) * 128], in_=src[:, c * L:(c + 1) * L], identity=idb[:]),
                            r=[skey, "idb"], w=[kb(bt)])
                    P.act(lambda e, bt=bt, g0=g0, g1=g1, dst3=dst3: e.copy(
                        out=dst3[:, g0:g1, :], in_=PSB(bt, L)[:, 0:(g1 - g0) * 128].rearrange("p (c n) -> p c n", n=128)),
                        r=[kb(bt), "XA"], w=(dkey if isinstance(dkey, list) else [dkey]))
            if kind == "s" and STOP == 31:
                return
            if kind == "s":
                btv = bank()
                P.pe(lambda e, btv=btv: e.transpose(out=PSB(btv)[:, 0:128], in_=vb[:, 0:128], identity=idb[:]), r=["vb", "idb"], w=[kb(btv)])
                P.act(lambda e, btv=btv, hp=hp: e.copy(out=vblk_s[:, hp * 128:(hp + 1) * 128], in_=PSB(btv)[:, 0:128]), r=[kb(btv)], w=["vblk_s"])
                Sst = hid32[0:64, 2048:4096]
                if VAR == 3:
                    return
                for s_ in range(NSEQ_S):
                    P.dma("sp", "sst_in", Sst.rearrange("i (s h j) -> i s h j", s=NSEQ_S, h=2)[:, s_, :, :],
                          swkv[s_, 2 * hp:2 * hp + 2, :, :].rearrange("h i j -> i h j"), r=["XA"], w=["Sstage"] + HID_ALL)
                if VAR == 4:
                    return
                for g in range(2):
                    bts = bank()
                    for s8 in range(8):
                        s_ = g * 8 + s8
                        P.pe(lambda e, bts=bts, s8=s8, s_=s_, Sst=Sst: e.transpose(out=PS(bts, 512)[:, s8 * 64:(s8 + 1) * 64],
                                                                                 in_=Sst[:, s_ * 128:(s_ + 1) * 128], identity=idf[0:64, 0:64]),
                             r=["Sstage", "XA"] + CONST, w=[kb(bts)])
                    if VAR == 5:
                        return
                    P.act(lambda e, bts=bts, g=g: e.copy(out=SmS[:, g * 8:(g + 1) * 8, :], in_=PS(bts, 512).rearrange("p (s i) -> p s i", i=64)),
                          r=[kb(bts)], w=[("SmS", s_) for s_ in range(g * 8, g * 8 + 8)])
                    if VAR == 6:
                        return
                    P.dve(lambda e, bts=bts, g=g: e.tensor_copy(out=SbS[:, g * 8:(g + 1) * 8, :], in_=PS(bts, 512).rearrange("p (s i) -> p s i", i=64)),
                          r=[kb(bts)], w=[("SbS", s_) for s_ in range(g * 8, g * 8 + 8)])
            if kind == "s" and STOP == 3:
                return
            yield "p2done"
            nu = 2 * nch
            W = nu * L
            NQv = NQ2[0:L, 0:2 * W]
            KQv = KQ2[0:L, 0:2 * W]
            NTv = NT2[0:L, 0:W]
            T32v = T32[0:L, 0:W]
            Tbv = Tb[0:L, 0:W]
            NQ4 = NQv.rearrange("p (u a l) -> p u a l", a=2, l=L)
            KQ4 = KQv.rearrange("p (u a l) -> p u a l", a=2, l=L)
            kNQ, kKQ, kNT, kT32, kTb = "NQ2", "KQ2", "NT2", "T32", "Tb"
            for p in range(2):
                P0 = 64 * p
                for (lhs, lkey, dst4, dkey, msk) in ((Bt, "Bt", NQ4, kNQ, m2b), (Kt, "Kt", KQ4, kKQ, m2k)):
                    bq = pair()
                    for c in range(nch):
                        col = c * 2 * L
                        P.pe(lambda e, P0=P0, c=c, bq=bq, lhs=lhs, col=col: e.matmul(
                            PSP(bq)[0:L, col:col + 2 * L], lhsT=lhs[P0:P0 + 64, c * L:(c + 1) * L],
                            rhs=KR[P0:P0 + 64, c * 2 * L:(c + 1) * 2 * L], start=True, stop=True),
                            r=[lkey, "KR"], w=[kb(bq), kb(bq + 1)])
                    P.dve(lambda e, bq=bq, dst4=dst4, msk=msk, p=p: e.tensor_tensor(
                        out=dst4[:, p * nch:(p + 1) * nch, :, :],
                        in0=PSP(bq)[0:L, 0:nch * 2 * L].rearrange("p (c a l) -> p c a l", a=2, l=L),
                        in1=msk[0:L, :, 0:L].unsqueeze(1).to_broadcast([L, nch, 2, L]),
                        op=ALU.mult), r=[kb(bq), kb(bq + 1), "cconst"], w=[(dkey, p)])
                    pump()
            pump()
            bT2 = pair()
            for p in range(2):
                P0 = 64 * p
                for c in range(nch):
                    u = p * nch + c
                    P.pe(lambda e, P0=P0, c=c, u=u, bT2=bT2: e.matmul(PSP(bT2)[0:L, u * L:(u + 1) * L],
                                                                     lhsT=KR[P0:P0 + 64, c * 2 * L:c * 2 * L + L],
                                                                     rhs=Bt[P0:P0 + 64, c * L:(c + 1) * L], start=True, stop=True),
                         r=["KR", "Bt"], w=[kb(bT2), kb(bT2 + 1)])
            P.dve(lambda e, bT2=bT2: e.tensor_tensor(out=NTv.rearrange("p (u l) -> p u l", l=L),
                                                     in0=PSP(bT2)[0:L, 0:W].rearrange("p (u l) -> p u l", l=L),
                                                     in1=mTl[0:L, 0:L].unsqueeze(1).to_broadcast([L, nu, L]), op=ALU.mult),
                  r=[kb(bT2), kb(bT2 + 1), "cconst"], w=[kNT])
            kNQb = [(kNQ, 0), (kNQ, 1)]
            kKQb = [(kKQ, 0), (kKQ, 1)]
            P.dve(lambda e: e.tensor_tensor(out=T32v.rearrange("p (u l) -> p u l", l=L),
                                            in0=idf[0:L, 0:L].unsqueeze(1).to_broadcast([L, nu, L]),
                                            in1=NQ4[:, :, 0, :], op=ALU.subtract), r=kNQb + CONST, w=[kT32])
            P.act(lambda e: e.copy(out=Tbv, in_=T32v), r=[kT32], w=[kTb])
            nlev = int(np.log2(L)) - 1

            def emit_sq(lastlev):
                bPT = pair()
                for u in range(nu):
                    P.pe(lambda e, u=u, bPT=bPT: e.matmul(PSP(bPT)[0:L, u * L:(u + 1) * L], lhsT=NQ4[:, u, 0, :],
                                                          rhs=NTv[:, u * L:(u + 1) * L], start=True, stop=True),
                         r=kNQb + [kNT], w=[kb(bPT), kb(bPT + 1)])
                bP = None
                if not lastlev:
                    bP = pair()
                    for u in range(nu):
                        P.pe(lambda e, u=u, bP=bP: e.matmul(PSP(bP)[0:L, u * L:(u + 1) * L], lhsT=NTv[:, u * L:(u + 1) * L],
                                                            rhs=NQ4[:, u, 0, :], start=True, stop=True),
                             r=kNQb + [kNT], w=[kb(bP), kb(bP + 1)])
                return bPT, bP

            def emit_sq_evac(bPT, bP):
                P.act(lambda e, bPT=bPT: e.copy(out=NTv, in_=PSP(bPT)[0:L, 0:W]), r=[kb(bPT), kb(bPT + 1)], w=[kNT])
                if bP is not None:
                    P.dve(lambda e, bP=bP: e.tensor_copy(out=NQ4[:, :, 0, :], in_=PSP(bP)[0:L, 0:W].rearrange("p (u l) -> p u l", l=L)),
                          r=[kb(bP), kb(bP + 1)], w=kNQb)

            bb = emit_sq(nlev == 1)
            emit_sq_evac(*bb)
            for lev in range(nlev):
                nxt = None
                if lev + 1 < nlev:
                    nxt = emit_sq(lev + 1 == nlev - 1)
                bT = pair()
                for u in range(nu):
                    P.pe(lambda e, u=u, bT=bT: e.matmul(PSP(bT)[0:L, u * L:(u + 1) * L], lhsT=NTv[:, u * L:(u + 1) * L],
                                                        rhs=Tbv[:, u * L:(u + 1) * L], start=True, stop=True),
                         r=[kNT, kTb], w=[kb(bT), kb(bT + 1)])
                if nxt is not None:
                    emit_sq_evac(*nxt)
                P.dve(lambda e, bT=bT: e.tensor_tensor(out=T32v, in0=T32v, in1=PSP(bT)[0:L, 0:W], op=ALU.add),
                      r=[kb(bT), kb(bT + 1), kT32], w=[kT32])
                P.act(lambda e: e.copy(out=Tbv, in_=T32v), r=[kT32], w=[kTb])
                pump()
            for c in range(nch):
                if kind == "p":
                    Smv, Sbv, skm, skb = Sm[:, hp, :], Sb[:, hp, :], ("Sm", hp), ("Sb", hp)
                    zero_state = first and c == 0
                    ydst_ap, ykey = ytok[:, c, hp * 128:(hp + 1) * 128], ("ytok", c)
                else:
                    Smv, Sbv, skm, skb = SmS[:, c, :], SbS[:, c, :], ("SmS", c), ("SbS", c)
                    zero_state = False
                    ydst_ap, ykey = ytok_s[0:L, c, :], "ytok_s"
                bX = bank()
                for p in range(2):
                    P0 = 64 * p
                    u = p * nch + c
                    if not zero_state:
                        P.pe(lambda e, P0=P0, bX=bX, c=c, p=p, Sbv=Sbv: e.matmul(PS(bX, 128, L)[:, p * 64:(p + 1) * 64],
                                                                       lhsT=KR[P0:P0 + 64, c * 2 * L:c * 2 * L + L], rhs=Sbv[P0:P0 + 64, :],
                                                                       start=True, stop=False), r=["KR", skb], w=[kb(bX)])
                    P.pe(lambda e, P0=P0, bX=bX, c=c, p=p, u=u, zs=zero_state, vtok_hp=vtok_hp: e.matmul(PS(bX, 128, L)[:, p * 64:(p + 1) * 64], lhsT=KQ4[:, u, 0, :],
                                                                                      rhs=vtok_hp[:, c, P0:P0 + 64], start=zs, stop=True),
                         r=kKQb + [vkey], w=[kb(bX)])
                P.act(lambda e, bX=bX: e.copy(out=XTs[0:L, :], in_=PS(bX, 128, L)), r=[kb(bX)], w=["XTs"])
                bS = bank()
                for p in range(2):
                    u = p * nch + c
                    P.pe(lambda e, bS=bS, p=p, u=u: e.matmul(PS(bS, 128, L)[:, p * 64:(p + 1) * 64], lhsT=Tbv[:, u * L:(u + 1) * L],
                                                            rhs=XTs[0:L, p * 64:(p + 1) * 64], start=True, stop=True), r=[kTb, "XTs"], w=[kb(bS)])
                P.act(lambda e, bS=bS: e.copy(out=SATs[0:L, :], in_=PS(bS, 128, L)), r=[kb(bS)], w=["SATs"])
                bY = bank()
                for p in range(2):
                    P0 = 64 * p
                    u = p * nch + c
                    if not zero_state:
                        P.pe(lambda e, P0=P0, bY=bY, c=c, p=p, Sbv=Sbv: e.matmul(PS(bY, 128, L)[:, p * 64:(p + 1) * 64],
                                                                       lhsT=KR[P0:P0 + 64, c * 2 * L + L:(c + 1) * 2 * L], rhs=Sbv[P0:P0 + 64, :],
                                                                       start=True, stop=False), r=["KR", skb], w=[kb(bY)])
                    P.pe(lambda e, P0=P0, bY=bY, c=c, p=p, u=u, zs=zero_state, vtok_hp=vtok_hp: e.matmul(PS(bY, 128, L)[:, p * 64:(p + 1) * 64], lhsT=KQ4[:, u, 1, :],
                                                                                      rhs=vtok_hp[:, c, P0:P0 + 64], start=zs, stop=False),
                         r=kKQb + [vkey], w=[kb(bY)])
                    P.pe(lambda e, bY=bY, p=p, u=u: e.matmul(PS(bY, 128, L)[:, p * 64:(p + 1) * 64], lhsT=NQ4[:, u, 1, :],
                                                            rhs=SATs[0:L, p * 64:(p + 1) * 64], start=False, stop=True),
                         r=kNQb + ["SATs"], w=[kb(bY)])
                P.dve(lambda e, bY=bY, ydst_ap=ydst_ap: e.tensor_copy(out=ydst_ap, in_=PS(bY, 128, L)),
                      r=[kb(bY)] + (["XA"] if kind == "s" else []), w=[ykey])
                bZ = bank()
                for p in range(2):
                    P0 = 64 * p
                    P.pe(lambda e, P0=P0, bZ=bZ, c=c, vtok_hp=vtok_hp, KgLtok3=KgLtok3: e.matmul(PS(bZ, 64, 64, P0), lhsT=KgLtok3[:, c, P0:P0 + 64], rhs=vtok_hp[:, c, P0:P0 + 64],
                                                               start=True, stop=False), r=["KgLtok", vkey, "XA"], w=[kb(bZ)])
                    P.pe(lambda e, P0=P0, bZ=bZ, c=c, p=p, BgLtok3=BgLtok3: e.matmul(PS(bZ, 64, 64, P0), lhsT=BgLtok3[:, c, P0:P0 + 64], rhs=SATs[0:L, p * 64:(p + 1) * 64],
                                                                   start=False, stop=True), r=kBg + ["SATs"], w=[kb(bZ)])
                if zero_state:
                    P.dve(lambda e, bZ=bZ, Smv=Smv: e.tensor_copy(out=Smv, in_=PS(bZ, 64)), r=[kb(bZ)], w=[skm])
                else:
                    P.dve(lambda e, bZ=bZ, Smv=Smv, c=c: e.scalar_tensor_tensor(out=Smv, in0=Smv, scalar=gLv[:, c:c + 1], in1=PS(bZ, 64),
                                                                                op0=ALU.mult, op1=ALU.add), r=[kb(bZ), skm, gk], w=[skm])
                P.act(lambda e, Smv=Smv, Sbv=Sbv: e.copy(out=Sbv, in_=Smv), r=[skm], w=[skb])
                pump()
            if kind == "s":
                for s_ in range(NSEQ_S):
                    P.dma("sp", "yrl", ytok[s_ * LS:(s_ + 1) * LS, 0, hp * 128:(hp + 1) * 128], ytok_s[0:LS, s_, :],
                          r=["ytok_s", "XA"] + HID_ALL, w=[("ytok", 0)])
                Sst = hid32[0:64, 2048:4096]
                for g in range(4):
                    bts = bank()
                    for s4 in range(4):
                        s_ = g * 4 + s4
                        P.pe(lambda e, bts=bts, s4=s4, s_=s_: e.transpose(out=PS(bts, 512, 64)[:, s4 * 128:(s4 + 1) * 128], in_=SmS[:, s_, :], identity=idf[:]),
                             r=[("SmS", s_)] + CONST, w=[kb(bts)])
                    P.act(lambda e, bts=bts, g=g, Sst=Sst: e.copy(out=Sst[:, g * 512:(g + 1) * 512], in_=PS(bts, 512, 64)), r=[kb(bts), "XA"], w=["Sstage"])
                for s_ in range(NSEQ_S):
                    P.dma("sp", "sst_out", wkvs[s_, 2 * hp:2 * hp + 2, :, :].rearrange("h i j -> i h j"),
                          Sst.rearrange("i (s h j) -> i s h j", s=NSEQ_S, h=2)[:, s_, :, :], r=["Sstage", "XA"] + HID_ALL, w=["o_wkvs"])


        nhp = HPS if kind == "s" else 4
        PIPE = (kind == "p")
        pstate = {"g": None, "done": True}

        def pump():
            if pstate["done"] or pstate["g"] is None:
                return
            try:
                v = next(pstate["g"])
            except StopIteration:
                pstate["done"] = True
                return
            if v == "p1done":
                pstate["done"] = True

        def run_until(g, tag):
            while True:
                try:
                    v = next(g)
                except StopIteration:
                    return False
                if v == tag:
                    return True

        gens = [hp_gen(h) for h in range(nhp)]
        alive = run_until(gens[0], "p2done")
        for h in range(nhp):
            if not alive:
                break
            nxt = gens[h + 1] if (h + 1 < nhp) else None
            if nxt is not None and PIPE:
                pstate["g"], pstate["done"] = nxt, False
            else:
                pstate["g"], pstate["done"] = None, True
            run_until(gens[h], "__end__")
            if nxt is not None:
                if PIPE and not pstate["done"]:
                    run_until(nxt, "p1done")
                pstate["done"] = True
                alive = run_until(nxt, "p2done")
        if kind == "s" and (STOP <= 4 or 30 <= STOP < 40):
            return
        bkk = inproj(("ak", 0), 128)
        bks = inproj(("aks", 0), 128)
        P.dve(lambda e: e.tensor_tensor(out=kf32[:, S_], in0=PS(bkk, NT), in1=cs_t[:, 0, S_], op=ALU.mult), r=[kb(bkk), "cs_t"], w=["t3"])
        P.dve(lambda e: e.tensor_tensor(out=dtmp[:, S_], in0=PS(bks, NT), in1=cs_t[:, 1, S_], op=ALU.mult), r=[kb(bks), "cs_t"], w=["dtmp"])
        P.dve(lambda e: e.tensor_tensor(out=kf32[:, S_], in0=kf32[:, S_], in1=dtmp[:, S_], op=ALU.add), r=["t3", "dtmp"], w=["t3"])
        P.act(lambda e: e.copy(out=kbuf[:, 128:128 + NT], in_=kf32[:, S_]), r=["t3"], w=["kbuf"])
        bv = inproj(("av", 0), 128)
        P.act(lambda e: e.copy(out=vT32[:, S_], in_=PS(bv, NT)), r=[kb(bv)], w=["t4"])
        if first or kind == "s":
            P.pool(lambda e: e.memset(Vaug[:], 1.0), w=["Vaug"])
        for b in range(NB):
            bt = bank()
            P.pe(lambda e, b=b, bt=bt: e.transpose(out=PS(bt, 128), in_=vT32[:, b * 128:(b + 1) * 128], identity=idf[:]),
                 r=["t4"] + CONST, w=[kb(bt)])
            P.act(lambda e, b=b, bt=bt: e.copy(out=Vaug[:, b + 1, :, 0:64], in_=PS(bt, 128).rearrange("p (k d) -> p k d", k=2)),
                  r=[kb(bt)], w=["Vaug"])
            if last and b == NB - 1:
                P.dve(lambda e, bt=bt: e.tensor_copy(out=vtokf[:], in_=PS(bt, 128)), r=[kb(bt)], w=["vtokf"])
                if kind == "p":
                    P.dma("pool", "o_vw", vwp, vtokf[:], r=["vtokf"], w=["o_vwp"])
                else:
                    for s in range(NSEQ_S):
                        P.dma("sp", "o_vws", vws[s, 120:128, :], vtokf[s * LS:(s + 1) * LS, :], r=["vtokf"], w=["o_vws"])
                    P.dma("sp", "o_vw2", vws[:, 0:120, :].rearrange("s r c -> s (r c)"), cv[:, 8:128, :].rearrange("s r c -> s (r c)"), w=["o_vws2"])
                bt2 = bank()
                P.pe(lambda e, b=b, bt2=bt2: e.transpose(out=PS(bt2, 128), in_=kf32[:, b * 128:(b + 1) * 128], identity=idf[:]),
                     r=["t3"] + CONST, w=[kb(bt2)])
                P.dve(lambda e, bt2=bt2: e.tensor_copy(out=yq[:, 0:128], in_=PS(bt2, 128)), r=[kb(bt2)], w=["t0"])
                if kind == "p":
                    P.dma("pool", "o_kw", kwp, yq[:, 0:128], r=["t0"], w=["o_kwp"])
                else:
                    for s in range(NSEQ_S):
                        P.dma("sp", "o_kws", kws[s, 120:128, :], yq[s * LS:(s + 1) * LS, 0:128], r=["t0"], w=["o_kws"])
                    P.dma("sp", "o_kw2", kws[:, 0:120, :].rearrange("s r c -> s (r c)"), ck[:, 8:128, :].rearrange("s r c -> s (r c)"), w=["o_kws2"])
        for c in range(4):
            bq_ = inproj(("q", c), 128)
            bqs = inproj(("qs", c), 128)
            P.dve(lambda e, bq_=bq_: e.tensor_tensor(out=yq[:, S_], in0=PS(bq_, NT), in1=cs_t[:, 0, S_], op=ALU.mult),
                  r=[kb(bq_), "cs_t", "t0"], w=["t0"])
            P.dve(lambda e, bqs=bqs: e.tensor_tensor(out=yq2[:, S_], in0=PS(bqs, NT), in1=cs_t[:, 1, S_], op=ALU.mult),
                  r=[kb(bqs), "cs_t"], w=["t1"])
            P.pool(lambda e, c=c: e.tensor_tensor(out=qT[:, c, S_], in0=yq[:, S_], in1=yq2[:, S_], op=ALU.add),
                   r=["t0", "t1"], w=[("qT", c)])
        qkeys = [("qT", c) for c in range(4)]

        def epilogue(b):
            y3 = ytok[:, b, :].rearrange("p (h d) -> p h d", d=64)
            yk = ("ytok", b)
            gs = gstat[:, 0, :]
            P.dve(lambda e, y3=y3: e.tensor_reduce(out=gstat[:, 0, :], in_=y3, axis=AX.X, op=ALU.add), r=[yk], w=["gstat"])
            P.act(lambda e, b=b: e.activation(out=yq[:, :], in_=ytok[:, b, :], func=AF.Square), r=[yk, "t0"], w=["t0"])
            P.dve(lambda e: e.tensor_reduce(out=gstat[:, 1, :], in_=yq[:, :].rearrange("p (h d) -> p h d", d=64), axis=AX.X, op=ALU.add),
                  r=["t0"], w=["gstat"])
            P.dve(lambda e: e.tensor_scalar(out=gstat[:, 0, :], in0=gstat[:, 0, :], scalar1=1.0 / 64, scalar2=None, op0=ALU.mult),
                  r=["gstat"], w=["gstat"])
            P.dve(lambda e: e.tensor_tensor(out=gstat[:, 2, :], in0=gstat[:, 0, :], in1=gstat[:, 0, :], op=ALU.mult), r=["gstat"], w=["gstat"])
            P.dve(lambda e: e.scalar_tensor_tensor(out=gstat[:, 1, :], in0=gstat[:, 1, :], scalar=1.0 / 64, in1=gstat[:, 2, :],
                                                   op0=ALU.mult, op1=ALU.subtract), r=["gstat"], w=["gstat"])
            P.act(lambda e: e.activation(out=gstat[:, 1, :], in_=gstat[:, 1, :], func=AF.Ln, bias=GN_EPS), r=["gstat"], w=["gstat"])
            P.act(lambda e: e.activation(out=gstat[:, 1, :], in_=gstat[:, 1, :], func=AF.Exp, scale=-0.5), r=["gstat"], w=["gstat"])
            for h in range(8):
                P.dve(lambda e, h=h, y3=y3: e.tensor_scalar(out=yq2[:, h * 64:(h + 1) * 64], in0=y3[:, h, :], scalar1=gstat[:, 0, h:h + 1],
                                                            scalar2=gstat[:, 1, h:h + 1], op0=ALU.subtract, op1=ALU.mult),
                      r=[yk, "gstat", "t1"], w=["t1"])
            P.dve(lambda e: e.tensor_tensor(out=yq2[:, :], in0=yq2[:, :], in1=gnw_bc[:], op=ALU.mult), r=["t1"] + CONST, w=["t1"])
            P.dve(lambda e: e.tensor_tensor(out=yq2[:, :], in0=yq2[:, :], in1=gnb_bc[:], op=ALU.add), r=["t1"] + CONST, w=["t1"])
            for hp in range(4):
                if kind == "p":
                    vt_blk = vtok[:, (hp * nch + b) * 128:(hp * nch + b + 1) * 128]
                    vkeys = [("vtok", hp)]
                else:
                    vt_blk = None
                    vkeys = []
                for p in range(2):
                    h = 2 * hp + p
                    if kind == "p":
                        P.dve(lambda e, h=h, p=p, vt_blk=vt_blk, b=b: e.scalar_tensor_tensor(
                            out=yq2[:, h * 64:(h + 1) * 64], in0=vt_blk[:, p * 64:(p + 1) * 64], scalar=rkb[:, b, h:h + 1],
                            in1=yq2[:, h * 64:(h + 1) * 64], op0=ALU.mult, op1=ALU.add),
                            r=vkeys + [("rkb", hp), "t1"], w=["t1"])
                    else:
                        P.dve(lambda e, h=h, b=b: e.scalar_tensor_tensor(
                            out=yq2[:, h * 64:(h + 1) * 64], in0=vblk_s[:, h * 64:(h + 1) * 64], scalar=rkb[:, b, h:h + 1],
                            in1=yq2[:, h * 64:(h + 1) * 64], op0=ALU.mult, op1=ALU.add),
                            r=["vblk_s", ("rkb", hp), "t1"], w=["t1"])
            bg = bank()
            P.pe(lambda e, b=b, bg=bg: e.matmul(PS(bg, 512), lhsT=sgb[:, b * 128:(b + 1) * 128], rhs=Wg_b[:], start=True, stop=True),
                 r=["sgb", "cconst"], w=[kb(bg)])
            P.dve(lambda e, b=b, bg=bg: e.tensor_tensor(out=mix[:, b, 512:1024], in0=yq2[:, :], in1=PS(bg, 512), op=ALU.mult),
                  r=["t1", kb(bg)], w=[("mixr", b)])


        def attn_group(b, kblocks, first_grp, only_grp, additive=False):
            ng = len(kblocks)
            for kv in range(2):
                P0 = 64 * kv
                for i, (kfn, vfn, msk, kkeys) in enumerate(kblocks):
                    bs = bank()
                    P.pe(lambda e, P0=P0, bs=bs, kfn=kfn, kv=kv: e.matmul(PS(bs, 512), lhsT=kfn(kv),
                                                                   rhs=qT[P0:P0 + 64, :, b * 128:(b + 1) * 128], start=True, stop=not additive),
                         r=qkeys + kkeys, w=[kb(bs)])
                    if additive:
                        P.pe(lambda e, bs=bs, msk=msk: e.matmul(PS(bs, 512), lhsT=idb[:], rhs=msk.rearrange("p a b -> p (a b)"),
                                                                start=False, stop=True), r=["idb", "cconst"], w=[kb(bs)])
                    Ei = Et[:, kv * 2 + i, :]
                    ek = ("Et", kv * 2 + i)
                    P.act(lambda e, bs=bs, Ei=Ei: e.activation(out=Ei, in_=PS(bs, 512), func=AF.Exp, scale=0.125), r=[kb(bs)], w=[ek])
                    if not additive:
                      P.pool(lambda e, Ei=Ei, msk=msk: e.tensor_tensor(out=Ei.rearrange("p (a b) -> p a b", a=4), in0=Ei.rearrange("p (a b) -> p a b", a=4), in1=msk, op=ALU.mult),
                           r=[ek, "cconst", "scconst"], w=[ek])
            bo = pair()
            for kv in range(2):
                for c4 in range(4):
                    for i, (kfn, vfn, msk, kkeys) in enumerate(kblocks):
                        P.pe(lambda e, kv=kv, c4=c4, i=i, vfn=vfn, bo=bo: e.matmul(
                            PS(bo + kv, 260)[:, c4 * 65:(c4 + 1) * 65], lhsT=Et[:, kv * 2 + i, c4 * 128:(c4 + 1) * 128], rhs=vfn(kv),
                            start=(i == 0), stop=(i == ng - 1)), r=[("Et", kv * 2 + i)] + kkeys, w=[kb(bo + kv)])
            return bo

        def attn_finish(b, src_fn, rkeys):
            for kv in range(2):
                s3 = src_fn(kv)
                P.dve(lambda e, kv=kv, s3=s3: e.tensor_tensor(out=den[:, kv * 4:(kv + 1) * 4].unsqueeze(2), in0=s3[:, :, 64:65],
                                                              in1=esink[:, kv * 4:(kv + 1) * 4].unsqueeze(2), op=ALU.add),
                      r=rkeys + ["esink"], w=[("den", kv)])
                P.dve(lambda e, kv=kv: e.reciprocal(out=den[:, kv * 4:(kv + 1) * 4], in_=den[:, kv * 4:(kv + 1) * 4]),
                      r=[("den", kv)], w=[("den", kv)])
                for c4 in range(4):
                    h = kv * 4 + c4
                    P.dve(lambda e, s3=s3, c4=c4, h=h: e.tensor_scalar(out=mix[:, b, h * 64:(h + 1) * 64], in0=s3[:, c4, 0:64],
                                                                       scalar1=den[:, h:h + 1], scalar2=None, op0=ALU.mult),
                          r=rkeys + [("den", kv)], w=[("mixa", b)])

        if kind == "p":
            for b in range(NB):
                gb = ti * NB + b
                kbl = []
                if gb > 0:
                    kbl.append((lambda kv, b=b: kbuf[64 * kv:64 * kv + 64, b * 128:(b + 1) * 128],
                                lambda kv, b=b: Vaug[:, b, kv, :], m_prev[:], ["kbuf", "Vaug"]))
                kbl.append((lambda kv, b=b: kbuf[64 * kv:64 * kv + 64, (b + 1) * 128:(b + 2) * 128],
                            lambda kv, b=b: Vaug[:, b + 1, kv, :], m_own[:], ["kbuf", "Vaug"]))
                bo = attn_group(b, kbl, True, True, additive=True)
                epilogue(b)
                attn_finish(b, lambda kv, bo=bo: PS(bo + kv, 260).rearrange("p (c d) -> p c d", d=65), [kb(bo), kb(bo + 1)])
            P.act(lambda e: e.copy(out=kbuf[:, 0:128], in_=kbuf[:, NB * 128:(NB + 1) * 128]), r=["kbuf"], w=["kbuf"])
            P.pool(lambda e: e.tensor_copy(out=Vaug[:, 0, :, :], in_=Vaug[:, NB, :, :]), r=["Vaug"], w=["Vaug"])
        else:
            kbl = [(lambda kv: kbuf[64 * kv:64 * kv + 64, 128:256], lambda kv: Vaug[:, 1, kv, :], m_sown[:], ["kbuf", "Vaug"])]
            bo = attn_group(0, kbl, True, False)
            for kv in range(2):
                P.dve(lambda e, kv=kv, bo=bo: e.tensor_copy(out=oacc[:, kv, :, :].rearrange("p c d -> p (c d)"), in_=PS(bo + kv, 260)),
                      r=[kb(bo + kv)], w=[("oacc", kv)])
            for s in range(NSEQ_S):
                i = s % 2
                P.dma("pool", "ck%d" % i, ckf[:, i, :], ck[s], w=[("ckf", i)])
                btk = bank()
                P.pe(lambda e, i=i, btk=btk: e.transpose(out=PS(btk, 128), in_=ckf[:, i, :], identity=idf[:]), r=[("ckf", i)] + CONST, w=[kb(btk)])
                P.act(lambda e, btk=btk: e.copy(out=ckT[:], in_=PS(btk, 128)), r=[kb(btk)], w=["ckT"])
                P.dma("pool", "cv%d" % i, ckf[:, i, :], cv[s], r=[], w=[("ckf", i)])
                P.pool(lambda e, i=i: e.memset(cVaug[:, i, :, 64:65], 1.0), w=[("cVaug", i)])
                P.act(lambda e, i=i: e.copy(out=cVaug[:, i, :, 0:64], in_=ckf[:, i, :].rearrange("p (k d) -> p k d", k=2)),
                      r=[("ckf", i)], w=[("cVaug", i)])
                kbl = [(lambda kv: ckT[64 * kv:64 * kv + 64, :], lambda kv, i=i: cVaug[:, i, kv, :],
                        m_scache[:, s:s + 1, :].to_broadcast([128, 4, 128]), ["ckT", ("cVaug", i)])]
                bo = attn_group(0, kbl, False, False)
                for kv in range(2):
                    P.dve(lambda e, kv=kv, bo=bo: e.tensor_tensor(out=oacc[:, kv, :, :].rearrange("p c d -> p (c d)"),
                                                                  in0=oacc[:, kv, :, :].rearrange("p c d -> p (c d)"),
                                                                  in1=PS(bo + kv, 260), op=ALU.add),
                          r=[kb(bo + kv), ("oacc", kv)], w=[("oacc", kv)])
            attn_finish(0, lambda kv: oacc[:, kv, :, :], [("oacc", 0), ("oacc", 1)])
            epilogue(0)

        if kind == "s" and STOP <= 5:
            return
        if kind == "s" and STOP <= 6:
            return
        for b in range(NB):
            bk = bank()
            for c in range(8):
                P.pe(lambda e, c=c, bk=bk, b=b: e.transpose(out=PSB(bk)[:, c * 128:(c + 1) * 128], in_=mix[:, b, c * 128:(c + 1) * 128],
                                                            identity=idb[:]), r=[("mixa", b), ("mixr", b), "idb"], w=[kb(bk)])
            P.act(lambda e, b=b, bk=bk: e.copy(out=bufA[:, :, b * 128:(b + 1) * 128], in_=PSB(bk).rearrange("p (c n) -> p c n", c=8)),
                  r=[kb(bk)], w=[("bufA", b)])
        nb[0] = 0
        for c in range(8):
            s = stream(wsc_out[c], ("wsc_out", c))
            for b in range(NB):
                for hf in range(2):
                    P.pe(lambda e, c=c, s=s, b=b, hf=hf: e.matmul(PS(2 * b + hf, 512), lhsT=bufA[:, c, b * 128:(b + 1) * 128],
                                                                 rhs=ring[:, s, hf * 512:(hf + 1) * 512], start=(c == 0), stop=(c == 7)),
                         r=[("ring", s), ("bufA", b)], w=[kb(2 * b + hf)])

        def ost(b):
            return hid32[:, 1536 + b * 1024:1536 + (b + 1) * 1024]

        def ostk(b):
            return [("hidT", fc_) for fc_ in range(6 + 4 * b, 10 + 4 * b)]

        def norm_res(b, src_pair, gbc, dst, dkey, xkey_r):
            sc = sstat[:, 4 + b:5 + b]
            pk = [kb(2 * b), kb(2 * b + 1)]
            P.act(lambda e: e.activation(out=junk[:], in_=src_pair, func=AF.Square, accum_out=sc), r=pk, w=["junk", ("ss2", b)])
            P.act(lambda e: e.activation(out=sc, in_=sc, func=AF.Ln, scale=1.0 / D, bias=RMS_EPS), r=[("ss2", b)], w=[("ss2", b)])
            P.act(lambda e: e.activation(out=sc, in_=sc, func=AF.Exp, scale=-0.5), r=[("ss2", b)], w=[("ss2", b)])
            tmp = ost(b)
            P.dve(lambda e: e.scalar_tensor_tensor(out=tmp, in0=src_pair, scalar=sc, in1=gbc[:], op0=ALU.mult, op1=ALU.mult),
                  r=pk + [("ss2", b)] + CONST, w=ostk(b))
            P.pool(lambda e: e.tensor_tensor(out=dst, in0=tmp, in1=x_t[:, b, :], op=ALU.add), r=ostk(b) + [xkey_r], w=dkey)

        for b in range(NB):
            norm_res(b, PSP(2 * b), gpost_bc, x_t[:, b, :], [("x", b)], ("x", b))
        for b in range(NB):
            sc = sstat[:, 8 + b:9 + b]
            P.act(lambda e, b=b, sc=sc: e.activation(out=junk[:], in_=x_t[:, b, :], func=AF.Square, accum_out=sc),
                  r=[("x", b)], w=["junk", ("ss3", b)])
            P.act(lambda e, sc=sc: e.activation(out=sc, in_=sc, func=AF.Ln, scale=1.0 / D, bias=RMS_EPS), r=[("ss3", b)], w=[("ss3", b)])
            P.act(lambda e, sc=sc: e.activation(out=sc, in_=sc, func=AF.Exp, scale=-0.5), r=[("ss3", b)], w=[("ss3", b)])
        for b in range(NB):
            sc = sstat[:, 8 + b:9 + b]
            P.dve(lambda e, b=b, sc=sc: e.tensor_scalar(out=xn[:], in0=x_t[:, b, :], scalar1=sc, scalar2=None, op0=ALU.mult),
                  r=[("x", b), ("ss3", b)], w=["xn"])
            bk = (2 * b) % 8
            for c in range(8):
                P.pe(lambda e, c=c, bk=bk: e.transpose(out=PSB(bk)[:, c * 128:(c + 1) * 128], in_=xn[:, c * 128:(c + 1) * 128],
                                                      identity=idb[:]), r=["xn", "idb"], w=[kb(bk)])
            P.act(lambda e, b=b, bk=bk: e.copy(out=bufA[:, :, b * 128:(b + 1) * 128], in_=PSB(bk).rearrange("p (c n) -> p c n", c=8)),
                  r=[kb(bk)], w=[("bufA", b)])
        nb[0] = 0

        if kind == "s" and STOP <= 7:
            return
        for fc in range(NFC):
            sz = stream(wsc_fin[2 * fc], ("wsc_fin", 2 * fc))
            su = stream(wsc_fin[2 * fc + 1], ("wsc_fin", 2 * fc + 1))
            bz = bank()
            bu = bank()
            for (s, bk_) in ((sz, bz), (su, bu)):
                for c in range(8):
                    P.pe(lambda e, c=c, s=s, bk_=bk_: e.matmul(PS(bk_, NT), lhsT=ring[:, s, c * 128:(c + 1) * 128], rhs=bufA[:, c, S_],
                                                               start=(c == 0), stop=(c == 7)), r=[("ring", s)] + bufA_keys, w=[kb(bk_)])
            i = 0
            zb = zbuf[:, i, 0:nseq * (Lseq + 2)].rearrange("p (s l) -> p s l", s=nseq)
            zk = ("zbuf", i)
            P.act(lambda e, bz=bz, zb=zb: e.copy(out=zb[:, :, 2:Lseq + 2], in_=PS(bz, NT).rearrange("p (s l) -> p s l", s=nseq)),
                  r=[kb(bz)], w=[zk])
            P.pool(lambda e, zb=zb, fc=fc: e.tensor_copy(out=zb[:, :, 0:2], in_=zcar[:, fc, :, :]), r=[("zcar", fc)], w=[zk])
            P.pool(lambda e, zb=zb, fc=fc: e.tensor_copy(out=zcar[:, fc, :, :], in_=zb[:, :, Lseq:Lseq + 2]), r=[zk], w=[("zcar", fc)])
            a3 = za[:, i, S_].rearrange("p (s l) -> p s l", s=nseq)
            ak = ("za", i)
            P.act(lambda e, bz=bz, a3=a3, fc=fc: e.activation(out=a3, in_=PS(bz, NT).rearrange("p (s l) -> p s l", s=nseq), func=AF.Identity,
                                                              scale=cwT[:, 2, fc:fc + 1], bias=cbT[:, fc:fc + 1]),
                  r=[kb(bz)] + CONST, w=[ak])
            P.dve(lambda e, zb=zb, a3=a3, fc=fc: e.scalar_tensor_tensor(out=a3, in0=zb[:, :, 1:Lseq + 1], scalar=cwT[:, 1, fc:fc + 1], in1=a3,
                                                                        op0=ALU.mult, op1=ALU.add), r=[zk, ak], w=[ak])
            P.dve(lambda e, zb=zb, a3=a3, fc=fc: e.scalar_tensor_tensor(out=a3, in0=zb[:, :, 0:Lseq], scalar=cwT[:, 0, fc:fc + 1], in1=a3,
                                                                        op0=ALU.mult, op1=ALU.add), r=[zk, ak], w=[ak])
            P.act(lambda e, i=i: e.activation(out=za[:, i, S_], in_=za[:, i, S_], func=AF.Silu), r=[ak], w=[ak])
            P.dve(lambda e, i=i, bu=bu, fc=fc: e.tensor_tensor(out=hidT[:, fc, S_], in0=za[:, i, S_], in1=PS(bu, NT), op=ALU.mult),
                  r=[ak, kb(bu)], w=[("hidT", fc)])
        nb[0] = 0
        for fc in range(NFC):
            s = stream(wsc_fout[fc], ("wsc_fout", fc))
            for b in range(NB):
                for hf in range(2):
                    P.pe(lambda e, fc=fc, s=s, b=b, hf=hf: e.matmul(PS(2 * b + hf, 512), lhsT=hidT[:, fc, b * 128:(b + 1) * 128],
                                                                   rhs=ring[:, s, hf * 512:(hf + 1) * 512], start=(fc == 0), stop=(fc == NFC - 1)),
                         r=[("ring", s), ("hidT", fc)], w=[kb(2 * b + hf)])
        for b in range(NB):
            oi = 0
            state["ost"] += 1
            norm_res(b, PSP(2 * b), gpostf_bc, ost(b), ostk(b), ("x", b))
        for b in range(NB):
            P.dma("pool", "oy%d" % b, ydst[t0 + b * 128:t0 + (b + 1) * 128, :] if kind == "p" else ydst, ost(b),
                  r=ostk(b), w=["o_y"])
        nb[0] = 0

        if last:
            emit_state_outputs(kind, nseq)

    def emit_state_outputs(kind, nseq):
        hcar = hcar_p if kind == "p" else hcar_s
        zcar = zcar_p if kind == "p" else zcar_s
        for rc in range(14):
            np_ = RC_NP.get(rc, 128)
            bt = bank()
            P.pe(lambda e, rc=rc, np_=np_, bt=bt: e.transpose(out=PS(bt, np_, nseq), in_=hcar[0:np_, rc, :], identity=idf[0:np_, 0:np_]),
                 r=[("hcar", rc)] + CONST, w=[kb(bt)])
            c0 = rc * 128 if rc < 13 else 1600
            P.act(lambda e, bt=bt, np_=np_, c0=c0: e.copy(out=rowbuf[0:nseq, c0:c0 + np_], in_=PS(bt, np_, nseq)), r=[kb(bt)], w=["rowbuf", "XA"] + (HID_ALL if rc == 0 else []))
        P.dma("pool", "o_sh", shp if kind == "p" else shs, rowbuf[0:nseq, 0:DSH], r=["rowbuf"] + HID_ALL, w=["o_sh" + kind, "XA"])
        for fc in range(NFC):
            bt = bank()
            P.pe(lambda e, fc=fc, bt=bt: e.transpose(out=PS(bt, 128, 2 * nseq), in_=zcar[:, fc, :, :].rearrange("p s j -> p (s j)"),
                                                     identity=idf[:]), r=[("zcar", fc)] + CONST, w=[kb(bt)])
            P.act(lambda e, fc=fc, bt=bt: e.copy(out=sst[0:2 * nseq, fc * 128:(fc + 1) * 128], in_=PS(bt, 128, 2 * nseq)), r=[kb(bt)], w=["sst", "XA"] + (HID_ALL if fc == 0 else []))
        P.dma("pool", "o_cv", convp if kind == "p" else convs, sst[0:2 * nseq, :], r=["sst"] + HID_ALL, w=["o_cv" + kind, "XA"])
        if kind == "p":
            for hp in range(4):
                bt = bank()
                P.pe(lambda e, hp=hp, bt=bt: e.transpose(out=PS(bt, 128, 64), in_=Sm[:, hp, :], identity=idf[:]),
                     r=[("Sm", hp)] + CONST, w=[kb(bt)])
                i = hp % 2
                P.act(lambda e, bt=bt, i=i: e.copy(out=Sld[:, i, :], in_=PS(bt, 128, 64)), r=[kb(bt)], w=["t5"])
                P.dma("pool", "o_wk%d" % i, wkvp[2 * hp:2 * hp + 2].rearrange("h i j -> i h j"),
                      Sld[:, i, :].rearrange("p (h j) -> p h j", h=2), r=["t5"], w=["o_wkvp"])

    SmS = T("SmS", [128, NSEQ_S, 64])
    SbS = T("SbS", [128, NSEQ_S, 64], BF16)

    P.pool(lambda e: e.memset(hcar_p[:], 0.0), w=[("hcar", rc) for rc in range(14)])
    P.pool(lambda e: e.memset(zcar_p[:], 0.0), w=[("zcar", fc) for fc in range(NFC)])
    P.pool(lambda e: e.memset(kbuf[:], 0.0), w=["kbuf"])

    for ti in range(N_TILES):
        emit_tile("p", ti)

    if DO_SAMPLE:
        P.dma("sp", "clss", scanm_s[:], c_scanm_s, w=["scconst"])
        cast_const(m_sown[:], c_mask_sown, 128, 128, bcast4=True)
        for g in range(4):
            cast_const(m_scache[:, 4 * g:4 * g + 4, :].rearrange("p a b -> p (a b)"), c_mask_scache[:, 4 * g:4 * g + 4, :].rearrange("p a b -> p (a b)"), 128, 512)
        P.ops["dve"][-1].deps.add(P.last_w["scconst"])
        P.dma("pool", "hst", rowbuf[0:NSEQ_S, :], sshift, r=[], w=["rowbuf", "XA"] + HID_ALL)
        for rc in range(14):
            np_ = RC_NP.get(rc, 128)
            c0 = rc * 128 if rc < 13 else 1600
            bt = bank()
            P.pe(lambda e, bt=bt, np_=np_, c0=c0: e.transpose(out=PS(bt, NSEQ_S, np_), in_=rowbuf[0:NSEQ_S, c0:c0 + np_], identity=idf[0:NSEQ_S, 0:NSEQ_S]),
                 r=["rowbuf"] + CONST, w=[kb(bt), "XA"])
            P.act(lambda e, bt=bt, np_=np_, rc=rc: e.copy(out=hcar_s[0:np_, rc, :], in_=PS(bt, NSEQ_S, np_)), r=[kb(bt)], w=[("hcar", rc)])
        P.dma("pool", "hst2", sst[0:2 * NSEQ_S, :], sconv, r=[], w=["sst", "XA"] + HID_ALL)
        for fc in range(NFC):
            bt = bank()
            P.pe(lambda e, bt=bt, fc=fc: e.transpose(out=PS(bt, 2 * NSEQ_S, 128), in_=sst[0:2 * NSEQ_S, fc * 128:(fc + 1) * 128],
                                                     identity=idf[0:2 * NSEQ_S, 0:2 * NSEQ_S]), r=["sst"] + CONST, w=[kb(bt), "XA"])
            P.act(lambda e, bt=bt, fc=fc: e.copy(out=zcar_s[:, fc, :, :].rearrange("p s j -> p (s j)"), in_=PS(bt, 2 * NSEQ_S, 128)),
                  r=[kb(bt)], w=[("zcar", fc)])
        emit_tile("s", 0)

    okeys = [k for k in P.last_w.keys() if isinstance(k, str) and k.startswith("o_")]
    P.add("sp", lambda e: None, reads=okeys)
    P.add("pool", lambda e: None, reads=okeys)
    P.emit()
    return nc, st


def _consts():
    c = {}
    c["c_ident"] = np.eye(128, dtype=np.float32)
    half = 32
    inv = (np.float32(10000.0) ** (-np.arange(half, dtype=np.float32) / np.float32(half))).astype(np.float32)
    p = np.arange(128)
    f = (p % 64) % 32
    sign = np.where((p % 64) < 32, -1.0, 1.0).astype(np.float32)

    def tabs(pos):
        ang = pos.astype(np.float32)[None, :] * inv[f][:, None]
        return np.cos(ang).astype(np.float32), (np.sin(ang).astype(np.float32) * sign[:, None]).astype(np.float32)

    c["c_cos_p"], c["c_sin_p"] = tabs(np.arange(SEQ))
    pos_s = 16384 + (np.arange(128) % LS)
    c["c_cos_s"], c["c_sin_s"] = tabs(pos_s)
    s = np.arange(128)[:, None]
    q = np.arange(128)[None, :]
    c["c_mask_own"] = (s <= q).astype(np.float32)
    c["c_mask_prev"] = (s >= q).astype(np.float32)
    c["c_negmask_own"] = np.where(s <= q, 0.0, -30000.0).astype(np.float32)
    c["c_negmask_prev"] = np.where(s >= q, 0.0, -30000.0).astype(np.float32)
    c["c_mask_sown"] = ((s // LS == q // LS) & (s <= q)).astype(np.float32)
    msc = np.zeros((128, NSEQ_S, 128), np.float32)
    for sq in range(NSEQ_S):
        msc[:, sq, :] = ((q // LS == sq) & (s >= (q % LS))).astype(np.float32)
    c["c_mask_scache"] = msc
    bo = np.zeros((128, 128), np.float32)
    bo[:64, :64] = 1
    bo[64:, 64:] = 1
    c["c_blockones"] = bo
    strict = (s < q).astype(np.float32)
    incl = (s <= q).astype(np.float32)
    c["c_m2b"] = np.stack([strict, -incl], axis=1).astype(np.float32)
    c["c_m2k"] = np.stack([strict, incl], axis=1).astype(np.float32)
    c["c_mT"] = (q < s).astype(np.float32)
    mp = np.ones((128, 512), np.float32)
    mp[:, 0::128] = 0
    c["c_scanm_p"] = mp
    ms = np.ones((128, 128), np.float32)
    ms[:, 0::LS] = 0
    c["c_scanm_s"] = ms
    return c


_CACHE = {}


def kernel(**inputs):
    f = lambda a: np.ascontiguousarray(np.asarray(a, dtype=np.float32))
    if "nc" not in _CACHE:
        _CACHE["nc"] = build_program()
        _CACHE["consts"] = _consts()
    nc, _st = _CACHE["nc"]
    consts = _CACHE["consts"]
    x_prompt = f(inputs["x_prompt"])
    x_sample = f(inputs["x_sample"])
    wnames = ["g_pre_mix", "w_in", "attn_sinks", "mu_shift", "w0", "w_decay_up", "a0", "w_a_up", "w_g_up", "k_k", "k_a", "r_k",
              "gn_w", "gn_b", "w_out", "g_post_mix", "g_pre_ffn", "w_ffn_in", "conv_w", "conv_b", "w_ffn_out", "g_post_ffn"]
    shared = {}
    for n in wnames:
        a = f(inputs[n])[0]
        if n == "r_k":
            a = a.reshape(512)
        shared[n] = np.ascontiguousarray(a)
    shared.update(consts)
    in_maps = []
    for c in range(8):
        m = dict(shared)
        m["xp"] = x_prompt[c % 4]
        sl = slice(c * NSEQ_S, (c + 1) * NSEQ_S)
        m["xs"] = np.ascontiguousarray(x_sample[sl].reshape(128, D))
        m["ck"] = np.ascontiguousarray(f(inputs["cache_k_win"])[0, sl].reshape(NSEQ_S, 128, 128))
        m["cv"] = np.ascontiguousarray(f(inputs["cache_v_win"])[0, sl].reshape(NSEQ_S, 128, 128))
        m["sshift"] = np.ascontiguousarray(f(inputs["state_shift"])[0, sl])
        m["swkv"] = np.ascontiguousarray(f(inputs["state_wkv"])[0, sl])
        m["sconv"] = np.ascontiguousarray(f(inputs["state_conv"])[0, sl].reshape(NSEQ_S * 2, DFF))
        in_maps.append(m)
    ncores = int(os.environ.get("MK_CORES", "8"))
    res = run_bass_kernel_spmd(nc, in_maps[:ncores], core_ids=list(range(ncores)))
    R = list(res.results) + [res.results[0]] * (8 - ncores)
    cat = lambda k, rng: np.stack([R[c][k] for c in rng], axis=0)
    y_prompt = cat("yp", range(4)).reshape(4, SEQ, D)
    y_sample = np.concatenate([R[c]["ys"].reshape(NSEQ_S, LS, D) for c in range(8)], axis=0)
    nkp = cat("kwp", range(4)).reshape(1, 4, 128, 2, 64)
    nvp = cat("vwp", range(4)).reshape(1, 4, 128, 2, 64)
    nsp = cat("shp", range(4)).reshape(1, 4, DSH)
    nwp = cat("wkvp", range(4)).reshape(1, 4, 8, 64, 64)
    ncp = cat("convp", range(4)).reshape(1, 4, 2, DFF)
    nks = np.concatenate([R[c]["kws"] for c in range(8)], axis=0).reshape(1, 128, 128, 2, 64)
    nvs = np.concatenate([R[c]["vws"] for c in range(8)], axis=0).reshape(1, 128, 128, 2, 64)
    nss = np.concatenate([R[c]["shs"] for c in range(8)], axis=0).reshape(1, 128, DSH)
    nws = np.concatenate([R[c]["wkvs"] for c in range(8)], axis=0).reshape(1, 128, 8, 64, 64)
    ncs = np.concatenate([R[c]["convs"].reshape(NSEQ_S, 2, DFF) for c in range(8)], axis=0).reshape(1, 128, 2, DFF)
    outs = (y_prompt, y_sample, nkp, nvp, nsp, nwp, ncp, nks, nvs, nss, nws, ncs)
    return tuple(np.ascontiguousarray(o.astype(np.float32)) for o in outs)
```

```python
import os
import contextlib
import numpy as np
import concourse.bass as bass
import concourse.mybir as mybir
from concourse.bass_utils import run_bass_kernel_spmd

F32 = mybir.dt.float32
BF16 = mybir.dt.bfloat16
ALU = mybir.AluOpType
AF = mybir.ActivationFunctionType
AX = mybir.AxisListType

ENGS = ("pe", "act", "dve", "pool", "sp")


class Op:
    __slots__ = ("eng", "fn", "deps", "dma", "signal", "sigval", "waits", "has_dependents")

    def __init__(self, eng, fn, dma):
        self.eng = eng
        self.fn = fn
        self.dma = dma
        self.deps = set()
        self.signal = False
        self.sigval = 0
        self.waits = []
        self.has_dependents = False


class Prog:
    def __init__(self, nc, same_engine_sync=True):
        self.nc = nc
        self.ops = {e: [] for e in ENGS}
        self.last_w = {}
        self.readers = {}
        self.same_engine_sync = same_engine_sync
        self.nops = 0

    def add(self, eng, fn, reads=(), writes=(), dma=None):
        op = Op(eng, fn, dma)
        deps = op.deps
        for k in reads:
            w = self.last_w.get(k)
            if w is not None:
                deps.add(w)
            if isinstance(k, tuple) and k[0] == "ps":
                for r in self.readers.get(k, ()):
                    if r.eng != eng:
                        deps.add(r)
        for k in writes:
            w = self.last_w.get(k)
            if w is not None:
                deps.add(w)
            for r in self.readers.get(k, ()):
                deps.add(r)
        deps.discard(op)
        for k in reads:
            lst = self.readers.setdefault(k, [])
            if dma is None:
                for i_, r_ in enumerate(lst):
                    if r_.dma is None and r_.eng == eng:
                        lst[i_] = op
                        break
                else:
                    lst.append(op)
            else:
                lst.append(op)
        for k in writes:
            self.last_w[k] = op
            self.readers[k] = []
        self.ops[eng].append(op)
        self.nops += 1
        return op

    def pe(self, fn, r=(), w=()):
        return self.add("pe", fn, r, w)

    def act(self, fn, r=(), w=()):
        return self.add("act", fn, r, w)

    def dve(self, fn, r=(), w=()):
        return self.add("dve", fn, r, w)

    def pool(self, fn, r=(), w=()):
        return self.add("pool", fn, r, w)

    def dma(self, q, sem, out, in_, r=(), w=(), **kw):
        return self.add(q, lambda e: e.dma_start(out=out, in_=in_, **kw), r, w, dma=sem)

    def _skip(self, d, op):
        return d.dma is None and d.eng == op.eng and (op.eng == "pe" or not self.same_engine_sync)

    def finalize(self):
        for e in ENGS:
            for op in self.ops[e]:
                for d in op.deps:
                    if not self._skip(d, op):
                        d.has_dependents = True
        eng_cnt = {e: 0 for e in ENGS}
        dma_cnt = {}
        for e in ENGS:
            for op in self.ops[e]:
                if op.dma is not None:
                    dma_cnt[op.dma] = dma_cnt.get(op.dma, 0) + 16
                    op.sigval = dma_cnt[op.dma]
                    op.signal = True
                elif op.has_dependents:
                    eng_cnt[e] += 1
                    op.sigval = eng_cnt[e]
                    op.signal = True
        for e in ENGS:
            for op in self.ops[e]:
                ws = {}
                for d in op.deps:
                    if self._skip(d, op):
                        continue
                    key = ("dma", d.dma) if d.dma is not None else ("eng", d.eng)
                    if ws.get(key, 0) < d.sigval:
                        ws[key] = d.sigval
                op.waits = list(ws.items())
        self.dma_names = sorted(dma_cnt.keys())

    def emit(self):
        nc = self.nc
        self.finalize()
        with contextlib.ExitStack() as st:
            sems = {}
            for e in ENGS:
                sems[("eng", e)] = st.enter_context(nc.semaphore("s_" + e))
            for n in self.dma_names:
                sems[("dma", n)] = st.enter_context(nc.semaphore("d_" + n))
            block = st.enter_context(nc.Block())

            def replay(eobj, ename):
                known = {}
                for op in self.ops[ename]:
                    for key, val in op.waits:
                        if known.get(key, 0) < val:
                            eobj.wait_ge(sems[key], val)
                            known[key] = val
                    ins = op.fn(eobj)
                    if op.signal and ins is not None:
                        if op.dma is not None:
                            ins.then_inc(sems[("dma", op.dma)], 16)
                        else:
                            ins.then_inc(sems[("eng", ename)], 1)

            @block.tensor
            def _(e):
                replay(e, "pe")

            @block.scalar
            def _(e):
                replay(e, "act")

            @block.vector
            def _(e):
                replay(e, "dve")

            @block.gpsimd
            def _(e):
                replay(e, "pool")

            @block.sync
            def _(e):
                replay(e, "sp")


D = 1024
DIN = 2464
DFF = 2816
NFC = 22
DSH = 1696
SEQ = 8192
NSEQ_S = 16
LS = 8
KAPPA = float(np.exp(-0.5))
RMS_EPS = 1e-6
GN_EPS = 64e-5
RING = 5
N_TILES = int(os.environ.get("MK_NTILES", "16"))
DO_SAMPLE = int(os.environ.get("MK_SAMPLE", "1"))
STOP = int(os.environ.get("MK_STOP", "99"))
HPS = int(os.environ.get("MK_HPS", "4"))
VAR = int(os.environ.get("MK_VAR", "0"))

RW0 = 768
WIN_CHUNKS = {}
for c in range(4):
    WIN_CHUNKS[("q", c)] = [(64 * c, 64, 0), (256 + 64 * c, 64, 64)]
    WIN_CHUNKS[("qs", c)] = [(64 * c + 32, 32, 0), (64 * c, 32, 32), (256 + 64 * c + 32, 32, 64), (256 + 64 * c, 32, 96)]
    WIN_CHUNKS[("r", c)] = [(RW0 + 128 * c, 128, 0)]
    WIN_CHUNKS[("k", c)] = [(RW0 + 512 + 128 * c, 128, 0)]
    WIN_CHUNKS[("v", c)] = [(RW0 + 1024 + 128 * c, 128, 0)]
WIN_CHUNKS[("ak", 0)] = [(512, 128, 0)]
WIN_CHUNKS[("aks", 0)] = [(544, 32, 0), (512, 32, 32), (608, 32, 64), (576, 32, 96)]
WIN_CHUNKS[("av", 0)] = [(640, 128, 0)]
WIN_CHUNKS[("lo", 0)] = [(RW0 + 1536, 64, 0)]
WIN_CHUNKS[("lg", 0)] = [(RW0 + 1600, 96, 0)]
WIN_ORDER = [("lo", 0), ("lg", 0)]
for c in range(4):
    WIN_ORDER += [("r", c), ("k", c), ("v", c)]
WIN_ORDER += [("ak", 0), ("aks", 0), ("av", 0)]
for c in range(4):
    WIN_ORDER += [("q", c), ("qs", c)]
WIN_IDX = {k: i for i, k in enumerate(WIN_ORDER)}
NWIN = len(WIN_ORDER)
RC = {}
for c in range(4):
    RC[("r", c)] = c
    RC[("k", c)] = 4 + c
    RC[("v", c)] = 8 + c
RC[("lo", 0)] = 12
RC[("lg", 0)] = 13
RC_NP = {12: 64, 13: 96}


def build_program():
    nc = bass.Bass("TRN2", target_bir_lowering=False)
    P = Prog(nc)
    st = contextlib.ExitStack()

    def din(name, shape, dt=F32):
        return nc.dram_tensor(name, list(shape), dt, kind="ExternalInput").ap()

    def dout(name, shape, dt=F32):
        return nc.dram_tensor(name, list(shape), dt, kind="ExternalOutput").ap()

    def dint(name, shape, dt=BF16):
        return nc.dram_tensor(name, list(shape), dt, kind="Internal").ap()

    def T(name, shape, dt=F32):
        return st.enter_context(nc.sbuf_tensor(name, list(shape), dt))

    xp = din("xp", [SEQ, D])
    xs = din("xs", [128, D])
    ck = din("ck", [NSEQ_S, 128, 128])
    cv = din("cv", [NSEQ_S, 128, 128])
    sshift = din("sshift", [NSEQ_S, DSH])
    swkv = din("swkv", [NSEQ_S, 8, 64, 64])
    sconv = din("sconv", [NSEQ_S * 2, DFF])
    g_pre_mix = din("g_pre_mix", [D])
    w_in = din("w_in", [D, DIN])
    attn_sinks = din("attn_sinks", [8])
    mu_shift = din("mu_shift", [DSH])
    w0 = din("w0", [512])
    w_decay_up = din("w_decay_up", [32, 512])
    a0 = din("a0", [512])
    w_a_up = din("w_a_up", [32, 512])
    w_g_up = din("w_g_up", [96, 512])
    k_k = din("k_k", [512])
    k_a = din("k_a", [512])
    r_k = din("r_k", [512])
    gn_w = din("gn_w", [512])
    gn_b = din("gn_b", [512])
    w_out = din("w_out", [D, D])
    g_post_mix = din("g_post_mix", [D])
    g_pre_ffn = din("g_pre_ffn", [D])
    w_ffn_in = din("w_ffn_in", [D, 2 * DFF])
    conv_w = din("conv_w", [3, DFF])
    conv_b = din("conv_b", [DFF])
    w_ffn_out = din("w_ffn_out", [DFF, D])
    g_post_ffn = din("g_post_ffn", [D])
    c_ident = din("c_ident", [128, 128])
    c_cos_p = din("c_cos_p", [128, SEQ])
    c_sin_p = din("c_sin_p", [128, SEQ])
    c_cos_s = din("c_cos_s", [128, 128])
    c_sin_s = din("c_sin_s", [128, 128])
    c_mask_own = din("c_mask_own", [128, 128])
    c_mask_prev = din("c_mask_prev", [128, 128])
    c_negmask_own = din("c_negmask_own", [128, 128])
    c_negmask_prev = din("c_negmask_prev", [128, 128])
    c_mask_sown = din("c_mask_sown", [128, 128])
    c_mask_scache = din("c_mask_scache", [128, NSEQ_S, 128])
    c_blockones = din("c_blockones", [128, 128])
    c_m2b = din("c_m2b", [128, 2, 128])
    c_m2k = din("c_m2k", [128, 2, 128])
    c_mT = din("c_mT", [128, 128])
    c_scanm_p = din("c_scanm_p", [128, 512])
    c_scanm_s = din("c_scanm_s", [128, 128])

    yp = dout("yp", [SEQ, D])
    ys = dout("ys", [128, D])
    kwp = dout("kwp", [128, 128])
    vwp = dout("vwp", [128, 128])
    shp = dout("shp", [1, DSH])
    wkvp = dout("wkvp", [8, 64, 64])
    convp = dout("convp", [2, DFF])
    kws = dout("kws", [NSEQ_S, 128, 128])
    vws = dout("vws", [NSEQ_S, 128, 128])
    shs = dout("shs", [NSEQ_S, DSH])
    wkvs = dout("wkvs", [NSEQ_S, 8, 64, 64])
    convs = dout("convs", [NSEQ_S * 2, DFF])

    wsc_in = dint("wsc_in", [NWIN, 128, 1024])
    wsc_out = dint("wsc_out", [8, 128, 1024])
    wsc_fin = dint("wsc_fin", [2 * NFC, 128, 1024])
    wsc_fout = dint("wsc_fout", [NFC, 128, 1024])

    pp = [st.enter_context(nc.psum_tensor("pp%d" % i, [128, 1024], F32)) for i in range(4)]
    nb = [0]

    def bank():
        b = nb[0] % 8
        nb[0] += 1
        return b

    def pair():
        if nb[0] % 2:
            nb[0] += 1
        b = nb[0] % 8
        nb[0] += 2
        return b

    def PS(b, n=512, np_=128, p0=0):
        o = (b % 2) * 512
        return pp[b // 2][p0:p0 + np_, o:o + n]

    def PSP(b):
        return pp[b // 2][:, :]

    def PSB(b, np_=128):
        return pp[b // 2].bitcast(BF16)[0:np_, (b % 2) * 1024:(b % 2) * 1024 + 1024]

    def kb(b):
        return ("ps", b)

    idf = T("idf", [128, 128])
    idb = T("idb", [128, 128], BF16)
    gTpre = T("gTpre", [128, 8])
    gTffn = T("gTffn", [128, 8])
    gpost_bc = T("gpost_bc", [128, D])
    gpostf_bc = T("gpostf_bc", [128, D])
    gnw_bc = T("gnw_bc", [128, 512])
    gnb_bc = T("gnb_bc", [128, 512])
    esink = T("esink", [128, 8])
    w0T = T("w0T", [128, 4])
    a0T = T("a0T", [128, 4])
    kkT = T("kkT", [128, 4])
    kaT = T("kaT", [128, 4])
    rkT = T("rkT", [128, 4])
    ln2x20 = T("ln2x20", [128, 1])
    muT = T("muT", [128, 14])
    ommT = T("ommT", [128, 14])
    cwT = T("cwT", [128, 3, NFC])
    cbT = T("cbT", [128, NFC])
    Wd_b = T("Wd_b", [32, 512], BF16)
    Wa_b = T("Wa_b", [64, 512], BF16)
    Wg_b = T("Wg_b", [96, 512], BF16)
    blk1 = T("blk1", [128, 128], BF16)
    m_own = T("m_own", [128, 4, 128], BF16)
    m_prev = T("m_prev", [128, 4, 128], BF16)
    m2b = T("m2b", [128, 2, 128], BF16)
    m2k = T("m2k", [128, 2, 128], BF16)
    mTl = T("mTl", [128, 128], BF16)
    scanm_p = T("scanm_p", [128, 512])
    cstage = T("cstage", [128, 512])

    ld_n = [0]

    def cload(dst, src, xw=(), **kw):
        i = ld_n[0]
        ld_n[0] += 1
        k = ("c", i)
        P.dma("sp", "cl%d" % i, dst, src, w=[k] + list(xw), **kw)
        return k

    def cload_cast(dst_bf, src, np_, shape_free, eng="dve"):
        nfree = int(np.prod(shape_free))
        stg = cstage[0:np_, 0:nfree]
        k = ("c", ld_n[0])
        ld_n[0] += 1
        P.dma("sp", "cst", stg, src, w=["cstage"])
        P.dve(lambda e: e.tensor_copy(out=dst_bf, in_=stg), r=["cstage"], w=[k, "cstage_rd"])
        return k

    NSC = ALLOW = dict(allow_slow_non_contiguous=True)
    CK = []
    CK.append(cload(idf[:], c_ident))
    P.dve(lambda e: e.tensor_copy(out=idb[:], in_=idf[:]), r=[CK[-1]], w=["idb"])
    CK.append(cload(gTpre[:], g_pre_mix.rearrange("(c p) -> p c", p=128), **NSC))
    kgpre = CK[-1]
    CK.append(cload(gTffn[:], g_pre_ffn.rearrange("(c p) -> p c", p=128), **NSC))
    kgffn = CK[-1]
    CK.append(cload(gpost_bc[:], g_post_mix.partition_broadcast(128)))
    CK.append(cload(gpostf_bc[:], g_post_ffn.partition_broadcast(128)))
    CK.append(cload(gnw_bc[:], gn_w.partition_broadcast(128)))
    CK.append(cload(gnb_bc[:], gn_b.partition_broadcast(128)))
    CK.append(cload(esink[:], attn_sinks.partition_broadcast(128)))
    P.act(lambda e: e.activation(out=esink[:], in_=esink[:], func=AF.Exp), r=[CK[-1]], w=["esink"])
    for tt, src in ((w0T, w0), (a0T, a0), (kkT, k_k), (kaT, k_a), (rkT, r_k)):
        CK.append(cload(tt[:], src.rearrange("(c p) -> p c", p=128), **NSC))
    P.pool(lambda e: e.memset(muT[:], 0.0), w=["muT"])
    P.pool(lambda e: e.memset(ln2x20[:], float(20.0 * np.log(2.0))), w=["ln2c"])
    CK.append(cload(muT[:, 0:12], mu_shift[0:1536].rearrange("(c p) -> p c", p=128), xw=["muT"], **NSC))
    CK.append(cload(muT[0:64, 12:13], mu_shift[1536:1600].rearrange("(p c) -> p c", c=1), xw=["muT"], **NSC))
    CK.append(cload(muT[0:96, 13:14], mu_shift[1600:1696].rearrange("(p c) -> p c", c=1), xw=["muT"], **NSC))
    P.dve(lambda e: e.tensor_scalar(out=ommT[:], in0=muT[:], scalar1=-1.0, scalar2=1.0, op0=ALU.mult, op1=ALU.add),
          r=["muT"], w=["muT2"])
    CK.append(cload(cwT[:], conv_w.rearrange("j (c p) -> p j c", p=128), **NSC))
    CK.append(cload(cbT[:], conv_b.rearrange("(c p) -> p c", p=128), **NSC))
    CK.append(cload(scanm_p[:], c_scanm_p))

    def cast_const(dst, src, np_, nfree, bcast4=False):
        stg = cstage[0:np_, 0:nfree]
        P.dma("sp", "cst", stg, src, w=["cstage"])
        if bcast4:
            P.dve(lambda e: e.tensor_copy(out=dst, in_=stg.unsqueeze(1).to_broadcast([np_, 4, nfree])),
                  r=["cstage"], w=["cconst", "cstage"])
        else:
            P.dve(lambda e: e.tensor_copy(out=dst, in_=stg), r=["cstage"], w=["cconst", "cstage"])

    cast_const(blk1[:], c_blockones, 128, 128)
    cast_const(m_own[:], c_negmask_own, 128, 128, bcast4=True)
    cast_const(m_prev[:], c_negmask_prev, 128, 128, bcast4=True)
    cast_const(m2b[:].rearrange("p a b -> p (a b)"), c_m2b.rearrange("p a b -> p (a b)"), 128, 256)
    cast_const(m2k[:].rearrange("p a b -> p (a b)"), c_m2k.rearrange("p a b -> p (a b)"), 128, 256)
    cast_const(mTl[:], c_mT, 128, 128)
    cast_const(Wd_b[:], w_decay_up, 32, 512)
    P.dma("sp", "cst", cstage[32:64, 0:512], w_a_up, w=["cstage"])
    P.dve(lambda e: e.tensor_copy(out=Wa_b[32:64, :], in_=cstage[32:64, 0:512]), r=["cstage"], w=["cconst", "cstage"])
    cast_const(Wg_b[:], w_g_up, 96, 512)
    CONST = CK + ["idb", "esink", "cconst", "muT", "muT2", "ln2c"]

    x_t = T("x_t", [128, 4, D])
    bufA = T("bufA", [128, 8, 512], BF16)
    BUFA_ALL = [("bufA", b) for b in range(4)]
    wp_n = [0]

    def prep_chunk(dst_dram, dkey, loads, gT=None, gkey=None):
        i = wp_n[0] % 4
        wp_n[0] += 1
        stage = x_t[:, i, :]
        wbs_i = bufA[:, 2 * i:2 * i + 2, :].rearrange("p a b -> p (a b)")
        for mk, src in loads:
            P.dma("sp", "wl%d" % i, mk(stage), src, w=[("x", i)])
        if gT is not None:
            P.dve(lambda e: e.tensor_tensor(out=wbs_i.rearrange("p (c n) -> p c n", c=8),
                                            in0=stage.rearrange("p (c n) -> p c n", c=8),
                                            in1=gT[:].unsqueeze(2).to_broadcast([128, 8, 128]), op=ALU.mult),
                  r=[("x", i), gkey], w=[("wbs", i)])
        else:
            P.dve(lambda e: e.tensor_copy(out=wbs_i, in_=stage), r=[("x", i)], w=[("wbs", i)])
        P.dma("pool", "ws%d" % i, dst_dram, wbs_i, r=[("wbs", i)], w=[dkey])

    P.pool(lambda e: e.memset(x_t[:, :, :], 0.0), w=[("x", b_) for b_ in range(4)])
    win3 = w_in.rearrange("(c p) n -> p c n", p=128)
    for ci, key in enumerate(WIN_ORDER):
        loads = []
        for (cs, n, o) in WIN_CHUNKS[key]:
            loads.append((lambda s, o=o, n=n: s.rearrange("p (c n) -> p c n", c=8)[:, :, o:o + n], win3[:, :, cs:cs + n]))
        if key[0] in ("lo", "lg"):
            pass
        prep_chunk(wsc_in[ci], ("wsc_in", ci), loads, gTpre, kgpre)
    wfi3 = w_ffn_in.rearrange("(c p) n -> p c n", p=128)
    for fc in range(NFC):
        for zu in range(2):
            cs = zu * DFF + fc * 128
            prep_chunk(wsc_fin[2 * fc + zu], ("wsc_fin", 2 * fc + zu),
                       [(lambda s: s.rearrange("p (c n) -> p c n", c=8), wfi3[:, :, cs:cs + 128])], gTffn, kgffn)
    for c in range(8):
        prep_chunk(wsc_out[c], ("wsc_out", c), [(lambda s: s, w_out[c * 128:(c + 1) * 128, :])])
    for c in range(NFC):
        prep_chunk(wsc_fout[c], ("wsc_fout", c), [(lambda s: s, w_ffn_out[c * 128:(c + 1) * 128, :])])

    P.dve(lambda e: e.memset(bufA[0:1, 0, 0:2], 0.0), r=[("wbs", i_) for i_ in range(4)], w=BUFA_ALL + [("wbs", i_) for i_ in range(4)])
    ring = T("ring", [128, RING, 1024], BF16)
    xn = T("xn", [128, D], BF16)
    junk = T("junk", [128, D], BF16)
    cs_t = T("cs_t", [128, 2, 512])
    qT = T("qT", [128, 4, 512], BF16)
    kbuf = T("kbuf", [128, 5 * 128], BF16)
    vtokf = T("vtokf", [128, 128])
    Vaug = T("Vaug", [128, 5, 2, 65], BF16)
    Et = T("Et", [128, 4, 512], BF16)
    oacc = T("oacc", [128, 2, 4, 65])
    den = T("den", [128, 8])
    mix = T("mix", [128, 4, D], BF16)
    hidT = T("hidT", [128, NFC, 512], BF16)
    zbuf = T("zbuf", [128, 1, 640])
    za = T("za", [128, 1, 512])
    zcar_p = T("zcar_p", [128, NFC, 1, 2])
    zcar_s = T("zcar_s", [128, NFC, NSEQ_S, 2])
    sstat = T("sstat", [128, 16])
    hbuf = T("hbuf", [128, 1, 640])
    hcar_p = T("hcar_p", [128, 14, 1])
    hcar_s = T("hcar_s", [128, 14, NSEQ_S])
    dtmp = T("dtmp", [128, 512])
    hs_lo = T("hs_lo", [64, 512])
    hs_lg = T("hs_lg", [96, 512])
    lorab = T("lorab", [64, 512], BF16)
    sgb = T("sgb", [96, 512], BF16)
    rkv = T("rkv", [128, 1, 3, 512])
    tq = [T("tq%d" % i, [128, 512]) for i in range(7)]
    yq, yq2, kf32, vT32 = tq[0], tq[1], tq[3], tq[4]
    vblk_s = T("vblk_s", [128, 512], BF16)
    Sld = tq[5][0:64, 0:256].rearrange("p (a b) -> p a b", a=2)
    sqb = T("sqb", [128, 512], BF16)
    Kt = T("Kt", [128, 512], BF16)
    Bt = T("Bt", [128, 512], BF16)
    KR = T("KR", [128, 1024], BF16)
    KgL = T("KgL", [128, 512], BF16)
    BgLn = T("BgLn", [128, 512], BF16)
    vb = T("vb", [128, 512], BF16)
    prodb = T("prodb", [128, 512], BF16)
    gL2 = T("gL2", [128, 2, 16])
    nkcl = T("nkcl", [128, 16])
    vtok = T("vtok", [128, 2048], BF16)
    KgLtok = T("KgLtok", [128, 512], BF16)
    BgLtok = T("BgLtok", [128, 512], BF16)
    NQ2 = T("NQ2", [128, 2048], BF16)
    KQ2 = T("KQ2", [128, 2048], BF16)
    NT2 = T("NT2", [128, 1024], BF16)
    T32 = T("T32", [128, 1024])
    Tb = T("Tb", [128, 1024], BF16)
    XTs = T("XTs", [128, 128], BF16)
    SATs = T("SATs", [128, 128], BF16)
    ytok = T("ytok", [128, 4, 512])
    rkb = T("rkb", [128, 4, 8])
    Sm = T("Sm", [128, 4, 64])
    Sb = T("Sb", [128, 4, 64], BF16)
    gstat = T("gstat", [128, 4, 8])
    hid32 = hidT[:].rearrange("p a b -> p (a b)").bitcast(F32)
    HID_ALL = [("hidT", fc) for fc in range(NFC)]
    rowbuf = hid32[0:32, 0:DSH]
    sst = hid32[0:32, 1792:1792 + DFF]
    scanm_s = T("scanm_s", [128, 128])
    m_sown = T("m_sown", [128, 4, 128], BF16)
    m_scache = T("m_scache", [128, NSEQ_S, 128], BF16)
    ckT = T("ckT", [128, 128], BF16)
    ckf = T("ckf", [128, 2, 128])
    cVaug = T("cVaug", [128, 2, 2, 65], BF16)
    ytok_s = hid32[0:8, 0:NSEQ_S * 128].rearrange("p (s n) -> p s n", s=NSEQ_S)

    state = dict(ring_n=0, xld=0, ost=0)

    def stream(src, skey):
        n = state["ring_n"]
        state["ring_n"] += 1
        s = n % RING
        P.dma("sp", "rg%d" % s, ring[:, s, :], src, r=[skey], w=[("ring", s)])
        return s

    def emit_tile(kind, ti):
        NT = 512 if kind == "p" else 128
        NB = NT // 128
        nseq, Lseq = (1, 512) if kind == "p" else (NSEQ_S, LS)
        L = 128 if kind == "p" else LS
        nch = NT // L
        t0 = ti * 512
        first = (kind == "p" and ti == 0)
        last = (kind == "s") or (ti == N_TILES - 1)
        xsrc = xp if kind == "p" else xs
        ydst = yp if kind == "p" else ys
        hcar = hcar_p if kind == "p" else hcar_s
        zcar = zcar_p if kind == "p" else zcar_s
        scanm = scanm_p if kind == "p" else scanm_s
        S_ = slice(0, NT)

        if kind == "p":
            P.dma("pool", "cs", cs_t[:, 0, S_], c_cos_p[:, t0:t0 + NT], w=["cs_t"])
            P.dma("pool", "cs", cs_t[:, 1, S_], c_sin_p[:, t0:t0 + NT], w=["cs_t"])
        else:
            P.dma("pool", "cs", cs_t[:, 0, S_], c_cos_s, w=["cs_t"])
            P.dma("pool", "cs", cs_t[:, 1, S_], c_sin_s, w=["cs_t"])

        for b in range(NB):
            P.dma("sp", "xl%d" % b, x_t[:, b, :], xsrc[t0 + b * 128:t0 + (b + 1) * 128, :] if kind == "p" else xsrc,
                  w=[("x", b)])
            sc = sstat[:, b:b + 1]
            P.act(lambda e, b=b, sc=sc: e.activation(out=junk[:], in_=x_t[:, b, :], func=AF.Square, accum_out=sc),
                  r=[("x", b)], w=["junk", ("ss", b)])
            P.act(lambda e, sc=sc: e.activation(out=sc, in_=sc, func=AF.Ln, scale=1.0 / D, bias=RMS_EPS),
                  r=[("ss", b)], w=[("ss", b)])
            P.act(lambda e, sc=sc: e.activation(out=sc, in_=sc, func=AF.Exp, scale=-0.5), r=[("ss", b)], w=[("ss", b)])
        for b in range(NB):
            sc = sstat[:, b:b + 1]
            P.dve(lambda e, b=b, sc=sc: e.tensor_scalar(out=xn[:], in0=x_t[:, b, :], scalar1=sc, scalar2=None, op0=ALU.mult),
                  r=[("x", b), ("ss", b)], w=["xn"])
            bk = bank()
            for c in range(8):
                P.pe(lambda e, c=c, bk=bk: e.transpose(out=PSB(bk)[:, c * 128:(c + 1) * 128], in_=xn[:, c * 128:(c + 1) * 128],
                                                      identity=idb[:]), r=["xn", "idb"], w=[kb(bk)])
            P.act(lambda e, b=b, bk=bk: e.copy(out=bufA[:, :, b * 128:(b + 1) * 128],
                                               in_=PSB(bk).rearrange("p (c n) -> p c n", c=8)),
                  r=[kb(bk)], w=[("bufA", b)])
        bufA_keys = [("bufA", b) for b in range(NB)]

        if kind == "s" and STOP <= 1:
            return
        def inproj(key, ncols):
            ci = WIN_IDX[key]
            s = stream(wsc_in[ci], ("wsc_in", ci))
            bk = bank()
            for c in range(8):
                P.pe(lambda e, c=c, s=s, bk=bk: e.matmul(PS(bk, NT, ncols), lhsT=ring[:, s, c * 128:c * 128 + ncols],
                                                         rhs=bufA[:, c, S_], start=(c == 0), stop=(c == 7)),
                     r=[("ring", s)] + bufA_keys, w=[kb(bk)])
            return bk

        hb_n = [0]

        def shift_evac(bk, np_, rc, out_ap, okey, xw=()):
            i = 0
            hb = hbuf[0:np_, i, 0:nseq * (Lseq + 1)].rearrange("p (s l) -> p s l", s=nseq)
            hk = ("hbuf", i)
            P.act(lambda e: e.copy(out=hb[:, :, 1:Lseq + 1], in_=PS(bk, NT, np_).rearrange("p (s l) -> p s l", s=nseq)),
                  r=[kb(bk)], w=[hk])
            if VAR != 1:
                P.pool(lambda e: e.tensor_copy(out=hb[:, :, 0:1], in_=hcar[0:np_, rc, :].unsqueeze(2)),
                       r=[("hcar", rc)], w=[hk])
                P.pool(lambda e: e.tensor_copy(out=hcar[0:np_, rc, :].unsqueeze(2), in_=hb[:, :, Lseq:Lseq + 1]),
                       r=[hk], w=[("hcar", rc)])
            if VAR == 2:
                return
            d3 = dtmp[0:np_, S_].rearrange("p (s l) -> p s l", s=nseq)
            P.act(lambda e: e.activation(out=d3, in_=hb[:, :, 0:Lseq], func=AF.Copy, scale=muT[0:np_, rc:rc + 1]),
                  r=[hk, "muT"], w=["dtmp"])
            P.dve(lambda e: e.scalar_tensor_tensor(out=out_ap.rearrange("p (s l) -> p s l", s=nseq), in0=hb[:, :, 1:Lseq + 1],
                                                   scalar=ommT[0:np_, rc:rc + 1], in1=d3,
                                                   op0=ALU.mult, op1=ALU.add),
                  r=["dtmp", hk, "muT", "muT2"], w=[okey] + list(xw))

        bk = inproj(("lo", 0), 64)
        shift_evac(bk, 64, 12, hs_lo[:, S_], "hs_lo")
        P.act(lambda e: e.activation(out=lorab[0:32, S_], in_=hs_lo[0:32, S_], func=AF.Tanh), r=["hs_lo"], w=["lorab0"])
        P.act(lambda e: e.copy(out=lorab[32:64, S_], in_=hs_lo[32:64, S_]), r=["hs_lo"], w=["lorab1"])
        bk = inproj(("lg", 0), 96)
        shift_evac(bk, 96, 13, hs_lg[:, S_], "hs_lg")
        P.act(lambda e: e.activation(out=sgb[:, S_], in_=hs_lg[:, S_], func=AF.Sigmoid), r=["hs_lg"], w=["sgb"])

        if kind == "s" and STOP <= 2:
            return
        def hp_gen(hp):
            rs_i = (hp % 2) if PIPE else 0
            gLv = gL2[:, rs_i, :]
            gk = ("gL", rs_i)
            rkv_v = (lambda j: rkv[:, 0, j, S_]) if rs_i == 0 else (lambda j: hid32[:, j * 512:j * 512 + NT])
            rs, ks, vs = rkv_v(0), rkv_v(1), rkv_v(2)
            for j, nm in enumerate(("r", "k", "v")):
                bk = inproj((nm, hp), 128)
                shift_evac(bk, 128, RC[(nm, hp)], rkv_v(j), ("rkv", rs_i, j), xw=([("hidT", fc_) for fc_ in range(6)] if rs_i == 1 else []))
                yield "p0"
            kr_, kk_, kv_ = ("rkv", rs_i, 0), ("rkv", rs_i, 1), ("rkv", rs_i, 2)
            t = [x[:, S_] for x in tq]
            if kind == "s" and STOP == 32:
                return
            bw = bank()
            P.pe(lambda e, bw=bw, hp=hp: e.matmul(PS(bw, NT), lhsT=Wd_b[0:32, hp * 128:(hp + 1) * 128], rhs=lorab[0:32, S_],
                                                  start=True, stop=True), r=["lorab0", "cconst"], w=[kb(bw)])
            P.act(lambda e, bw=bw, hp=hp: e.activation(out=t[0], in_=PS(bw, NT), func=AF.Sigmoid, bias=w0T[:, hp:hp + 1]),
                  r=[kb(bw)] + CONST, w=["t0"])
            ba = bank()
            P.pe(lambda e, ba=ba, hp=hp: e.matmul(PS(ba, NT), lhsT=Wa_b[32:64, hp * 128:(hp + 1) * 128], rhs=lorab[32:64, S_],
                                                  start=True, stop=True), r=["lorab1", "cconst"], w=[kb(ba)])
            P.act(lambda e, ba=ba, hp=hp: e.activation(out=t[6], in_=PS(ba, NT), func=AF.Sigmoid, bias=a0T[:, hp:hp + 1]),
                  r=[kb(ba)] + CONST, w=["t6"])
            P.dve(lambda e: e.tensor_tensor_scan(out=t[1], data0=scanm[:, S_], data1=t[0], initial=0.0, op0=ALU.mult, op1=ALU.add),
                  r=["t0"] + CONST, w=["t1"])
            P.dve(lambda e: e.tensor_tensor(out=t[2], in0=t[1], in1=t[0], op=ALU.subtract), r=["t0", "t1"], w=["t2"])
            yield "p1"
            P.act(lambda e: e.activation(out=t[0], in_=t[1], func=AF.Exp, scale=-KAPPA), r=["t1", "t2"], w=["t0"])
            P.act(lambda e: e.activation(out=t[3], in_=t[1], func=AF.Exp, scale=KAPPA), r=["t1"], w=["t3"])
            P.act(lambda e: e.activation(out=t[2], in_=t[2], func=AF.Exp, scale=-KAPPA), r=["t2"], w=["t2"])
            yield "p1"
            cp3 = t[1].rearrange("p (c l) -> p c l", l=L)
            eg3 = t[0].rearrange("p (c l) -> p c l", l=L)
            P.dve(lambda e: e.tensor_copy(out=gLv[:, 0:nch].unsqueeze(2), in_=eg3[:, :, L - 1:L]), r=["t0"], w=[gk])
            P.dve(lambda e: e.tensor_scalar(out=nkcl[:, 0:nch].unsqueeze(2), in0=cp3[:, :, L - 1:L], scalar1=-KAPPA, scalar2=None,
                                            op0=ALU.mult), r=["t1"], w=["nkcl"])
            P.dve(lambda e: e.scalar_tensor_tensor(out=t[4].rearrange("p (c l) -> p c l", l=L), in0=cp3, scalar=KAPPA,
                                                   in1=nkcl[:, 0:nch].unsqueeze(2).to_broadcast([128, nch, L]),
                                                   op0=ALU.mult, op1=ALU.add), r=["t1", "nkcl"], w=["t4"])
            P.act(lambda e: e.activation(out=t[4], in_=t[4], func=AF.Exp), r=["t4"], w=["t4"])
            if kind == "s" and STOP == 33:
                return
            yield "p1"
            yield "p1"
            P.dve(lambda e, hp=hp: e.tensor_scalar(out=t[5], in0=ks, scalar1=kkT[:, hp:hp + 1], scalar2=None, op0=ALU.mult),
                  r=[kk_] + CONST, w=["t5"])
            P.act(lambda e: e.activation(out=sqb[:, S_], in_=t[5], func=AF.Square), r=["t5"], w=["sqb"])
            bn = bank()
            P.pe(lambda e, bn=bn: e.matmul(PS(bn, NT), lhsT=blk1[:], rhs=sqb[:, S_], start=True, stop=True),
                 r=["sqb", "cconst"], w=[kb(bn)])
            yield "p1"
            P.dve(lambda e, bn=bn: e.tensor_scalar(out=t[1], in0=PS(bn, NT), scalar1=float(2.0 ** 40), scalar2=float(1e-24 * 2.0 ** 40),
                                                   op0=ALU.mult, op1=ALU.max),
                  r=[kb(bn), "t4", gk, "nkcl"], w=["t1"])
            P.act(lambda e: e.activation(out=t[1], in_=t[1], func=AF.Ln), r=["t1"], w=["t1"])
            P.act(lambda e: e.activation(out=t[1], in_=t[1], func=AF.Exp, scale=-0.5, bias=ln2x20[:, 0:1]), r=["t1"] + CONST, w=["t1"])
            P.dve(lambda e: e.tensor_tensor(out=t[5], in0=t[5], in1=t[1], op=ALU.mult), r=["t5", "t1"], w=["t5"])
            yield "p1"
            P.dve(lambda e: e.tensor_tensor(out=t[1], in0=t[5], in1=t[6], op=ALU.mult), r=["t5", "t6", "t1"], w=["t1"])
            yield "p1"
            P.dve(lambda e, hp=hp: e.tensor_scalar(out=t[6], in0=t[6], scalar1=1.0, scalar2=kaT[:, hp:hp + 1],
                                                   op0=ALU.subtract, op1=ALU.mult), r=["t6", "t1"] + CONST, w=["t6"])
            P.dve(lambda e: e.scalar_tensor_tensor(out=t[6], in0=t[6], scalar=1.0, in1=ks, op0=ALU.add, op1=ALU.mult),
                  r=["t6", kk_], w=["t6"])
            if kind == "s" and STOP == 34:
                return
            yield "p1done"
            KR4 = KR[:, 0:2 * NT].rearrange("p (c a l) -> p c a l", a=2, l=L)
            P.dve(lambda e: e.tensor_tensor(out=Kt[:, S_], in0=t[6], in1=t[3], op=ALU.mult), r=["t6", "t3"], w=["Kt"])
            P.dve(lambda e: e.tensor_tensor(out=Bt[:, S_], in0=t[1], in1=t[3], op=ALU.mult), r=["t1", "t3"], w=["Bt"])
            P.dve(lambda e: e.tensor_tensor(out=KR4[:, :, 0, :], in0=t[5].rearrange("p (c l) -> p c l", l=L),
                                            in1=t[2].rearrange("p (c l) -> p c l", l=L), op=ALU.mult), r=["t5", "t2"], w=["KR"])
            P.dve(lambda e: e.tensor_tensor(out=KR4[:, :, 1, :], in0=rs.rearrange("p (c l) -> p c l", l=L),
                                            in1=t[0].rearrange("p (c l) -> p c l", l=L), op=ALU.mult), r=[kr_, "t0"], w=["KR"])
            P.dve(lambda e: e.tensor_tensor(out=KgL[:, S_], in0=t[6], in1=t[4], op=ALU.mult), r=["t6", "t4"], w=["KgL"])
            P.dve(lambda e: e.scalar_tensor_tensor(out=BgLn[:, S_], in0=t[1], scalar=-1.0, in1=t[4], op0=ALU.mult, op1=ALU.mult),
                  r=["t1", "t4"], w=["BgLn"])
            P.act(lambda e: e.copy(out=vb[:, S_], in_=vs), r=[kv_], w=["vb"])
            P.dve(lambda e, hp=hp: e.scalar_tensor_tensor(out=prodb[:, S_], in0=rs, scalar=rkT[:, hp:hp + 1], in1=t[6],
                                                          op0=ALU.mult, op1=ALU.mult), r=[kr_, "t6"] + CONST, w=["prodb"])
            if kind == "s" and STOP == 35:
                return
            brk = bank()
            for b in range(NB):
                P.pe(lambda e, b=b, brk=brk: e.matmul(PS(brk, 2 * NB)[:, 2 * b:2 * b + 2], lhsT=prodb[:, b * 128:(b + 1) * 128],
                                                      rhs=blk1[:, 0:128:64], start=True, stop=True),
                     r=["prodb", "cconst"], w=[kb(brk)])
            P.act(lambda e, brk=brk, hp=hp: e.copy(out=rkb[:, 0:NB, 2 * hp:2 * hp + 2],
                                                   in_=PS(brk, 2 * NB).rearrange("p (b a) -> p b a", a=2)),
                  r=[kb(brk)], w=[("rkb", hp)])
            if kind == "s" and STOP == 36:
                return
            if kind == "p":
                vtok_hp = vtok[0:L, hp * nch * 128:(hp + 1) * nch * 128].rearrange("p (c n) -> p c n", n=128)
                vkey = ("vtok", hp)
            else:
                vtok_hp = vtok[0:L, 0:nch * 128].rearrange("p (c n) -> p c n", n=128)
                vkey = "vtok_s"
            if kind == "p":
                KgLtok3 = KgLtok[0:L, 0:nch * 128].rearrange("p (c n) -> p c n", n=128)
                BgLtok3 = BgLtok[0:L, 0:nch * 128].rearrange("p (c n) -> p c n", n=128)
                kBg = ["BgLtok"]
            else:
                KgLtok3 = hidT[:].rearrange("p a b -> p (a b)")[0:L, 8192:8192 + nch * 128].rearrange("p (c n) -> p c n", n=128)
                BgLtok3 = Et[:].rearrange("p a b -> p (a b)")[0:L, 0:nch * 128].rearrange("p (c n) -> p c n", n=128)
                kBg = ["BgLtok"] + [("Et", i_) for i_ in range(4)]
            for src, skey, dst3, dkey in ((vb, "vb", vtok_hp, vkey), (KgL, "KgL", KgLtok3, "KgLtok"), (BgLn, "BgLn", BgLtok3, kBg)):
                for g0 in range(0, nch, 8):
                    g1 = min(nch, g0 + 8)
                    bt = bank()
                    for c in range(g0, g1):
                        P.pe(lambda e, c=c, bt=bt, src=src, g0=g0: e.transpose(
                            out=PSB(bt, L)[:, (c - g0) * 128:(c - g0 + 1) * 128], in_=src[:, c * L:(c + 1) * L], identity=idb[:]),
                            r=[skey, "idb"], w=[kb(bt)])
                    P.act(lambda e, bt=bt, g0=g0, g1=g1, dst3=dst3: e.copy(
                        out=dst3[:, g0:g1, :], in_=PSB(bt, L)[:, 0:(g1 - g0) * 128].rearrange("p (c n) -> p c n", n=128)),
                        r=[kb(bt), "XA"], w=(dkey if isinstance(dkey, list) else [dkey]))
            if kind == "s" and STOP == 31:
                return
            if kind == "s":
                btv = bank()
                P.pe(lambda e, btv=btv: e.transpose(out=PSB(btv)[:, 0:128], in_=vb[:, 0:128], identity=idb[:]), r=["vb", "idb"], w=[kb(btv)])
                P.act(lambda e, btv=btv, hp=hp: e.copy(out=vblk_s[:, hp * 128:(hp + 1) * 128], in_=PSB(btv)[:, 0:128]), r=[kb(btv)], w=["vblk_s"])
                Sst = hid32[0:64, 2048:4096]
                if VAR == 3:
                    return
                for s_ in range(NSEQ_S):
                    P.dma("sp", "sst_in", Sst.rearrange("i (s h j) -> i s h j", s=NSEQ_S, h=2)[:, s_, :, :],
                          swkv[s_, 2 * hp:2 * hp + 2, :, :].rearrange("h i j -> i h j"), r=["XA"], w=["Sstage"] + HID_ALL)
                if VAR == 4:
                    return
                for g in range(2):
                    bts = bank()
                    for s8 in range(8):
                        s_ = g * 8 + s8
                        P.pe(lambda e, bts=bts, s8=s8, s_=s_, Sst=Sst: e.transpose(out=PS(bts, 512)[:, s8 * 64:(s8 + 1) * 64],
                                                                                 in_=Sst[:, s_ * 128:(s_ + 1) * 128], identity=idf[0:64, 0:64]),
                             r=["Sstage", "XA"] + CONST, w=[kb(bts)])
                    if VAR == 5:
                        return
                    P.act(lambda e, bts=bts, g=g: e.copy(out=SmS[:, g * 8:(g + 1) * 8, :], in_=PS(bts, 512).rearrange("p (s i) -> p s i", i=64)),
                          r=[kb(bts)], w=[("SmS", s_) for s_ in range(g * 8, g * 8 + 8)])
                    if VAR == 6:
                        return
                    P.dve(lambda e, bts=bts, g=g: e.tensor_copy(out=SbS[:, g * 8:(g + 1) * 8, :], in_=PS(bts, 512).rearrange("p (s i) -> p s i", i=64)),
                          r=[kb(bts)], w=[("SbS", s_) for s_ in range(g * 8, g * 8 + 8)])
            if kind == "s" and STOP == 3:
                return
            yield "p2done"
            nu = 2 * nch
            W = nu * L
            NQv = NQ2[0:L, 0:2 * W]
            KQv = KQ2[0:L, 0:2 * W]
            NTv = NT2[0:L, 0:W]
            T32v = T32[0:L, 0:W]
            Tbv = Tb[0:L, 0:W]
            NQ4 = NQv.rearrange("p (u a l) -> p u a l", a=2, l=L)
            KQ4 = KQv.rearrange("p (u a l) -> p u a l", a=2, l=L)
            kNQ, kKQ, kNT, kT32, kTb = "NQ2", "KQ2", "NT2", "T32", "Tb"
            for p in range(2):
                P0 = 64 * p
                for (lhs, lkey, dst4, dkey, msk) in ((Bt, "Bt", NQ4, kNQ, m2b), (Kt, "Kt", KQ4, kKQ, m2k)):
                    bq = pair()
                    for c in range(nch):
                        col = c * 2 * L
                        P.pe(lambda e, P0=P0, c=c, bq=bq, lhs=lhs, col=col: e.matmul(
                            PSP(bq)[0:L, col:col + 2 * L], lhsT=lhs[P0:P0 + 64, c * L:(c + 1) * L],
                            rhs=KR[P0:P0 + 64, c * 2 * L:(c + 1) * 2 * L], start=True, stop=True),
                            r=[lkey, "KR"], w=[kb(bq), kb(bq + 1)])
                    P.dve(lambda e, bq=bq, dst4=dst4, msk=msk, p=p: e.tensor_tensor(
                        out=dst4[:, p * nch:(p + 1) * nch, :, :],
                        in0=PSP(bq)[0:L, 0:nch * 2 * L].rearrange("p (c a l) -> p c a l", a=2, l=L),
                        in1=msk[0:L, :, 0:L].unsqueeze(1).to_broadcast([L, nch, 2, L]),
                        op=ALU.mult), r=[kb(bq), kb(bq + 1), "cconst"], w=[(dkey, p)])
                    pump()
            pump()
            bT2 = pair()
            for p in range(2):
                P0 = 64 * p
                for c in range(nch):
                    u = p * nch + c
                    P.pe(lambda e, P0=P0, c=c, u=u, bT2=bT2: e.matmul(PSP(bT2)[0:L, u * L:(u + 1) * L],
                                                                     lhsT=KR[P0:P0 + 64, c * 2 * L:c * 2 * L + L],
                                                                     rhs=Bt[P0:P0 + 64, c * L:(c + 1) * L], start=True, stop=True),
                         r=["KR", "Bt"], w=[kb(bT2), kb(bT2 + 1)])
            P.dve(lambda e, bT2=bT2: e.tensor_tensor(out=NTv.rearrange("p (u l) -> p u l", l=L),
                                                     in0=PSP(bT2)[0:L, 0:W].rearrange("p (u l) -> p u l", l=L),
                                                     in1=mTl[0:L, 0:L].unsqueeze(1).to_broadcast([L, nu, L]), op=ALU.mult),
                  r=[kb(bT2), kb(bT2 + 1), "cconst"], w=[kNT])
            kNQb = [(kNQ, 0), (kNQ, 1)]
            kKQb = [(kKQ, 0), (kKQ, 1)]
            P.dve(lambda e: e.tensor_tensor(out=T32v.rearrange("p (u l) -> p u l", l=L),
                                            in0=idf[0:L, 0:L].unsqueeze(1).to_broadcast([L, nu, L]),
                                            in1=NQ4[:, :, 0, :], op=ALU.subtract), r=kNQb + CONST, w=[kT32])
            P.act(lambda e: e.copy(out=Tbv, in_=T32v), r=[kT32], w=[kTb])
            nlev = int(np.log2(L)) - 1

            def emit_sq(lastlev):
                bPT = pair()
                for u in range(nu):
                    P.pe(lambda e, u=u, bPT=bPT: e.matmul(PSP(bPT)[0:L, u * L:(u + 1) * L], lhsT=NQ4[:, u, 0, :],
                                                          rhs=NTv[:, u * L:(u + 1) * L], start=True, stop=True),
                         r=kNQb + [kNT], w=[kb(bPT), kb(bPT + 1)])
                bP = None
                if not lastlev:
                    bP = pair()
                    for u in range(nu):
                        P.pe(lambda e, u=u, bP=bP: e.matmul(PSP(bP)[0:L, u * L:(u + 1) * L], lhsT=NTv[:, u * L:(u + 1) * L],
                                                            rhs=NQ4[:, u, 0, :], start=True, stop=True),
                             r=kNQb + [kNT], w=[kb(bP), kb(bP + 1)])
                return bPT, bP

            def emit_sq_evac(bPT, bP):
                P.act(lambda e, bPT=bPT: e.copy(out=NTv, in_=PSP(bPT)[0:L, 0:W]), r=[kb(bPT), kb(bPT + 1)], w=[kNT])
                if bP is not None:
                    P.dve(lambda e, bP=bP: e.tensor_copy(out=NQ4[:, :, 0, :], in_=PSP(bP)[0:L, 0:W].rearrange("p (u l) -> p u l", l=L)),
                          r=[kb(bP), kb(bP + 1)], w=kNQb)

            bb = emit_sq(nlev == 1)
            emit_sq_evac(*bb)
            for lev in range(nlev):
                nxt = None
                if lev + 1 < nlev:
                    nxt = emit_sq(lev + 1 == nlev - 1)
                bT = pair()
                for u in range(nu):
                    P.pe(lambda e, u=u, bT=bT: e.matmul(PSP(bT)[0:L, u * L:(u + 1) * L], lhsT=NTv[:, u * L:(u + 1) * L],
                                                        rhs=Tbv[:, u * L:(u + 1) * L], start=True, stop=True),
                         r=[kNT, kTb], w=[kb(bT), kb(bT + 1)])
                if nxt is not None:
                    emit_sq_evac(*nxt)
                P.dve(lambda e, bT=bT: e.tensor_tensor(out=T32v, in0=T32v, in1=PSP(bT)[0:L, 0:W], op=ALU.add),
                      r=[kb(bT), kb(bT + 1), kT32], w=[kT32])
                P.act(lambda e: e.copy(out=Tbv, in_=T32v), r=[kT32], w=[kTb])
                pump()
            for c in range(nch):
                if kind == "p":
                    Smv, Sbv, skm, skb = Sm[:, hp, :], Sb[:, hp, :], ("Sm", hp), ("Sb", hp)
                    zero_state = first and c == 0
                    ydst_ap, ykey = ytok[:, c, hp * 128:(hp + 1) * 128], ("ytok", c)
                else:
                    Smv, Sbv, skm, skb = SmS[:, c, :], SbS[:, c, :], ("SmS", c), ("SbS", c)
                    zero_state = False
                    ydst_ap, ykey = ytok_s[0:L, c, :], "ytok_s"
                bX = bank()
                for p in range(2):
                    P0 = 64 * p
                    u = p * nch + c
                    if not zero_state:
                        P.pe(lambda e, P0=P0, bX=bX, c=c, p=p, Sbv=Sbv: e.matmul(PS(bX, 128, L)[:, p * 64:(p + 1) * 64],
                                                                       lhsT=KR[P0:P0 + 64, c * 2 * L:c * 2 * L + L], rhs=Sbv[P0:P0 + 64, :],
                                                                       start=True, stop=False), r=["KR", skb], w=[kb(bX)])
                    P.pe(lambda e, P0=P0, bX=bX, c=c, p=p, u=u, zs=zero_state, vtok_hp=vtok_hp: e.matmul(PS(bX, 128, L)[:, p * 64:(p + 1) * 64], lhsT=KQ4[:, u, 0, :],
                                                                                      rhs=vtok_hp[:, c, P0:P0 + 64], start=zs, stop=True),
                         r=kKQb + [vkey], w=[kb(bX)])
                P.act(lambda e, bX=bX: e.copy(out=XTs[0:L, :], in_=PS(bX, 128, L)), r=[kb(bX)], w=["XTs"])
                bS = bank()
                for p in range(2):
                    u = p * nch + c
                    P.pe(lambda e, bS=bS, p=p, u=u: e.matmul(PS(bS, 128, L)[:, p * 64:(p + 1) * 64], lhsT=Tbv[:, u * L:(u + 1) * L],
                                                            rhs=XTs[0:L, p * 64:(p + 1) * 64], start=True, stop=True), r=[kTb, "XTs"], w=[kb(bS)])
                P.act(lambda e, bS=bS: e.copy(out=SATs[0:L, :], in_=PS(bS, 128, L)), r=[kb(bS)], w=["SATs"])
                bY = bank()
                for p in range(2):
                    P0 = 64 * p
                    u = p * nch + c
                    if not zero_state:
                        P.pe(lambda e, P0=P0, bY=bY, c=c, p=p, Sbv=Sbv: e.matmul(PS(bY, 128, L)[:, p * 64:(p + 1) * 64],
                                                                       lhsT=KR[P0:P0 + 64, c * 2 * L + L:(c + 1) * 2 * L], rhs=Sbv[P0:P0 + 64, :],
                                                                       start=True, stop=False), r=["KR", skb], w=[kb(bY)])
                    P.pe(lambda e, P0=P0, bY=bY, c=c, p=p, u=u, zs=zero_state, vtok_hp=vtok_hp: e.matmul(PS(bY, 128, L)[:, p * 64:(p + 1) * 64], lhsT=KQ4[:, u, 1, :],
                                                                                      rhs=vtok_hp[:, c, P0:P0 + 64], start=zs, stop=False),
                         r=kKQb + [vkey], w=[kb(bY)])
                    P.pe(lambda e, bY=bY, p=p, u=u: e.matmul(PS(bY, 128, L)[:, p * 64:(p + 1) * 64], lhsT=NQ4[:, u, 1, :],
                                                            rhs=SATs[0:L, p * 64:(p + 1) * 64], start=False, stop=True),
                         r=kNQb + ["SATs"], w=[kb(bY)])
                P.dve(lambda e, bY=bY, ydst_ap=ydst_ap: e.tensor_copy(out=ydst_ap, in_=PS(bY, 128, L)),
                      r=[kb(bY)] + (["XA"] if kind == "s" else []), w=[ykey])
                bZ = bank()
                for p in range(2):
                    P0 = 64 * p
                    P.pe(lambda e, P0=P0, bZ=bZ, c=c, vtok_hp=vtok_hp, KgLtok3=KgLtok3: e.matmul(PS(bZ, 64, 64, P0), lhsT=KgLtok3[:, c, P0:P0 + 64], rhs=vtok_hp[:, c, P0:P0 + 64],
                                                               start=True, stop=False), r=["KgLtok", vkey, "XA"], w=[kb(bZ)])
                    P.pe(lambda e, P0=P0, bZ=bZ, c=c, p=p, BgLtok3=BgLtok3: e.matmul(PS(bZ, 64, 64, P0), lhsT=BgLtok3[:, c, P0:P0 + 64], rhs=SATs[0:L, p * 64:(p + 1) * 64],
                                                                   start=False, stop=True), r=kBg + ["SATs"], w=[kb(bZ)])
                if zero_state:
                    P.dve(lambda e, bZ=bZ, Smv=Smv: e.tensor_copy(out=Smv, in_=PS(bZ, 64)), r=[kb(bZ)], w=[skm])
                else:
                    P.dve(lambda e, bZ=bZ, Smv=Smv, c=c: e.scalar_tensor_tensor(out=Smv, in0=Smv, scalar=gLv[:, c:c + 1], in1=PS(bZ, 64),
                                                                                op0=ALU.mult, op1=ALU.add), r=[kb(bZ), skm, gk], w=[skm])
                P.act(lambda e, Smv=Smv, Sbv=Sbv: e.copy(out=Sbv, in_=Smv), r=[skm], w=[skb])
                pump()
            if kind == "s":
                for s_ in range(NSEQ_S):
                    P.dma("sp", "yrl", ytok[s_ * LS:(s_ + 1) * LS, 0, hp * 128:(hp + 1) * 128], ytok_s[0:LS, s_, :],
                          r=["ytok_s", "XA"] + HID_ALL, w=[("ytok", 0)])
                Sst = hid32[0:64, 2048:4096]
                for g in range(4):
                    bts = bank()
                    for s4 in range(4):
                        s_ = g * 4 + s4
                        P.pe(lambda e, bts=bts, s4=s4, s_=s_: e.transpose(out=PS(bts, 512, 64)[:, s4 * 128:(s4 + 1) * 128], in_=SmS[:, s_, :], identity=idf[:]),
                             r=[("SmS", s_)] + CONST, w=[kb(bts)])
                    P.act(lambda e, bts=bts, g=g, Sst=Sst: e.copy(out=Sst[:, g * 512:(g + 1) * 512], in_=PS(bts, 512, 64)), r=[kb(bts), "XA"], w=["Sstage"])
                for s_ in range(NSEQ_S):
                    P.dma("sp", "sst_out", wkvs[s_, 2 * hp:2 * hp + 2, :, :].rearrange("h i j -> i h j"),
                          Sst.rearrange("i (s h j) -> i s h j", s=NSEQ_S, h=2)[:, s_, :, :], r=["Sstage", "XA"] + HID_ALL, w=["o_wkvs"])


        nhp = HPS if kind == "s" else 4
        PIPE = (kind == "p")
        pstate = {"g": None, "done": True}

        def pump():
            if pstate["done"] or pstate["g"] is None:
                return
            try:
                v = next(pstate["g"])
            except StopIteration:
                pstate["done"] = True
                return
            if v == "p1done":
                pstate["done"] = True

        def run_until(g, tag):
            while True:
                try:
                    v = next(g)
                except StopIteration:
                    return False
                if v == tag:
                    return True

        gens = [hp_gen(h) for h in range(nhp)]
        alive = run_until(gens[0], "p2done")
        for h in range(nhp):
            if not alive:
                break
            nxt = gens[h + 1] if (h + 1 < nhp) else None
            if nxt is not None and PIPE:
                pstate["g"], pstate["done"] = nxt, False
            else:
                pstate["g"], pstate["done"] = None, True
            run_until(gens[h], "__end__")
            if nxt is not None:
                if PIPE and not pstate["done"]:
                    run_until(nxt, "p1done")
                pstate["done"] = True
                alive = run_until(nxt, "p2done")
        if kind == "s" and (STOP <= 4 or 30 <= STOP < 40):
            return
        bkk = inproj(("ak", 0), 128)
        bks = inproj(("aks", 0), 128)
        P.dve(lambda e: e.tensor_tensor(out=kf32[:, S_], in0=PS(bkk, NT), in1=cs_t[:, 0, S_], op=ALU.mult), r=[kb(bkk), "cs_t"], w=["t3"])
        P.dve(lambda e: e.tensor_tensor(out=dtmp[:, S_], in0=PS(bks, NT), in1=cs_t[:, 1, S_], op=ALU.mult), r=[kb(bks), "cs_t"], w=["dtmp"])
        P.dve(lambda e: e.tensor_tensor(out=kf32[:, S_], in0=kf32[:, S_], in1=dtmp[:, S_], op=ALU.add), r=["t3", "dtmp"], w=["t3"])
        P.act(lambda e: e.copy(out=kbuf[:, 128:128 + NT], in_=kf32[:, S_]), r=["t3"], w=["kbuf"])
        bv = inproj(("av", 0), 128)
        P.act(lambda e: e.copy(out=vT32[:, S_], in_=PS(bv, NT)), r=[kb(bv)], w=["t4"])
        if first or kind == "s":
            P.pool(lambda e: e.memset(Vaug[:], 1.0), w=["Vaug"])
        for b in range(NB):
            bt = bank()
            P.pe(lambda e, b=b, bt=bt: e.transpose(out=PS(bt, 128), in_=vT32[:, b * 128:(b + 1) * 128], identity=idf[:]),
                 r=["t4"] + CONST, w=[kb(bt)])
            P.act(lambda e, b=b, bt=bt: e.copy(out=Vaug[:, b + 1, :, 0:64], in_=PS(bt, 128).rearrange("p (k d) -> p k d", k=2)),
                  r=[kb(bt)], w=["Vaug"])
            if last and b == NB - 1:
                P.dve(lambda e, bt=bt: e.tensor_copy(out=vtokf[:], in_=PS(bt, 128)), r=[kb(bt)], w=["vtokf"])
                if kind == "p":
                    P.dma("pool", "o_vw", vwp, vtokf[:], r=["vtokf"], w=["o_vwp"])
                else:
                    for s in range(NSEQ_S):
                        P.dma("sp", "o_vws", vws[s, 120:128, :], vtokf[s * LS:(s + 1) * LS, :], r=["vtokf"], w=["o_vws"])
                    P.dma("sp", "o_vw2", vws[:, 0:120, :].rearrange("s r c -> s (r c)"), cv[:, 8:128, :].rearrange("s r c -> s (r c)"), w=["o_vws2"])
                bt2 = bank()
                P.pe(lambda e, b=b, bt2=bt2: e.transpose(out=PS(bt2, 128), in_=kf32[:, b * 128:(b + 1) * 128], identity=idf[:]),
                     r=["t3"] + CONST, w=[kb(bt2)])
                P.dve(lambda e, bt2=bt2: e.tensor_copy(out=yq[:, 0:128], in_=PS(bt2, 128)), r=[kb(bt2)], w=["t0"])
                if kind == "p":
                    P.dma("pool", "o_kw", kwp, yq[:, 0:128], r=["t0"], w=["o_kwp"])
                else:
                    for s in range(NSEQ_S):
                        P.dma("sp", "o_kws", kws[s, 120:128, :], yq[s * LS:(s + 1) * LS, 0:128], r=["t0"], w=["o_kws"])
                    P.dma("sp", "o_kw2", kws[:, 0:120, :].rearrange("s r c -> s (r c)"), ck[:, 8:128, :].rearrange("s r c -> s (r c)"), w=["o_kws2"])
        for c in range(4):
            bq_ = inproj(("q", c), 128)
            bqs = inproj(("qs", c), 128)
            P.dve(lambda e, bq_=bq_: e.tensor_tensor(out=yq[:, S_], in0=PS(bq_, NT), in1=cs_t[:, 0, S_], op=ALU.mult),
                  r=[kb(bq_), "cs_t", "t0"], w=["t0"])
            P.dve(lambda e, bqs=bqs: e.tensor_tensor(out=yq2[:, S_], in0=PS(bqs, NT), in1=cs_t[:, 1, S_], op=ALU.mult),
                  r=[kb(bqs), "cs_t"], w=["t1"])
            P.pool(lambda e, c=c: e.tensor_tensor(out=qT[:, c, S_], in0=yq[:, S_], in1=yq2[:, S_], op=ALU.add),
                   r=["t0", "t1"], w=[("qT", c)])
        qkeys = [("qT", c) for c in range(4)]

        def epilogue(b):
            y3 = ytok[:, b, :].rearrange("p (h d) -> p h d", d=64)
            yk = ("ytok", b)
            gs = gstat[:, 0, :]
            P.dve(lambda e, y3=y3: e.tensor_reduce(out=gstat[:, 0, :], in_=y3, axis=AX.X, op=ALU.add), r=[yk], w=["gstat"])
            P.act(lambda e, b=b: e.activation(out=yq[:, :], in_=ytok[:, b, :], func=AF.Square), r=[yk, "t0"], w=["t0"])
            P.dve(lambda e: e.tensor_reduce(out=gstat[:, 1, :], in_=yq[:, :].rearrange("p (h d) -> p h d", d=64), axis=AX.X, op=ALU.add),
                  r=["t0"], w=["gstat"])
            P.dve(lambda e: e.tensor_scalar(out=gstat[:, 0, :], in0=gstat[:, 0, :], scalar1=1.0 / 64, scalar2=None, op0=ALU.mult),
                  r=["gstat"], w=["gstat"])
            P.dve(lambda e: e.tensor_tensor(out=gstat[:, 2, :], in0=gstat[:, 0, :], in1=gstat[:, 0, :], op=ALU.mult), r=["gstat"], w=["gstat"])
            P.dve(lambda e: e.scalar_tensor_tensor(out=gstat[:, 1, :], in0=gstat[:, 1, :], scalar=1.0 / 64, in1=gstat[:, 2, :],
                                                   op0=ALU.mult, op1=ALU.subtract), r=["gstat"], w=["gstat"])
            P.act(lambda e: e.activation(out=gstat[:, 1, :], in_=gstat[:, 1, :], func=AF.Ln, bias=GN_EPS), r=["gstat"], w=["gstat"])
            P.act(lambda e: e.activation(out=gstat[:, 1, :], in_=gstat[:, 1, :], func=AF.Exp, scale=-0.5), r=["gstat"], w=["gstat"])
            yq2v = yq2[:, :].rearrange("p (h d) -> p h d", d=64)
            P.dve(lambda e, y3=y3, yq2v=yq2v: e.tensor_tensor(out=yq2v, in0=y3, in1=gstat[:, 0, :].unsqueeze(2).to_broadcast([128, 8, 64]),
                                                              op=ALU.subtract), r=[yk, "gstat", "t1"], w=["t1"])
            P.dve(lambda e, yq2v=yq2v: e.tensor_tensor(out=yq2v, in0=yq2v, in1=gstat[:, 1, :].unsqueeze(2).to_broadcast([128, 8, 64]),
                                                       op=ALU.mult), r=["gstat", "t1"], w=["t1"])
            P.dve(lambda e: e.tensor_tensor(out=yq2[:, :], in0=yq2[:, :], in1=gnw_bc[:], op=ALU.mult), r=["t1"] + CONST, w=["t1"])
            P.dve(lambda e: e.tensor_tensor(out=yq2[:, :], in0=yq2[:, :], in1=gnb_bc[:], op=ALU.add), r=["t1"] + CONST, w=["t1"])
            for hp in range(4):
                if kind == "p":
                    vt_blk = vtok[:, (hp * nch + b) * 128:(hp * nch + b + 1) * 128]
                    vkeys = [("vtok", hp)]
                else:
                    vt_blk = None
                    vkeys = []
                for p in range(2):
                    h = 2 * hp + p
                    if kind == "p":
                        P.dve(lambda e, h=h, p=p, vt_blk=vt_blk, b=b: e.scalar_tensor_tensor(
                            out=yq2[:, h * 64:(h + 1) * 64], in0=vt_blk[:, p * 64:(p + 1) * 64], scalar=rkb[:, b, h:h + 1],
                            in1=yq2[:, h * 64:(h + 1) * 64], op0=ALU.mult, op1=ALU.add),
                            r=vkeys + [("rkb", hp), "t1"], w=["t1"])
                    else:
                        P.dve(lambda e, h=h, b=b: e.scalar_tensor_tensor(
                            out=yq2[:, h * 64:(h + 1) * 64], in0=vblk_s[:, h * 64:(h + 1) * 64], scalar=rkb[:, b, h:h + 1],
                            in1=yq2[:, h * 64:(h + 1) * 64], op0=ALU.mult, op1=ALU.add),
                            r=["vblk_s", ("rkb", hp), "t1"], w=["t1"])
            bg = bank()
            P.pe(lambda e, b=b, bg=bg: e.matmul(PS(bg, 512), lhsT=sgb[:, b * 128:(b + 1) * 128], rhs=Wg_b[:], start=True, stop=True),
                 r=["sgb", "cconst"], w=[kb(bg)])
            P.dve(lambda e, b=b, bg=bg: e.tensor_tensor(out=mix[:, b, 512:1024], in0=yq2[:, :], in1=PS(bg, 512), op=ALU.mult),
                  r=["t1", kb(bg)], w=[("mixr", b)])


        def attn_group(b, kblocks, first_grp, only_grp, additive=False):
            ng = len(kblocks)
            for kv in range(2):
                P0 = 64 * kv
                for i, (kfn, vfn, msk, kkeys) in enumerate(kblocks):
                    bs = bank()
                    P.pe(lambda e, P0=P0, bs=bs, kfn=kfn, kv=kv: e.matmul(PS(bs, 512), lhsT=kfn(kv),
                                                                   rhs=qT[P0:P0 + 64, :, b * 128:(b + 1) * 128], start=True, stop=not additive),
                         r=qkeys + kkeys, w=[kb(bs)])
                    if additive:
                        P.pe(lambda e, bs=bs, msk=msk: e.matmul(PS(bs, 512), lhsT=idb[:], rhs=msk.rearrange("p a b -> p (a b)"),
                                                                start=False, stop=True), r=["idb", "cconst"], w=[kb(bs)])
                    Ei = Et[:, kv * 2 + i, :]
                    ek = ("Et", kv * 2 + i)
                    P.act(lambda e, bs=bs, Ei=Ei: e.activation(out=Ei, in_=PS(bs, 512), func=AF.Exp, scale=0.125), r=[kb(bs)], w=[ek])
                    if not additive:
                      P.pool(lambda e, Ei=Ei, msk=msk: e.tensor_tensor(out=Ei.rearrange("p (a b) -> p a b", a=4), in0=Ei.rearrange("p (a b) -> p a b", a=4), in1=msk, op=ALU.mult),
                           r=[ek, "cconst", "scconst"], w=[ek])
            bo = pair()
            for kv in range(2):
                for c4 in range(4):
                    for i, (kfn, vfn, msk, kkeys) in enumerate(kblocks):
                        P.pe(lambda e, kv=kv, c4=c4, i=i, vfn=vfn, bo=bo: e.matmul(
                            PS(bo + kv, 260)[:, c4 * 65:(c4 + 1) * 65], lhsT=Et[:, kv * 2 + i, c4 * 128:(c4 + 1) * 128], rhs=vfn(kv),
                            start=(i == 0), stop=(i == ng - 1)), r=[("Et", kv * 2 + i)] + kkeys, w=[kb(bo + kv)])
            return bo

        def attn_finish(b, src_fn, rkeys):
            for kv in range(2):
                s3 = src_fn(kv)
                P.dve(lambda e, kv=kv, s3=s3: e.tensor_tensor(out=den[:, kv * 4:(kv + 1) * 4].unsqueeze(2), in0=s3[:, :, 64:65],
                                                              in1=esink[:, kv * 4:(kv + 1) * 4].unsqueeze(2), op=ALU.add),
                      r=rkeys + ["esink"], w=[("den", kv)])
                P.dve(lambda e, kv=kv: e.reciprocal(out=den[:, kv * 4:(kv + 1) * 4], in_=den[:, kv * 4:(kv + 1) * 4]),
                      r=[("den", kv)], w=[("den", kv)])
                for c4 in range(4):
                    h = kv * 4 + c4
                    P.dve(lambda e, s3=s3, c4=c4, h=h: e.tensor_scalar(out=mix[:, b, h * 64:(h + 1) * 64], in0=s3[:, c4, 0:64],
                                                                       scalar1=den[:, h:h + 1], scalar2=None, op0=ALU.mult),
                          r=rkeys + [("den", kv)], w=[("mixa", b)])

        if kind == "p":
            for b in range(NB):
                gb = ti * NB + b
                kbl = []
                if gb > 0:
                    kbl.append((lambda kv, b=b: kbuf[64 * kv:64 * kv + 64, b * 128:(b + 1) * 128],
                                lambda kv, b=b: Vaug[:, b, kv, :], m_prev[:], ["kbuf", "Vaug"]))
                kbl.append((lambda kv, b=b: kbuf[64 * kv:64 * kv + 64, (b + 1) * 128:(b + 2) * 128],
                            lambda kv, b=b: Vaug[:, b + 1, kv, :], m_own[:], ["kbuf", "Vaug"]))
                bo = attn_group(b, kbl, True, True, additive=True)
                epilogue(b)
                attn_finish(b, lambda kv, bo=bo: PS(bo + kv, 260).rearrange("p (c d) -> p c d", d=65), [kb(bo), kb(bo + 1)])
            P.act(lambda e: e.copy(out=kbuf[:, 0:128], in_=kbuf[:, NB * 128:(NB + 1) * 128]), r=["kbuf"], w=["kbuf"])
            P.pool(lambda e: e.tensor_copy(out=Vaug[:, 0, :, :], in_=Vaug[:, NB, :, :]), r=["Vaug"], w=["Vaug"])
        else:
            kbl = [(lambda kv: kbuf[64 * kv:64 * kv + 64, 128:256], lambda kv: Vaug[:, 1, kv, :], m_sown[:], ["kbuf", "Vaug"])]
            bo = attn_group(0, kbl, True, False)
            for kv in range(2):
                P.dve(lambda e, kv=kv, bo=bo: e.tensor_copy(out=oacc[:, kv, :, :].rearrange("p c d -> p (c d)"), in_=PS(bo + kv, 260)),
                      r=[kb(bo + kv)], w=[("oacc", kv)])
            for s in range(NSEQ_S):
                i = s % 2
                P.dma("pool", "ck%d" % i, ckf[:, i, :], ck[s], w=[("ckf", i)])
                btk = bank()
                P.pe(lambda e, i=i, btk=btk: e.transpose(out=PS(btk, 128), in_=ckf[:, i, :], identity=idf[:]), r=[("ckf", i)] + CONST, w=[kb(btk)])
                P.act(lambda e, btk=btk: e.copy(out=ckT[:], in_=PS(btk, 128)), r=[kb(btk)], w=["ckT"])
                P.dma("pool", "cv%d" % i, ckf[:, i, :], cv[s], r=[], w=[("ckf", i)])
                P.pool(lambda e, i=i: e.memset(cVaug[:, i, :, 64:65], 1.0), w=[("cVaug", i)])
                P.act(lambda e, i=i: e.copy(out=cVaug[:, i, :, 0:64], in_=ckf[:, i, :].rearrange("p (k d) -> p k d", k=2)),
                      r=[("ckf", i)], w=[("cVaug", i)])
                kbl = [(lambda kv: ckT[64 * kv:64 * kv + 64, :], lambda kv, i=i: cVaug[:, i, kv, :],
                        m_scache[:, s:s + 1, :].to_broadcast([128, 4, 128]), ["ckT", ("cVaug", i)])]
                bo = attn_group(0, kbl, False, False)
                for kv in range(2):
                    P.dve(lambda e, kv=kv, bo=bo: e.tensor_tensor(out=oacc[:, kv, :, :].rearrange("p c d -> p (c d)"),
                                                                  in0=oacc[:, kv, :, :].rearrange("p c d -> p (c d)"),
                                                                  in1=PS(bo + kv, 260), op=ALU.add),
                          r=[kb(bo + kv), ("oacc", kv)], w=[("oacc", kv)])
            attn_finish(0, lambda kv: oacc[:, kv, :, :], [("oacc", 0), ("oacc", 1)])
            epilogue(0)

        if kind == "s" and STOP <= 5:
            return
        if kind == "s" and STOP <= 6:
            return
        for b in range(NB):
            bk = bank()
            for c in range(8):
                P.pe(lambda e, c=c, bk=bk, b=b: e.transpose(out=PSB(bk)[:, c * 128:(c + 1) * 128], in_=mix[:, b, c * 128:(c + 1) * 128],
                                                            identity=idb[:]), r=[("mixa", b), ("mixr", b), "idb"], w=[kb(bk)])
            P.act(lambda e, b=b, bk=bk: e.copy(out=bufA[:, :, b * 128:(b + 1) * 128], in_=PSB(bk).rearrange("p (c n) -> p c n", c=8)),
                  r=[kb(bk)], w=[("bufA", b)])
        nb[0] = 0
        for c in range(8):
            s = stream(wsc_out[c], ("wsc_out", c))
            for b in range(NB):
                for hf in range(2):
                    P.pe(lambda e, c=c, s=s, b=b, hf=hf: e.matmul(PS(2 * b + hf, 512), lhsT=bufA[:, c, b * 128:(b + 1) * 128],
                                                                 rhs=ring[:, s, hf * 512:(hf + 1) * 512], start=(c == 0), stop=(c == 7)),
                         r=[("ring", s), ("bufA", b)], w=[kb(2 * b + hf)])

        def ost(b):
            return hid32[:, 1536 + b * 1024:1536 + (b + 1) * 1024]

        def ostk(b):
            return [("hidT", fc_) for fc_ in range(6 + 4 * b, 10 + 4 * b)]

        def norm_res(b, src_pair, gbc, dst, dkey, xkey_r):
            sc = sstat[:, 4 + b:5 + b]
            pk = [kb(2 * b), kb(2 * b + 1)]
            P.act(lambda e: e.activation(out=junk[:], in_=src_pair, func=AF.Square, accum_out=sc), r=pk, w=["junk", ("ss2", b)])
            P.act(lambda e: e.activation(out=sc, in_=sc, func=AF.Ln, scale=1.0 / D, bias=RMS_EPS), r=[("ss2", b)], w=[("ss2", b)])
            P.act(lambda e: e.activation(out=sc, in_=sc, func=AF.Exp, scale=-0.5), r=[("ss2", b)], w=[("ss2", b)])
            tmp = ost(b)
            P.dve(lambda e: e.scalar_tensor_tensor(out=tmp, in0=src_pair, scalar=sc, in1=gbc[:], op0=ALU.mult, op1=ALU.mult),
                  r=pk + [("ss2", b)] + CONST, w=ostk(b))
            P.pool(lambda e: e.tensor_tensor(out=dst, in0=tmp, in1=x_t[:, b, :], op=ALU.add), r=ostk(b) + [xkey_r], w=dkey)

        for b in range(NB):
            norm_res(b, PSP(2 * b), gpost_bc, x_t[:, b, :], [("x", b)], ("x", b))
        for b in range(NB):
            sc = sstat[:, 8 + b:9 + b]
            P.act(lambda e, b=b, sc=sc: e.activation(out=junk[:], in_=x_t[:, b, :], func=AF.Square, accum_out=sc),
                  r=[("x", b)], w=["junk", ("ss3", b)])
            P.act(lambda e, sc=sc: e.activation(out=sc, in_=sc, func=AF.Ln, scale=1.0 / D, bias=RMS_EPS), r=[("ss3", b)], w=[("ss3", b)])
            P.act(lambda e, sc=sc: e.activation(out=sc, in_=sc, func=AF.Exp, scale=-0.5), r=[("ss3", b)], w=[("ss3", b)])
        for b in range(NB):
            sc = sstat[:, 8 + b:9 + b]
            P.dve(lambda e, b=b, sc=sc: e.tensor_scalar(out=xn[:], in0=x_t[:, b, :], scalar1=sc, scalar2=None, op0=ALU.mult),
                  r=[("x", b), ("ss3", b)], w=["xn"])
            bk = (2 * b) % 8
            for c in range(8):
                P.pe(lambda e, c=c, bk=bk: e.transpose(out=PSB(bk)[:, c * 128:(c + 1) * 128], in_=xn[:, c * 128:(c + 1) * 128],
                                                      identity=idb[:]), r=["xn", "idb"], w=[kb(bk)])
            P.act(lambda e, b=b, bk=bk: e.copy(out=bufA[:, :, b * 128:(b + 1) * 128], in_=PSB(bk).rearrange("p (c n) -> p c n", c=8)),
                  r=[kb(bk)], w=[("bufA", b)])
        nb[0] = 0

        if kind == "s" and STOP <= 7:
            return
        for fc in range(NFC):
            sz = stream(wsc_fin[2 * fc], ("wsc_fin", 2 * fc))
            su = stream(wsc_fin[2 * fc + 1], ("wsc_fin", 2 * fc + 1))
            bz = bank()
            bu = bank()
            for (s, bk_) in ((sz, bz), (su, bu)):
                for c in range(8):
                    P.pe(lambda e, c=c, s=s, bk_=bk_: e.matmul(PS(bk_, NT), lhsT=ring[:, s, c * 128:(c + 1) * 128], rhs=bufA[:, c, S_],
                                                               start=(c == 0), stop=(c == 7)), r=[("ring", s)] + bufA_keys, w=[kb(bk_)])
            i = 0
            zb = zbuf[:, i, 0:nseq * (Lseq + 2)].rearrange("p (s l) -> p s l", s=nseq)
            zk = ("zbuf", i)
            P.act(lambda e, bz=bz, zb=zb: e.copy(out=zb[:, :, 2:Lseq + 2], in_=PS(bz, NT).rearrange("p (s l) -> p s l", s=nseq)),
                  r=[kb(bz)], w=[zk])
            P.pool(lambda e, zb=zb, fc=fc: e.tensor_copy(out=zb[:, :, 0:2], in_=zcar[:, fc, :, :]), r=[("zcar", fc)], w=[zk])
            P.pool(lambda e, zb=zb, fc=fc: e.tensor_copy(out=zcar[:, fc, :, :], in_=zb[:, :, Lseq:Lseq + 2]), r=[zk], w=[("zcar", fc)])
            a3 = za[:, i, S_].rearrange("p (s l) -> p s l", s=nseq)
            ak = ("za", i)
            P.act(lambda e, bz=bz, a3=a3, fc=fc: e.activation(out=a3, in_=PS(bz, NT).rearrange("p (s l) -> p s l", s=nseq), func=AF.Identity,
                                                              scale=cwT[:, 2, fc:fc + 1], bias=cbT[:, fc:fc + 1]),
                  r=[kb(bz)] + CONST, w=[ak])
            P.dve(lambda e, zb=zb, a3=a3, fc=fc: e.scalar_tensor_tensor(out=a3, in0=zb[:, :, 1:Lseq + 1], scalar=cwT[:, 1, fc:fc + 1], in1=a3,
                                                                        op0=ALU.mult, op1=ALU.add), r=[zk, ak], w=[ak])
            P.dve(lambda e, zb=zb, a3=a3, fc=fc: e.scalar_tensor_tensor(out=a3, in0=zb[:, :, 0:Lseq], scalar=cwT[:, 0, fc:fc + 1], in1=a3,
                                                                        op0=ALU.mult, op1=ALU.add), r=[zk, ak], w=[ak])
            P.act(lambda e, i=i: e.activation(out=za[:, i, S_], in_=za[:, i, S_], func=AF.Silu), r=[ak], w=[ak])
            P.dve(lambda e, i=i, bu=bu, fc=fc: e.tensor_tensor(out=hidT[:, fc, S_], in0=za[:, i, S_], in1=PS(bu, NT), op=ALU.mult),
                  r=[ak, kb(bu)], w=[("hidT", fc)])
        nb[0] = 0
        for fc in range(NFC):
            s = stream(wsc_fout[fc], ("wsc_fout", fc))
            for b in range(NB):
                for hf in range(2):
                    P.pe(lambda e, fc=fc, s=s, b=b, hf=hf: e.matmul(PS(2 * b + hf, 512), lhsT=hidT[:, fc, b * 128:(b + 1) * 128],
                                                                   rhs=ring[:, s, hf * 512:(hf + 1) * 512], start=(fc == 0), stop=(fc == NFC - 1)),
                         r=[("ring", s), ("hidT", fc)], w=[kb(2 * b + hf)])
        for b in range(NB):
            oi = 0
            state["ost"] += 1
            norm_res(b, PSP(2 * b), gpostf_bc, ost(b), ostk(b), ("x", b))
        for b in range(NB):
            P.dma("pool", "oy%d" % b, ydst[t0 + b * 128:t0 + (b + 1) * 128, :] if kind == "p" else ydst, ost(b),
                  r=ostk(b), w=["o_y"])
        nb[0] = 0

        if last:
            emit_state_outputs(kind, nseq)

    def emit_state_outputs(kind, nseq):
        hcar = hcar_p if kind == "p" else hcar_s
        zcar = zcar_p if kind == "p" else zcar_s
        for rc in range(14):
            np_ = RC_NP.get(rc, 128)
            bt = bank()
            P.pe(lambda e, rc=rc, np_=np_, bt=bt: e.transpose(out=PS(bt, np_, nseq), in_=hcar[0:np_, rc, :], identity=idf[0:np_, 0:np_]),
                 r=[("hcar", rc)] + CONST, w=[kb(bt)])
            c0 = rc * 128 if rc < 13 else 1600
            P.act(lambda e, bt=bt, np_=np_, c0=c0: e.copy(out=rowbuf[0:nseq, c0:c0 + np_], in_=PS(bt, np_, nseq)), r=[kb(bt)], w=["rowbuf", "XA"] + (HID_ALL if rc == 0 else []))
        P.dma("pool", "o_sh", shp if kind == "p" else shs, rowbuf[0:nseq, 0:DSH], r=["rowbuf"] + HID_ALL, w=["o_sh" + kind, "XA"])
        for fc in range(NFC):
            bt = bank()
            P.pe(lambda e, fc=fc, bt=bt: e.transpose(out=PS(bt, 128, 2 * nseq), in_=zcar[:, fc, :, :].rearrange("p s j -> p (s j)"),
                                                     identity=idf[:]), r=[("zcar", fc)] + CONST, w=[kb(bt)])
            P.act(lambda e, fc=fc, bt=bt: e.copy(out=sst[0:2 * nseq, fc * 128:(fc + 1) * 128], in_=PS(bt, 128, 2 * nseq)), r=[kb(bt)], w=["sst", "XA"] + (HID_ALL if fc == 0 else []))
        P.dma("pool", "o_cv", convp if kind == "p" else convs, sst[0:2 * nseq, :], r=["sst"] + HID_ALL, w=["o_cv" + kind, "XA"])
        if kind == "p":
            for hp in range(4):
                bt = bank()
                P.pe(lambda e, hp=hp, bt=bt: e.transpose(out=PS(bt, 128, 64), in_=Sm[:, hp, :], identity=idf[:]),
                     r=[("Sm", hp)] + CONST, w=[kb(bt)])
                i = hp % 2
                P.act(lambda e, bt=bt, i=i: e.copy(out=Sld[:, i, :], in_=PS(bt, 128, 64)), r=[kb(bt)], w=["t5"])
                P.dma("pool", "o_wk%d" % i, wkvp[2 * hp:2 * hp + 2].rearrange("h i j -> i h j"),
                      Sld[:, i, :].rearrange("p (h j) -> p h j", h=2), r=["t5"], w=["o_wkvp"])

    SmS = T("SmS", [128, NSEQ_S, 64])
    SbS = T("SbS", [128, NSEQ_S, 64], BF16)

    P.pool(lambda e: e.memset(hcar_p[:], 0.0), w=[("hcar", rc) for rc in range(14)])
    P.pool(lambda e: e.memset(zcar_p[:], 0.0), w=[("zcar", fc) for fc in range(NFC)])
    P.pool(lambda e: e.memset(kbuf[:], 0.0), w=["kbuf"])

    for ti in range(N_TILES):
        emit_tile("p", ti)

    if DO_SAMPLE:
        P.dma("sp", "clss", scanm_s[:], c_scanm_s, w=["scconst"])
        cast_const(m_sown[:], c_mask_sown, 128, 128, bcast4=True)
        for g in range(4):
            cast_const(m_scache[:, 4 * g:4 * g + 4, :].rearrange("p a b -> p (a b)"), c_mask_scache[:, 4 * g:4 * g + 4, :].rearrange("p a b -> p (a b)"), 128, 512)
        P.ops["dve"][-1].deps.add(P.last_w["scconst"])
        P.dma("pool", "hst", rowbuf[0:NSEQ_S, :], sshift, r=[], w=["rowbuf", "XA"] + HID_ALL)
        for rc in range(14):
            np_ = RC_NP.get(rc, 128)
            c0 = rc * 128 if rc < 13 else 1600
            bt = bank()
            P.pe(lambda e, bt=bt, np_=np_, c0=c0: e.transpose(out=PS(bt, NSEQ_S, np_), in_=rowbuf[0:NSEQ_S, c0:c0 + np_], identity=idf[0:NSEQ_S, 0:NSEQ_S]),
                 r=["rowbuf"] + CONST, w=[kb(bt), "XA"])
            P.act(lambda e, bt=bt, np_=np_, rc=rc: e.copy(out=hcar_s[0:np_, rc, :], in_=PS(bt, NSEQ_S, np_)), r=[kb(bt)], w=[("hcar", rc)])
        P.dma("pool", "hst2", sst[0:2 * NSEQ_S, :], sconv, r=[], w=["sst", "XA"] + HID_ALL)
        for fc in range(NFC):
            bt = bank()
            P.pe(lambda e, bt=bt, fc=fc: e.transpose(out=PS(bt, 2 * NSEQ_S, 128), in_=sst[0:2 * NSEQ_S, fc * 128:(fc + 1) * 128],
                                                     identity=idf[0:2 * NSEQ_S, 0:2 * NSEQ_S]), r=["sst"] + CONST, w=[kb(bt), "XA"])
            P.act(lambda e, bt=bt, fc=fc: e.copy(out=zcar_s[:, fc, :, :].rearrange("p s j -> p (s j)"), in_=PS(bt, 2 * NSEQ_S, 128)),
                  r=[kb(bt)], w=[("zcar", fc)])
        emit_tile("s", 0)

    okeys = [k for k in P.last_w.keys() if isinstance(k, str) and k.startswith("o_")]
    P.add("sp", lambda e: None, reads=okeys)
    P.add("pool", lambda e: None, reads=okeys)
    P.emit()
    return nc, st


def _consts():
    c = {}
    c["c_ident"] = np.eye(128, dtype=np.float32)
    half = 32
    inv = (np.float32(10000.0) ** (-np.arange(half, dtype=np.float32) / np.float32(half))).astype(np.float32)
    p = np.arange(128)
    f = (p % 64) % 32
    sign = np.where((p % 64) < 32, -1.0, 1.0).astype(np.float32)

    def tabs(pos):
        ang = pos.astype(np.float32)[None, :] * inv[f][:, None]
        return np.cos(ang).astype(np.float32), (np.sin(ang).astype(np.float32) * sign[:, None]).astype(np.float32)

    c["c_cos_p"], c["c_sin_p"] = tabs(np.arange(SEQ))
    pos_s = 16384 + (np.arange(128) % LS)
    c["c_cos_s"], c["c_sin_s"] = tabs(pos_s)
    s = np.arange(128)[:, None]
    q = np.arange(128)[None, :]
    c["c_mask_own"] = (s <= q).astype(np.float32)
    c["c_mask_prev"] = (s >= q).astype(np.float32)
    c["c_negmask_own"] = np.where(s <= q, 0.0, -30000.0).astype(np.float32)
    c["c_negmask_prev"] = np.where(s >= q, 0.0, -30000.0).astype(np.float32)
    c["c_mask_sown"] = ((s // LS == q // LS) & (s <= q)).astype(np.float32)
    msc = np.zeros((128, NSEQ_S, 128), np.float32)
    for sq in range(NSEQ_S):
        msc[:, sq, :] = ((q // LS == sq) & (s >= (q % LS))).astype(np.float32)
    c["c_mask_scache"] = msc
    bo = np.zeros((128, 128), np.float32)
    bo[:64, :64] = 1
    bo[64:, 64:] = 1
    c["c_blockones"] = bo
    strict = (s < q).astype(np.float32)
    incl = (s <= q).astype(np.float32)
    c["c_m2b"] = np.stack([strict, -incl], axis=1).astype(np.float32)
    c["c_m2k"] = np.stack([strict, incl], axis=1).astype(np.float32)
    c["c_mT"] = (q < s).astype(np.float32)
    mp = np.ones((128, 512), np.float32)
    mp[:, 0::128] = 0
    c["c_scanm_p"] = mp
    ms = np.ones((128, 128), np.float32)
    ms[:, 0::LS] = 0
    c["c_scanm_s"] = ms
    return c


_CACHE = {}


def kernel(**inputs):
    f = lambda a: np.ascontiguousarray(np.asarray(a, dtype=np.float32))
    if "nc" not in _CACHE:
        _CACHE["nc"] = build_program()
        _CACHE["consts"] = _consts()
    nc, _st = _CACHE["nc"]
    consts = _CACHE["consts"]
    x_prompt = f(inputs["x_prompt"])
    x_sample = f(inputs["x_sample"])
    wnames = ["g_pre_mix", "w_in", "attn_sinks", "mu_shift", "w0", "w_decay_up", "a0", "w_a_up", "w_g_up", "k_k", "k_a", "r_k",
              "gn_w", "gn_b", "w_out", "g_post_mix", "g_pre_ffn", "w_ffn_in", "conv_w", "conv_b", "w_ffn_out", "g_post_ffn"]
    shared = {}
    for n in wnames:
        a = f(inputs[n])[0]
        if n == "r_k":
            a = a.reshape(512)
        shared[n] = np.ascontiguousarray(a)
    shared.update(consts)
    in_maps = []
    for c in range(8):
        m = dict(shared)
        m["xp"] = x_prompt[c % 4]
        sl = slice(c * NSEQ_S, (c + 1) * NSEQ_S)
        m["xs"] = np.ascontiguousarray(x_sample[sl].reshape(128, D))
        m["ck"] = np.ascontiguousarray(f(inputs["cache_k_win"])[0, sl].reshape(NSEQ_S, 128, 128))
        m["cv"] = np.ascontiguousarray(f(inputs["cache_v_win"])[0, sl].reshape(NSEQ_S, 128, 128))
        m["sshift"] = np.ascontiguousarray(f(inputs["state_shift"])[0, sl])
        m["swkv"] = np.ascontiguousarray(f(inputs["state_wkv"])[0, sl])
        m["sconv"] = np.ascontiguousarray(f(inputs["state_conv"])[0, sl].reshape(NSEQ_S * 2, DFF))
        in_maps.append(m)
    ncores = int(os.environ.get("MK_CORES", "8"))
    res = run_bass_kernel_spmd(nc, in_maps[:ncores], core_ids=list(range(ncores)))
    R = list(res.results) + [res.results[0]] * (8 - ncores)
    cat = lambda k, rng: np.stack([R[c][k] for c in rng], axis=0)
    y_prompt = cat("yp", range(4)).reshape(4, SEQ, D)
    y_sample = np.concatenate([R[c]["ys"].reshape(NSEQ_S, LS, D) for c in range(8)], axis=0)
    nkp = cat("kwp", range(4)).reshape(1, 4, 128, 2, 64)
    nvp = cat("vwp", range(4)).reshape(1, 4, 128, 2, 64)
    nsp = cat("shp", range(4)).reshape(1, 4, DSH)
    nwp = cat("wkvp", range(4)).reshape(1, 4, 8, 64, 64)
    ncp = cat("convp", range(4)).reshape(1, 4, 2, DFF)
    nks = np.concatenate([R[c]["kws"] for c in range(8)], axis=0).reshape(1, 128, 128, 2, 64)
    nvs = np.concatenate([R[c]["vws"] for c in range(8)], axis=0).reshape(1, 128, 128, 2, 64)
    nss = np.concatenate([R[c]["shs"] for c in range(8)], axis=0).reshape(1, 128, DSH)
    nws = np.concatenate([R[c]["wkvs"] for c in range(8)], axis=0).reshape(1, 128, 8, 64, 64)
    ncs = np.concatenate([R[c]["convs"].reshape(NSEQ_S, 2, DFF) for c in range(8)], axis=0).reshape(1, 128, 2, DFF)
    outs = (y_prompt, y_sample, nkp, nvp, nsp, nwp, ncp, nks, nvs, nss, nws, ncs)
    return tuple(np.ascontiguousarray(o.astype(np.float32)) for o in outs)
```
